# Optimizing a Trainium2 kernel written in Bass

```python
import jax
import jax.numpy as jnp
from jax import lax

D_MODEL = 2048
BATCH = 8
SEQ = 4096
DEPTH = 4

GRID_W = 64
CTX_LEN = 256
F32 = jnp.float32

FOURIER_W = D_MODEL // 4
FOURIER_GROUPS = 4
FOURIER_GW = FOURIER_W // FOURIER_GROUPS

RWKV_W = D_MODEL // 4
RWKV_HEAD = 64
RWKV_HEADS = RWKV_W // RWKV_HEAD
DECAY_RANK = 64
ICLR_RANK = 64
GATE_RANK = 128
LNX_EPS = 64e-5
NORM_EPS = 1e-12

ATTN_W = D_MODEL // 2
ATTN_HEAD_DIM = 128
ATTN_Q_HEADS = ATTN_W // ATTN_HEAD_DIM
ATTN_KV_HEADS = 2
ATTN_GROUP = ATTN_Q_HEADS // ATTN_KV_HEADS
KV_W = ATTN_KV_HEADS * ATTN_HEAD_DIM
WINDOW = 128
ATTN_BLOCK = 128
ATTN_SCALE = ATTN_HEAD_DIM ** -0.5
ROPE_THETA = 10000.0
ROPE_PAIRS = ATTN_HEAD_DIM // 4
NEG_INF = -1e30

D_FF = 11 * D_MODEL // 4
RMS_EPS = 1e-6

OFF_F = 0
OFF_Q = OFF_F + FOURIER_W
OFF_R = OFF_Q + ATTN_W
OFF_G = OFF_R + RWKV_W
OFF_K = OFF_G + GATE_RANK
OFF_V = OFF_K + RWKV_W
OFF_WD = OFF_V + RWKV_W
OFF_AD = OFF_WD + 2 * DECAY_RANK
OFF_AK = OFF_AD + 2 * ICLR_RANK
OFF_AV = OFF_AK + KV_W
IN_W = OFF_AV + KV_W
RWKV_SHIFT_W = OFF_AK - OFF_R

kernel_name = 'hymba_fnet_rwkv7_swa_flow_block'


def rms_norm(x, g):
    xf = x.astype(F32)
    y = xf * lax.rsqrt(jnp.mean(xf * xf, axis=-1, keepdims=True) + RMS_EPS)
    return (y * g.astype(F32)).astype(x.dtype)


def modulate(h, shift, scale):
    return h * (1.0 + scale) + shift


def neighbours(t):
    zero = jnp.zeros_like(t[:, :1])
    return (jnp.concatenate([zero, t[:, :-1]], axis=1), jnp.concatenate([t[:, 1:], zero], axis=1))


def shift_mix(t, mu):
    prev, nxt = neighbours(t)
    return t + mu[0] * (prev - t) + mu[1] * (nxt - t)


def dwconv3(t, w, b):
    prev, nxt = neighbours(t)
    return prev * w[0] + t * w[1] + nxt * w[2] + b


def axial_rope_tables(n_tokens):
    rows = n_tokens // GRID_W
    row = jnp.repeat(jnp.arange(rows, dtype=F32), GRID_W)
    col = jnp.tile(jnp.arange(GRID_W, dtype=F32), rows)
    inv_freq = ROPE_THETA ** (-jnp.arange(ROPE_PAIRS, dtype=F32) / ROPE_PAIRS)
    ang = jnp.stack([row[:, None] * inv_freq, col[:, None] * inv_freq], axis=1)
    return jnp.cos(ang), jnp.sin(ang)


def apply_axial_rope(t, cos, sin):
    tf = t.astype(F32).reshape(t.shape[:-1] + (2, 2, ROPE_PAIRS))
    a, b = tf[..., 0, :], tf[..., 1, :]
    cs, sn = cos[None, :, None], sin[None, :, None]
    out = jnp.stack([a * cs - b * sn, a * sn + b * cs], axis=-2)
    return out.reshape(t.shape).astype(t.dtype)


def fourier_mix(t):
    B, L, _ = t.shape
    z = t.astype(F32).reshape(B, L, FOURIER_GROUPS, FOURIER_GW).transpose(0, 2, 1, 3)
    y = jnp.fft.fft2(z, norm='ortho').real
    return y.transpose(0, 2, 1, 3).reshape(B, L, FOURIER_W)


def rwkv_heads(t):
    return t.astype(F32).reshape(t.shape[:2] + (RWKV_HEADS, RWKV_HEAD))


def rwkv_state_inputs(k, v, wd, ad, w0, w2, a0, a2, kk_scale, ka):
    B, L = k.shape[:2]
    hshape = (B, L, 2, RWKV_HEADS, RWKV_HEAD)
    k = rwkv_heads(k)
    v = rwkv_heads(v)
    wd = jnp.tanh(wd.astype(F32).reshape(B, L, 2, DECAY_RANK))
    log_w = -jax.nn.softplus(-(w0.astype(F32) + jnp.einsum('bldr,drc->bldc', wd, w2.astype(F32)))) - 0.5
    decay = jnp.exp(-jnp.exp(log_w)).reshape(hshape)
    ad = ad.astype(F32).reshape(B, L, 2, ICLR_RANK)
    a = jax.nn.sigmoid(a0.astype(F32) + jnp.einsum('bldr,drc->bldc', ad, a2.astype(F32))).reshape(hshape)
    kk = k * kk_scale.astype(F32).reshape(RWKV_HEADS, RWKV_HEAD)
    kk = kk / jnp.maximum(jnp.sqrt(jnp.sum(kk * kk, axis=-1, keepdims=True)), NORM_EPS)
    k_dir = k[:, :, None] * (1.0 + (a - 1.0) * ka.astype(F32).reshape(RWKV_HEADS, RWKV_HEAD))
    return decay, k_dir, v, kk, kk[:, :, None] * a


def to_scan(t):
    t = jnp.stack([t[:, :, 0], jnp.flip(t[:, :, 1], axis=1)], axis=0)
    return jnp.moveaxis(t, 2, 0)


def from_scan(ys):
    ys = jnp.moveaxis(ys, 0, 2)
    return ys[0] + jnp.flip(ys[1], axis=1)


def wkv7_scan(s0, decay, k_dir, v, kk, b, r):
    B, L = v.shape[:2]
    full = (B, L, 2, RWKV_HEADS, RWKV_HEAD)
    xs = (to_scan(decay), to_scan(k_dir), to_scan(jnp.broadcast_to(v[:, :, None], full)),
          to_scan(jnp.broadcast_to(kk[:, :, None], full)), to_scan(b))
    if r is not None:
        xs = xs + (to_scan(jnp.broadcast_to(r[:, :, None], full)),)

    def step(s, inp):
        w_t, k_t, v_t, kk_t, b_t = inp[:5]
        sa = jnp.einsum('dbhij,dbhj->dbhi', s, kk_t)
        s = s * w_t[..., None, :] - sa[..., :, None] * b_t[..., None, :] + v_t[..., :, None] * k_t[..., None, :]
        if r is None:
            return s, None
        return s, jnp.einsum('dbhij,dbhj->dbhi', s, inp[5])

    return lax.scan(step, s0, xs)


def rwkv_output(y, r, k_dir, v, gd, g2, rk, lnx_w, lnx_b):
    B, L = y.shape[:2]
    mean = jnp.mean(y, axis=-1, keepdims=True)
    var = jnp.mean(jnp.square(y - mean), axis=-1, keepdims=True)
    yn = ((y - mean) * lax.rsqrt(var + LNX_EPS)).reshape(B, L, RWKV_W) * lnx_w.astype(F32) + lnx_b.astype(F32)
    bonus = jnp.sum(r[:, :, None] * k_dir * rk.astype(F32), axis=(2, 4))[..., None] * v
    g = jax.nn.sigmoid(gd.astype(F32)) @ g2.astype(F32)
    return (yn + bonus.reshape(B, L, RWKV_W)) * g


def window_attention(q, k, v, k_ctx, v_ctx, sink):
    B, L = q.shape[:2]
    C = k_ctx.shape[1]
    nb = L // ATTN_BLOCK
    nk = 3 * ATTN_BLOCK
    qb = q.reshape(B, nb, ATTN_BLOCK, ATTN_KV_HEADS, ATTN_GROUP, ATTN_HEAD_DIM)

    def band(t):
        tp = jnp.pad(t, ((0, 0), (ATTN_BLOCK, ATTN_BLOCK), (0, 0), (0, 0)))
        tp = tp.reshape(B, nb + 2, ATTN_BLOCK, ATTN_KV_HEADS, ATTN_HEAD_DIM)
        return jnp.concatenate([tp[:, :-2], tp[:, 1:-1], tp[:, 2:]], axis=2)

    kb, vb = band(k), band(v)
    s_loc = jnp.einsum('bnqhgd,bnkhd->bnhgqk', qb, kb).astype(F32) * ATTN_SCALE
    blk = jnp.arange(nb)[:, None, None] * ATTN_BLOCK
    qpos = blk + jnp.arange(ATTN_BLOCK)[None, :, None]
    kpos = blk - ATTN_BLOCK + jnp.arange(nk)[None, None, :]
    valid = (jnp.abs(kpos - qpos) <= WINDOW) & (kpos >= 0) & (kpos < L)
    s_loc = jnp.where(valid[None, :, None, None], s_loc, NEG_INF)
    s_ctx = jnp.einsum('bnqhgd,bchd->bnhgqc', qb, k_ctx).astype(F32) * ATTN_SCALE
    s_sink = jnp.broadcast_to(sink.astype(F32).reshape(ATTN_KV_HEADS, ATTN_GROUP, 1, 1), s_loc.shape[:-1] + (1,))
    p = jax.nn.softmax(jnp.concatenate([s_loc, s_ctx, s_sink], axis=-1), axis=-1).astype(v.dtype)
    o = (jnp.einsum('bnhgqk,bnkhd->bnqhgd', p[..., :nk], vb)
         + jnp.einsum('bnhgqc,bchd->bnqhgd', p[..., nk:nk + C], v_ctx))
    return o.reshape(B, L, ATTN_W)


def context_attention(q_ctx, k_ctx, v_ctx, sink):
    B, C = q_ctx.shape[:2]
    qg = q_ctx.reshape(B, C, ATTN_KV_HEADS, ATTN_GROUP, ATTN_HEAD_DIM)
    s = jnp.einsum('bqhgd,bkhd->bhgqk', qg, k_ctx).astype(F32) * ATTN_SCALE
    s_sink = jnp.broadcast_to(sink.astype(F32).reshape(ATTN_KV_HEADS, ATTN_GROUP, 1, 1), s.shape[:-1] + (1,))
    p = jax.nn.softmax(jnp.concatenate([s, s_sink], axis=-1), axis=-1).astype(v_ctx.dtype)
    o = jnp.einsum('bhgqk,bkhd->bqhgd', p[..., :C], v_ctx)
    return o.reshape(B, C, ATTN_W)


def token_mixer(u, uc, w_in, w_out, mu, w0, w2, a0, a2, g2, kk_scale, ka, rk, lnx_w, lnx_b,
                sink, f_g, a_g, cos, sin, ctx_out):
    B, L, _ = u.shape
    C = uc.shape[1]
    dt = u.dtype
    p = u @ w_in
    ps = shift_mix(p[..., OFF_R:OFF_AK], mu)
    base = 0 if ctx_out else OFF_K
    pc = uc @ w_in[:, base:]
    sbase = max(base, OFF_R)
    psc = shift_mix(pc[..., sbase - base:OFF_AK - base], mu[:, sbase - OFF_R:])

    def cols(t, origin, lo, hi):
        return t[..., lo - origin:hi - origin]

    lat = rwkv_state_inputs(cols(ps, OFF_R, OFF_K, OFF_V), cols(ps, OFF_R, OFF_V, OFF_WD),
                            cols(ps, OFF_R, OFF_WD, OFF_AD), cols(ps, OFF_R, OFF_AD, OFF_AK),
                            w0, w2, a0, a2, kk_scale, ka)
    cst = rwkv_state_inputs(cols(psc, sbase, OFF_K, OFF_V), cols(psc, sbase, OFF_V, OFF_WD),
                            cols(psc, sbase, OFF_WD, OFF_AD), cols(psc, sbase, OFF_AD, OFF_AK),
                            w0, w2, a0, a2, kk_scale, ka)
    r = rwkv_heads(cols(ps, OFF_R, OFF_R, OFF_G))
    r_c = rwkv_heads(cols(psc, sbase, OFF_R, OFF_G)) if ctx_out else None
    s_init = jnp.zeros((2, B, RWKV_HEADS, RWKV_HEAD, RWKV_HEAD), F32)
    s_ctx, ys_c = wkv7_scan(s_init, cst[0], cst[1], cst[2], cst[3], cst[4], r_c)
    _, ys = wkv7_scan(s_ctx, lat[0], lat[1], lat[2], lat[3], lat[4], r)
    rw = rwkv_output(from_scan(ys), r, lat[1], lat[2], cols(ps, OFF_R, OFF_G, OFF_K), g2, rk, lnx_w, lnx_b)

    q = apply_axial_rope(cols(p, 0, OFF_Q, OFF_R).reshape(B, L, ATTN_Q_HEADS, ATTN_HEAD_DIM), cos, sin)
    k = apply_axial_rope(cols(p, 0, OFF_AK, OFF_AV).reshape(B, L, ATTN_KV_HEADS, ATTN_HEAD_DIM), cos, sin)
    v = cols(p, 0, OFF_AV, IN_W).reshape(B, L, ATTN_KV_HEADS, ATTN_HEAD_DIM)
    k_c = cols(pc, base, OFF_AK, OFF_AV).reshape(B, C, ATTN_KV_HEADS, ATTN_HEAD_DIM)
    v_c = cols(pc, base, OFF_AV, IN_W).reshape(B, C, ATTN_KV_HEADS, ATTN_HEAD_DIM)
    at = window_attention(q, k, v, k_c, v_c, sink)

    fo = fourier_mix(cols(p, 0, OFF_F, OFF_Q))

    y = jnp.concatenate([rms_norm(fo, f_g).astype(dt), rw.astype(dt), rms_norm(at, a_g).astype(dt)], axis=-1) @ w_out
    if not ctx_out:
        return y, None

    fo_c = fourier_mix(cols(pc, 0, OFF_F, OFF_Q))
    rw_c = rwkv_output(from_scan(ys_c), r_c, cst[1], cst[2], cols(psc, sbase, OFF_G, OFF_K), g2, rk, lnx_w, lnx_b)
    at_c = context_attention(cols(pc, 0, OFF_Q, OFF_R).reshape(B, C, ATTN_Q_HEADS, ATTN_HEAD_DIM), k_c, v_c, sink)
    y_c = jnp.concatenate([rms_norm(fo_c, f_g).astype(dt), rw_c.astype(dt), rms_norm(at_c, a_g).astype(dt)], axis=-1) @ w_out
    return y, y_c


def conv_ffn(u, w_in, conv_w, conv_b, w_out):
    h = u @ w_in
    gate = dwconv3(h[..., :D_FF], conv_w, conv_b)
    return (jax.nn.gelu(gate, approximate=False) * h[..., D_FF:]) @ w_out


def setup_inputs(seed: int = 0) -> dict:
    key = jax.random.key(seed)
    ks = iter(jax.random.split(key, 32))

    def nrm(shape, scale):
        return jax.random.normal(next(ks), shape, F32) * scale

    def unif(shape, lo, hi):
        return jax.random.uniform(next(ks), shape, F32, lo, hi)

    D = D_MODEL
    return {
        'x': nrm((BATCH, SEQ, D), 1.0),
        'c': nrm((BATCH, D), 1.0),
        'ctx': nrm((BATCH, CTX_LEN, D), 1.0),
        'c_ctx': nrm((D,), 1.0),
        'norm1_g': 1.0 + nrm((DEPTH, D), 0.02),
        'norm2_g': 1.0 + nrm((DEPTH, D), 0.02),
        'w_ada': nrm((DEPTH, D, 6 * D), 0.5 * D ** -0.5),
        'b_ada': nrm((DEPTH, 6 * D), 0.01),
        'w_in': nrm((DEPTH, D, IN_W), D ** -0.5),
        'rwkv_mu': unif((DEPTH, 2, RWKV_SHIFT_W), 0.0, 0.5),
        'rwkv_w0': unif((DEPTH, 2, RWKV_W), -6.0, -1.0),
        'rwkv_w2': nrm((DEPTH, 2, DECAY_RANK, RWKV_W), 0.5 * DECAY_RANK ** -0.5),
        'rwkv_a0': nrm((DEPTH, 2, RWKV_W), 0.1),
        'rwkv_a2': nrm((DEPTH, 2, ICLR_RANK, RWKV_W), ICLR_RANK ** -0.5),
        'rwkv_g2': nrm((DEPTH, GATE_RANK, RWKV_W), GATE_RANK ** -0.5),
        'rwkv_kk_scale': 0.85 + nrm((DEPTH, RWKV_W), 0.02),
        'rwkv_ka': 1.0 + nrm((DEPTH, RWKV_W), 0.02),
        'rwkv_rk': nrm((DEPTH, RWKV_HEADS, RWKV_HEAD), 0.1),
        'rwkv_lnx_w': 1.0 + nrm((DEPTH, RWKV_W), 0.02),
        'rwkv_lnx_b': nrm((DEPTH, RWKV_W), 0.01),
        'attn_sink': nrm((DEPTH, ATTN_Q_HEADS), 0.5),
        'fourier_out_g': 1.0 + nrm((DEPTH, FOURIER_W), 0.02),
        'attn_out_g': 1.0 + nrm((DEPTH, ATTN_W), 0.02),
        'w_out': nrm((DEPTH, D, D), D ** -0.5),
        'w_ffn_in': nrm((DEPTH, D, 2 * D_FF), D ** -0.5),
        'ffn_conv_w': nrm((DEPTH, 3, D_FF), 3 ** -0.5),
        'ffn_conv_b': nrm((DEPTH, D_FF), 0.01),
        'w_ffn_out': nrm((DEPTH, D_FF, D), D_FF ** -0.5),
        'final_g': 1.0 + nrm((D,), 0.02),
    }


def reference(x, c, ctx, c_ctx, norm1_g, norm2_g, w_ada, b_ada, w_in, rwkv_mu, rwkv_w0, rwkv_w2,
              rwkv_a0, rwkv_a2, rwkv_g2, rwkv_kk_scale, rwkv_ka, rwkv_rk, rwkv_lnx_w, rwkv_lnx_b,
              attn_sink, fourier_out_g, attn_out_g, w_out, w_ffn_in, ffn_conv_w, ffn_conv_b, w_ffn_out, final_g):
    L = x.shape[1]
    D = D_MODEL
    cos, sin = axial_rope_tables(L)
    silu_c = jax.nn.silu(c)
    silu_cc = jax.nn.silu(c_ctx)
    h_ctx = ctx
    for l in range(DEPTH):
        last = l == DEPTH - 1
        mod = (silu_c @ w_ada[l] + b_ada[l])[:, None, :]
        sh1, sc1, ga1, sh2, sc2, ga2 = jnp.split(mod, 6, axis=-1)
        n_ctx_mod = 2 if last else 6
        mc = jnp.split(silu_cc @ w_ada[l][:, :n_ctx_mod * D] + b_ada[l][:n_ctx_mod * D], n_ctx_mod, axis=-1)

        u = modulate(rms_norm(x, norm1_g[l]), sh1, sc1)
        uc = modulate(rms_norm(h_ctx, norm1_g[l]), mc[0], mc[1])
        y, y_c = token_mixer(u, uc, w_in[l], w_out[l], rwkv_mu[l], rwkv_w0[l], rwkv_w2[l], rwkv_a0[l],
                             rwkv_a2[l], rwkv_g2[l], rwkv_kk_scale[l], rwkv_ka[l], rwkv_rk[l],
                             rwkv_lnx_w[l], rwkv_lnx_b[l], attn_sink[l], fourier_out_g[l], attn_out_g[l],
                             cos, sin, not last)
        x = x + ga1 * y
        x = x + ga2 * conv_ffn(modulate(rms_norm(x, norm2_g[l]), sh2, sc2),
                               w_ffn_in[l], ffn_conv_w[l], ffn_conv_b[l], w_ffn_out[l])
        if not last:
            h_ctx = h_ctx + mc[2] * y_c
            h_ctx = h_ctx + mc[5] * conv_ffn(modulate(rms_norm(h_ctx, norm2_g[l]), mc[3], mc[4]),
                                             w_ffn_in[l], ffn_conv_w[l], ffn_conv_b[l], w_ffn_out[l])
    return rms_norm(x, final_g)
```

```python
import numpy as np
from contextlib import ExitStack
import ml_dtypes
import concourse.bass as bass
import concourse.mybir as mybir
from concourse.bass_utils import run_bass_kernel_spmd

F32 = mybir.dt.float32
BF16 = mybir.dt.bfloat16
AF = mybir.ActivationFunctionType
ALU = mybir.AluOpType
AX = mybir.AxisListType

D = 2048
NCTX = 256
NLAT = 4096
TOK = NCTX + NLAT
DEPTH = 4
INW = 3968
DFF = 5632
TILES = [(0, 256)] + [(256 + 512 * i, 512) for i in range(8)]
SEQS = [(0, NCTX), (NCTX, TOK)]
RMS_EPS = 1e-6

C_F, C_Q, C_R, C_G, C_K, C_V, C_WD, C_AD, C_AK, C_AV = 0, 4, 12, 16, 17, 21, 25, 26, 27, 29
NPC = 31

PCOLS = {}
_off = 0
for _n, _w in [("n1", 16), ("n2", 16), ("mu0", 15), ("mu1", 15), ("w0", 8), ("a0", 8), ("kks", 4), ("ka", 4), ("rk", 4),
               ("lnw", 4), ("lnb", 4), ("fg", 4), ("ag", 8), ("cw0", 44), ("cw1", 44), ("cw2", 44), ("cb", 44),
               ("bada", 96), ("sink", 8)]:
    PCOLS[_n] = (_off, _w)
    _off += _w
PL = _off
P_FINAL = DEPTH * PL
NPAR = P_FINAL + 16


def _col(v):
    v = np.asarray(v, np.float32).reshape(-1)
    return np.ascontiguousarray(v.reshape(-1, 128).T)


def pack_params(inp):
    P = np.zeros((128, NPAR), np.float32)
    for l in range(DEPTH):
        def put(name, arr):
            o, w = PCOLS[name]
            P[:, l * PL + o: l * PL + o + w] = arr
        put("n1", _col(inp["norm1_g"][l])); put("n2", _col(inp["norm2_g"][l]))
        put("mu0", _col(inp["rwkv_mu"][l][0])); put("mu1", _col(inp["rwkv_mu"][l][1]))
        put("w0", _col(inp["rwkv_w0"][l])); put("a0", _col(inp["rwkv_a0"][l]))
        put("kks", _col(inp["rwkv_kk_scale"][l])); put("ka", _col(inp["rwkv_ka"][l])); put("rk", _col(inp["rwkv_rk"][l]))
        put("lnw", _col(inp["rwkv_lnx_w"][l])); put("lnb", _col(inp["rwkv_lnx_b"][l]))
        put("fg", _col(inp["fourier_out_g"][l])); put("ag", _col(inp["attn_out_g"][l]))
        for j in range(3):
            put("cw%d" % j, _col(inp["ffn_conv_w"][l][j]))
        put("cb", _col(inp["ffn_conv_b"][l])); put("bada", _col(inp["b_ada"][l]))
        put("sink", np.broadcast_to(np.asarray(inp["attn_sink"][l], np.float32)[None, :], (128, 8)))
    P[:, P_FINAL:P_FINAL + 16] = _col(inp["final_g"])
    return P


class Dep:
    __slots__ = ("w", "r", "x")

    def __init__(self, x=False):
        self.w = None
        self.r = {}
        self.x = x


class T:
    def __init__(self, t):
        self.t = t
        self.d = Dep()

    def __getitem__(self, idx):
        return self.t[idx]


def _d(x):
    return getattr(x, "d", x)


class Sched:
    def __init__(self, nc, es):
        self.nc = nc
        self.E = {"pe": nc.tensor, "dve": nc.vector, "act": nc.scalar, "pool": nc.gpsimd, "sp": nc.sync}
        self.csem = {k: es.enter_context(nc.semaphore("cs_" + k)) for k in ("pe", "dve", "act", "pool")}
        self.cnt = {k: 0 for k in self.csem}
        self.seen = {k: {} for k in self.E}
        self.dq = {}
        for q, n in (("sp", 16), ("pool", 8), ("act", 8)):
            self.dq[q] = dict(sems=[es.enter_context(nc.semaphore("d_%s%d" % (q, i))) for i in range(n)],
                              cnt=[0] * n, nxt=0)
        self.nins = 0

    def _wait(self, e, tok):
        if tok is None:
            return
        key, sem, val = tok
        if e == "pe" and key == "pe":
            return
        if self.seen[e].get(key, 0) >= val:
            return
        self.E[e].wait_ge(sem, val)
        self.seen[e][key] = val

    def _deps(self, e, r, w):
        for d in r:
            self._wait(e, _d(d).w)
        for d in w:
            d = _d(d)
            self._wait(e, d.w)
            for t in list(d.r.values()):
                self._wait(e, t)

    def _mark(self, tok, r, w):
        for d in r:
            _d(d).r[tok[0]] = tok
        for d in w:
            d = _d(d)
            d.w = tok
            d.r = {}

    def op(self, e, fn, r=(), w=()):
        xs = [d for d in r if _d(d).x]
        if xs:
            r = [d for d in r if not _d(d).x]
            w = list(w) + xs
        self._deps(e, r, w)
        ins = fn(self.E[e])
        self.cnt[e] += 1
        ins.then_inc(self.csem[e], 1)
        self._mark((e, self.csem[e], self.cnt[e]), r, w)
        self.nins += 1

    def dma(self, q, out, in_, r=(), w=(), **kw):
        Q = self.dq[q]
        i = Q["nxt"]
        Q["nxt"] = (i + 1) % len(Q["sems"])
        key = (q, i)
        if Q["cnt"][i]:
            self._wait(q, (key, Q["sems"][i], 16 * Q["cnt"][i]))
        self._deps(q, r, w)
        ins = self.E[q].dma_start(out=out, in_=in_, **kw)
        Q["cnt"][i] += 1
        ins.then_inc(Q["sems"][i], 16)
        self._mark((key, Q["sems"][i], 16 * Q["cnt"][i]), r, w)
        self.nins += 1

    def barrier(self, engines=("pe", "dve", "act", "pool", "sp")):
        for e in engines:
            for k in self.csem:
                if self.cnt[k]:
                    self._wait(e, (k, self.csem[k], self.cnt[k]))
            for q, Q in self.dq.items():
                for i, s in enumerate(Q["sems"]):
                    if Q["cnt"][i]:
                        self._wait(e, ((q, i), s, 16 * Q["cnt"][i]))


class Ctx:
    pass


def build_nc(nl=DEPTH, dbg=None):
    nc = bass.Bass("TRN2", target_bir_lowering=False)
    G = Ctx()
    G.nc = nc
    G.dbg = dbg
    dt_in = lambda name, shape, dt=F32: nc.dram_tensor(name, shape, dt, kind="ExternalInput").ap()
    dt_sc = lambda name, shape, dt=F32: nc.dram_tensor(name, shape, dt, kind="Internal").ap()
    G.xin = dt_in("xin", [D, TOK])
    G.cvec = dt_in("cvec", [128, 32])
    G.params = dt_in("params", [128, NPAR])
    G.w_ada = dt_in("w_ada", [DEPTH, D, 6 * D])
    G.w_in = dt_in("w_in", [DEPTH, D, INW])
    G.w_out = dt_in("w_out", [DEPTH, D, D])
    G.w_ffn_in = dt_in("w_ffn_in", [DEPTH, D, 2 * DFF])
    G.w_ffn_out = dt_in("w_ffn_out", [DEPTH, DFF, D])
    G.out = nc.dram_tensor("out", [D, NLAT], F32, kind="ExternalOutput").ap()
    G.XT = dt_sc("XT", [D, TOK])
    G.PT = dt_sc("PT", [NPC * 128, TOK])
    G.MIX = dt_sc("MIX", [D, TOK], BF16)
    G.U2 = dt_sc("U2", [D, TOK], BF16)
    if dbg and "mixin" in dbg:
        G.mixin = dt_in("mixin", [D, TOK])
    G.C = {k: dt_in(k, sh, dt) for k, (sh, dt) in CONST_SHAPES.items()}
    G.w2r = dt_in("w2r", [DEPTH, 128, 512])
    G.a2r = dt_in("a2r", [DEPTH, 128, 512])
    G.g2 = dt_in("g2", [DEPTH, 128, 512])
    G.GT = dt_sc("GT", [512, TOK])
    G.VS = dt_sc("VS", [512, TOK])
    G.BON = dt_sc("BON", [512, TOK])
    G.PTOT = [dt_sc("PTOT%d" % d, [512, NCHK]) for d in range(2)]
    G.DER = [[dt_sc("DER%d_%d" % (d, a), [512, TOK]) for a in range(DER_N)] for d in range(2)]
    G.YD = [dt_sc("YD%d" % d, [512, TOK]) for d in range(2)]
    G.wb_in = [dt_sc("wb_in%d" % l, [D, INW], BF16) for l in range(nl)]
    G.wb_out = [dt_sc("wb_out%d" % l, [D, D], BF16) for l in range(nl)]
    G.wb_fi = [dt_sc("wb_fi%d" % l, [D, 2 * DFF], BF16) for l in range(nl)]
    G.wb_fo = [dt_sc("wb_fo%d" % l, [DFF, D], BF16) for l in range(nl)]
    if dbg:
        G.dbg_out = {}
        for name, shape in dbg.items():
            if name == "mixin":
                continue
            G.dbg_out[name] = nc.dram_tensor("dbg_" + name, shape, F32, kind="ExternalOutput").ap()

    with ExitStack() as es:
        S = Sched(nc, es)
        G.S = S
        G.es = es
        G.uid = 0

        def nuid():
            G.uid += 1
            return G.uid
        G.nuid = nuid

        def sb(name, shape, dt=F32, st=es):
            G.uid += 1
            return T(st.enter_context(nc.sbuf_tensor("%s_%d" % (name, G.uid), shape, dt)))
        G.sb = sb
        G.par = sb("par", [128, NPAR])
        G.mod = sb("mod", [128, nl * 96 * 2])
        G.ones = sb("ones_f", [128, 128])
        G.onesb = sb("ones_b", [128, 128], BF16)
        G.wcast = Dep()
        G.epsc = sb("epsc", [128, 1])
        S.op("dve", lambda e: e.memset(G.epsc[:], RMS_EPS), w=[G.epsc])
        G.lin_it = 0
        G.ps_it = 0
        G.stg_it = 0
        G.xc_it = 0
        S.dma("sp", G.par[:], G.params[:, :], w=[G.par])
        S.op("dve", lambda e: e.memset(G.ones[:], 1.0), w=[G.ones])
        S.op("dve", lambda e: e.memset(G.onesb[:], 1.0), w=[G.onesb])
        for l in range(nl):
            for src, dst, K in ((G.w_in, G.wb_in, D), (G.w_out, G.wb_out, D), (G.w_ffn_in, G.wb_fi, D), (G.w_ffn_out, G.wb_fo, DFF)):
                for kc in range(K // 128):
                    S.dma("pool", dst[l][kc * 128:(kc + 1) * 128, :], src[l, kc * 128:(kc + 1) * 128, :], w=[G.wcast],
                          max_dma_last_dim=4096)
        xd = Dep()
        for c in range(16):
            S.dma("sp", G.XT[c * 128:(c + 1) * 128, :], G.xin[c * 128:(c + 1) * 128, :], w=[xd])
        prologue_adaln(G, nl)
        S.barrier()
        if dbg and "mod" in dbg:
            S.dma("sp", G.dbg_out["mod"][:, :], G.mod[:], r=[G.mod])
        for l in range(nl):
            layer(G, l)
        final_norm(G)
        S.barrier()
    G.nins = S.nins
    return nc, G


def pcol(G, l, name, c=0, n=1):
    o, w = PCOLS[name]
    return G.par[:, l * PL + o + c: l * PL + o + c + n]


def mcol(G, l, idx, c, which):
    j = ((l * 96) + idx * 16 + c) * 2 + which
    return G.mod[:, j:j + 1]


def prologue_adaln(G, nl):
    nc, S = G.nc, G.S
    with ExitStack() as st:
        sb = lambda name, shape, dt=F32: G.sb(name, shape, dt, st)
        cv = sb("cv", [128, 32])
        s2 = sb("s2", [128, 32])
        wts = [sb("wada%d" % i, [128, 16, 512]) for i in range(2)]
        ps = [T(st.enter_context(nc.psum_tensor("ps_ada%d" % i, [128, 512], F32))) for i in range(4)]
        S.dma("sp", cv[:], G.cvec[:, :], w=[cv])
        S.op("act", lambda e: e.activation(out=s2[:].rearrange("p (k w) -> p w k", w=2),
                                           in_=cv[:].rearrange("p (w k) -> p w k", w=2), func=AF.Silu), r=[cv], w=[s2])
        it = 0
        for l in range(nl):
            wv = G.w_ada[l].rearrange("(kc p) n -> p kc n", p=128)
            for cg in range(24):
                wt = wts[it % 2]
                S.dma("sp", wt[:], wv[:, :, cg * 512:(cg + 1) * 512], w=[wt])
                for j in range(4):
                    ch = cg * 4 + j
                    p_ = ps[(it * 4 + j) % 4]
                    for kc in range(16):
                        S.op("pe", lambda e, kc=kc, j=j, p_=p_, wt=wt: e.matmul(
                            p_[:, 0:2], lhsT=wt[:, kc, j * 128:(j + 1) * 128], rhs=s2[:, kc * 2:kc * 2 + 2],
                            start=(kc == 0), stop=(kc == 15)), r=[wt, s2], w=[p_])
                    o = (l * 96 + ch) * 2
                    S.op("dve", lambda e, p_=p_, o=o, ch=ch, l=l: e.tensor_scalar(
                        out=G.mod[:, o:o + 2], in0=p_[:, 0:2], scalar1=pcol(G, l, "bada", ch), scalar2=None, op0=ALU.add),
                        r=[p_, G.par], w=[G.mod])
                it += 1


def rms_modulate(G, xt, W, scale_col, bias_col, out_fn, sq, rstd, ps_ss, nch=16, dim=D, ones=None):
    S = G.S
    ones = ones or G.ones
    S.op("act", lambda e: e.activation(out=sq[:, 0:nch, 0:W], in_=xt[:, 0:nch, 0:W], func=AF.Square), r=[xt], w=[sq])
    for c in range(nch):
        S.op("pe", lambda e, c=c: e.matmul(ps_ss[:, 0:W], lhsT=ones[:], rhs=sq[:, c, 0:W], start=(c == 0), stop=(c == nch - 1)),
             r=[sq, ones], w=[ps_ss])
    S.op("act", lambda e: e.activation(out=rstd[:, 0:W], in_=ps_ss[:, 0:W], func=AF.Sqrt, scale=1.0 / dim, bias=G.epsc[:, 0:1]),
         r=[ps_ss, G.epsc], w=[rstd])
    S.op("dve", lambda e: e.reciprocal(out=rstd[:, 0:W], in_=rstd[:, 0:W]), r=[rstd], w=[rstd])
    S.op("dve", lambda e: e.tensor_tensor(out=sq[:, 0:nch, 0:W], in0=xt[:, 0:nch, 0:W],
                                          in1=rstd[:, 0:W].unsqueeze(1).broadcast_to([128, nch, W]), op=ALU.mult),
         r=[xt, rstd], w=[sq])
    for c in range(nch):
        o, od = out_fn(c)
        b = bias_col(c) if bias_col is not None else 0.0
        S.op("act", lambda e, c=c, o=o, b=b: e.activation(out=o, in_=sq[:, c, 0:W], func=AF.Identity, scale=scale_col(c), bias=b),
             r=[sq, G.par, G.mod] + list(od), w=od)


def linear(G, act, KC, W, wdram, n_oc, epilogue, ps, wts, gsz=4, col0=0, a0=0):
    S = G.S
    wv = wdram.rearrange("(kc p) n -> p kc n", p=128)
    ng = (n_oc + gsz - 1) // gsz
    st = G.lin_it
    for g in range(ng):
        wt = wts[(st + g) % len(wts)]
        n = min(gsz, n_oc - g * gsz)
        S.dma("sp", wt[:, 0:KC, 0:n * 128], wv[:, :, col0 + g * gsz * 128: col0 + (g * gsz + n) * 128], w=[wt])
        for j in range(n):
            oc = g * gsz + j
            p_ = ps[G.ps_it % len(ps)]
            G.ps_it += 1
            for kc in range(KC):
                S.op("pe", lambda e, kc=kc, j=j, p_=p_, wt=wt: e.matmul(
                    p_[:, 0:W], lhsT=wt[:, kc, j * 128:(j + 1) * 128], rhs=act[:, kc, a0:a0 + W],
                    start=(kc == 0), stop=(kc == KC - 1)), r=[wt, act], w=[p_])
            epilogue(oc, p_)
    G.lin_it += ng


def gmod_cols(G, l, gm, nname, sc_idx):
    S = G.S
    mv = G.mod[:, l * 192:(l + 1) * 192].rearrange("p (i c w) -> p i c w", i=6, c=16)
    o, _ = PCOLS[nname]
    for which in range(2):
        S.op("dve", lambda e, which=which: e.scalar_tensor_tensor(
            out=gm[:, which * 16:(which + 1) * 16], in0=mv[:, sc_idx, :, which], scalar=1.0,
            in1=G.par[:, l * PL + o:l * PL + o + 16], op0=ALU.add, op1=ALU.mult), r=[G.mod, G.par], w=[gm])


def phase_norm_proj(G, l):
    nc, S = G.nc, G.S
    with ExitStack() as st:
        sb = lambda name, shape, dt=F32: G.sb(name, shape, dt, st)
        pst = lambda name: T(st.enter_context(nc.psum_tensor(name + "_%d" % G.nuid(), [128, 512], F32)))
        xts = [sb("np_x%d" % i, [128, 16, 512]) for i in range(2)]
        sq = sb("np_sq", [128, 16, 512])
        rstd = sb("np_rstd", [128, 512])
        us = [sb("np_u%d" % i, [128, 16, 512], BF16) for i in range(2)]
        wts = [sb("np_w%d" % i, [128, 16, 512], BF16) for i in range(3)]
        stg = [sb("np_stg%d" % i, [128, 4, 512]) for i in range(2)]
        gm = sb("np_gm", [128, 32])
        ps_ss = pst("np_pss")
        ps = [pst("np_ps%d" % i) for i in range(6)]
        gmod_cols(G, l, gm, "n1", 1)
        xv = G.XT.rearrange("(c p) t -> p c t", p=128)
        pv = G.PT.rearrange("(c p) t -> p c t", p=128)
        def do_norm(ti):
            t0, W = TILES[ti]
            which = 1 if ti == 0 else 0
            xt, u = xts[ti % 2], us[ti % 2]
            S.dma("sp", xt[:, :, 0:W], xv[:, :, t0:t0 + W], w=[xt])
            rms_modulate(G, xt, W, lambda c: gm[:, which * 16 + c:which * 16 + c + 1], lambda c: mcol(G, l, 0, c, which),
                         lambda c: (u[:, c, 0:W], [u]), sq, rstd, ps_ss)
        do_norm(0)
        for ti, (t0, W) in enumerate(TILES):
            u = us[ti % 2]
            if ti + 1 < len(TILES):
                do_norm(ti + 1)
            state = {"k": 0}

            def epi(oc, p_, t0=t0, W=W):
                k = G.stg_it
                sg = stg[(k // 4) % 2]
                j = oc % 4
                eng = "act" if oc % 2 == 0 else "dve"
                if eng == "act":
                    S.op("act", lambda e: e.activation(out=sg[:, j, 0:W], in_=p_[:, 0:W], func=AF.Copy), r=[p_], w=[sg])
                else:
                    S.op("dve", lambda e: e.tensor_copy(out=sg[:, j, 0:W], in_=p_[:, 0:W]), r=[p_], w=[sg])
                G.stg_it += 1
                if j == 3 or oc == NPC - 1:
                    o0 = oc - j
                    S.dma("sp", pv[:, o0:oc + 1, t0:t0 + W], sg[:, 0:j + 1, 0:W], r=[sg])
                    G.stg_it = ((G.stg_it + 3) // 4) * 4
            linear(G, u, 16, W, G.wb_in[l], NPC, epi, ps, wts)


def phase_out_proj(G, l):
    nc, S = G.nc, G.S
    with ExitStack() as st:
        sb = lambda name, shape, dt=F32: G.sb(name, shape, dt, st)
        pst = lambda name: T(st.enter_context(nc.psum_tensor(name + "_%d" % G.nuid(), [128, 512], F32)))
        xts = [sb("op_x%d" % i, [128, 16, 512]) for i in range(2)]
        ms = [sb("op_m%d" % i, [128, 16, 512], BF16) for i in range(2)]
        sq = sb("op_sq", [128, 16, 512])
        rstd = sb("op_rstd", [128, 512])
        us = [sb("op_u%d" % i, [128, 16, 512], BF16) for i in range(1)]
        wts = [sb("op_w%d" % i, [128, 16, 512], BF16) for i in range(2)]
        gm = sb("op_gm", [128, 32])
        ps_ss = pst("op_pss")
        ps = [pst("op_ps%d" % i) for i in range(6)]
        gmod_cols(G, l, gm, "n2", 4)
        xv = G.XT.rearrange("(c p) t -> p c t", p=128)
        mv = G.MIX.rearrange("(c p) t -> p c t", p=128)
        uv = G.U2.rearrange("(c p) t -> p c t", p=128)
        for ti, (t0, W) in enumerate(TILES):
            which = 1 if ti == 0 else 0
            xt, u, m = xts[ti % 2], us[0], ms[ti % 2]
            S.dma("sp", xt[:, :, 0:W], xv[:, :, t0:t0 + W], w=[xt])
            S.dma("sp", m[:, :, 0:W], mv[:, :, t0:t0 + W], w=[m])

            def epi(oc, p_, W=W, xt=xt, which=which):
                S.op("dve", lambda e: e.scalar_tensor_tensor(out=xt[:, oc, 0:W], in0=p_[:, 0:W], scalar=mcol(G, l, 2, oc, which),
                                                             in1=xt[:, oc, 0:W], op0=ALU.mult, op1=ALU.add),
                     r=[p_, G.mod, xt], w=[xt])
            linear(G, m, 16, W, G.wb_out[l], 16, epi, ps, wts)
            S.dma("sp", xv[:, :, t0:t0 + W], xt[:, :, 0:W], r=[xt])
            rms_modulate(G, xt, W, lambda c: gm[:, which * 16 + c:which * 16 + c + 1], lambda c: mcol(G, l, 3, c, which),
                         lambda c: (u[:, c, 0:W], [u]), sq, rstd, ps_ss)
            S.dma("sp", uv[:, :, t0:t0 + W], u[:, :, 0:W], r=[u])


def phase_ffn(G, l):
    nc, S = G.nc, G.S
    with ExitStack() as st:
        sb = lambda name, shape, dt=F32: G.sb(name, shape, dt, st)
        pst = lambda name: T(st.enter_context(nc.psum_tensor(name + "_%d" % G.nuid(), [128, 512], F32)))
        uh = [sb("ff_u%d" % i, [128, 16, 514], BF16) for i in range(2)]
        gt = sb("ff_g", [128, 44, 512], BF16)
        wg = [sb("ff_wg%d" % i, [128, 16, 256], BF16) for i in range(2)]
        wu = [sb("ff_wu%d" % i, [128, 16, 256], BF16) for i in range(2)]
        wo = [sb("ff_wo%d" % i, [128, 44, 256], BF16) for i in range(2)]
        xc = [sb("ff_x%d" % i, [128, 512]) for i in range(3)]
        hh = [sb("ff_hh%d" % i, [128, 514]) for i in range(2)]
        tm = [sb("ff_tm%d" % i, [128, 512]) for i in range(2)]
        ge = [sb("ff_ge%d" % i, [128, 512]) for i in range(2)]
        psg = [pst("ff_pg%d" % i) for i in range(2)]
        psh = pst("ff_ph")
        psu = [pst("ff_pu%d" % i) for i in range(2)]
        pso = [pst("ff_po%d" % i) for i in range(3)]
        xv = G.XT.rearrange("(c p) t -> p c t", p=128)
        uv = G.U2.rearrange("(c p) t -> p c t", p=128)
        wiv = G.wb_fi[l].rearrange("(kc p) n -> p kc n", p=128)
        it = 0
        for ti, (t0, W) in enumerate(TILES):
            which = 1 if ti == 0 else 0
            u = uh[ti % 2]
            lz = any(t0 == a for a, b in SEQS)
            rz = any(t0 + W == b for a, b in SEQS)
            lo = t0 - (0 if lz else 1)
            hi = t0 + W + (0 if rz else 1)
            S.dma("sp", u[:, :, (1 if lz else 0):(1 if lz else 0) + hi - lo], uv[:, :, lo:hi], w=[u])
            if lz:
                S.op("dve", lambda e, u=u: e.memset(u[:, :, 0:1], 0.0), w=[u])
            if rz:
                S.op("dve", lambda e, u=u, W=W: e.memset(u[:, :, W + 1:W + 2], 0.0), w=[u])
            for g in range(22):
                a, b = wg[it % 2], wu[it % 2]
                S.dma("sp", a[:], wiv[:, :, g * 256:(g + 1) * 256], w=[a])
                S.dma("sp", b[:], wiv[:, :, DFF + g * 256:DFF + (g + 1) * 256], w=[b])
                it += 1
                for j in range(2):
                    ch = g * 2 + j
                    pg, pu = psg[ch % 2], psu[ch % 2]
                    h, t_, g_ = hh[ch % 2], tm[ch % 2], ge[ch % 2]
                    for kc in range(16):
                        S.op("pe", lambda e, kc=kc, j=j, a=a, pg=pg: e.matmul(pg[:, 0:W], lhsT=a[:, kc, j * 128:(j + 1) * 128],
                                                                         rhs=u[:, kc, 1:W + 1], start=(kc == 0), stop=(kc == 15)),
                             r=[a, u], w=[pg])
                    hs = psh[:, (ch % 8) * 2:(ch % 8) * 2 + 2]
                    for kc in range(16):
                        S.op("pe", lambda e, kc=kc, j=j, a=a, hs=hs: e.matmul(hs, lhsT=a[:, kc, j * 128:(j + 1) * 128],
                                                                         rhs=u[:, kc, 0:W + 2:W + 1], start=(kc == 0), stop=(kc == 15)),
                             r=[a, u], w=[psh])
                    for kc in range(16):
                        S.op("pe", lambda e, kc=kc, j=j, b=b, pu=pu: e.matmul(pu[:, 0:W], lhsT=b[:, kc, j * 128:(j + 1) * 128],
                                                                         rhs=u[:, kc, 1:W + 1], start=(kc == 0), stop=(kc == 15)),
                             r=[b, u], w=[pu])
                    S.op("act", lambda e, h=h, pg=pg: e.activation(out=h[:, 1:W + 1], in_=pg[:, 0:W], func=AF.Copy), r=[pg], w=[h])
                    S.op("act", lambda e, h=h, hs=hs: e.activation(out=h[:, 0:W + 2:W + 1], in_=hs, func=AF.Copy), r=[psh], w=[h])
                    S.op("act", lambda e, h=h, t_=t_, ch=ch: e.activation(out=t_[:, 0:W], in_=h[:, 1:W + 1], func=AF.Identity,
                                                                    scale=pcol(G, l, "cw1", ch), bias=pcol(G, l, "cb", ch)),
                         r=[h, G.par], w=[t_])
                    S.op("dve", lambda e, h=h, t_=t_, ch=ch: e.scalar_tensor_tensor(out=t_[:, 0:W], in0=h[:, 0:W], scalar=pcol(G, l, "cw0", ch),
                                                                             in1=t_[:, 0:W], op0=ALU.mult, op1=ALU.add),
                         r=[h, t_, G.par], w=[t_])
                    S.op("dve", lambda e, h=h, t_=t_, ch=ch: e.scalar_tensor_tensor(out=t_[:, 0:W], in0=h[:, 2:W + 2], scalar=pcol(G, l, "cw2", ch),
                                                                             in1=t_[:, 0:W], op0=ALU.mult, op1=ALU.add),
                         r=[h, t_, G.par], w=[t_])
                    S.op("act", lambda e, t_=t_, g_=g_: e.activation(out=g_[:, 0:W], in_=t_[:, 0:W], func=AF.Gelu), r=[t_], w=[g_])
                    S.op("dve", lambda e, g_=g_, pu=pu, ch=ch: e.tensor_tensor(out=gt[:, ch, 0:W], in0=g_[:, 0:W], in1=pu[:, 0:W], op=ALU.mult),
                         r=[g_, pu], w=[gt])

            def epi(oc, p_, W=W, t0=t0, which=which):
                x_ = xc[G.xc_it % 3]
                G.xc_it += 1
                S.dma("sp", x_[:, 0:W], xv[:, oc, t0:t0 + W], w=[x_])
                S.op("dve", lambda e: e.scalar_tensor_tensor(out=x_[:, 0:W], in0=p_[:, 0:W], scalar=mcol(G, l, 5, oc, which),
                                                             in1=x_[:, 0:W], op0=ALU.mult, op1=ALU.add),
                     r=[p_, G.mod, x_], w=[x_])
                S.dma("sp", xv[:, oc, t0:t0 + W], x_[:, 0:W], r=[x_])
            linear(G, gt, 44, W, G.wb_fo[l], 16, epi, pso, wo, gsz=2)


def final_norm(G):
    nc, S = G.nc, G.S
    with ExitStack() as st:
        sb = lambda name, shape, dt=F32: G.sb(name, shape, dt, st)
        xts = [sb("fn_x%d" % i, [128, 16, 512]) for i in range(2)]
        os_ = [sb("fn_o%d" % i, [128, 16, 512]) for i in range(2)]
        sq = sb("fn_sq", [128, 16, 512])
        rstd = sb("fn_rstd", [128, 512])
        ps_ss = T(st.enter_context(nc.psum_tensor("fin_pss", [128, 512], F32)))
        xv = G.XT.rearrange("(c p) t -> p c t", p=128)
        ov = G.out.rearrange("(c p) t -> p c t", p=128)
        for ti, (t0, W) in enumerate(TILES[1:]):
            xt, o = xts[ti % 2], os_[ti % 2]
            S.dma("sp", xt[:, :, 0:W], xv[:, :, t0:t0 + W], w=[xt])
            rms_modulate(G, xt, W, lambda c: G.par[:, P_FINAL + c:P_FINAL + c + 1], None,
                         lambda c: (o[:, c, 0:W], [o]), sq, rstd, ps_ss)
            S.dma("sp", ov[:, :, t0 - NCTX:t0 - NCTX + W], o[:, :, 0:W], r=[o])


def layer(G, l):
    S = G.S
    dbg = G.dbg or {}
    phase_norm_proj(G, l)
    S.barrier()
    if "PT" in dbg and l == 0:
        S.dma("sp", G.dbg_out["PT"][:, :], G.PT[:, :])
        S.barrier()
    if "mixin" in dbg:
        for c in range(16):
            S.dma("pool", G.MIX[c * 128:(c + 1) * 128, :], G.mixin[c * 128:(c + 1) * 128, :])
    else:
        phase_mixers(G, l)
    S.barrier()
    phase_out_proj(G, l)
    S.barrier()
    phase_ffn(G, l)
    S.barrier()
    if "YD" in dbg and l == 0:
        for d in range(2):
            for hh in range(4):
                S.dma("sp", G.dbg_out["YD"][d * 512 + hh * 128:d * 512 + (hh + 1) * 128, :], G.YD[d][hh * 128:(hh + 1) * 128, :])
        S.barrier()
    if "DER" in dbg and l == 0:
        for d in range(2):
            for a in range(DER_N):
                for hh in range(4):
                    S.dma("sp", G.dbg_out["DER"][(d * DER_N + a) * 512 + hh * 128:(d * DER_N + a) * 512 + (hh + 1) * 128, :], G.DER[d][a][hh * 128:(hh + 1) * 128, :])
        S.barrier()
    if "MIX" in dbg and l == 0:
        for c in range(16):
            S.dma("pool", G.dbg_out["MIX"][c * 128:(c + 1) * 128, :], G.MIX[c * 128:(c + 1) * 128, :])
        S.barrier()
    if "XT" in dbg and l == 0:
        S.dma("sp", G.dbg_out["XT"][:, :], G.XT[:, :])
        S.barrier()


_CONST = None


def const_inputs():
    global _CONST
    if _CONST is not None:
        return _CONST
    bf = ml_dtypes.bfloat16
    t = np.arange(NLAT, dtype=np.int64)
    tk = (t[:, None] * t[None, :]) % NLAT
    ang = 2.0 * np.pi * tk.astype(np.float64) / NLAT
    C = {}
    C["cosL"] = np.cos(ang).astype(np.float32).astype(bf)
    C["sinL"] = np.sin(ang).astype(np.float32).astype(bf)
    t = np.arange(NCTX, dtype=np.int64)
    ang = 2.0 * np.pi * ((t[:, None] * t[None, :]) % NCTX).astype(np.float64) / NCTX
    C["cosC"] = np.cos(ang).astype(np.float32).astype(bf)
    C["sinC"] = np.sin(ang).astype(np.float32).astype(bf)
    c = np.arange(128, dtype=np.int64)
    ang = 2.0 * np.pi * ((c[:, None] * c[None, :]) % 128).astype(np.float64) / 128
    C["cs128"] = np.concatenate([np.cos(ang), -np.sin(ang)], axis=1).astype(np.float32).astype(bf)
    pos = np.arange(NLAT)
    row = (pos // 64).astype(np.float64)
    colp = (pos % 64).astype(np.float64)
    inv = 10000.0 ** (-np.arange(32, dtype=np.float64) / 32)
    rc = np.zeros((128, NLAT), np.float64)
    rs = np.zeros((128, NLAT), np.float64)
    for d in range(128):
        axis, ab, pr = d // 64, (d % 64) // 32, d % 32
        a = (row if axis == 0 else colp) * inv[pr]
        rc[d] = np.cos(a)
        rs[d] = np.sin(a) * (-1.0 if ab == 0 else 1.0)
    C["ropeC"] = rc.astype(np.float32)
    C["ropeS"] = rs.astype(np.float32)
    j = np.arange(128)
    am = np.zeros((128, 3, 128), np.float32)
    am[:, 0, :] = (j[:, None] >= j[None, :])
    am[:, 1, :] = (j[:, None] <= j[None, :])
    am[:, 2, :] = np.eye(128)
    C["amask"] = am.astype(bf)
    j = np.arange(64)
    rwm = np.zeros((64, 5, 64), np.float32)
    rwm[:, 0, :] = (j[:, None] < j[None, :])
    rwm[:, 1, :] = (j[:, None] <= j[None, :])
    rwm[:, 2, :] = (j[None, :] < j[:, None])
    rwm[:, 3, :] = (j[None, :] <= j[:, None])
    rwm[:, 4, :] = np.eye(64)
    C["rwm2"] = np.ascontiguousarray(np.concatenate([rwm, rwm], axis=0))
    C["ident128"] = np.eye(128, dtype=np.float32)
    rm = np.ones((128, 512), np.float32)
    rm[:, ::64] = 0.0
    C["rmask"] = rm
    _CONST = C
    return C


CONST_SHAPES = {"rwm2": ([128, 5, 64], F32), "ident128": ([128, 128], F32), "rmask": ([128, 512], F32), "cosL": ([NLAT, NLAT], BF16), "sinL": ([NLAT, NLAT], BF16), "cosC": ([NCTX, NCTX], BF16), "sinC": ([NCTX, NCTX], BF16),
                "cs128": ([128, 256], BF16), "ropeC": ([128, NLAT], F32), "ropeS": ([128, NLAT], F32), "amask": ([128, 3, 128], BF16)}


def phase_fnet(G, l):
    nc, S = G.nc, G.S
    pv = G.PT.rearrange("(c p) t -> p c t", p=128)
    mv = G.MIX.rearrange("(c p) t -> p c t", p=128)
    with ExitStack() as st:
        sb = lambda name, shape, dt=F32, s_=st: G.sb(name, shape, dt, s_)
        pst = lambda name: T(st.enter_context(nc.psum_tensor(name + "_%d" % G.nuid(), [128, 512], F32)))
        AT = sb("fn_AT", [128, 34, 4, 256], BF16)
        cs = sb("fn_cs", [128, 256], BF16)
        S.dma("sp", cs[:], G.C["cs128"][:, :], w=[cs])
        ps = [pst("fnp%d" % i) for i in range(6)]
        ps_ss = pst("fnpss")
        with ExitStack() as st1:
            zf = [sb("fn_zf%d" % i, [128, TOK], F32, st1) for i in range(2)]
            zb = sb("fn_zb", [128, 4, TOK], BF16, st1)
            for g in range(4):
                S.dma("sp", zf[g % 2][:], pv[:, C_F + g, :], w=[zf[g % 2]])
                S.op("act" if g % 2 == 0 else "dve", (lambda e, g=g: e.activation(out=zb[:, g, :], in_=zf[g % 2][:], func=AF.Copy)) if g % 2 == 0
                     else (lambda e, g=g: e.tensor_copy(out=zb[:, g, :], in_=zf[g % 2][:])), r=[zf[g % 2]], w=[zb])
            for tb in range(34):
                pa, pb = ps[(2 * tb) % 6], ps[(2 * tb + 1) % 6]
                for g in range(4):
                    p_ = pa if g < 2 else pb
                    S.op("pe", lambda e, g=g, tb=tb, p_=p_: e.matmul(p_[:, (g % 2) * 256:(g % 2) * 256 + 256], lhsT=zb[:, g, tb * 128:(tb + 1) * 128],
                                                                 rhs=cs[:, :], start=True, stop=True), r=[zb, cs], w=[p_])
                S.op("act", lambda e, tb=tb, pa=pa: e.activation(out=AT[:, tb, 0:2, :], in_=pa[:, :].rearrange("p (g c) -> p g c", g=2), func=AF.Copy),
                     r=[pa], w=[AT])
                S.op("dve", lambda e, tb=tb, pb=pb: e.tensor_copy(out=AT[:, tb, 2:4, :], in_=pb[:, :].rearrange("p (g c) -> p g c", g=2)),
                     r=[pb], w=[AT])
            S.barrier()
        tc_ = [sb("fn_tc%d" % i, [128, 16, 512], BF16) for i in range(2)]
        ts_ = [sb("fn_ts%d" % i, [128, 16, 512], BF16) for i in range(2)]
        fo = [sb("fn_fo%d" % i, [128, 4, 512]) for i in range(2)]
        sq = sb("fn_sq", [128, 4, 512])
        rstd = sb("fn_rstd", [128, 512])
        ob = [sb("fn_ob%d" % i, [128, 4, 512], BF16) for i in range(2)]
        it = 0
        for (base, L, ntb, tb0, KW, ctab, stab) in ((0, NCTX, 2, 0, 256, "cosC", "sinC"), (NCTX, NLAT, 32, 2, 512, "cosL", "sinL")):
            cv = G.C[ctab].rearrange("(tb p) k -> p tb k", p=128)
            sv = G.C[stab].rearrange("(tb p) k -> p tb k", p=128)
            scale = float(1.0 / np.sqrt(L * 128.0))
            for kt in range(L // KW):
                f_, o_ = fo[kt % 2], ob[kt % 2]
                for half in range((ntb + 15) // 16):
                    nb = min(16, ntb - half * 16)
                    a, b = tc_[it % 2], ts_[it % 2]
                    it += 1
                    S.dma("sp", a[:, 0:nb, 0:KW], cv[:, half * 16:half * 16 + nb, kt * KW:(kt + 1) * KW], w=[a])
                    S.dma("sp", b[:, 0:nb, 0:KW], sv[:, half * 16:half * 16 + nb, kt * KW:(kt + 1) * KW], w=[b])
                    for i in range(nb):
                        tb = half * 16 + i
                        for g in range(4):
                            S.op("pe", lambda e, g=g, i=i, tb=tb, a=a: e.matmul(ps[g][:, 0:KW], lhsT=AT[:, tb0 + tb, g, 0:128], rhs=a[:, i, 0:KW],
                                                                         start=(tb == 0), stop=False), r=[AT, a], w=[ps[g]])
                            S.op("pe", lambda e, g=g, i=i, tb=tb, b=b: e.matmul(ps[g][:, 0:KW], lhsT=AT[:, tb0 + tb, g, 128:256], rhs=b[:, i, 0:KW],
                                                                         start=False, stop=(tb == ntb - 1)), r=[AT, b], w=[ps[g]])
                for g in range(4):
                    S.op("act", lambda e, g=g, f_=f_: e.activation(out=f_[:, g, 0:KW], in_=ps[g][:, 0:KW], func=AF.Copy, scale=scale),
                         r=[ps[g]], w=[f_])
                rms_modulate(G, f_, KW, lambda c: pcol(G, l, "fg", c), None, lambda c: (o_[:, c, 0:KW], [o_]), sq, rstd, ps_ss, nch=4, dim=512)
                S.dma("sp", mv[:, 0:4, base + kt * KW:base + (kt + 1) * KW], o_[:, :, 0:KW], r=[o_])


def phase_attn(G, l):
    nc, S = G.nc, G.S
    pv = G.PT.rearrange("(c p) t -> p c t", p=128)
    mv = G.MIX.rearrange("(c p) t -> p c t", p=128)
    SCALE = 128.0 ** -0.5
    with ExitStack() as st:
        sb = lambda name, shape, dt=F32, s_=st: G.sb(name, shape, dt, s_)
        pst = lambda name: T(st.enter_context(nc.psum_tensor(name + "_%d" % G.nuid(), [128, 512], F32)))
        QR = sb("at_QR", [128, 8, TOK], BF16)
        KR = sb("at_KR", [128, 2, TOK], BF16)
        VT = sb("at_VT", [128, 34, 2, 128], BF16)
        am = sb("at_am", [128, 3, 128], BF16)
        esink = sb("at_es", [128, 8])
        S.dma("sp", am[:], G.C["amask"][:, :, :], w=[am])
        S.op("act", lambda e: e.activation(out=esink[:], in_=pcol(G, l, "sink", 0, 8), func=AF.Exp), r=[G.par], w=[esink])
        pT = T(st.enter_context(nc.psum_tensor("at_pT_%d" % G.nuid(), [128, 1024], BF16)))
        with ExitStack() as st1:
            rc = sb("at_rc", [128, NLAT], F32, st1)
            rs = sb("at_rs", [128, NLAT], F32, st1)
            S.dma("sp", rc[:], G.C["ropeC"][:, :], w=[rc])
            S.dma("sp", rs[:], G.C["ropeS"][:, :], w=[rs])
            qf = [sb("at_qf%d" % i, [128, 512], F32, st1) for i in range(2)]
            qs = [sb("at_qs%d" % i, [128, 512], F32, st1) for i in range(2)]
            t1 = [sb("at_t1%d" % i, [128, 512], F32, st1) for i in range(2)]
            t2 = [sb("at_t2%d" % i, [128, 512], F32, st1) for i in range(2)]
            vb = [sb("at_vb%d" % i, [128, 512], BF16, st1) for i in range(2)]
            it = 0
            for ch in range(10):
                src = C_Q + ch if ch < 8 else C_AK + (ch - 8)
                dst = (lambda a, b: QR[:, ch, a:b]) if ch < 8 else (lambda a, b: KR[:, ch - 8, a:b])
                dT = QR if ch < 8 else KR
                for ti, (t0, W) in enumerate(TILES):
                    q_, s_, a_, b_ = qf[it % 2], qs[it % 2], t1[it % 2], t2[it % 2]
                    it += 1
                    S.dma("sp", q_[:, 0:W], pv[:, src, t0:t0 + W], w=[q_])
                    if ti == 0:
                        S.op("act", lambda e, q_=q_, W=W, dst=dst, t0=t0: e.activation(out=dst(t0, t0 + W), in_=q_[:, 0:W], func=AF.Copy), r=[q_], w=[dT])
                        continue
                    for blk in range(4):
                        sp = (blk ^ 1) * 32
                        S.dma("sp", s_[blk * 32:(blk + 1) * 32, 0:W], G.PT[src * 128 + sp:src * 128 + sp + 32, t0:t0 + W], w=[s_])
                    p0 = t0 - NCTX
                    S.op("dve", lambda e, q_=q_, a_=a_, p0=p0, W=W: e.tensor_tensor(out=a_[:, 0:W], in0=q_[:, 0:W], in1=rc[:, p0:p0 + W], op=ALU.mult),
                         r=[q_, rc], w=[a_])
                    S.op("pool", lambda e, s_=s_, b_=b_, p0=p0, W=W: e.tensor_tensor(out=b_[:, 0:W], in0=s_[:, 0:W], in1=rs[:, p0:p0 + W], op=ALU.mult),
                         r=[s_, rs], w=[b_])
                    S.op("dve", lambda e, a_=a_, b_=b_, W=W, dst=dst, t0=t0: e.tensor_tensor(out=dst(t0, t0 + W), in0=a_[:, 0:W], in1=b_[:, 0:W], op=ALU.add),
                         r=[a_, b_], w=[dT])
            for g in range(2):
                for ti, (t0, W) in enumerate(TILES):
                    q_, v_ = qf[it % 2], vb[it % 2]
                    it += 1
                    S.dma("sp", q_[:, 0:W], pv[:, C_AV + g, t0:t0 + W], w=[q_])
                    S.op("act", lambda e, q_=q_, v_=v_, W=W: e.activation(out=v_[:, 0:W], in_=q_[:, 0:W], func=AF.Copy), r=[q_], w=[v_])
                    nb = W // 128
                    for i in range(nb):
                        S.op("pe", lambda e, i=i, v_=v_: e.transpose(pT[:, i * 128:(i + 1) * 128], v_[:, i * 128:(i + 1) * 128], am[:, 2, :]),
                             r=[v_, am], w=[pT])
                    b0 = t0 // 128
                    S.op("dve", lambda e, g=g, b0=b0, nb=nb: e.tensor_copy(out=VT[:, b0:b0 + nb, g, :],
                                                                     in_=pT[:, 0:nb * 128].rearrange("p (b d) -> p b d", b=nb)), r=[pT], w=[VT])
            S.barrier()
        ao = [sb("at_ao%d" % i, [128, 8, 512]) for i in range(2)]
        sq = sb("at_sq", [128, 8, 512])
        rstd = sb("at_rstd", [128, 512])
        ob = [sb("at_ob%d" % i, [128, 8, 512], BF16) for i in range(2)]
        pts = [sb("at_pt%d" % i, [128, 4, 128], BF16) for i in range(4)]
        den = [sb("at_den%d" % i, [128, 4, 128]) for i in range(2)]
        ps_s = [pst("at_ps%d" % i) for i in range(3)]
        ps_n = [pst("at_pn%d" % i) for i in range(2)]
        ps_d = [pst("at_pd%d" % i) for i in range(2)]
        it = 0
        ib = 0
        for ti, (t0, W) in enumerate(TILES):
            a_, o_ = ao[ti % 2], ob[ti % 2]
            for g in range(2):
                for n in range(W // 128):
                    q0 = t0 + n * 128
                    gb = q0 // 128
                    kbs = [(0, None), (1, None)]
                    if ti > 0:
                        if gb > 2:
                            kbs.append((gb - 1, 0))
                        kbs.append((gb, None))
                        if gb < 33:
                            kbs.append((gb + 1, 1))
                    pn, pd = ps_n[ib % 2], ps_d[ib % 2]
                    dn = den[ib % 2]
                    ib += 1
                    for ki, (kb, mk) in enumerate(kbs):
                        p_s, pt = ps_s[it % 3], pts[it % 4]
                        it += 1
                        S.op("pe", lambda e, kb=kb, p_s=p_s, q0=q0, g=g: e.matmul(p_s[:, :].rearrange("p (h q) -> p h q", h=4), lhsT=KR[:, g, kb * 128:(kb + 1) * 128],
                                                                          rhs=QR[:, 4 * g:4 * g + 4, q0:q0 + 128], start=True, stop=True), r=[KR, QR], w=[p_s])
                        S.op("act", lambda e, p_s=p_s, pt=pt: e.activation(out=pt[:], in_=p_s[:, :].rearrange("p (h q) -> p h q", h=4), func=AF.Exp, scale=SCALE),
                             r=[p_s], w=[pt])
                        if mk is not None:
                            S.op("dve", lambda e, pt=pt, mk=mk: e.tensor_tensor(out=pt[:], in0=pt[:], in1=am[:, mk:mk + 1, :].broadcast_to([128, 4, 128]), op=ALU.mult),
                                 r=[pt, am], w=[pt])
                        S.op("pe", lambda e, kb=kb, pt=pt, pn=pn, ki=ki, g=g: e.matmul(pn[:, :].rearrange("p (h q) -> p h q", h=4), lhsT=VT[:, kb, g, :], rhs=pt[:],
                                                                             start=(ki == 0), stop=(ki == len(kbs) - 1)), r=[VT, pt], w=[pn])
                        S.op("pe", lambda e, pt=pt, pd=pd, ki=ki: e.matmul(pd[:, :].rearrange("p (h q) -> p h q", h=4), lhsT=G.onesb[:], rhs=pt[:],
                                                                       start=(ki == 0), stop=(ki == len(kbs) - 1)), r=[G.onesb, pt], w=[pd])
                    S.op("dve", lambda e, pd=pd, dn=dn, g=g: e.tensor_tensor(out=dn[:], in0=pd[:, :].rearrange("p (h q) -> p h q", h=4),
                                                                       in1=esink[:, 4 * g:4 * g + 4].unsqueeze(2).broadcast_to([128, 4, 128]), op=ALU.add),
                         r=[pd, esink], w=[dn])
                    S.op("dve", lambda e, dn=dn: e.reciprocal(out=dn[:], in_=dn[:]), r=[dn], w=[dn])
                    S.op("dve", lambda e, pn=pn, dn=dn, a_=a_, g=g, n=n: e.tensor_tensor(out=a_[:, 4 * g:4 * g + 4, n * 128:(n + 1) * 128],
                                                                                 in0=pn[:, :].rearrange("p (h q) -> p h q", h=4), in1=dn[:], op=ALU.mult),
                         r=[pn, dn], w=[a_])
            rms_modulate(G, a_, W, lambda c: pcol(G, l, "ag", c), None, lambda c: (o_[:, c, 0:W], [o_]), sq, rstd, ps_s[0], nch=8, dim=1024)
            S.dma("sp", mv[:, 8:16, t0:t0 + W], o_[:, :, 0:W], r=[o_])


LD = 0.6065306597126334
NCHK = TOK // 64
SCW = 256
DER_N = 6


def phase_rwkv(G, l):
    S = G.S
    rwkv_r1(G, l)
    S.barrier()
    rwkv_r2(G, l)
    S.barrier()
    rwkv_r3(G, l)


def rwkv_r1(G, l):
    nc, S = G.nc, G.S
    pv = G.PT.rearrange("(c p) t -> p c t", p=128)
    with ExitStack() as st:
        sb = lambda name, shape, dt=F32, s_=st: G.sb(name, shape, dt, s_)
        pst = lambda name: T(st.enter_context(nc.psum_tensor(name + "_%d" % G.nuid(), [128, 512], F32)))
        w2t = sb("r1_w2", [128, 512]); a2t = sb("r1_a2", [128, 512]); g2t = sb("r1_g2", [128, 512])
        S.dma("sp", w2t[:], G.w2r[l], w=[w2t]); S.dma("sp", a2t[:], G.a2r[l], w=[a2t]); S.dma("sp", g2t[:], G.g2[l], w=[g2t])
        rmask = sb("r1_rm", [128, 512])
        S.dma("sp", rmask[:], G.C["rmask"][:, :], w=[rmask])
        bones = sb("r1_bo", [128, 128])
        S.op("dve", lambda e: e.memset(bones[:], 0.0), w=[bones])
        S.op("dve", lambda e: e.memset(bones[0:64, 0:64], 1.0), w=[bones])
        S.op("dve", lambda e: e.memset(bones[64:128, 64:128], 1.0), w=[bones])
        mmc = sb("r1_mmc", [128, 15]); omka = sb("r1_omka", [128, 4])
        S.op("dve", lambda e: e.tensor_tensor(out=mmc[:], in0=pcol(G, l, "mu0", 0, 15), in1=pcol(G, l, "mu1", 0, 15), op=ALU.add), r=[G.par], w=[mmc])
        S.op("dve", lambda e: e.tensor_scalar(out=mmc[:], in0=mmc[:], scalar1=-1.0, scalar2=1.0, op0=ALU.mult, op1=ALU.add), r=[mmc], w=[mmc])
        S.op("dve", lambda e: e.tensor_scalar(out=omka[:], in0=pcol(G, l, "ka", 0, 4), scalar1=-1.0, scalar2=1.0, op0=ALU.mult, op1=ALU.add), r=[G.par], w=[omka])
        rh = [sb("r1_rh%d" % i, [128, 514]) for i in range(3)]
        psx = sb("r1_psx", [128, 15, 512])
        tw = sb("r1_tw", [128, 512]); sgd = sb("r1_sgd", [128, 512])
        NT = 12
        tp = [sb("r1_t%d" % i, [128, 512]) for i in range(NT)]
        fixed = {n: sb("r1_f" + n, [128, 512]) for n in ("kk0", "sq", "kk", "kds", "bq")}
        ob = [sb("r1_o%d" % i, [128, 512]) for i in range(8)]
        ptt = [sb("r1_pt%d" % i, [128, 8]) for i in range(2)]
        ps = [pst("r1p%d" % i) for i in range(6)]
        cnt = {"t": 0, "o": 0, "p": 0, "rh": 0}

        def tmp():
            cnt["t"] += 1
            return tp[cnt["t"] % NT]

        def otile():
            cnt["o"] += 1
            return ob[cnt["o"] % 8]

        def psn():
            cnt["p"] += 1
            return ps[cnt["p"] % 6]

        def tt(eng, out, a, b, op, r, w):
            S.op(eng, lambda e: e.tensor_tensor(out=out, in0=a, in1=b, op=op), r=r, w=w)

        for ti, (t0, W) in enumerate(TILES):
            nck = W // 64
            lz = any(t0 == a for a, b in SEQS)
            rz = any(t0 + W == b for a, b in SEQS)
            lo = t0 - (0 if lz else 1)
            hi = t0 + W + (0 if rz else 1)
            for ci in range(15):
                cnt["rh"] += 1
                h = rh[cnt["rh"] % 3]
                S.dma("sp", h[:, (1 if lz else 0):(1 if lz else 0) + hi - lo], pv[:, C_R + ci, lo:hi], w=[h])
                if lz:
                    S.op("dve", lambda e, h=h: e.memset(h[:, 0:1], 0.0), w=[h])
                if rz:
                    S.op("dve", lambda e, h=h, W=W: e.memset(h[:, W + 1:W + 2], 0.0), w=[h])
                S.op("act", lambda e, h=h, ci=ci, W=W: e.activation(out=psx[:, ci, 0:W], in_=h[:, 1:W + 1], func=AF.Identity, scale=mmc[:, ci:ci + 1]),
                     r=[h, mmc], w=[psx])
                S.op("dve", lambda e, h=h, ci=ci, W=W: e.scalar_tensor_tensor(out=psx[:, ci, 0:W], in0=h[:, 0:W], scalar=pcol(G, l, "mu0", ci),
                                                                        in1=psx[:, ci, 0:W], op0=ALU.mult, op1=ALU.add), r=[h, psx, G.par], w=[psx])
                S.op("dve", lambda e, h=h, ci=ci, W=W: e.scalar_tensor_tensor(out=psx[:, ci, 0:W], in0=h[:, 2:W + 2], scalar=pcol(G, l, "mu1", ci),
                                                                        in1=psx[:, ci, 0:W], op0=ALU.mult, op1=ALU.add), r=[h, psx, G.par], w=[psx])
            S.op("act", lambda e, W=W: e.activation(out=tw[:, 0:W], in_=psx[:, 13, 0:W], func=AF.Tanh), r=[psx], w=[tw])
            S.op("act", lambda e, W=W: e.activation(out=sgd[:, 0:W], in_=psx[:, 4, 0:W], func=AF.Sigmoid), r=[psx], w=[sgd])
            for hc in range(4):
                r_, k_, v_ = psx[:, hc, 0:W], psx[:, 5 + hc, 0:W], psx[:, 9 + hc, 0:W]
                rows = slice(hc * 128, (hc + 1) * 128)
                p_ = psn()
                S.op("pe", lambda e, p_=p_, hc=hc, W=W: e.matmul(p_[:, 0:W], lhsT=g2t[:, hc * 128:(hc + 1) * 128], rhs=sgd[:, 0:W], start=True, stop=True),
                     r=[g2t, sgd], w=[p_])
                o = otile()
                S.op("act", lambda e, p_=p_, o=o, W=W: e.activation(out=o[:, 0:W], in_=p_[:, 0:W], func=AF.Copy), r=[p_], w=[o])
                S.dma("sp", G.GT[rows, t0:t0 + W], o[:, 0:W], r=[o])
                o = otile()
                S.op("act", lambda e, o=o, v_=v_, W=W: e.activation(out=o[:, 0:W], in_=v_, func=AF.Copy), r=[psx], w=[o])
                S.dma("sp", G.VS[rows, t0:t0 + W], o[:, 0:W], r=[o])
                kk0, sq, kk = fixed["kk0"], fixed["sq"], fixed["kk"]
                S.op("act", lambda e, kk0=kk0, k_=k_, hc=hc, W=W: e.activation(out=kk0[:, 0:W], in_=k_, func=AF.Identity, scale=pcol(G, l, "kks", hc)),
                     r=[psx, G.par], w=[kk0])
                S.op("act", lambda e, kk0=kk0, sq=sq, W=W: e.activation(out=sq[:, 0:W], in_=kk0[:, 0:W], func=AF.Square), r=[kk0], w=[sq])
                p_ = psn()
                S.op("pe", lambda e, p_=p_, sq=sq, W=W: e.matmul(p_[:, 0:W], lhsT=bones[:], rhs=sq[:, 0:W], start=True, stop=True), r=[bones, sq], w=[p_])
                S.op("act", lambda e, p_=p_, sq=sq, W=W: e.activation(out=sq[:, 0:W], in_=p_[:, 0:W], func=AF.Sqrt), r=[p_], w=[sq])
                S.op("dve", lambda e, sq=sq, W=W: e.tensor_scalar(out=sq[:, 0:W], in0=sq[:, 0:W], scalar1=1e-12, scalar2=None, op0=ALU.max), r=[sq], w=[sq])
                S.op("dve", lambda e, sq=sq, W=W: e.reciprocal(out=sq[:, 0:W], in_=sq[:, 0:W]), r=[sq], w=[sq])
                tt("dve", kk[:, 0:W], kk0[:, 0:W], sq[:, 0:W], ALU.mult, [kk0, sq], [kk])
                kds = fixed["kds"]
                for dr in range(2):
                    cnt["t"] = 0
                    prt = slice(dr * 64, (dr + 1) * 64)
                    pw, pa = psn(), psn()
                    S.op("pe", lambda e, pw=pw, dr=dr, hc=hc, W=W, prt=prt: e.matmul(pw[:, 0:W], lhsT=w2t[prt, hc * 128:(hc + 1) * 128], rhs=tw[prt, 0:W], start=True, stop=True),
                         r=[w2t, tw], w=[pw])
                    S.op("pe", lambda e, pa=pa, dr=dr, hc=hc, W=W, prt=prt: e.matmul(pa[:, 0:W], lhsT=a2t[prt, hc * 128:(hc + 1) * 128], rhs=psx[prt, 14, 0:W], start=True, stop=True),
                         r=[a2t, psx], w=[pa])
                    sg = tmp(); a_ = tmp()
                    S.op("act", lambda e, pw=pw, sg=sg, W=W, dr=dr, hc=hc: e.activation(out=sg[:, 0:W], in_=pw[:, 0:W], func=AF.Sigmoid, bias=pcol(G, l, "w0", dr * 4 + hc)),
                         r=[pw, G.par], w=[sg])
                    S.op("act", lambda e, pa=pa, a_=a_, W=W, dr=dr, hc=hc: e.activation(out=a_[:, 0:W], in_=pa[:, 0:W], func=AF.Sigmoid, bias=pcol(G, l, "a0", dr * 4 + hc)),
                         r=[pa, G.par], w=[a_])
                    kd = tmp(); nb = tmp()
                    S.op("act", lambda e, a_=a_, kd=kd, W=W, hc=hc: e.activation(out=kd[:, 0:W], in_=a_[:, 0:W], func=AF.Identity, scale=pcol(G, l, "ka", hc), bias=omka[:, hc:hc + 1]),
                         r=[a_, G.par, omka], w=[kd])
                    tt("dve", kd[:, 0:W], kd[:, 0:W], k_, ALU.mult, [kd, psx], [kd])
                    if dr == 0:
                        S.op("dve", lambda e, kds=kds, kd=kd, W=W: e.tensor_copy(out=kds[:, 0:W], in_=kd[:, 0:W]), r=[kd], w=[kds])
                    else:
                        tt("dve", kds[:, 0:W], kds[:, 0:W], kd[:, 0:W], ALU.add, [kds, kd], [kds])
                    S.op("dve", lambda e, a_=a_, nb=nb, kk=kk, W=W: e.scalar_tensor_tensor(out=nb[:, 0:W], in0=a_[:, 0:W], scalar=-1.0, in1=kk[:, 0:W],
                                                                                 op0=ALU.mult, op1=ALU.mult), r=[a_, kk], w=[nb])
                    c_ = tmp()
                    S.op("dve", lambda e, c_=c_, sg=sg, W=W: e.tensor_tensor_scan(out=c_[:, 0:W], data0=rmask[:, 0:W], data1=sg[:, 0:W], initial=0.0,
                                                                            op0=ALU.mult, op1=ALU.add), r=[rmask, sg], w=[c_])
                    c3 = c_[:, 0:W].rearrange("p (c t) -> p c t", t=64)
                    if dr == 1:
                        d_ = tmp()
                        tt("dve", d_[:, 0:W], sg[:, 0:W], c_[:, 0:W], ALU.subtract, [sg, c_], [d_])
                        tt("dve", d_[:, 0:W].rearrange("p (c t) -> p c t", t=64), d_[:, 0:W].rearrange("p (c t) -> p c t", t=64),
                           c3[:, :, 63:64].broadcast_to([128, nck, 64]), ALU.add, [d_, c_], [d_])
                        c_ = d_
                        c3 = c_[:, 0:W].rearrange("p (c t) -> p c t", t=64)
                        endi = 0
                    else:
                        endi = 63
                    e1 = tmp(); e2 = tmp(); e3 = tmp(); e4 = tmp()
                    S.op("act", lambda e, c_=c_, e1=e1, W=W: e.activation(out=e1[:, 0:W], in_=c_[:, 0:W], func=AF.Exp, scale=-LD), r=[c_], w=[e1])
                    S.op("act", lambda e, c_=c_, e2=e2, W=W: e.activation(out=e2[:, 0:W], in_=c_[:, 0:W], func=AF.Exp, scale=LD), r=[c_], w=[e2])
                    tt("dve", e3[:, 0:W], c_[:, 0:W], sg[:, 0:W], ALU.subtract, [c_, sg], [e3])
                    S.op("act", lambda e, e3=e3, W=W: e.activation(out=e3[:, 0:W], in_=e3[:, 0:W], func=AF.Exp, scale=-LD), r=[e3], w=[e3])
                    tt("dve", e4[:, 0:W].rearrange("p (c t) -> p c t", t=64), c3[:, :, endi:endi + 1].broadcast_to([128, nck, 64]), c3, ALU.subtract, [c_], [e4])
                    S.op("act", lambda e, e4=e4, W=W: e.activation(out=e4[:, 0:W], in_=e4[:, 0:W], func=AF.Exp, scale=-LD), r=[e4], w=[e4])
                    pt_ = ptt[(hc * 2 + dr) % 2]
                    S.op("dve", lambda e, pt_=pt_, e1=e1, W=W, endi=endi, nck=nck: e.tensor_copy(
                        out=pt_[:, 0:nck].unsqueeze(2), in_=e1[:, 0:W].rearrange("p (c t) -> p c t", t=64)[:, :, endi:endi + 1]), r=[e1], w=[pt_])
                    S.dma("sp", G.PTOT[dr][rows, t0 // 64:t0 // 64 + nck], pt_[:, 0:nck], r=[pt_])
                    for ai, (x_, y_, xd, yd) in enumerate(((kk, e3, kk, e3), (None, e1, psx, e1), (kd, e2, kd, e2), (nb, e2, nb, e2), (kd, e4, kd, e4), (nb, e4, nb, e4))):
                        o = otile()
                        xin_ = r_ if x_ is None else x_[:, 0:W]
                        tt("dve", o[:, 0:W], xin_, y_[:, 0:W], ALU.mult, [xd, yd], [o])
                        S.dma("sp", G.DER[dr][ai][rows, t0:t0 + W], o[:, 0:W], r=[o])
                bq = fixed["bq"]
                S.op("dve", lambda e, bq=bq, r_=r_, kds=kds, hc=hc, W=W: e.scalar_tensor_tensor(out=bq[:, 0:W], in0=r_, scalar=pcol(G, l, "rk", hc), in1=kds[:, 0:W],
                                                                                      op0=ALU.mult, op1=ALU.mult), r=[psx, kds, G.par], w=[bq])
                p_ = psn()
                S.op("pe", lambda e, p_=p_, bq=bq, W=W: e.matmul(p_[:, 0:W], lhsT=bones[:], rhs=bq[:, 0:W], start=True, stop=True), r=[bones, bq], w=[p_])
                o = otile()
                tt("dve", o[:, 0:W], p_[:, 0:W], v_, ALU.mult, [p_, psx], [o])
                S.dma("sp", G.BON[rows, t0:t0 + W], o[:, 0:W], r=[o])


class PV:
    def __init__(self, ap, d):
        self.ap = ap
        self.d = d


def rwkv_r2(G, l, NCH=8):
    nc, S = G.nc, G.S
    SC2 = 128
    with ExitStack() as st:
        sb = lambda name, shape, dt=F32, s_=st: G.sb(name, shape, dt, s_)
        banks = [st.enter_context(nc.psum_tensor("r2b%d_%d" % (i, G.nuid()), [128, 512], F32)) for i in range(8)]
        bdeps = [Dep(x=True) for i in range(8)]
        rwm = sb("r2_rwm", [128, 5, 64])
        S.dma("sp", rwm[:], G.C["rwm2"][:, :, :], w=[rwm])
        i128 = sb("r2_i128", [128, 128])
        S.dma("sp", i128[:], G.C["ident128"][:, :], w=[i128])
        identS = rwm[:, 4, :]
        m12 = [rwm[:, 0:2, :], rwm[:, 2:4, :]]
        mX = [rwm[:, 2, :], rwm[:, 0, :]]
        order = [list(range(NCHK)), [3, 2, 1, 0] + list(range(NCHK - 1, 3, -1))]
        BI = {0: 0, 2: 1, 3: 2, 4: 3, 5: 4, 6: 5}

        class Chain:
            pass
        chains = []
        for i in range(NCH):
            c = Chain()
            c.curS = sb("r2_cs%d" % i, [128, 7, SC2])
            c.curB = sb("r2_cb%d" % i, [128, 6, 2, 128])
            c.tmB = sb("r2_tm%d" % i, [128, 2, 128]); c.vB = sb("r2_vb%d" % i, [128, 128]); c.vS = sb("r2_vs%d" % i, [128, 64])
            c.AM1 = sb("r2_am1_%d" % i, [128, 2, 64]); c.AM2 = sb("r2_am2_%d" % i, [128, 2, 64]); c.AkB = sb("r2_akb%d" % i, [128, 128])
            c.XS = [sb("r2_xs%d_%d" % (i, j), [128, 64]) for j in range(2)]
            c.XB = [sb("r2_xb%d_%d" % (i, j), [128, 128]) for j in range(2)]
            c.WS = [sb("r2_ws%d_%d" % (i, j), [128, 2, 64]) for j in range(2)]
            c.LB = [sb("r2_lb%d_%d" % (i, j), [128, 128]) for j in range(2)]
            c.TiS = sb("r2_tis%d" % i, [128, 64]); c.TiB = sb("r2_tib%d" % i, [128, 128])
            c.Z = sb("r2_z%d" % i, [128, 64]); c.US = sb("r2_us%d" % i, [128, 64]); c.UB = sb("r2_ub%d" % i, [128, 128])
            c.HS = sb("r2_hs%d" % i, [128, 64]); c.HB = sb("r2_hb%d" % i, [128, 128])
            c.ys = sb("r2_ys%d" % i, [128, SC2])
            c.pt = sb("r2_ptot%d" % i, [128, NCHK])
            c.regs = [PV(banks[i][:, k * 128:(k + 1) * 128], bdeps[i]) for k in range(4)]
            c.ri = 0
            for t_ in [c.curB, c.AkB, c.TiB, c.UB] + c.XB + c.LB:
                S.op("pool", lambda e, t_=t_: e.memset(t_[:], 0.0), w=[t_])
            chains.append(c)

        def prc(c):
            c.ri += 1
            return c.regs[c.ri % 4]

        def cp(eng, out, in_, r, w):
            if eng == "act":
                S.op("act", lambda e: e.activation(out=out, in_=in_, func=AF.Copy), r=r, w=w)
            else:
                S.op(eng, lambda e: e.tensor_copy(out=out, in_=in_), r=r, w=w)

        def bd(dstT, src_ap, srcT):
            for h in range(2):
                cp("pool", dstT[64 * h:64 * h + 64, 64 * h:64 * h + 64], src_ap[64 * h:64 * h + 64, :], [srcT], [dstT])

        def mm(out_pv, out_ap, lhsT, rhs, start, stop, r):
            S.op("pe", lambda e: e.matmul(out_ap, lhsT=lhsT, rhs=rhs, start=start, stop=stop), r=r, w=[out_pv])

        a3 = lambda ap: ap.rearrange("p (a t) -> p a t", a=2)
        npair_groups = (4 * 2) // NCH
        for pg in range(npair_groups):
            for i, c in enumerate(chains):
                c.dr = i % 2
                c.pair = pg * (NCH // 2) + i // 2
                c.rows = slice(c.pair * 128, (c.pair + 1) * 128)
                S.op("dve", lambda e, c=c: e.memset(c.HS[:], 0.0), w=[c.HS])
                S.op("pool", lambda e, c=c: e.memset(c.HB[:], 0.0), w=[c.HB])
                S.dma("sp", c.pt[:], G.PTOT[c.dr][c.rows, :], w=[c.pt])
            for step in range(NCHK):
                for c in chains:
                    c.ck = order[c.dr][step]
                    c.cs = c.ck % 2
                    c.cols = slice(c.cs * 64, c.cs * 64 + 64)
                    if step % 2 == 0:
                        sc = c.ck // 2
                        tk = slice(sc * SC2, (sc + 1) * SC2)
                        for ai in range(6):
                            S.dma("sp", c.curS[:, ai, :], G.DER[c.dr][ai][c.rows, tk], w=[c.curS])
                        S.dma("sp", c.curS[:, 6, :], G.VS[c.rows, tk], w=[c.curS])
                        for ai, bi in BI.items():
                            srcT = G.VS if ai == 6 else G.DER[c.dr][ai]
                            for h in range(2):
                                S.dma("sp", c.curB[64 * h:64 * h + 64, bi, :, 64 * h:64 * h + 64],
                                      srcT[c.pair * 128 + 64 * h:c.pair * 128 + 64 * h + 64, tk].rearrange("p (c t) -> p c t", c=2), w=[c.curB])
                kS = lambda c, ai: c.curS[:, ai, c.cols]
                kB = lambda c, ai: c.curB[:, BI[ai], c.cs, :]
                for c in chains:
                    c.p1, c.p2, c.p3 = prc(c), prc(c), prc(c)
                    for p_, ai in ((c.p1, 4), (c.p2, 5), (c.p3, 6)):
                        S.op("pe", lambda e, c=c, p_=p_, ai=ai: e.transpose(p_.ap[:, :], kB(c, ai), i128[:]), r=[c.curB, i128], w=[p_])
                for c in chains:
                    cp("act", c.tmB[:, 0, :], c.p1.ap[:, :], [c.p1], [c.tmB])
                    cp("dve", c.tmB[:, 1, :], c.p2.ap[:, :], [c.p2], [c.tmB])
                    cp("act", c.vB[:], c.p3.ap[:, :], [c.p3], [c.vB])
                    for h in range(2):
                        cp("dve", c.vS[64 * h:64 * h + 64, :], c.p3.ap[64 * h:64 * h + 64, 64 * h:64 * h + 64], [c.p3], [c.vS])
                for c in chains:
                    c.p1, c.p2, c.p3 = prc(c), prc(c), prc(c)
                    kr = c.curS[:, 0:2, c.cols]
                    mm(c.p1, a3(c.p1.ap[:, :]), kB(c, 2), kr, True, True, [c.curB, c.curS])
                    mm(c.p2, a3(c.p2.ap[:, :]), kB(c, 3), kr, True, True, [c.curB, c.curS])
                    mm(c.p3, c.p3.ap[:, 0:64], kB(c, 0), kS(c, 3), True, True, [c.curB, c.curS])
                for c in chains:
                    S.op("dve", lambda e, c=c: e.tensor_tensor(out=c.AM1[:], in0=a3(c.p1.ap[:, :]), in1=m12[c.dr], op=ALU.mult), r=[c.p1, rwm], w=[c.AM1])
                    S.op("dve", lambda e, c=c: e.tensor_tensor(out=c.AM2[:], in0=a3(c.p2.ap[:, :]), in1=m12[c.dr], op=ALU.mult), r=[c.p2, rwm], w=[c.AM2])
                    S.op("dve", lambda e, c=c: e.tensor_tensor(out=c.XS[0][:], in0=c.p3.ap[:, 0:64], in1=mX[c.dr], op=ALU.mult), r=[c.p3, rwm], w=[c.XS[0]])
                for c in chains:
                    bd(c.AkB, c.AM1[:, 0, :], c.AM1)
                    bd(c.LB[0], c.AM2[:, 0, :], c.AM2)
                    bd(c.XB[0], c.XS[0][:, :], c.XS[0])
                for c in chains:
                    c.p1, c.p2 = prc(c), prc(c)
                    mm(c.p1, c.p1.ap[:, 0:64], c.XB[0][:], c.AM2[:, 0, :], True, True, [c.XB[0], c.AM2])
                    mm(c.p2, c.p2.ap[:, 0:64], c.LB[0][:], c.XS[0][:], True, True, [c.LB[0], c.XS[0]])
                for c in chains:
                    cp("act", c.WS[1][:, 0, :], c.p1.ap[:, 0:64], [c.p1], [c.WS[1]])
                    S.op("dve", lambda e, c=c: e.tensor_tensor(out=c.WS[1][:, 1, :], in0=c.AM2[:, 0, :], in1=identS, op=ALU.add), r=[c.AM2, rwm], w=[c.WS[1]])
                    cp("act", c.XS[1][:], c.p2.ap[:, 0:64], [c.p2], [c.XS[1]])
                for c in chains:
                    bd(c.LB[1], c.WS[1][:, 0, :], c.WS[1])
                    bd(c.XB[1], c.XS[1][:, :], c.XS[1])
                for k in range(1, 5):
                    a, b = k % 2, (k + 1) % 2
                    for c in chains:
                        c.p1, c.p2 = prc(c), prc(c)
                        mm(c.p1, a3(c.p1.ap[:, :]), c.XB[a][:], c.WS[a][:], True, False, [c.XB[a], c.WS[a]])
                        mm(c.p1, c.p1.ap[:, 64:128], i128[:], c.WS[a][:, 1, :], False, True, [i128, c.WS[a]])
                        mm(c.p2, c.p2.ap[:, 0:64], c.LB[a][:], c.XS[a][:], True, True, [c.LB[a], c.XS[a]])
                    for c in chains:
                        cp("act", c.WS[b][:], a3(c.p1.ap[:, :]), [c.p1], [c.WS[b]])
                        cp("dve", c.XS[b][:], c.p2.ap[:, 0:64], [c.p2], [c.XS[b]])
                    for c in chains:
                        if k < 4:
                            bd(c.LB[b], c.WS[b][:, 0, :], c.WS[b])
                        bd(c.XB[b], c.XS[b][:, :], c.XS[b])
                for c in chains:
                    c.p1 = prc(c)
                    mm(c.p1, c.p1.ap[:, 0:64], c.XB[1][:], c.WS[1][:, 1, :], True, False, [c.XB[1], c.WS[1]])
                    mm(c.p1, c.p1.ap[:, 0:64], i128[:], c.WS[1][:, 1, :], False, True, [i128, c.WS[1]])
                for c in chains:
                    cp("act", c.TiS[:], c.p1.ap[:, 0:64], [c.p1], [c.TiS])
                for c in chains:
                    bd(c.TiB, c.TiS[:, :], c.TiS)
                for c in chains:
                    c.p1 = prc(c)
                    mm(c.p1, c.p1.ap[:, 0:64], kB(c, 0), c.HS[:], True, False, [c.curB, c.HS])
                    mm(c.p1, c.p1.ap[:, 0:64], c.AkB[:], c.vS[:], False, True, [c.AkB, c.vS])
                for c in chains:
                    cp("act", c.Z[:], c.p1.ap[:, 0:64], [c.p1], [c.Z])
                for c in chains:
                    c.p1 = prc(c)
                    mm(c.p1, c.p1.ap[:, 0:64], c.TiB[:], c.Z[:], True, True, [c.TiB, c.Z])
                for c in chains:
                    cp("dve", c.US[:], c.p1.ap[:, 0:64], [c.p1], [c.US])
                for c in chains:
                    bd(c.UB, c.US[:, :], c.US)
                for c in chains:
                    c.p1, c.p2 = prc(c), prc(c)
                    mm(c.p1, c.p1.ap[:, 0:64], c.HB[:], kS(c, 1), True, False, [c.HB, c.curS])
                    mm(c.p1, c.p1.ap[:, 0:64], c.vB[:], c.AM1[:, 1, :], False, False, [c.vB, c.AM1])
                    mm(c.p1, c.p1.ap[:, 0:64], c.UB[:], c.AM2[:, 1, :], False, True, [c.UB, c.AM2])
                    mm(c.p2, c.p2.ap[:, 0:64], c.tmB[:, 0, :], c.vS[:], True, False, [c.tmB, c.vS])
                    mm(c.p2, c.p2.ap[:, 0:64], c.tmB[:, 1, :], c.US[:], False, True, [c.tmB, c.US])
                for c in chains:
                    cp("act", c.ys[:, c.cols], c.p1.ap[:, 0:64], [c.p1], [c.ys])
                    S.op("dve", lambda e, c=c: e.scalar_tensor_tensor(out=c.HS[:], in0=c.HS[:], scalar=c.pt[:, c.ck:c.ck + 1], in1=c.p2.ap[:, 0:64],
                                                                  op0=ALU.mult, op1=ALU.add), r=[c.HS, c.pt, c.p2], w=[c.HS])
                for c in chains:
                    bd(c.HB, c.HS[:, :], c.HS)
                if step % 2 == 1:
                    for c in chains:
                        sc = c.ck // 2
                        S.dma("sp", G.YD[c.dr][c.rows, sc * SC2:(sc + 1) * SC2], c.ys[:], r=[c.ys])


def rwkv_r3(G, l):
    nc, S = G.nc, G.S
    mv = G.MIX.rearrange("(c p) t -> p c t", p=128)
    with ExitStack() as st:
        sb = lambda name, shape, dt=F32, s_=st: G.sb(name, shape, dt, s_)
        pst = lambda name: T(st.enter_context(nc.psum_tensor(name + "_%d" % G.nuid(), [128, 512], F32)))
        bo = sb("r3_bo", [128, 128])
        S.op("dve", lambda e: e.memset(bo[:], 0.0), w=[bo])
        S.op("dve", lambda e: e.memset(bo[0:64, 0:64], 1.0 / 64), w=[bo])
        S.op("dve", lambda e: e.memset(bo[64:128, 64:128], 1.0 / 64), w=[bo])
        lnx = sb("r3_eps", [128, 1])
        S.op("dve", lambda e: e.memset(lnx[:], 64e-5), w=[lnx])
        ya = [sb("r3_ya%d" % i, [128, 512]) for i in range(2)]
        yb = [sb("r3_yb%d" % i, [128, 512]) for i in range(2)]
        bn = [sb("r3_bn%d" % i, [128, 512]) for i in range(2)]
        gg = [sb("r3_gg%d" % i, [128, 512]) for i in range(2)]
        t1 = [sb("r3_t1%d" % i, [128, 512]) for i in range(2)]
        t2 = [sb("r3_t2%d" % i, [128, 512]) for i in range(2)]
        ob = [sb("r3_ob%d" % i, [128, 512], BF16) for i in range(2)]
        ps = [pst("r3p%d" % i) for i in range(4)]
        it = 0
        for ti, (t0, W) in enumerate(TILES):
            for hc in range(4):
                rows = slice(hc * 128, (hc + 1) * 128)
                a, b, n_, g_, x1, x2, o = ya[it % 2], yb[it % 2], bn[it % 2], gg[it % 2], t1[it % 2], t2[it % 2], ob[it % 2]
                pm, pvv = ps[(2 * it) % 4], ps[(2 * it + 1) % 4]
                it += 1
                S.dma("sp", a[:, 0:W], G.YD[0][rows, t0:t0 + W], w=[a])
                S.dma("sp", b[:, 0:W], G.YD[1][rows, t0:t0 + W], w=[b])
                S.dma("sp", n_[:, 0:W], G.BON[rows, t0:t0 + W], w=[n_])
                S.dma("sp", g_[:, 0:W], G.GT[rows, t0:t0 + W], w=[g_])
                S.op("dve", lambda e, a=a, b=b, W=W: e.tensor_tensor(out=a[:, 0:W], in0=a[:, 0:W], in1=b[:, 0:W], op=ALU.add), r=[a, b], w=[a])
                S.op("pe", lambda e, a=a, pm=pm, W=W: e.matmul(pm[:, 0:W], lhsT=bo[:], rhs=a[:, 0:W], start=True, stop=True), r=[bo, a], w=[pm])
                S.op("dve", lambda e, a=a, pm=pm, x1=x1, W=W: e.tensor_tensor(out=x1[:, 0:W], in0=a[:, 0:W], in1=pm[:, 0:W], op=ALU.subtract), r=[a, pm], w=[x1])
                S.op("act", lambda e, x1=x1, x2=x2, W=W: e.activation(out=x2[:, 0:W], in_=x1[:, 0:W], func=AF.Square), r=[x1], w=[x2])
                S.op("pe", lambda e, x2=x2, pvv=pvv, W=W: e.matmul(pvv[:, 0:W], lhsT=bo[:], rhs=x2[:, 0:W], start=True, stop=True), r=[bo, x2], w=[pvv])
                S.op("act", lambda e, x2=x2, pvv=pvv, W=W: e.activation(out=x2[:, 0:W], in_=pvv[:, 0:W], func=AF.Sqrt, bias=lnx[:, 0:1]), r=[pvv, lnx], w=[x2])
                S.op("dve", lambda e, x2=x2, W=W: e.reciprocal(out=x2[:, 0:W], in_=x2[:, 0:W]), r=[x2], w=[x2])
                S.op("dve", lambda e, x1=x1, x2=x2, W=W: e.tensor_tensor(out=x1[:, 0:W], in0=x1[:, 0:W], in1=x2[:, 0:W], op=ALU.mult), r=[x1, x2], w=[x1])
                S.op("act", lambda e, x1=x1, W=W, hc=hc: e.activation(out=x1[:, 0:W], in_=x1[:, 0:W], func=AF.Identity, scale=pcol(G, l, "lnw", hc), bias=pcol(G, l, "lnb", hc)),
                     r=[x1, G.par], w=[x1])
                S.op("dve", lambda e, x1=x1, n_=n_, W=W: e.tensor_tensor(out=x1[:, 0:W], in0=x1[:, 0:W], in1=n_[:, 0:W], op=ALU.add), r=[x1, n_], w=[x1])
                S.op("dve", lambda e, x1=x1, g_=g_, o=o, W=W: e.tensor_tensor(out=o[:, 0:W], in0=x1[:, 0:W], in1=g_[:, 0:W], op=ALU.mult), r=[x1, g_], w=[o])
                S.dma("sp", mv[:, 4 + hc, t0:t0 + W], o[:, 0:W], r=[o])

def phase_mixers(G, l):
    S = G.S
    phase_fnet(G, l)
    S.barrier()
    phase_attn(G, l)
    S.barrier()
    phase_rwkv(G, l)
    S.barrier()


EXTRA_W = ("w2r", "a2r", "g2")


def extra_w(inp, k):
    if k == "w2r":
        return np.ascontiguousarray(np.asarray(inp["rwkv_w2"], np.float32).reshape(DEPTH, 128, 512))
    if k == "a2r":
        return np.ascontiguousarray(np.asarray(inp["rwkv_a2"], np.float32).reshape(DEPTH, 128, 512))
    return np.ascontiguousarray(np.asarray(inp["rwkv_g2"], np.float32))


def make_inputs(inp, b):
    x = np.asarray(inp["x"][b], np.float32)
    cx = np.asarray(inp["ctx"][b], np.float32)
    xin = np.ascontiguousarray(np.concatenate([cx, x], axis=0).T)
    cvec = np.concatenate([_col(inp["c"][b]), _col(inp["c_ctx"])], axis=1)
    return {"xin": xin, "cvec": np.ascontiguousarray(cvec)}


def kernel(**inp):
    nc, G = build_nc()
    shared = {"params": pack_params(inp)}
    shared.update(const_inputs())
    for k in ("w_ada", "w_in", "w_out", "w_ffn_in", "w_ffn_out"):
        shared[k] = np.ascontiguousarray(np.asarray(inp[k], np.float32))
    for k in EXTRA_W:
        shared[k] = extra_w(inp, k)
    in_maps = []
    for b in range(8):
        m = dict(shared)
        m.update(make_inputs(inp, b))
        in_maps.append(m)
    res = run_bass_kernel_spmd(nc, in_maps, core_ids=list(range(8)))
    out = np.stack([np.ascontiguousarray(r["out"].T) for r in res.results], axis=0)
    return out.astype(np.float32)
```

```python
import numpy as np
from contextlib import ExitStack
import ml_dtypes
import concourse.bass as bass
import concourse.mybir as mybir
from concourse.bass_utils import run_bass_kernel_spmd

F32 = mybir.dt.float32
F32R = mybir.dt.float32r
BF16 = mybir.dt.bfloat16
AF = mybir.ActivationFunctionType
ALU = mybir.AluOpType
AX = mybir.AxisListType

D = 2048
NCTX = 256
NLAT = 4096
TOK = NCTX + NLAT
DEPTH = 4
INW = 3968
DFF = 5632
TILES = [(0, 256)] + [(256 + 512 * i, 512) for i in range(8)]
SEQS = [(0, NCTX), (NCTX, TOK)]
RMS_EPS = 1e-6

C_F, C_Q, C_R, C_G, C_K, C_V, C_WD, C_AD, C_AK, C_AV = 0, 4, 12, 16, 17, 21, 25, 26, 27, 29
NPC = 31

PCOLS = {}
_off = 0
for _n, _w in [("n1", 16), ("n2", 16), ("mu0", 15), ("mu1", 15), ("w0", 8), ("a0", 8), ("kks", 4), ("ka", 4), ("rk", 4),
               ("lnw", 4), ("lnb", 4), ("fg", 4), ("ag", 8), ("cw0", 44), ("cw1", 44), ("cw2", 44), ("cb", 44),
               ("bada", 96), ("sink", 8)]:
    PCOLS[_n] = (_off, _w)
    _off += _w
PL = _off
P_FINAL = DEPTH * PL
NPAR = P_FINAL + 16


def _col(v):
    v = np.asarray(v, np.float32).reshape(-1)
    return np.ascontiguousarray(v.reshape(-1, 128).T)


def pack_params(inp):
    P = np.zeros((128, NPAR), np.float32)
    for l in range(DEPTH):
        def put(name, arr):
            o, w = PCOLS[name]
            P[:, l * PL + o: l * PL + o + w] = arr
        put("n1", _col(inp["norm1_g"][l])); put("n2", _col(inp["norm2_g"][l]))
        put("mu0", _col(inp["rwkv_mu"][l][0])); put("mu1", _col(inp["rwkv_mu"][l][1]))
        put("w0", _col(inp["rwkv_w0"][l])); put("a0", _col(inp["rwkv_a0"][l]))
        put("kks", _col(inp["rwkv_kk_scale"][l])); put("ka", _col(inp["rwkv_ka"][l])); put("rk", _col(inp["rwkv_rk"][l]))
        put("lnw", _col(inp["rwkv_lnx_w"][l])); put("lnb", _col(inp["rwkv_lnx_b"][l]))
        put("fg", _col(inp["fourier_out_g"][l])); put("ag", _col(inp["attn_out_g"][l]))
        for j in range(3):
            put("cw%d" % j, _col(inp["ffn_conv_w"][l][j]))
        put("cb", _col(inp["ffn_conv_b"][l])); put("bada", _col(inp["b_ada"][l]))
        put("sink", np.broadcast_to(np.asarray(inp["attn_sink"][l], np.float32)[None, :], (128, 8)))
    P[:, P_FINAL:P_FINAL + 16] = _col(inp["final_g"])
    return P


class Dep:
    __slots__ = ("w", "r", "x")

    def __init__(self, x=False):
        self.w = None
        self.r = {}
        self.x = x


class T:
    def __init__(self, t):
        self.t = t
        self.d = Dep()

    def __getitem__(self, idx):
        return self.t[idx]


def _d(x):
    return getattr(x, "d", x)


class Sched:
    def __init__(self, nc, es):
        self.nc = nc
        self.E = {"pe": nc.tensor, "dve": nc.vector, "act": nc.scalar, "pool": nc.gpsimd, "sp": nc.sync}
        self.csem = {k: es.enter_context(nc.semaphore("cs_" + k)) for k in ("pe", "dve", "act", "pool")}
        self.cnt = {k: 0 for k in self.csem}
        self.seen = {k: {} for k in self.E}
        self.dq = {}
        for q, n in (("sp", 16), ("pool", 8), ("act", 8)):
            self.dq[q] = dict(sems=[es.enter_context(nc.semaphore("d_%s%d" % (q, i))) for i in range(n)],
                              cnt=[0] * n, nxt=0)
        self.nins = 0

    def _wait(self, e, tok):
        if tok is None:
            return
        key, sem, val = tok
        if e == "pe" and key == "pe":
            return
        if self.seen[e].get(key, 0) >= val:
            return
        self.E[e].wait_ge(sem, val)
        self.seen[e][key] = val

    def _deps(self, e, r, w):
        for d in r:
            self._wait(e, _d(d).w)
        for d in w:
            d = _d(d)
            self._wait(e, d.w)
            for t in list(d.r.values()):
                self._wait(e, t)

    def _mark(self, tok, r, w):
        for d in r:
            _d(d).r[tok[0]] = tok
        for d in w:
            d = _d(d)
            d.w = tok
            d.r = {}

    def op(self, e, fn, r=(), w=()):
        xs = [d for d in r if _d(d).x]
        if xs:
            r = [d for d in r if not _d(d).x]
            w = list(w) + xs
        self._deps(e, r, w)
        ins = fn(self.E[e])
        self.cnt[e] += 1
        ins.then_inc(self.csem[e], 1)
        self._mark((e, self.csem[e], self.cnt[e]), r, w)
        self.nins += 1

    def dma(self, q, out, in_, r=(), w=(), **kw):
        Q = self.dq[q]
        i = Q["nxt"]
        Q["nxt"] = (i + 1) % len(Q["sems"])
        key = (q, i)
        if Q["cnt"][i]:
            self._wait(q, (key, Q["sems"][i], 16 * Q["cnt"][i]))
        self._deps(q, r, w)
        ins = self.E[q].dma_start(out=out, in_=in_, **kw)
        Q["cnt"][i] += 1
        ins.then_inc(Q["sems"][i], 16)
        self._mark((key, Q["sems"][i], 16 * Q["cnt"][i]), r, w)
        self.nins += 1

    def barrier(self, engines=("pe", "dve", "act", "pool", "sp")):
        for e in engines:
            for k in self.csem:
                if self.cnt[k]:
                    self._wait(e, (k, self.csem[k], self.cnt[k]))
            for q, Q in self.dq.items():
                for i, s in enumerate(Q["sems"]):
                    if Q["cnt"][i]:
                        self._wait(e, ((q, i), s, 16 * Q["cnt"][i]))


class Ctx:
    pass


def build_nc(nl=DEPTH, dbg=None):
    nc = bass.Bass("TRN2", target_bir_lowering=False)
    G = Ctx()
    G.nc = nc
    G.dbg = dbg
    dt_in = lambda name, shape, dt=F32: nc.dram_tensor(name, shape, dt, kind="ExternalInput").ap()
    dt_sc = lambda name, shape, dt=F32: nc.dram_tensor(name, shape, dt, kind="Internal").ap()
    G.xin = dt_in("xin", [D, TOK])
    G.cvec = dt_in("cvec", [128, 32])
    G.params = dt_in("params", [128, NPAR])
    G.w_ada = dt_in("w_ada", [DEPTH, D, 6 * D])
    G.w_in = dt_in("w_in", [DEPTH, D, INW])
    G.w_out = dt_in("w_out", [DEPTH, D, D])
    G.w_ffn_in = dt_in("w_ffn_in", [DEPTH, D, 2 * DFF])
    G.w_ffn_out = dt_in("w_ffn_out", [DEPTH, DFF, D])
    G.out = nc.dram_tensor("out", [D, NLAT], F32, kind="ExternalOutput").ap()
    G.XT = dt_sc("XT", [D, TOK])
    G.PT = dt_sc("PT", [NPC * 128, TOK])
    G.MIX = dt_sc("MIX", [D, TOK], BF16)
    G.U2 = dt_sc("U2", [D, TOK], BF16)
    if dbg and "mixin" in dbg:
        G.mixin = dt_in("mixin", [D, TOK])
    G.C = {k: dt_in(k, sh, dt) for k, (sh, dt) in CONST_SHAPES.items()}
    G.w2r = dt_in("w2r", [DEPTH, 128, 512])
    G.a2r = dt_in("a2r", [DEPTH, 128, 512])
    G.g2 = dt_in("g2", [DEPTH, 128, 512])
    G.GT = dt_sc("GT", [512, TOK])
    G.VS = dt_sc("VS", [512, TOK])
    G.BON = dt_sc("BON", [512, TOK])
    G.PTOT = [dt_sc("PTOT%d" % d, [512, NCHK]) for d in range(2)]
    G.DER = [[dt_sc("DER%d_%d" % (d, a), [512, TOK]) for a in range(DER_N)] for d in range(2)]
    G.YD = [dt_sc("YD%d" % d, [512, TOK]) for d in range(2)]
    G.wb_in = [dt_sc("wb_in%d" % l, [D, INW], BF16) for l in range(nl)]
    G.wb_out = [dt_sc("wb_out%d" % l, [D, D], BF16) for l in range(nl)]
    G.wb_fi = [dt_sc("wb_fi%d" % l, [D, 2 * DFF], BF16) for l in range(nl)]
    G.wb_fo = [dt_sc("wb_fo%d" % l, [DFF, D], BF16) for l in range(nl)]
    if dbg:
        G.dbg_out = {}
        for name, shape in dbg.items():
            if name == "mixin":
                continue
            G.dbg_out[name] = nc.dram_tensor("dbg_" + name, shape, F32, kind="ExternalOutput").ap()

    with ExitStack() as es:
        S = Sched(nc, es)
        G.S = S
        G.es = es
        G.uid = 0

        def nuid():
            G.uid += 1
            return G.uid
        G.nuid = nuid

        def sb(name, shape, dt=F32, st=es):
            G.uid += 1
            return T(st.enter_context(nc.sbuf_tensor("%s_%d" % (name, G.uid), shape, dt)))
        G.sb = sb
        G.par = sb("par", [128, NPAR])
        G.mod = sb("mod", [128, nl * 96 * 2])
        G.ones = sb("ones_f", [128, 128])
        G.onesb = sb("ones_b", [128, 128], BF16)
        G.wcast = Dep()
        G.epsc = sb("epsc", [128, 1])
        S.op("dve", lambda e: e.memset(G.epsc[:], RMS_EPS), w=[G.epsc])
        G.lin_it = 0
        G.ps_it = 0
        G.stg_it = 0
        G.xc_it = 0
        S.dma("sp", G.par[:], G.params[:, :], w=[G.par])
        S.op("dve", lambda e: e.memset(G.ones[:], 1.0), w=[G.ones])
        S.op("dve", lambda e: e.memset(G.onesb[:], 1.0), w=[G.onesb])
        for l in range(nl):
            for src, dst, K in ((G.w_in, G.wb_in, D), (G.w_out, G.wb_out, D), (G.w_ffn_in, G.wb_fi, D), (G.w_ffn_out, G.wb_fo, DFF)):
                for kc in range(K // 128):
                    S.dma("pool", dst[l][kc * 128:(kc + 1) * 128, :], src[l, kc * 128:(kc + 1) * 128, :], w=[G.wcast],
                          max_dma_last_dim=4096)
        xd = Dep()
        for c in range(16):
            S.dma("sp", G.XT[c * 128:(c + 1) * 128, :], G.xin[c * 128:(c + 1) * 128, :], w=[xd])
        prologue_adaln(G, nl)
        S.barrier()
        if dbg and "mod" in dbg:
            S.dma("sp", G.dbg_out["mod"][:, :], G.mod[:], r=[G.mod])
        for l in range(nl):
            layer(G, l)
        final_norm(G)
        S.barrier()
    G.nins = S.nins
    return nc, G


def pcol(G, l, name, c=0, n=1):
    o, w = PCOLS[name]
    return G.par[:, l * PL + o + c: l * PL + o + c + n]


def mcol(G, l, idx, c, which):
    j = ((l * 96) + idx * 16 + c) * 2 + which
    return G.mod[:, j:j + 1]


def prologue_adaln(G, nl):
    nc, S = G.nc, G.S
    with ExitStack() as st:
        sb = lambda name, shape, dt=F32: G.sb(name, shape, dt, st)
        cv = sb("cv", [128, 32])
        s2 = sb("s2", [128, 32])
        wts = [sb("wada%d" % i, [128, 16, 512]) for i in range(2)]
        ps = [T(st.enter_context(nc.psum_tensor("ps_ada%d" % i, [128, 512], F32))) for i in range(4)]
        S.dma("sp", cv[:], G.cvec[:, :], w=[cv])
        S.op("act", lambda e: e.activation(out=s2[:].rearrange("p (k w) -> p w k", w=2),
                                           in_=cv[:].rearrange("p (w k) -> p w k", w=2), func=AF.Silu), r=[cv], w=[s2])
        it = 0
        for l in range(nl):
            wv = G.w_ada[l].rearrange("(kc p) n -> p kc n", p=128)
            for cg in range(24):
                wt = wts[it % 2]
                S.dma("sp", wt[:], wv[:, :, cg * 512:(cg + 1) * 512], w=[wt])
                for j in range(4):
                    ch = cg * 4 + j
                    p_ = ps[(it * 4 + j) % 4]
                    for kc in range(16):
                        S.op("pe", lambda e, kc=kc, j=j, p_=p_, wt=wt: e.matmul(
                            p_[:, 0:2], lhsT=wt[:, kc, j * 128:(j + 1) * 128], rhs=s2[:, kc * 2:kc * 2 + 2],
                            start=(kc == 0), stop=(kc == 15)), r=[wt, s2], w=[p_])
                    o = (l * 96 + ch) * 2
                    S.op("dve", lambda e, p_=p_, o=o, ch=ch, l=l: e.tensor_scalar(
                        out=G.mod[:, o:o + 2], in0=p_[:, 0:2], scalar1=pcol(G, l, "bada", ch), scalar2=None, op0=ALU.add),
                        r=[p_, G.par], w=[G.mod])
                it += 1


def rms_modulate(G, xt, W, scale_col, bias_col, out_fn, sq, rstd, ps_ss, nch=16, dim=D, ones=None):
    S = G.S
    ones = ones or G.ones
    S.op("act", lambda e: e.activation(out=sq[:, 0:nch, 0:W], in_=xt[:, 0:nch, 0:W], func=AF.Square), r=[xt], w=[sq])
    for c in range(nch):
        S.op("pe", lambda e, c=c: e.matmul(ps_ss[:, 0:W], lhsT=ones[:], rhs=sq[:, c, 0:W], start=(c == 0), stop=(c == nch - 1)),
             r=[sq, ones], w=[ps_ss])
    S.op("act", lambda e: e.activation(out=rstd[:, 0:W], in_=ps_ss[:, 0:W], func=AF.Sqrt, scale=1.0 / dim, bias=G.epsc[:, 0:1]),
         r=[ps_ss, G.epsc], w=[rstd])
    S.op("dve", lambda e: e.reciprocal(out=rstd[:, 0:W], in_=rstd[:, 0:W]), r=[rstd], w=[rstd])
    S.op("dve", lambda e: e.tensor_tensor(out=sq[:, 0:nch, 0:W], in0=xt[:, 0:nch, 0:W],
                                          in1=rstd[:, 0:W].unsqueeze(1).broadcast_to([128, nch, W]), op=ALU.mult),
         r=[xt, rstd], w=[sq])
    for c in range(nch):
        o, od = out_fn(c)
        b = bias_col(c) if bias_col is not None else 0.0
        S.op("act", lambda e, c=c, o=o, b=b: e.activation(out=o, in_=sq[:, c, 0:W], func=AF.Identity, scale=scale_col(c), bias=b),
             r=[sq, G.par, G.mod] + list(od), w=od)


def linear(G, act, KC, W, wdram, n_oc, epilogue, ps, wts, gsz=4, col0=0, a0=0):
    S = G.S
    wv = wdram.rearrange("(kc p) n -> p kc n", p=128)
    ng = (n_oc + gsz - 1) // gsz
    st = G.lin_it
    for g in range(ng):
        wt = wts[(st + g) % len(wts)]
        n = min(gsz, n_oc - g * gsz)
        S.dma("sp", wt[:, 0:KC, 0:n * 128], wv[:, :, col0 + g * gsz * 128: col0 + (g * gsz + n) * 128], w=[wt])
        for j in range(n):
            oc = g * gsz + j
            p_ = ps[G.ps_it % len(ps)]
            G.ps_it += 1
            for kc in range(KC):
                S.op("pe", lambda e, kc=kc, j=j, p_=p_, wt=wt: e.matmul(
                    p_[:, 0:W], lhsT=wt[:, kc, j * 128:(j + 1) * 128], rhs=act[:, kc, a0:a0 + W],
                    start=(kc == 0), stop=(kc == KC - 1)), r=[wt, act], w=[p_])
            epilogue(oc, p_)
    G.lin_it += ng


def gmod_cols(G, l, gm, nname, sc_idx):
    S = G.S
    mv = G.mod[:, l * 192:(l + 1) * 192].rearrange("p (i c w) -> p i c w", i=6, c=16)
    o, _ = PCOLS[nname]
    for which in range(2):
        S.op("dve", lambda e, which=which: e.scalar_tensor_tensor(
            out=gm[:, which * 16:(which + 1) * 16], in0=mv[:, sc_idx, :, which], scalar=1.0,
            in1=G.par[:, l * PL + o:l * PL + o + 16], op0=ALU.add, op1=ALU.mult), r=[G.mod, G.par], w=[gm])


def phase_norm_proj(G, l):
    nc, S = G.nc, G.S
    with ExitStack() as st:
        sb = lambda name, shape, dt=F32: G.sb(name, shape, dt, st)
        pst = lambda name: T(st.enter_context(nc.psum_tensor(name + "_%d" % G.nuid(), [128, 512], F32)))
        xts = [sb("np_x%d" % i, [128, 16, 512]) for i in range(2)]
        sq = sb("np_sq", [128, 16, 512])
        rstd = sb("np_rstd", [128, 512])
        us = [sb("np_u%d" % i, [128, 16, 512], BF16) for i in range(2)]
        wts = [sb("np_w%d" % i, [128, 16, 512], BF16) for i in range(3)]
        stg = [sb("np_stg%d" % i, [128, 4, 512]) for i in range(2)]
        gm = sb("np_gm", [128, 32])
        ps_ss = pst("np_pss")
        ps = [pst("np_ps%d" % i) for i in range(6)]
        gmod_cols(G, l, gm, "n1", 1)
        xv = G.XT.rearrange("(c p) t -> p c t", p=128)
        pv = G.PT.rearrange("(c p) t -> p c t", p=128)
        def do_norm(ti):
            t0, W = TILES[ti]
            which = 1 if ti == 0 else 0
            xt, u = xts[ti % 2], us[ti % 2]
            S.dma("sp", xt[:, :, 0:W], xv[:, :, t0:t0 + W], w=[xt])
            rms_modulate(G, xt, W, lambda c: gm[:, which * 16 + c:which * 16 + c + 1], lambda c: mcol(G, l, 0, c, which),
                         lambda c: (u[:, c, 0:W], [u]), sq, rstd, ps_ss)
        do_norm(0)
        for ti, (t0, W) in enumerate(TILES):
            u = us[ti % 2]
            if ti + 1 < len(TILES):
                do_norm(ti + 1)
            state = {"k": 0}

            def epi(oc, p_, t0=t0, W=W):
                k = G.stg_it
                sg = stg[(k // 4) % 2]
                j = oc % 4
                eng = "act" if oc % 2 == 0 else "dve"
                if eng == "act":
                    S.op("act", lambda e: e.activation(out=sg[:, j, 0:W], in_=p_[:, 0:W], func=AF.Copy), r=[p_], w=[sg])
                else:
                    S.op("dve", lambda e: e.tensor_copy(out=sg[:, j, 0:W], in_=p_[:, 0:W]), r=[p_], w=[sg])
                G.stg_it += 1
                if j == 3 or oc == NPC - 1:
                    o0 = oc - j
                    S.dma("sp", pv[:, o0:oc + 1, t0:t0 + W], sg[:, 0:j + 1, 0:W], r=[sg])
                    G.stg_it = ((G.stg_it + 3) // 4) * 4
            linear(G, u, 16, W, G.wb_in[l], NPC, epi, ps, wts)


def phase_out_proj(G, l):
    nc, S = G.nc, G.S
    with ExitStack() as st:
        sb = lambda name, shape, dt=F32: G.sb(name, shape, dt, st)
        pst = lambda name: T(st.enter_context(nc.psum_tensor(name + "_%d" % G.nuid(), [128, 512], F32)))
        xts = [sb("op_x%d" % i, [128, 16, 512]) for i in range(2)]
        ms = [sb("op_m%d" % i, [128, 16, 512], BF16) for i in range(2)]
        sq = sb("op_sq", [128, 16, 512])
        rstd = sb("op_rstd", [128, 512])
        us = [sb("op_u%d" % i, [128, 16, 512], BF16) for i in range(1)]
        wts = [sb("op_w%d" % i, [128, 16, 512], BF16) for i in range(2)]
        gm = sb("op_gm", [128, 32])
        ps_ss = pst("op_pss")
        ps = [pst("op_ps%d" % i) for i in range(6)]
        gmod_cols(G, l, gm, "n2", 4)
        xv = G.XT.rearrange("(c p) t -> p c t", p=128)
        mv = G.MIX.rearrange("(c p) t -> p c t", p=128)
        uv = G.U2.rearrange("(c p) t -> p c t", p=128)
        for ti, (t0, W) in enumerate(TILES):
            which = 1 if ti == 0 else 0
            xt, u, m = xts[ti % 2], us[0], ms[ti % 2]
            S.dma("sp", xt[:, :, 0:W], xv[:, :, t0:t0 + W], w=[xt])
            S.dma("sp", m[:, :, 0:W], mv[:, :, t0:t0 + W], w=[m])

            def epi(oc, p_, W=W, xt=xt, which=which):
                S.op("dve", lambda e: e.scalar_tensor_tensor(out=xt[:, oc, 0:W], in0=p_[:, 0:W], scalar=mcol(G, l, 2, oc, which),
                                                             in1=xt[:, oc, 0:W], op0=ALU.mult, op1=ALU.add),
                     r=[p_, G.mod, xt], w=[xt])
            linear(G, m, 16, W, G.wb_out[l], 16, epi, ps, wts)
            S.dma("sp", xv[:, :, t0:t0 + W], xt[:, :, 0:W], r=[xt])
            rms_modulate(G, xt, W, lambda c: gm[:, which * 16 + c:which * 16 + c + 1], lambda c: mcol(G, l, 3, c, which),
                         lambda c: (u[:, c, 0:W], [u]), sq, rstd, ps_ss)
            S.dma("sp", uv[:, :, t0:t0 + W], u[:, :, 0:W], r=[u])


def phase_ffn(G, l):
    nc, S = G.nc, G.S
    with ExitStack() as st:
        sb = lambda name, shape, dt=F32: G.sb(name, shape, dt, st)
        pst = lambda name: T(st.enter_context(nc.psum_tensor(name + "_%d" % G.nuid(), [128, 512], F32)))
        uh = [sb("ff_u%d" % i, [128, 16, 514], BF16) for i in range(2)]
        gt = sb("ff_g", [128, 44, 512], BF16)
        wg = [sb("ff_wg%d" % i, [128, 16, 256], BF16) for i in range(2)]
        wu = [sb("ff_wu%d" % i, [128, 16, 256], BF16) for i in range(2)]
        wo = [sb("ff_wo%d" % i, [128, 44, 256], BF16) for i in range(2)]
        xc = [sb("ff_x%d" % i, [128, 512]) for i in range(3)]
        hh = [sb("ff_hh%d" % i, [128, 514]) for i in range(2)]
        tm = [sb("ff_tm%d" % i, [128, 512]) for i in range(2)]
        ge = [sb("ff_ge%d" % i, [128, 512]) for i in range(2)]
        psg = [pst("ff_pg%d" % i) for i in range(2)]
        psh = pst("ff_ph")
        psu = [pst("ff_pu%d" % i) for i in range(2)]
        pso = [pst("ff_po%d" % i) for i in range(3)]
        xv = G.XT.rearrange("(c p) t -> p c t", p=128)
        uv = G.U2.rearrange("(c p) t -> p c t", p=128)
        wiv = G.wb_fi[l].rearrange("(kc p) n -> p kc n", p=128)
        it = 0
        for ti, (t0, W) in enumerate(TILES):
            which = 1 if ti == 0 else 0
            u = uh[ti % 2]
            lz = any(t0 == a for a, b in SEQS)
            rz = any(t0 + W == b for a, b in SEQS)
            lo = t0 - (0 if lz else 1)
            hi = t0 + W + (0 if rz else 1)
            S.dma("sp", u[:, :, (1 if lz else 0):(1 if lz else 0) + hi - lo], uv[:, :, lo:hi], w=[u])
            if lz:
                S.op("dve", lambda e, u=u: e.memset(u[:, :, 0:1], 0.0), w=[u])
            if rz:
                S.op("dve", lambda e, u=u, W=W: e.memset(u[:, :, W + 1:W + 2], 0.0), w=[u])
            for g in range(22):
                a, b = wg[it % 2], wu[it % 2]
                S.dma("sp", a[:], wiv[:, :, g * 256:(g + 1) * 256], w=[a])
                S.dma("sp", b[:], wiv[:, :, DFF + g * 256:DFF + (g + 1) * 256], w=[b])
                it += 1
                for j in range(2):
                    ch = g * 2 + j
                    pg, pu = psg[ch % 2], psu[ch % 2]
                    h, t_, g_ = hh[ch % 2], tm[ch % 2], ge[ch % 2]
                    for kc in range(16):
                        S.op("pe", lambda e, kc=kc, j=j, a=a, pg=pg: e.matmul(pg[:, 0:W], lhsT=a[:, kc, j * 128:(j + 1) * 128],
                                                                         rhs=u[:, kc, 1:W + 1], start=(kc == 0), stop=(kc == 15)),
                             r=[a, u], w=[pg])
                    hs = psh[:, (ch % 8) * 2:(ch % 8) * 2 + 2]
                    for kc in range(16):
                        S.op("pe", lambda e, kc=kc, j=j, a=a, hs=hs: e.matmul(hs, lhsT=a[:, kc, j * 128:(j + 1) * 128],
                                                                         rhs=u[:, kc, 0:W + 2:W + 1], start=(kc == 0), stop=(kc == 15)),
                             r=[a, u], w=[psh])
                    for kc in range(16):
                        S.op("pe", lambda e, kc=kc, j=j, b=b, pu=pu: e.matmul(pu[:, 0:W], lhsT=b[:, kc, j * 128:(j + 1) * 128],
                                                                         rhs=u[:, kc, 1:W + 1], start=(kc == 0), stop=(kc == 15)),
                             r=[b, u], w=[pu])
                    S.op("act", lambda e, h=h, pg=pg: e.activation(out=h[:, 1:W + 1], in_=pg[:, 0:W], func=AF.Copy), r=[pg], w=[h])
                    S.op("act", lambda e, h=h, hs=hs: e.activation(out=h[:, 0:W + 2:W + 1], in_=hs, func=AF.Copy), r=[psh], w=[h])
                    S.op("act", lambda e, h=h, t_=t_, ch=ch: e.activation(out=t_[:, 0:W], in_=h[:, 1:W + 1], func=AF.Identity,
                                                                    scale=pcol(G, l, "cw1", ch), bias=pcol(G, l, "cb", ch)),
                         r=[h, G.par], w=[t_])
                    S.op("dve", lambda e, h=h, t_=t_, ch=ch: e.scalar_tensor_tensor(out=t_[:, 0:W], in0=h[:, 0:W], scalar=pcol(G, l, "cw0", ch),
                                                                             in1=t_[:, 0:W], op0=ALU.mult, op1=ALU.add),
                         r=[h, t_, G.par], w=[t_])
                    S.op("dve", lambda e, h=h, t_=t_, ch=ch: e.scalar_tensor_tensor(out=t_[:, 0:W], in0=h[:, 2:W + 2], scalar=pcol(G, l, "cw2", ch),
                                                                             in1=t_[:, 0:W], op0=ALU.mult, op1=ALU.add),
                         r=[h, t_, G.par], w=[t_])
                    S.op("act", lambda e, t_=t_, g_=g_: e.activation(out=g_[:, 0:W], in_=t_[:, 0:W], func=AF.Gelu), r=[t_], w=[g_])
                    S.op("dve", lambda e, g_=g_, pu=pu, ch=ch: e.tensor_tensor(out=gt[:, ch, 0:W], in0=g_[:, 0:W], in1=pu[:, 0:W], op=ALU.mult),
                         r=[g_, pu], w=[gt])

            def epi(oc, p_, W=W, t0=t0, which=which):
                x_ = xc[G.xc_it % 3]
                G.xc_it += 1
                S.dma("sp", x_[:, 0:W], xv[:, oc, t0:t0 + W], w=[x_])
                S.op("dve", lambda e: e.scalar_tensor_tensor(out=x_[:, 0:W], in0=p_[:, 0:W], scalar=mcol(G, l, 5, oc, which),
                                                             in1=x_[:, 0:W], op0=ALU.mult, op1=ALU.add),
                     r=[p_, G.mod, x_], w=[x_])
                S.dma("sp", xv[:, oc, t0:t0 + W], x_[:, 0:W], r=[x_])
            linear(G, gt, 44, W, G.wb_fo[l], 16, epi, pso, wo, gsz=2)


def final_norm(G):
    nc, S = G.nc, G.S
    with ExitStack() as st:
        sb = lambda name, shape, dt=F32: G.sb(name, shape, dt, st)
        xts = [sb("fn_x%d" % i, [128, 16, 512]) for i in range(2)]
        os_ = [sb("fn_o%d" % i, [128, 16, 512]) for i in range(2)]
        sq = sb("fn_sq", [128, 16, 512])
        rstd = sb("fn_rstd", [128, 512])
        ps_ss = T(st.enter_context(nc.psum_tensor("fin_pss", [128, 512], F32)))
        xv = G.XT.rearrange("(c p) t -> p c t", p=128)
        ov = G.out.rearrange("(c p) t -> p c t", p=128)
        for ti, (t0, W) in enumerate(TILES[1:]):
            xt, o = xts[ti % 2], os_[ti % 2]
            S.dma("sp", xt[:, :, 0:W], xv[:, :, t0:t0 + W], w=[xt])
            rms_modulate(G, xt, W, lambda c: G.par[:, P_FINAL + c:P_FINAL + c + 1], None,
                         lambda c: (o[:, c, 0:W], [o]), sq, rstd, ps_ss)
            S.dma("sp", ov[:, :, t0 - NCTX:t0 - NCTX + W], o[:, :, 0:W], r=[o])


def layer(G, l):
    S = G.S
    dbg = G.dbg or {}
    phase_norm_proj(G, l)
    S.barrier()
    if "PT" in dbg and l == 0:
        S.dma("sp", G.dbg_out["PT"][:, :], G.PT[:, :])
        S.barrier()
    if "mixin" in dbg:
        for c in range(16):
            S.dma("pool", G.MIX[c * 128:(c + 1) * 128, :], G.mixin[c * 128:(c + 1) * 128, :])
    else:
        phase_mixers(G, l)
    S.barrier()
    phase_out_proj(G, l)
    S.barrier()
    phase_ffn(G, l)
    S.barrier()
    if "YD" in dbg and l == 0:
        for d in range(2):
            for hh in range(4):
                S.dma("sp", G.dbg_out["YD"][d * 512 + hh * 128:d * 512 + (hh + 1) * 128, :], G.YD[d][hh * 128:(hh + 1) * 128, :])
        S.barrier()
    if "DER" in dbg and l == 0:
        for d in range(2):
            for a in range(DER_N):
                for hh in range(4):
                    S.dma("sp", G.dbg_out["DER"][(d * DER_N + a) * 512 + hh * 128:(d * DER_N + a) * 512 + (hh + 1) * 128, :], G.DER[d][a][hh * 128:(hh + 1) * 128, :])
        S.barrier()
    if "MIX" in dbg and l == 0:
        for c in range(16):
            S.dma("pool", G.dbg_out["MIX"][c * 128:(c + 1) * 128, :], G.MIX[c * 128:(c + 1) * 128, :])
        S.barrier()
    if "XT" in dbg and l == 0:
        S.dma("sp", G.dbg_out["XT"][:, :], G.XT[:, :])
        S.barrier()


_CONST = None


def const_inputs():
    global _CONST
    if _CONST is not None:
        return _CONST
    bf = ml_dtypes.bfloat16
    t = np.arange(NLAT, dtype=np.int64)
    tk = (t[:, None] * t[None, :]) % NLAT
    ang = 2.0 * np.pi * tk.astype(np.float64) / NLAT
    C = {}
    C["cosL"] = np.cos(ang).astype(np.float32).astype(bf)
    C["sinL"] = np.sin(ang).astype(np.float32).astype(bf)
    t = np.arange(NCTX, dtype=np.int64)
    ang = 2.0 * np.pi * ((t[:, None] * t[None, :]) % NCTX).astype(np.float64) / NCTX
    C["cosC"] = np.cos(ang).astype(np.float32).astype(bf)
    C["sinC"] = np.sin(ang).astype(np.float32).astype(bf)
    c = np.arange(128, dtype=np.int64)
    ang = 2.0 * np.pi * ((c[:, None] * c[None, :]) % 128).astype(np.float64) / 128
    C["cs128"] = np.concatenate([np.cos(ang), -np.sin(ang)], axis=1).astype(np.float32).astype(bf)
    pos = np.arange(NLAT)
    row = (pos // 64).astype(np.float64)
    colp = (pos % 64).astype(np.float64)
    inv = 10000.0 ** (-np.arange(32, dtype=np.float64) / 32)
    rc = np.zeros((128, NLAT), np.float64)
    rs = np.zeros((128, NLAT), np.float64)
    for d in range(128):
        axis, ab, pr = d // 64, (d % 64) // 32, d % 32
        a = (row if axis == 0 else colp) * inv[pr]
        rc[d] = np.cos(a)
        rs[d] = np.sin(a) * (-1.0 if ab == 0 else 1.0)
    C["ropeC"] = rc.astype(np.float32)
    C["ropeS"] = rs.astype(np.float32)
    j = np.arange(128)
    am = np.zeros((128, 3, 128), np.float32)
    am[:, 0, :] = (j[:, None] >= j[None, :])
    am[:, 1, :] = (j[:, None] <= j[None, :])
    am[:, 2, :] = np.eye(128)
    C["amask"] = am.astype(bf)
    j = np.arange(64)
    rwm = np.zeros((64, 5, 64), np.float32)
    rwm[:, 0, :] = (j[:, None] < j[None, :])
    rwm[:, 1, :] = (j[:, None] <= j[None, :])
    rwm[:, 2, :] = (j[None, :] < j[:, None])
    rwm[:, 3, :] = (j[None, :] <= j[:, None])
    rwm[:, 4, :] = np.eye(64)
    C["rwm"] = rwm
    rm = np.ones((128, 512), np.float32)
    rm[:, ::64] = 0.0
    C["rmask"] = rm
    _CONST = C
    return C


CONST_SHAPES = {"rwm": ([64, 5, 64], F32), "rmask": ([128, 512], F32), "cosL": ([NLAT, NLAT], BF16), "sinL": ([NLAT, NLAT], BF16), "cosC": ([NCTX, NCTX], BF16), "sinC": ([NCTX, NCTX], BF16),
                "cs128": ([128, 256], BF16), "ropeC": ([128, NLAT], F32), "ropeS": ([128, NLAT], F32), "amask": ([128, 3, 128], BF16)}


def phase_fnet(G, l):
    nc, S = G.nc, G.S
    pv = G.PT.rearrange("(c p) t -> p c t", p=128)
    mv = G.MIX.rearrange("(c p) t -> p c t", p=128)
    with ExitStack() as st:
        sb = lambda name, shape, dt=F32, s_=st: G.sb(name, shape, dt, s_)
        pst = lambda name: T(st.enter_context(nc.psum_tensor(name + "_%d" % G.nuid(), [128, 512], F32)))
        AT = sb("fn_AT", [128, 34, 4, 256], BF16)
        cs = sb("fn_cs", [128, 256], BF16)
        S.dma("sp", cs[:], G.C["cs128"][:, :], w=[cs])
        ps = [pst("fnp%d" % i) for i in range(6)]
        ps_ss = pst("fnpss")
        with ExitStack() as st1:
            zf = [sb("fn_zf%d" % i, [128, TOK], F32, st1) for i in range(2)]
            zb = sb("fn_zb", [128, 4, TOK], BF16, st1)
            for g in range(4):
                S.dma("sp", zf[g % 2][:], pv[:, C_F + g, :], w=[zf[g % 2]])
                S.op("act" if g % 2 == 0 else "dve", (lambda e, g=g: e.activation(out=zb[:, g, :], in_=zf[g % 2][:], func=AF.Copy)) if g % 2 == 0
                     else (lambda e, g=g: e.tensor_copy(out=zb[:, g, :], in_=zf[g % 2][:])), r=[zf[g % 2]], w=[zb])
            for tb in range(34):
                pa, pb = ps[(2 * tb) % 6], ps[(2 * tb + 1) % 6]
                for g in range(4):
                    p_ = pa if g < 2 else pb
                    S.op("pe", lambda e, g=g, tb=tb, p_=p_: e.matmul(p_[:, (g % 2) * 256:(g % 2) * 256 + 256], lhsT=zb[:, g, tb * 128:(tb + 1) * 128],
                                                                 rhs=cs[:, :], start=True, stop=True), r=[zb, cs], w=[p_])
                S.op("act", lambda e, tb=tb, pa=pa: e.activation(out=AT[:, tb, 0:2, :], in_=pa[:, :].rearrange("p (g c) -> p g c", g=2), func=AF.Copy),
                     r=[pa], w=[AT])
                S.op("dve", lambda e, tb=tb, pb=pb: e.tensor_copy(out=AT[:, tb, 2:4, :], in_=pb[:, :].rearrange("p (g c) -> p g c", g=2)),
                     r=[pb], w=[AT])
            S.barrier()
        tc_ = [sb("fn_tc%d" % i, [128, 16, 512], BF16) for i in range(2)]
        ts_ = [sb("fn_ts%d" % i, [128, 16, 512], BF16) for i in range(2)]
        fo = [sb("fn_fo%d" % i, [128, 4, 512]) for i in range(2)]
        sq = sb("fn_sq", [128, 4, 512])
        rstd = sb("fn_rstd", [128, 512])
        ob = [sb("fn_ob%d" % i, [128, 4, 512], BF16) for i in range(2)]
        it = 0
        for (base, L, ntb, tb0, KW, ctab, stab) in ((0, NCTX, 2, 0, 256, "cosC", "sinC"), (NCTX, NLAT, 32, 2, 512, "cosL", "sinL")):
            cv = G.C[ctab].rearrange("(tb p) k -> p tb k", p=128)
            sv = G.C[stab].rearrange("(tb p) k -> p tb k", p=128)
            scale = float(1.0 / np.sqrt(L * 128.0))
            for kt in range(L // KW):
                f_, o_ = fo[kt % 2], ob[kt % 2]
                for half in range((ntb + 15) // 16):
                    nb = min(16, ntb - half * 16)
                    a, b = tc_[it % 2], ts_[it % 2]
                    it += 1
                    S.dma("sp", a[:, 0:nb, 0:KW], cv[:, half * 16:half * 16 + nb, kt * KW:(kt + 1) * KW], w=[a])
                    S.dma("sp", b[:, 0:nb, 0:KW], sv[:, half * 16:half * 16 + nb, kt * KW:(kt + 1) * KW], w=[b])
                    for i in range(nb):
                        tb = half * 16 + i
                        for g in range(4):
                            S.op("pe", lambda e, g=g, i=i, tb=tb, a=a: e.matmul(ps[g][:, 0:KW], lhsT=AT[:, tb0 + tb, g, 0:128], rhs=a[:, i, 0:KW],
                                                                         start=(tb == 0), stop=False), r=[AT, a], w=[ps[g]])
                            S.op("pe", lambda e, g=g, i=i, tb=tb, b=b: e.matmul(ps[g][:, 0:KW], lhsT=AT[:, tb0 + tb, g, 128:256], rhs=b[:, i, 0:KW],
                                                                         start=False, stop=(tb == ntb - 1)), r=[AT, b], w=[ps[g]])
                for g in range(4):
                    S.op("act", lambda e, g=g, f_=f_: e.activation(out=f_[:, g, 0:KW], in_=ps[g][:, 0:KW], func=AF.Copy, scale=scale),
                         r=[ps[g]], w=[f_])
                rms_modulate(G, f_, KW, lambda c: pcol(G, l, "fg", c), None, lambda c: (o_[:, c, 0:KW], [o_]), sq, rstd, ps_ss, nch=4, dim=512)
                S.dma("sp", mv[:, 0:4, base + kt * KW:base + (kt + 1) * KW], o_[:, :, 0:KW], r=[o_])


def phase_attn(G, l):
    nc, S = G.nc, G.S
    pv = G.PT.rearrange("(c p) t -> p c t", p=128)
    mv = G.MIX.rearrange("(c p) t -> p c t", p=128)
    SCALE = 128.0 ** -0.5
    with ExitStack() as st:
        sb = lambda name, shape, dt=F32, s_=st: G.sb(name, shape, dt, s_)
        pst = lambda name: T(st.enter_context(nc.psum_tensor(name + "_%d" % G.nuid(), [128, 512], F32)))
        QR = sb("at_QR", [128, 8, TOK], BF16)
        KR = sb("at_KR", [128, 2, TOK], BF16)
        VT = sb("at_VT", [128, 34, 2, 128], BF16)
        am = sb("at_am", [128, 3, 128], BF16)
        esink = sb("at_es", [128, 8])
        S.dma("sp", am[:], G.C["amask"][:, :, :], w=[am])
        S.op("act", lambda e: e.activation(out=esink[:], in_=pcol(G, l, "sink", 0, 8), func=AF.Exp), r=[G.par], w=[esink])
        pT = T(st.enter_context(nc.psum_tensor("at_pT_%d" % G.nuid(), [128, 1024], BF16)))
        with ExitStack() as st1:
            rc = sb("at_rc", [128, NLAT], F32, st1)
            rs = sb("at_rs", [128, NLAT], F32, st1)
            S.dma("sp", rc[:], G.C["ropeC"][:, :], w=[rc])
            S.dma("sp", rs[:], G.C["ropeS"][:, :], w=[rs])
            qf = [sb("at_qf%d" % i, [128, 512], F32, st1) for i in range(2)]
            qs = [sb("at_qs%d" % i, [128, 512], F32, st1) for i in range(2)]
            t1 = [sb("at_t1%d" % i, [128, 512], F32, st1) for i in range(2)]
            t2 = [sb("at_t2%d" % i, [128, 512], F32, st1) for i in range(2)]
            vb = [sb("at_vb%d" % i, [128, 512], BF16, st1) for i in range(2)]
            it = 0
            for ch in range(10):
                src = C_Q + ch if ch < 8 else C_AK + (ch - 8)
                dst = (lambda a, b: QR[:, ch, a:b]) if ch < 8 else (lambda a, b: KR[:, ch - 8, a:b])
                dT = QR if ch < 8 else KR
                for ti, (t0, W) in enumerate(TILES):
                    q_, s_, a_, b_ = qf[it % 2], qs[it % 2], t1[it % 2], t2[it % 2]
                    it += 1
                    S.dma("sp", q_[:, 0:W], pv[:, src, t0:t0 + W], w=[q_])
                    if ti == 0:
                        S.op("act", lambda e, q_=q_, W=W, dst=dst, t0=t0: e.activation(out=dst(t0, t0 + W), in_=q_[:, 0:W], func=AF.Copy), r=[q_], w=[dT])
                        continue
                    for blk in range(4):
                        sp = (blk ^ 1) * 32
                        S.dma("sp", s_[blk * 32:(blk + 1) * 32, 0:W], G.PT[src * 128 + sp:src * 128 + sp + 32, t0:t0 + W], w=[s_])
                    p0 = t0 - NCTX
                    S.op("dve", lambda e, q_=q_, a_=a_, p0=p0, W=W: e.tensor_tensor(out=a_[:, 0:W], in0=q_[:, 0:W], in1=rc[:, p0:p0 + W], op=ALU.mult),
                         r=[q_, rc], w=[a_])
                    S.op("pool", lambda e, s_=s_, b_=b_, p0=p0, W=W: e.tensor_tensor(out=b_[:, 0:W], in0=s_[:, 0:W], in1=rs[:, p0:p0 + W], op=ALU.mult),
                         r=[s_, rs], w=[b_])
                    S.op("dve", lambda e, a_=a_, b_=b_, W=W, dst=dst, t0=t0: e.tensor_tensor(out=dst(t0, t0 + W), in0=a_[:, 0:W], in1=b_[:, 0:W], op=ALU.add),
                         r=[a_, b_], w=[dT])
            for g in range(2):
                for ti, (t0, W) in enumerate(TILES):
                    q_, v_ = qf[it % 2], vb[it % 2]
                    it += 1
                    S.dma("sp", q_[:, 0:W], pv[:, C_AV + g, t0:t0 + W], w=[q_])
                    S.op("act", lambda e, q_=q_, v_=v_, W=W: e.activation(out=v_[:, 0:W], in_=q_[:, 0:W], func=AF.Copy), r=[q_], w=[v_])
                    nb = W // 128
                    for i in range(nb):
                        S.op("pe", lambda e, i=i, v_=v_: e.transpose(pT[:, i * 128:(i + 1) * 128], v_[:, i * 128:(i + 1) * 128], am[:, 2, :]),
                             r=[v_, am], w=[pT])
                    b0 = t0 // 128
                    S.op("dve", lambda e, g=g, b0=b0, nb=nb: e.tensor_copy(out=VT[:, b0:b0 + nb, g, :],
                                                                     in_=pT[:, 0:nb * 128].rearrange("p (b d) -> p b d", b=nb)), r=[pT], w=[VT])
            S.barrier()
        ao = [sb("at_ao%d" % i, [128, 8, 512]) for i in range(2)]
        sq = sb("at_sq", [128, 8, 512])
        rstd = sb("at_rstd", [128, 512])
        ob = [sb("at_ob%d" % i, [128, 8, 512], BF16) for i in range(2)]
        pts = [sb("at_pt%d" % i, [128, 4, 128], BF16) for i in range(4)]
        den = [sb("at_den%d" % i, [128, 4, 128]) for i in range(2)]
        ps_s = [pst("at_ps%d" % i) for i in range(3)]
        ps_n = [pst("at_pn%d" % i) for i in range(2)]
        ps_d = [pst("at_pd%d" % i) for i in range(2)]
        it = 0
        ib = 0
        for ti, (t0, W) in enumerate(TILES):
            a_, o_ = ao[ti % 2], ob[ti % 2]
            for g in range(2):
                for n in range(W // 128):
                    q0 = t0 + n * 128
                    gb = q0 // 128
                    kbs = [(0, None), (1, None)]
                    if ti > 0:
                        if gb > 2:
                            kbs.append((gb - 1, 0))
                        kbs.append((gb, None))
                        if gb < 33:
                            kbs.append((gb + 1, 1))
                    pn, pd = ps_n[ib % 2], ps_d[ib % 2]
                    dn = den[ib % 2]
                    ib += 1
                    for ki, (kb, mk) in enumerate(kbs):
                        p_s, pt = ps_s[it % 3], pts[it % 4]
                        it += 1
                        S.op("pe", lambda e, kb=kb, p_s=p_s, q0=q0, g=g: e.matmul(p_s[:, :].rearrange("p (h q) -> p h q", h=4), lhsT=KR[:, g, kb * 128:(kb + 1) * 128],
                                                                          rhs=QR[:, 4 * g:4 * g + 4, q0:q0 + 128], start=True, stop=True), r=[KR, QR], w=[p_s])
                        S.op("act", lambda e, p_s=p_s, pt=pt: e.activation(out=pt[:], in_=p_s[:, :].rearrange("p (h q) -> p h q", h=4), func=AF.Exp, scale=SCALE),
                             r=[p_s], w=[pt])
                        if mk is not None:
                            S.op("dve", lambda e, pt=pt, mk=mk: e.tensor_tensor(out=pt[:], in0=pt[:], in1=am[:, mk:mk + 1, :].broadcast_to([128, 4, 128]), op=ALU.mult),
                                 r=[pt, am], w=[pt])
                        S.op("pe", lambda e, kb=kb, pt=pt, pn=pn, ki=ki, g=g: e.matmul(pn[:, :].rearrange("p (h q) -> p h q", h=4), lhsT=VT[:, kb, g, :], rhs=pt[:],
                                                                             start=(ki == 0), stop=(ki == len(kbs) - 1)), r=[VT, pt], w=[pn])
                        S.op("pe", lambda e, pt=pt, pd=pd, ki=ki: e.matmul(pd[:, :].rearrange("p (h q) -> p h q", h=4), lhsT=G.onesb[:], rhs=pt[:],
                                                                       start=(ki == 0), stop=(ki == len(kbs) - 1)), r=[G.onesb, pt], w=[pd])
                    S.op("dve", lambda e, pd=pd, dn=dn, g=g: e.tensor_tensor(out=dn[:], in0=pd[:, :].rearrange("p (h q) -> p h q", h=4),
                                                                       in1=esink[:, 4 * g:4 * g + 4].unsqueeze(2).broadcast_to([128, 4, 128]), op=ALU.add),
                         r=[pd, esink], w=[dn])
                    S.op("dve", lambda e, dn=dn: e.reciprocal(out=dn[:], in_=dn[:]), r=[dn], w=[dn])
                    S.op("dve", lambda e, pn=pn, dn=dn, a_=a_, g=g, n=n: e.tensor_tensor(out=a_[:, 4 * g:4 * g + 4, n * 128:(n + 1) * 128],
                                                                                 in0=pn[:, :].rearrange("p (h q) -> p h q", h=4), in1=dn[:], op=ALU.mult),
                         r=[pn, dn], w=[a_])
            rms_modulate(G, a_, W, lambda c: pcol(G, l, "ag", c), None, lambda c: (o_[:, c, 0:W], [o_]), sq, rstd, ps_s[0], nch=8, dim=1024)
            S.dma("sp", mv[:, 8:16, t0:t0 + W], o_[:, :, 0:W], r=[o_])


LD = 0.6065306597126334
NCHK = TOK // 64
SCW = 256
DER_N = 6


def phase_rwkv(G, l):
    S = G.S
    rwkv_r1(G, l)
    S.barrier()
    rwkv_r2(G, l)
    S.barrier()
    rwkv_r3(G, l)


def rwkv_r1(G, l):
    nc, S = G.nc, G.S
    pv = G.PT.rearrange("(c p) t -> p c t", p=128)
    with ExitStack() as st:
        sb = lambda name, shape, dt=F32, s_=st: G.sb(name, shape, dt, s_)
        pst = lambda name: T(st.enter_context(nc.psum_tensor(name + "_%d" % G.nuid(), [128, 512], F32)))
        w2t = sb("r1_w2", [128, 512]); a2t = sb("r1_a2", [128, 512]); g2t = sb("r1_g2", [128, 512])
        S.dma("sp", w2t[:], G.w2r[l], w=[w2t]); S.dma("sp", a2t[:], G.a2r[l], w=[a2t]); S.dma("sp", g2t[:], G.g2[l], w=[g2t])
        rmask = sb("r1_rm", [128, 512])
        S.dma("sp", rmask[:], G.C["rmask"][:, :], w=[rmask])
        bones = sb("r1_bo", [128, 128])
        S.op("dve", lambda e: e.memset(bones[:], 0.0), w=[bones])
        S.op("dve", lambda e: e.memset(bones[0:64, 0:64], 1.0), w=[bones])
        S.op("dve", lambda e: e.memset(bones[64:128, 64:128], 1.0), w=[bones])
        mmc = sb("r1_mmc", [128, 15]); omka = sb("r1_omka", [128, 4])
        S.op("dve", lambda e: e.tensor_tensor(out=mmc[:], in0=pcol(G, l, "mu0", 0, 15), in1=pcol(G, l, "mu1", 0, 15), op=ALU.add), r=[G.par], w=[mmc])
        S.op("dve", lambda e: e.tensor_scalar(out=mmc[:], in0=mmc[:], scalar1=-1.0, scalar2=1.0, op0=ALU.mult, op1=ALU.add), r=[mmc], w=[mmc])
        S.op("dve", lambda e: e.tensor_scalar(out=omka[:], in0=pcol(G, l, "ka", 0, 4), scalar1=-1.0, scalar2=1.0, op0=ALU.mult, op1=ALU.add), r=[G.par], w=[omka])
        rh = [sb("r1_rh%d" % i, [128, 514]) for i in range(3)]
        psx = sb("r1_psx", [128, 15, 512])
        tw = sb("r1_tw", [128, 512]); sgd = sb("r1_sgd", [128, 512])
        NT = 12
        tp = [sb("r1_t%d" % i, [128, 512]) for i in range(NT)]
        fixed = {n: sb("r1_f" + n, [128, 512]) for n in ("kk0", "sq", "kk", "kds", "bq")}
        ob = [sb("r1_o%d" % i, [128, 512]) for i in range(8)]
        ptt = [sb("r1_pt%d" % i, [128, 8]) for i in range(2)]
        ps = [pst("r1p%d" % i) for i in range(6)]
        cnt = {"t": 0, "o": 0, "p": 0, "rh": 0}

        def tmp():
            cnt["t"] += 1
            return tp[cnt["t"] % NT]

        def otile():
            cnt["o"] += 1
            return ob[cnt["o"] % 8]

        def psn():
            cnt["p"] += 1
            return ps[cnt["p"] % 6]

        def tt(eng, out, a, b, op, r, w):
            S.op(eng, lambda e: e.tensor_tensor(out=out, in0=a, in1=b, op=op), r=r, w=w)

        for ti, (t0, W) in enumerate(TILES):
            nck = W // 64
            lz = any(t0 == a for a, b in SEQS)
            rz = any(t0 + W == b for a, b in SEQS)
            lo = t0 - (0 if lz else 1)
            hi = t0 + W + (0 if rz else 1)
            for ci in range(15):
                cnt["rh"] += 1
                h = rh[cnt["rh"] % 3]
                S.dma("sp", h[:, (1 if lz else 0):(1 if lz else 0) + hi - lo], pv[:, C_R + ci, lo:hi], w=[h])
                if lz:
                    S.op("dve", lambda e, h=h: e.memset(h[:, 0:1], 0.0), w=[h])
                if rz:
                    S.op("dve", lambda e, h=h, W=W: e.memset(h[:, W + 1:W + 2], 0.0), w=[h])
                S.op("act", lambda e, h=h, ci=ci, W=W: e.activation(out=psx[:, ci, 0:W], in_=h[:, 1:W + 1], func=AF.Identity, scale=mmc[:, ci:ci + 1]),
                     r=[h, mmc], w=[psx])
                S.op("dve", lambda e, h=h, ci=ci, W=W: e.scalar_tensor_tensor(out=psx[:, ci, 0:W], in0=h[:, 0:W], scalar=pcol(G, l, "mu0", ci),
                                                                        in1=psx[:, ci, 0:W], op0=ALU.mult, op1=ALU.add), r=[h, psx, G.par], w=[psx])
                S.op("dve", lambda e, h=h, ci=ci, W=W: e.scalar_tensor_tensor(out=psx[:, ci, 0:W], in0=h[:, 2:W + 2], scalar=pcol(G, l, "mu1", ci),
                                                                        in1=psx[:, ci, 0:W], op0=ALU.mult, op1=ALU.add), r=[h, psx, G.par], w=[psx])
            S.op("act", lambda e, W=W: e.activation(out=tw[:, 0:W], in_=psx[:, 13, 0:W], func=AF.Tanh), r=[psx], w=[tw])
            S.op("act", lambda e, W=W: e.activation(out=sgd[:, 0:W], in_=psx[:, 4, 0:W], func=AF.Sigmoid), r=[psx], w=[sgd])
            for hc in range(4):
                r_, k_, v_ = psx[:, hc, 0:W], psx[:, 5 + hc, 0:W], psx[:, 9 + hc, 0:W]
                rows = slice(hc * 128, (hc + 1) * 128)
                p_ = psn()
                S.op("pe", lambda e, p_=p_, hc=hc, W=W: e.matmul(p_[:, 0:W], lhsT=g2t[:, hc * 128:(hc + 1) * 128], rhs=sgd[:, 0:W], start=True, stop=True),
                     r=[g2t, sgd], w=[p_])
                o = otile()
                S.op("act", lambda e, p_=p_, o=o, W=W: e.activation(out=o[:, 0:W], in_=p_[:, 0:W], func=AF.Copy), r=[p_], w=[o])
                S.dma("sp", G.GT[rows, t0:t0 + W], o[:, 0:W], r=[o])
                o = otile()
                S.op("act", lambda e, o=o, v_=v_, W=W: e.activation(out=o[:, 0:W], in_=v_, func=AF.Copy), r=[psx], w=[o])
                S.dma("sp", G.VS[rows, t0:t0 + W], o[:, 0:W], r=[o])
                kk0, sq, kk = fixed["kk0"], fixed["sq"], fixed["kk"]
                S.op("act", lambda e, kk0=kk0, k_=k_, hc=hc, W=W: e.activation(out=kk0[:, 0:W], in_=k_, func=AF.Identity, scale=pcol(G, l, "kks", hc)),
                     r=[psx, G.par], w=[kk0])
                S.op("act", lambda e, kk0=kk0, sq=sq, W=W: e.activation(out=sq[:, 0:W], in_=kk0[:, 0:W], func=AF.Square), r=[kk0], w=[sq])
                p_ = psn()
                S.op("pe", lambda e, p_=p_, sq=sq, W=W: e.matmul(p_[:, 0:W], lhsT=bones[:], rhs=sq[:, 0:W], start=True, stop=True), r=[bones, sq], w=[p_])
                S.op("act", lambda e, p_=p_, sq=sq, W=W: e.activation(out=sq[:, 0:W], in_=p_[:, 0:W], func=AF.Sqrt), r=[p_], w=[sq])
                S.op("dve", lambda e, sq=sq, W=W: e.tensor_scalar(out=sq[:, 0:W], in0=sq[:, 0:W], scalar1=1e-12, scalar2=None, op0=ALU.max), r=[sq], w=[sq])
                S.op("dve", lambda e, sq=sq, W=W: e.reciprocal(out=sq[:, 0:W], in_=sq[:, 0:W]), r=[sq], w=[sq])
                tt("dve", kk[:, 0:W], kk0[:, 0:W], sq[:, 0:W], ALU.mult, [kk0, sq], [kk])
                kds = fixed["kds"]
                for dr in range(2):
                    cnt["t"] = 0
                    prt = slice(dr * 64, (dr + 1) * 64)
                    pw, pa = psn(), psn()
                    S.op("pe", lambda e, pw=pw, dr=dr, hc=hc, W=W, prt=prt: e.matmul(pw[:, 0:W], lhsT=w2t[prt, hc * 128:(hc + 1) * 128], rhs=tw[prt, 0:W], start=True, stop=True),
                         r=[w2t, tw], w=[pw])
                    S.op("pe", lambda e, pa=pa, dr=dr, hc=hc, W=W, prt=prt: e.matmul(pa[:, 0:W], lhsT=a2t[prt, hc * 128:(hc + 1) * 128], rhs=psx[prt, 14, 0:W], start=True, stop=True),
                         r=[a2t, psx], w=[pa])
                    sg = tmp(); a_ = tmp()
                    S.op("act", lambda e, pw=pw, sg=sg, W=W, dr=dr, hc=hc: e.activation(out=sg[:, 0:W], in_=pw[:, 0:W], func=AF.Sigmoid, bias=pcol(G, l, "w0", dr * 4 + hc)),
                         r=[pw, G.par], w=[sg])
                    S.op("act", lambda e, pa=pa, a_=a_, W=W, dr=dr, hc=hc: e.activation(out=a_[:, 0:W], in_=pa[:, 0:W], func=AF.Sigmoid, bias=pcol(G, l, "a0", dr * 4 + hc)),
                         r=[pa, G.par], w=[a_])
                    kd = tmp(); nb = tmp()
                    S.op("act", lambda e, a_=a_, kd=kd, W=W, hc=hc: e.activation(out=kd[:, 0:W], in_=a_[:, 0:W], func=AF.Identity, scale=pcol(G, l, "ka", hc), bias=omka[:, hc:hc + 1]),
                         r=[a_, G.par, omka], w=[kd])
                    tt("dve", kd[:, 0:W], kd[:, 0:W], k_, ALU.mult, [kd, psx], [kd])
                    if dr == 0:
                        S.op("dve", lambda e, kds=kds, kd=kd, W=W: e.tensor_copy(out=kds[:, 0:W], in_=kd[:, 0:W]), r=[kd], w=[kds])
                    else:
                        tt("dve", kds[:, 0:W], kds[:, 0:W], kd[:, 0:W], ALU.add, [kds, kd], [kds])
                    S.op("dve", lambda e, a_=a_, nb=nb, kk=kk, W=W: e.scalar_tensor_tensor(out=nb[:, 0:W], in0=a_[:, 0:W], scalar=-1.0, in1=kk[:, 0:W],
                                                                                 op0=ALU.mult, op1=ALU.mult), r=[a_, kk], w=[nb])
                    c_ = tmp()
                    S.op("dve", lambda e, c_=c_, sg=sg, W=W: e.tensor_tensor_scan(out=c_[:, 0:W], data0=rmask[:, 0:W], data1=sg[:, 0:W], initial=0.0,
                                                                            op0=ALU.mult, op1=ALU.add), r=[rmask, sg], w=[c_])
                    c3 = c_[:, 0:W].rearrange("p (c t) -> p c t", t=64)
                    if dr == 1:
                        d_ = tmp()
                        tt("dve", d_[:, 0:W], sg[:, 0:W], c_[:, 0:W], ALU.subtract, [sg, c_], [d_])
                        tt("dve", d_[:, 0:W].rearrange("p (c t) -> p c t", t=64), d_[:, 0:W].rearrange("p (c t) -> p c t", t=64),
                           c3[:, :, 63:64].broadcast_to([128, nck, 64]), ALU.add, [d_, c_], [d_])
                        c_ = d_
                        c3 = c_[:, 0:W].rearrange("p (c t) -> p c t", t=64)
                        endi = 0
                    else:
                        endi = 63
                    e1 = tmp(); e2 = tmp(); e3 = tmp(); e4 = tmp()
                    S.op("act", lambda e, c_=c_, e1=e1, W=W: e.activation(out=e1[:, 0:W], in_=c_[:, 0:W], func=AF.Exp, scale=-LD), r=[c_], w=[e1])
                    S.op("act", lambda e, c_=c_, e2=e2, W=W: e.activation(out=e2[:, 0:W], in_=c_[:, 0:W], func=AF.Exp, scale=LD), r=[c_], w=[e2])
                    tt("dve", e3[:, 0:W], c_[:, 0:W], sg[:, 0:W], ALU.subtract, [c_, sg], [e3])
                    S.op("act", lambda e, e3=e3, W=W: e.activation(out=e3[:, 0:W], in_=e3[:, 0:W], func=AF.Exp, scale=-LD), r=[e3], w=[e3])
                    tt("dve", e4[:, 0:W].rearrange("p (c t) -> p c t", t=64), c3[:, :, endi:endi + 1].broadcast_to([128, nck, 64]), c3, ALU.subtract, [c_], [e4])
                    S.op("act", lambda e, e4=e4, W=W: e.activation(out=e4[:, 0:W], in_=e4[:, 0:W], func=AF.Exp, scale=-LD), r=[e4], w=[e4])
                    pt_ = ptt[(hc * 2 + dr) % 2]
                    S.op("dve", lambda e, pt_=pt_, e1=e1, W=W, endi=endi, nck=nck: e.tensor_copy(
                        out=pt_[:, 0:nck].unsqueeze(2), in_=e1[:, 0:W].rearrange("p (c t) -> p c t", t=64)[:, :, endi:endi + 1]), r=[e1], w=[pt_])
                    S.dma("sp", G.PTOT[dr][rows, t0 // 64:t0 // 64 + nck], pt_[:, 0:nck], r=[pt_])
                    for ai, (x_, y_, xd, yd) in enumerate(((kk, e3, kk, e3), (None, e1, psx, e1), (kd, e2, kd, e2), (nb, e2, nb, e2), (kd, e4, kd, e4), (nb, e4, nb, e4))):
                        o = otile()
                        xin_ = r_ if x_ is None else x_[:, 0:W]
                        tt("dve", o[:, 0:W], xin_, y_[:, 0:W], ALU.mult, [xd, yd], [o])
                        S.dma("sp", G.DER[dr][ai][rows, t0:t0 + W], o[:, 0:W], r=[o])
                bq = fixed["bq"]
                S.op("dve", lambda e, bq=bq, r_=r_, kds=kds, hc=hc, W=W: e.scalar_tensor_tensor(out=bq[:, 0:W], in0=r_, scalar=pcol(G, l, "rk", hc), in1=kds[:, 0:W],
                                                                                      op0=ALU.mult, op1=ALU.mult), r=[psx, kds, G.par], w=[bq])
                p_ = psn()
                S.op("pe", lambda e, p_=p_, bq=bq, W=W: e.matmul(p_[:, 0:W], lhsT=bones[:], rhs=bq[:, 0:W], start=True, stop=True), r=[bones, bq], w=[p_])
                o = otile()
                tt("dve", o[:, 0:W], p_[:, 0:W], v_, ALU.mult, [p_, psx], [o])
                S.dma("sp", G.BON[rows, t0:t0 + W], o[:, 0:W], r=[o])


class PV:
    def __init__(self, ap, d):
        self.ap = ap
        self.d = d


def rwkv_r2(G, l, HG=4):
    nc, S = G.nc, G.S
    with ExitStack() as st:
        sb = lambda name, shape, dt=F32, s_=st: G.sb(name, shape, dt, s_)
        banks = [st.enter_context(nc.psum_tensor("r2b%d_%d" % (i, G.nuid()), [128, 512], F32)) for i in range(8)]
        bdeps = [Dep(x=True) for i in range(8)]
        rwm = sb("r2_rwm", [64, 5, 64])
        S.dma("sp", rwm[:], G.C["rwm"][:, :, :], w=[rwm])
        ident = rwm[:, 4, :]
        m12 = [rwm[:, 0:2, :], rwm[:, 2:4, :]]
        mX = [rwm[:, 2, :], rwm[:, 0, :]]
        order = [list(range(NCHK)), [3, 2, 1, 0] + list(range(NCHK - 1, 3, -1))]
        nch = 2 * HG

        class Chain:
            pass
        chains = []
        for i in range(nch):
            c = Chain()
            c.inb = [sb("r2_in%d_%d" % (i, j), [64, 7, SCW]) for j in range(2)]
            c.tm = sb("r2_tm%d" % i, [64, 3, 64])
            c.AM1 = sb("r2_am1_%d" % i, [64, 2, 64]); c.AM2 = sb("r2_am2_%d" % i, [64, 2, 64])
            c.X = [sb("r2_x%d_%d" % (i, j), [64, 64]) for j in range(2)]
            c.Wk = [sb("r2_w%d_%d" % (i, j), [64, 2, 64]) for j in range(2)]
            c.Ti = sb("r2_ti%d" % i, [64, 64]); c.Z = sb("r2_z%d" % i, [64, 64]); c.U = sb("r2_u%d" % i, [64, 64])
            c.H = sb("r2_h%d" % i, [64, 64])
            c.ys = [sb("r2_ys%d_%d" % (i, j), [64, SCW]) for j in range(2)]
            c.pt = sb("r2_ptot%d" % i, [64, NCHK])
            c.regs = [PV(banks[i][0:64, k * 128:(k + 1) * 128], bdeps[i]) for k in range(4)]
            c.ri = 0
            chains.append(c)

        def prc(c):
            c.ri += 1
            return c.regs[c.ri % 4]

        def cp(eng, out, in_, r, w):
            if eng == "act":
                S.op("act", lambda e: e.activation(out=out, in_=in_, func=AF.Copy), r=r, w=w)
            else:
                S.op("dve", lambda e: e.tensor_copy(out=out, in_=in_), r=r, w=w)

        def mm(out_pv, out_ap, lhsT, rhs, start, stop, r):
            S.op("pe", lambda e: e.matmul(out_ap, lhsT=lhsT, rhs=rhs, start=start, stop=stop), r=r, w=[out_pv])

        for hg in range(8 // HG):
            for i, c in enumerate(chains):
                c.dr = i // HG
                c.head = hg * HG + i % HG
                c.rows = slice(c.head * 64, (c.head + 1) * 64)
                S.op("dve", lambda e, c=c: e.memset(c.H[:], 0.0), w=[c.H])
                S.dma("sp", c.pt[:], G.PTOT[c.dr][c.rows, :], w=[c.pt])
                c.nsc = 0
            for step in range(NCHK):
                for c in chains:
                    c.ck = order[c.dr][step]
                    c.off = (c.ck % 4) * 64
                    if step % 4 == 0:
                        c.nsc += 1
                        c.cur = c.inb[c.nsc % 2]
                        c.ysc = c.ys[c.nsc % 2]
                        sc = c.ck // 4
                        for ai in range(6):
                            S.dma("sp", c.cur[:, ai, :], G.DER[c.dr][ai][c.rows, sc * SCW:(sc + 1) * SCW], w=[c.cur])
                        S.dma("sp", c.cur[:, 6, :], G.VS[c.rows, sc * SCW:(sc + 1) * SCW], w=[c.cur])
                    c.cols = slice(c.off, c.off + 64)
                for c in chains:
                    c.p = prc(c)
                    for j, ai in enumerate((4, 5)):
                        S.op("pe", lambda e, c=c, j=j, ai=ai: e.transpose(c.p.ap[:, j * 64:(j + 1) * 64], c.cur[:, ai, c.cols], ident), r=[c.cur, rwm], w=[c.p])
                for c in chains:
                    pass
                for c in chains:
                    cp("act", c.tm[:, 0:2, :], c.p.ap[:, 0:128].rearrange("p (a t) -> p a t", a=2), [c.p], [c.tm])
                for c in chains:
                    c.p = prc(c)
                    S.op("pe", lambda e, c=c: e.transpose(c.p.ap[:, 0:64], c.cur[:, 6, c.cols], ident), r=[c.cur, rwm], w=[c.p])
                for c in chains:
                    cp("dve", c.tm[:, 2, :], c.p.ap[:, 0:64], [c.p], [c.tm])
                for c in chains:
                    c.p1, c.p2, c.p3 = prc(c), prc(c), prc(c)
                    kr = c.cur[:, 0:2, c.cols]
                    mm(c.p1, c.p1.ap[:, :].rearrange("p (a t) -> p a t", a=2), c.cur[:, 2, c.cols], kr, True, True, [c.cur])
                    mm(c.p2, c.p2.ap[:, :].rearrange("p (a t) -> p a t", a=2), c.cur[:, 3, c.cols], kr, True, True, [c.cur])
                    mm(c.p3, c.p3.ap[:, 0:64], c.cur[:, 0, c.cols], c.cur[:, 3, c.cols], True, True, [c.cur])
                for c in chains:
                    c.xi = 0
                    S.op("dve", lambda e, c=c: e.tensor_tensor(out=c.AM1[:], in0=c.p1.ap[:, :].rearrange("p (a t) -> p a t", a=2), in1=m12[c.dr], op=ALU.mult), r=[c.p1, rwm], w=[c.AM1])
                    S.op("dve", lambda e, c=c: e.tensor_tensor(out=c.AM2[:], in0=c.p2.ap[:, :].rearrange("p (a t) -> p a t", a=2), in1=m12[c.dr], op=ALU.mult), r=[c.p2, rwm], w=[c.AM2])
                    S.op("dve", lambda e, c=c: e.tensor_tensor(out=c.X[0][:], in0=c.p3.ap[:, 0:64], in1=mX[c.dr], op=ALU.mult), r=[c.p3, rwm], w=[c.X[0]])
                for c in chains:
                    c.p1, c.p2 = prc(c), prc(c)
                    mm(c.p1, c.p1.ap[:, 0:64], c.X[0][:], c.AM2[:, 0, :], True, True, [c.X[0], c.AM2])
                    mm(c.p2, c.p2.ap[:, 0:64], c.AM2[:, 0, :], c.X[0][:], True, True, [c.X[0], c.AM2])
                for c in chains:
                    cp("act", c.Wk[1][:, 0, :], c.p1.ap[:, 0:64], [c.p1], [c.Wk[1]])
                    S.op("dve", lambda e, c=c: e.tensor_tensor(out=c.Wk[1][:, 1, :], in0=c.AM2[:, 0, :], in1=ident, op=ALU.add), r=[c.AM2, rwm], w=[c.Wk[1]])
                    cp("act", c.X[1][:], c.p2.ap[:, 0:64], [c.p2], [c.X[1]])
                for k in range(1, 5):
                    a, b = k % 2, (k + 1) % 2
                    for c in chains:
                        c.p1, c.p2 = prc(c), prc(c)
                        mm(c.p1, c.p1.ap[:, :].rearrange("p (a t) -> p a t", a=2), c.X[a][:], c.Wk[a][:], True, True, [c.X[a], c.Wk[a]])
                        mm(c.p2, c.p2.ap[:, 0:64], c.Wk[a][:, 0, :], c.X[a][:], True, True, [c.X[a], c.Wk[a]])
                    for c in chains:
                        cp("act", c.Wk[b][:, 0, :], c.p1.ap[:, 0:64], [c.p1], [c.Wk[b]])
                        S.op("dve", lambda e, c=c, a=a, b=b: e.tensor_tensor(out=c.Wk[b][:, 1, :], in0=c.p1.ap[:, 64:128], in1=c.Wk[a][:, 1, :], op=ALU.add),
                             r=[c.p1, c.Wk[a]], w=[c.Wk[b]])
                        cp("dve", c.X[b][:], c.p2.ap[:, 0:64], [c.p2], [c.X[b]])
                for c in chains:
                    c.p1 = prc(c)
                    mm(c.p1, c.p1.ap[:, 0:64], c.X[1][:], c.Wk[1][:, 1, :], True, True, [c.X[1], c.Wk[1]])
                for c in chains:
                    S.op("dve", lambda e, c=c: e.tensor_tensor(out=c.Ti[:], in0=c.p1.ap[:, 0:64], in1=c.Wk[1][:, 1, :], op=ALU.add), r=[c.p1, c.Wk[1]], w=[c.Ti])
                for c in chains:
                    c.p1 = prc(c)
                    mm(c.p1, c.p1.ap[:, 0:64], c.cur[:, 0, c.cols], c.H[:], True, False, [c.cur, c.H])
                    mm(c.p1, c.p1.ap[:, 0:64], c.AM1[:, 0, :], c.tm[:, 2, :], False, True, [c.AM1, c.tm])
                for c in chains:
                    cp("act", c.Z[:], c.p1.ap[:, 0:64], [c.p1], [c.Z])
                for c in chains:
                    c.p1 = prc(c)
                    mm(c.p1, c.p1.ap[:, 0:64], c.Ti[:], c.Z[:], True, True, [c.Ti, c.Z])
                for c in chains:
                    cp("dve", c.U[:], c.p1.ap[:, 0:64], [c.p1], [c.U])
                for c in chains:
                    c.p1, c.p2 = prc(c), prc(c)
                    mm(c.p1, c.p1.ap[:, 0:64], c.H[:], c.cur[:, 1, c.cols], True, False, [c.H, c.cur])
                    mm(c.p1, c.p1.ap[:, 0:64], c.tm[:, 2, :], c.AM1[:, 1, :], False, False, [c.tm, c.AM1])
                    mm(c.p1, c.p1.ap[:, 0:64], c.U[:], c.AM2[:, 1, :], False, True, [c.U, c.AM2])
                    mm(c.p2, c.p2.ap[:, 0:64], c.tm[:, 0, :], c.tm[:, 2, :], True, False, [c.tm])
                    mm(c.p2, c.p2.ap[:, 0:64], c.tm[:, 1, :], c.U[:], False, True, [c.tm, c.U])
                for c in chains:
                    cp("act", c.ysc[:, c.cols], c.p1.ap[:, 0:64], [c.p1], [c.ysc])
                    S.op("dve", lambda e, c=c: e.scalar_tensor_tensor(out=c.H[:], in0=c.H[:], scalar=c.pt[:, c.ck:c.ck + 1], in1=c.p2.ap[:, 0:64],
                                                                  op0=ALU.mult, op1=ALU.add), r=[c.H, c.pt, c.p2], w=[c.H])
                if step % 4 == 3:
                    for c in chains:
                        sc = c.ck // 4
                        S.dma("sp", G.YD[c.dr][c.rows, sc * SCW:(sc + 1) * SCW], c.ysc[:], r=[c.ysc])


def rwkv_r3(G, l):
    nc, S = G.nc, G.S
    mv = G.MIX.rearrange("(c p) t -> p c t", p=128)
    with ExitStack() as st:
        sb = lambda name, shape, dt=F32, s_=st: G.sb(name, shape, dt, s_)
        pst = lambda name: T(st.enter_context(nc.psum_tensor(name + "_%d" % G.nuid(), [128, 512], F32)))
        bo = sb("r3_bo", [128, 128])
        S.op("dve", lambda e: e.memset(bo[:], 0.0), w=[bo])
        S.op("dve", lambda e: e.memset(bo[0:64, 0:64], 1.0 / 64), w=[bo])
        S.op("dve", lambda e: e.memset(bo[64:128, 64:128], 1.0 / 64), w=[bo])
        lnx = sb("r3_eps", [128, 1])
        S.op("dve", lambda e: e.memset(lnx[:], 64e-5), w=[lnx])
        ya = [sb("r3_ya%d" % i, [128, 512]) for i in range(2)]
        yb = [sb("r3_yb%d" % i, [128, 512]) for i in range(2)]
        bn = [sb("r3_bn%d" % i, [128, 512]) for i in range(2)]
        gg = [sb("r3_gg%d" % i, [128, 512]) for i in range(2)]
        t1 = [sb("r3_t1%d" % i, [128, 512]) for i in range(2)]
        t2 = [sb("r3_t2%d" % i, [128, 512]) for i in range(2)]
        ob = [sb("r3_ob%d" % i, [128, 512], BF16) for i in range(2)]
        ps = [pst("r3p%d" % i) for i in range(4)]
        it = 0
        for ti, (t0, W) in enumerate(TILES):
            for hc in range(4):
                rows = slice(hc * 128, (hc + 1) * 128)
                a, b, n_, g_, x1, x2, o = ya[it % 2], yb[it % 2], bn[it % 2], gg[it % 2], t1[it % 2], t2[it % 2], ob[it % 2]
                pm, pvv = ps[(2 * it) % 4], ps[(2 * it + 1) % 4]
                it += 1
                S.dma("sp", a[:, 0:W], G.YD[0][rows, t0:t0 + W], w=[a])
                S.dma("sp", b[:, 0:W], G.YD[1][rows, t0:t0 + W], w=[b])
                S.dma("sp", n_[:, 0:W], G.BON[rows, t0:t0 + W], w=[n_])
                S.dma("sp", g_[:, 0:W], G.GT[rows, t0:t0 + W], w=[g_])
                S.op("dve", lambda e, a=a, b=b, W=W: e.tensor_tensor(out=a[:, 0:W], in0=a[:, 0:W], in1=b[:, 0:W], op=ALU.add), r=[a, b], w=[a])
                S.op("pe", lambda e, a=a, pm=pm, W=W: e.matmul(pm[:, 0:W], lhsT=bo[:], rhs=a[:, 0:W], start=True, stop=True), r=[bo, a], w=[pm])
                S.op("dve", lambda e, a=a, pm=pm, x1=x1, W=W: e.tensor_tensor(out=x1[:, 0:W], in0=a[:, 0:W], in1=pm[:, 0:W], op=ALU.subtract), r=[a, pm], w=[x1])
                S.op("act", lambda e, x1=x1, x2=x2, W=W: e.activation(out=x2[:, 0:W], in_=x1[:, 0:W], func=AF.Square), r=[x1], w=[x2])
                S.op("pe", lambda e, x2=x2, pvv=pvv, W=W: e.matmul(pvv[:, 0:W], lhsT=bo[:], rhs=x2[:, 0:W], start=True, stop=True), r=[bo, x2], w=[pvv])
                S.op("act", lambda e, x2=x2, pvv=pvv, W=W: e.activation(out=x2[:, 0:W], in_=pvv[:, 0:W], func=AF.Sqrt, bias=lnx[:, 0:1]), r=[pvv, lnx], w=[x2])
                S.op("dve", lambda e, x2=x2, W=W: e.reciprocal(out=x2[:, 0:W], in_=x2[:, 0:W]), r=[x2], w=[x2])
                S.op("dve", lambda e, x1=x1, x2=x2, W=W: e.tensor_tensor(out=x1[:, 0:W], in0=x1[:, 0:W], in1=x2[:, 0:W], op=ALU.mult), r=[x1, x2], w=[x1])
                S.op("act", lambda e, x1=x1, W=W, hc=hc: e.activation(out=x1[:, 0:W], in_=x1[:, 0:W], func=AF.Identity, scale=pcol(G, l, "lnw", hc), bias=pcol(G, l, "lnb", hc)),
                     r=[x1, G.par], w=[x1])
                S.op("dve", lambda e, x1=x1, n_=n_, W=W: e.tensor_tensor(out=x1[:, 0:W], in0=x1[:, 0:W], in1=n_[:, 0:W], op=ALU.add), r=[x1, n_], w=[x1])
                S.op("dve", lambda e, x1=x1, g_=g_, o=o, W=W: e.tensor_tensor(out=o[:, 0:W], in0=x1[:, 0:W], in1=g_[:, 0:W], op=ALU.mult), r=[x1, g_], w=[o])
                S.dma("sp", mv[:, 4 + hc, t0:t0 + W], o[:, 0:W], r=[o])

def phase_mixers(G, l):
    S = G.S
    phase_fnet(G, l)
    S.barrier()
    phase_attn(G, l)
    S.barrier()
    phase_rwkv(G, l)
    S.barrier()


EXTRA_W = ("w2r", "a2r", "g2")


def extra_w(inp, k):
    if k == "w2r":
        return np.ascontiguousarray(np.asarray(inp["rwkv_w2"], np.float32).reshape(DEPTH, 128, 512))
    if k == "a2r":
        return np.ascontiguousarray(np.asarray(inp["rwkv_a2"], np.float32).reshape(DEPTH, 128, 512))
    return np.ascontiguousarray(np.asarray(inp["rwkv_g2"], np.float32))


def make_inputs(inp, b):
    x = np.asarray(inp["x"][b], np.float32)
    cx = np.asarray(inp["ctx"][b], np.float32)
    xin = np.ascontiguousarray(np.concatenate([cx, x], axis=0).T)
    cvec = np.concatenate([_col(inp["c"][b]), _col(inp["c_ctx"])], axis=1)
    return {"xin": xin, "cvec": np.ascontiguousarray(cvec)}


def kernel(**inp):
    nc, G = build_nc()
    shared = {"params": pack_params(inp)}
    shared.update(const_inputs())
    for k in ("w_ada", "w_in", "w_out", "w_ffn_in", "w_ffn_out"):
        shared[k] = np.ascontiguousarray(np.asarray(inp[k], np.float32))
    for k in EXTRA_W:
        shared[k] = extra_w(inp, k)
    in_maps = []
    for b in range(8):
        m = dict(shared)
        m.update(make_inputs(inp, b))
        in_maps.append(m)
    res = run_bass_kernel_spmd(nc, in_maps, core_ids=list(range(8)))
    out = np.stack([np.ascontiguousarray(r["out"].T) for r in res.results], axis=0)
    return out.astype(np.float32)
```

```python
import numpy as np
from contextlib import ExitStack
import ml_dtypes
import concourse.bass as bass
import concourse.mybir as mybir
from concourse.bass_utils import run_bass_kernel_spmd

F32 = mybir.dt.float32
F32R = mybir.dt.float32r
BF16 = mybir.dt.bfloat16
AF = mybir.ActivationFunctionType
ALU = mybir.AluOpType
AX = mybir.AxisListType

D = 2048
NCTX = 256
NLAT = 4096
TOK = NCTX + NLAT
DEPTH = 4
INW = 3968
DFF = 5632
TILES = [(0, 256)] + [(256 + 512 * i, 512) for i in range(8)]
SEQS = [(0, NCTX), (NCTX, TOK)]
RMS_EPS = 1e-6

C_F, C_Q, C_R, C_G, C_K, C_V, C_WD, C_AD, C_AK, C_AV = 0, 4, 12, 16, 17, 21, 25, 26, 27, 29
NPC = 31

PCOLS = {}
_off = 0
for _n, _w in [("n1", 16), ("n2", 16), ("mu0", 15), ("mu1", 15), ("w0", 8), ("a0", 8), ("kks", 4), ("ka", 4), ("rk", 4),
               ("lnw", 4), ("lnb", 4), ("fg", 4), ("ag", 8), ("cw0", 44), ("cw1", 44), ("cw2", 44), ("cb", 44),
               ("bada", 96), ("sink", 8)]:
    PCOLS[_n] = (_off, _w)
    _off += _w
PL = _off
P_FINAL = DEPTH * PL
NPAR = P_FINAL + 16


def _col(v):
    v = np.asarray(v, np.float32).reshape(-1)
    return np.ascontiguousarray(v.reshape(-1, 128).T)


def pack_params(inp):
    P = np.zeros((128, NPAR), np.float32)
    for l in range(DEPTH):
        def put(name, arr):
            o, w = PCOLS[name]
            P[:, l * PL + o: l * PL + o + w] = arr
        put("n1", _col(inp["norm1_g"][l])); put("n2", _col(inp["norm2_g"][l]))
        put("mu0", _col(inp["rwkv_mu"][l][0])); put("mu1", _col(inp["rwkv_mu"][l][1]))
        put("w0", _col(inp["rwkv_w0"][l])); put("a0", _col(inp["rwkv_a0"][l]))
        put("kks", _col(inp["rwkv_kk_scale"][l])); put("ka", _col(inp["rwkv_ka"][l])); put("rk", _col(inp["rwkv_rk"][l]))
        put("lnw", _col(inp["rwkv_lnx_w"][l])); put("lnb", _col(inp["rwkv_lnx_b"][l]))
        put("fg", _col(inp["fourier_out_g"][l])); put("ag", _col(inp["attn_out_g"][l]))
        for j in range(3):
            put("cw%d" % j, _col(inp["ffn_conv_w"][l][j]))
        put("cb", _col(inp["ffn_conv_b"][l])); put("bada", _col(inp["b_ada"][l]))
        put("sink", np.broadcast_to(np.asarray(inp["attn_sink"][l], np.float32)[None, :], (128, 8)))
    P[:, P_FINAL:P_FINAL + 16] = _col(inp["final_g"])
    return P


class Dep:
    __slots__ = ("w", "r", "x")

    def __init__(self, x=False):
        self.w = None
        self.r = {}
        self.x = x


class T:
    def __init__(self, t):
        self.t = t
        self.d = Dep()

    def __getitem__(self, idx):
        return self.t[idx]


def _d(x):
    return getattr(x, "d", x)


class Sched:
    def __init__(self, nc, es):
        self.nc = nc
        self.E = {"pe": nc.tensor, "dve": nc.vector, "act": nc.scalar, "pool": nc.gpsimd, "sp": nc.sync}
        self.csem = {k: es.enter_context(nc.semaphore("cs_" + k)) for k in ("pe", "dve", "act", "pool")}
        self.cnt = {k: 0 for k in self.csem}
        self.seen = {k: {} for k in self.E}
        self.dq = {}
        for q, n in (("sp", 16), ("pool", 8), ("act", 8)):
            self.dq[q] = dict(sems=[es.enter_context(nc.semaphore("d_%s%d" % (q, i))) for i in range(n)],
                              cnt=[0] * n, nxt=0)
        self.nins = 0

    def _wait(self, e, tok):
        if tok is None:
            return
        key, sem, val = tok
        if e == "pe" and key == "pe":
            return
        if self.seen[e].get(key, 0) >= val:
            return
        self.E[e].wait_ge(sem, val)
        self.seen[e][key] = val

    def _deps(self, e, r, w):
        for d in r:
            self._wait(e, _d(d).w)
        for d in w:
            d = _d(d)
            self._wait(e, d.w)
            for t in list(d.r.values()):
                self._wait(e, t)

    def _mark(self, tok, r, w):
        for d in r:
            _d(d).r[tok[0]] = tok
        for d in w:
            d = _d(d)
            d.w = tok
            d.r = {}

    def op(self, e, fn, r=(), w=()):
        xs = [d for d in r if _d(d).x]
        if xs:
            r = [d for d in r if not _d(d).x]
            w = list(w) + xs
        self._deps(e, r, w)
        ins = fn(self.E[e])
        self.cnt[e] += 1
        ins.then_inc(self.csem[e], 1)
        self._mark((e, self.csem[e], self.cnt[e]), r, w)
        self.nins += 1

    def dma(self, q, out, in_, r=(), w=(), **kw):
        Q = self.dq[q]
        i = Q["nxt"]
        Q["nxt"] = (i + 1) % len(Q["sems"])
        key = (q, i)
        if Q["cnt"][i]:
            self._wait(q, (key, Q["sems"][i], 16 * Q["cnt"][i]))
        self._deps(q, r, w)
        ins = self.E[q].dma_start(out=out, in_=in_, **kw)
        Q["cnt"][i] += 1
        ins.then_inc(Q["sems"][i], 16)
        self._mark((key, Q["sems"][i], 16 * Q["cnt"][i]), r, w)
        self.nins += 1

    def barrier(self, engines=("pe", "dve", "act", "pool", "sp")):
        for e in engines:
            for k in self.csem:
                if self.cnt[k]:
                    self._wait(e, (k, self.csem[k], self.cnt[k]))
            for q, Q in self.dq.items():
                for i, s in enumerate(Q["sems"]):
                    if Q["cnt"][i]:
                        self._wait(e, ((q, i), s, 16 * Q["cnt"][i]))


class Ctx:
    pass


def build_nc(nl=DEPTH, dbg=None):
    nc = bass.Bass("TRN2", target_bir_lowering=False)
    G = Ctx()
    G.nc = nc
    G.dbg = dbg
    dt_in = lambda name, shape, dt=F32: nc.dram_tensor(name, shape, dt, kind="ExternalInput").ap()
    dt_sc = lambda name, shape, dt=F32: nc.dram_tensor(name, shape, dt, kind="Internal").ap()
    G.xin = dt_in("xin", [D, TOK])
    G.cvec = dt_in("cvec", [128, 32])
    G.params = dt_in("params", [128, NPAR])
    G.w_ada = dt_in("w_ada", [DEPTH, D, 6 * D])
    G.w_in = dt_in("w_in", [DEPTH, D, INW])
    G.w_out = dt_in("w_out", [DEPTH, D, D])
    G.w_ffn_in = dt_in("w_ffn_in", [DEPTH, D, 2 * DFF])
    G.w_ffn_out = dt_in("w_ffn_out", [DEPTH, DFF, D])
    G.out = nc.dram_tensor("out", [D, NLAT], F32, kind="ExternalOutput").ap()
    G.XT = dt_sc("XT", [D, TOK])
    G.PT = dt_sc("PT", [NPC * 128, TOK])
    G.MIX = dt_sc("MIX", [D, TOK], BF16)
    G.U2 = dt_sc("U2", [D, TOK], BF16)
    if dbg and "mixin" in dbg:
        G.mixin = dt_in("mixin", [D, TOK])
    G.C = {k: dt_in(k, sh, dt) for k, (sh, dt) in CONST_SHAPES.items()}
    G.w2r = dt_in("w2r", [DEPTH, 128, 512])
    G.a2r = dt_in("a2r", [DEPTH, 128, 512])
    G.g2 = dt_in("g2", [DEPTH, 128, 512])
    G.GT = dt_sc("GT", [512, TOK])
    G.VS = dt_sc("VS", [512, TOK])
    G.BON = dt_sc("BON", [512, TOK])
    G.PTOT = [dt_sc("PTOT%d" % d, [512, NCHK]) for d in range(2)]
    G.DER = [[dt_sc("DER%d_%d" % (d, a), [512, TOK]) for a in range(DER_N)] for d in range(2)]
    G.YD = [dt_sc("YD%d" % d, [512, TOK]) for d in range(2)]
    G.wb_in = [dt_sc("wb_in%d" % l, [8 * 128, 16 * 512], BF16) for l in range(nl)]
    G.wb_out = [dt_sc("wb_out%d" % l, [4 * 128, 16 * 512], BF16) for l in range(nl)]
    G.wb_fi = [dt_sc("wb_fi%d" % l, [44 * 128, 16 * 256], BF16) for l in range(nl)]
    G.wb_fo = [dt_sc("wb_fo%d" % l, [8 * 128, 44 * 256], BF16) for l in range(nl)]
    if dbg:
        G.dbg_out = {}
        for name, shape in dbg.items():
            if name == "mixin":
                continue
            G.dbg_out[name] = nc.dram_tensor("dbg_" + name, shape, F32, kind="ExternalOutput").ap()

    with ExitStack() as es:
        S = Sched(nc, es)
        G.S = S
        G.es = es
        G.uid = 0

        def nuid():
            G.uid += 1
            return G.uid
        G.nuid = nuid

        def sb(name, shape, dt=F32, st=es):
            G.uid += 1
            return T(st.enter_context(nc.sbuf_tensor("%s_%d" % (name, G.uid), shape, dt)))
        G.sb = sb
        G.par = sb("par", [128, NPAR])
        G.mod = sb("mod", [128, nl * 96 * 2])
        G.ones = sb("ones_f", [128, 128])
        G.onesb = sb("ones_b", [128, 128], BF16)
        G.wcast = Dep()
        G.epsc = sb("epsc", [128, 1])
        S.op("dve", lambda e: e.memset(G.epsc[:], RMS_EPS), w=[G.epsc])
        G.lin_it = 0
        G.ps_it = 0
        G.stg_it = 0
        G.xc_it = 0
        S.dma("sp", G.par[:], G.params[:, :], w=[G.par])
        S.op("dve", lambda e: e.memset(G.ones[:], 1.0), w=[G.ones])
        S.op("dve", lambda e: e.memset(G.onesb[:], 1.0), w=[G.onesb])
        for l in range(nl):
            for src, dst, K, N, gs in ((G.w_in, G.wb_in, D, INW, 512), (G.w_out, G.wb_out, D, D, 512),
                                       (G.w_ffn_in, G.wb_fi, D, 2 * DFF, 256), (G.w_ffn_out, G.wb_fo, DFF, D, 256)):
                dv = dst[l].rearrange("(g p) (kc c) -> p g kc c", p=128, c=gs)
                nf = N // gs
                for kc in range(K // 128):
                    S.dma("pool", dv[:, 0:nf, kc, :], src[l, kc * 128:(kc + 1) * 128, 0:nf * gs].rearrange("p (g c) -> p g c", c=gs),
                          w=[G.wcast], max_dma_last_dim=4096)
                    if N > nf * gs:
                        S.dma("pool", dv[:, nf, kc, 0:N - nf * gs], src[l, kc * 128:(kc + 1) * 128, nf * gs:N], w=[G.wcast], max_dma_last_dim=4096)
        xd = Dep()
        for c in range(16):
            S.dma("sp", G.XT[c * 128:(c + 1) * 128, :], G.xin[c * 128:(c + 1) * 128, :], w=[xd])
        prologue_adaln(G, nl)
        S.barrier()
        if dbg and "mod" in dbg:
            S.dma("sp", G.dbg_out["mod"][:, :], G.mod[:], r=[G.mod])
        for l in range(nl):
            layer(G, l)
        final_norm(G)
        S.barrier()
    G.nins = S.nins
    return nc, G


def pcol(G, l, name, c=0, n=1):
    o, w = PCOLS[name]
    return G.par[:, l * PL + o + c: l * PL + o + c + n]


def mcol(G, l, idx, c, which):
    j = ((l * 96) + idx * 16 + c) * 2 + which
    return G.mod[:, j:j + 1]


def prologue_adaln(G, nl):
    nc, S = G.nc, G.S
    with ExitStack() as st:
        sb = lambda name, shape, dt=F32: G.sb(name, shape, dt, st)
        cv = sb("cv", [128, 32])
        s2 = sb("s2", [128, 32])
        wts = [sb("wada%d" % i, [128, 16, 512]) for i in range(2)]
        ps = [T(st.enter_context(nc.psum_tensor("ps_ada%d" % i, [128, 512], F32))) for i in range(4)]
        S.dma("sp", cv[:], G.cvec[:, :], w=[cv])
        S.op("act", lambda e: e.activation(out=s2[:].rearrange("p (k w) -> p w k", w=2),
                                           in_=cv[:].rearrange("p (w k) -> p w k", w=2), func=AF.Silu), r=[cv], w=[s2])
        it = 0
        for l in range(nl):
            wv = G.w_ada[l].rearrange("(kc p) n -> p kc n", p=128)
            for cg in range(24):
                wt = wts[it % 2]
                S.dma("sp", wt[:], wv[:, :, cg * 512:(cg + 1) * 512], w=[wt])
                for j in range(4):
                    ch = cg * 4 + j
                    p_ = ps[(it * 4 + j) % 4]
                    for kc in range(16):
                        S.op("pe", lambda e, kc=kc, j=j, p_=p_, wt=wt: e.matmul(
                            p_[:, 0:2], lhsT=wt[:, kc, j * 128:(j + 1) * 128], rhs=s2[:, kc * 2:kc * 2 + 2],
                            start=(kc == 0), stop=(kc == 15)), r=[wt, s2], w=[p_])
                    o = (l * 96 + ch) * 2
                    S.op("dve", lambda e, p_=p_, o=o, ch=ch, l=l: e.tensor_scalar(
                        out=G.mod[:, o:o + 2], in0=p_[:, 0:2], scalar1=pcol(G, l, "bada", ch), scalar2=None, op0=ALU.add),
                        r=[p_, G.par], w=[G.mod])
                it += 1


def rms_modulate(G, xt, W, scale_col, bias_col, out_fn, sq, rstd, ps_ss, nch=16, dim=D, ones=None):
    S = G.S
    ones = ones or G.ones
    S.op("act", lambda e: e.activation(out=sq[:, 0:nch, 0:W], in_=xt[:, 0:nch, 0:W], func=AF.Square), r=[xt], w=[sq])
    for c in range(nch):
        S.op("pe", lambda e, c=c: e.matmul(ps_ss[:, 0:W], lhsT=ones[:], rhs=sq[:, c, 0:W], start=(c == 0), stop=(c == nch - 1)),
             r=[sq, ones], w=[ps_ss])
    S.op("act", lambda e: e.activation(out=rstd[:, 0:W], in_=ps_ss[:, 0:W], func=AF.Sqrt, scale=1.0 / dim, bias=G.epsc[:, 0:1]),
         r=[ps_ss, G.epsc], w=[rstd])
    S.op("dve", lambda e: e.reciprocal(out=rstd[:, 0:W], in_=rstd[:, 0:W]), r=[rstd], w=[rstd])
    S.op("dve", lambda e: e.tensor_tensor(out=sq[:, 0:nch, 0:W], in0=xt[:, 0:nch, 0:W],
                                          in1=rstd[:, 0:W].unsqueeze(1).broadcast_to([128, nch, W]), op=ALU.mult),
         r=[xt, rstd], w=[sq])
    for c in range(nch):
        o, od = out_fn(c)
        b = bias_col(c) if bias_col is not None else 0.0
        S.op("act", lambda e, c=c, o=o, b=b: e.activation(out=o, in_=sq[:, c, 0:W], func=AF.Identity, scale=scale_col(c), bias=b),
             r=[sq, G.par, G.mod] + list(od), w=od)


def linear(G, act, KC, W, wdram, n_oc, epilogue, ps, wts, gsz=4, col0=0, a0=0):
    S = G.S
    wv = wdram.rearrange("(g p) (kc c) -> p g kc c", p=128, c=gsz * 128)
    ng = (n_oc + gsz - 1) // gsz
    st = G.lin_it
    for g in range(ng):
        wt = wts[(st + g) % len(wts)]
        n = min(gsz, n_oc - g * gsz)
        S.dma("sp", wt[:, 0:KC, 0:n * 128], wv[:, g, :, 0:n * 128], w=[wt])
        for j in range(n):
            oc = g * gsz + j
            p_ = ps[G.ps_it % len(ps)]
            G.ps_it += 1
            for kc in range(KC):
                S.op("pe", lambda e, kc=kc, j=j, p_=p_, wt=wt: e.matmul(
                    p_[:, 0:W], lhsT=wt[:, kc, j * 128:(j + 1) * 128], rhs=act[:, kc, a0:a0 + W],
                    start=(kc == 0), stop=(kc == KC - 1)), r=[wt, act], w=[p_])
            epilogue(oc, p_)
    G.lin_it += ng


def gmod_cols(G, l, gm, nname, sc_idx):
    S = G.S
    mv = G.mod[:, l * 192:(l + 1) * 192].rearrange("p (i c w) -> p i c w", i=6, c=16)
    o, _ = PCOLS[nname]
    for which in range(2):
        S.op("dve", lambda e, which=which: e.scalar_tensor_tensor(
            out=gm[:, which * 16:(which + 1) * 16], in0=mv[:, sc_idx, :, which], scalar=1.0,
            in1=G.par[:, l * PL + o:l * PL + o + 16], op0=ALU.add, op1=ALU.mult), r=[G.mod, G.par], w=[gm])


def phase_norm_proj(G, l):
    nc, S = G.nc, G.S
    with ExitStack() as st:
        sb = lambda name, shape, dt=F32: G.sb(name, shape, dt, st)
        pst = lambda name: T(st.enter_context(nc.psum_tensor(name + "_%d" % G.nuid(), [128, 512], F32)))
        xts = [sb("np_x%d" % i, [128, 16, 512]) for i in range(2)]
        sq = sb("np_sq", [128, 16, 512])
        rstd = sb("np_rstd", [128, 512])
        us = [sb("np_u%d" % i, [128, 16, 512], BF16) for i in range(2)]
        wts = [sb("np_w%d" % i, [128, 16, 512], BF16) for i in range(3)]
        stg = [sb("np_stg%d" % i, [128, 4, 512]) for i in range(2)]
        gm = sb("np_gm", [128, 32])
        ps_ss = pst("np_pss")
        ps = [pst("np_ps%d" % i) for i in range(6)]
        gmod_cols(G, l, gm, "n1", 1)
        xv = G.XT.rearrange("(c p) t -> p c t", p=128)
        pv = G.PT.rearrange("(c p) t -> p c t", p=128)
        def do_norm(ti):
            t0, W = TILES[ti]
            which = 1 if ti == 0 else 0
            xt, u = xts[ti % 2], us[ti % 2]
            S.dma("sp", xt[:, :, 0:W], xv[:, :, t0:t0 + W], w=[xt])
            rms_modulate(G, xt, W, lambda c: gm[:, which * 16 + c:which * 16 + c + 1], lambda c: mcol(G, l, 0, c, which),
                         lambda c: (u[:, c, 0:W], [u]), sq, rstd, ps_ss)
        do_norm(0)
        for ti, (t0, W) in enumerate(TILES):
            u = us[ti % 2]
            if ti + 1 < len(TILES):
                do_norm(ti + 1)
            state = {"k": 0}

            def epi(oc, p_, t0=t0, W=W):
                k = G.stg_it
                sg = stg[(k // 4) % 2]
                j = oc % 4
                eng = "act" if oc % 2 == 0 else "dve"
                if eng == "act":
                    S.op("act", lambda e: e.activation(out=sg[:, j, 0:W], in_=p_[:, 0:W], func=AF.Copy), r=[p_], w=[sg])
                else:
                    S.op("dve", lambda e: e.tensor_copy(out=sg[:, j, 0:W], in_=p_[:, 0:W]), r=[p_], w=[sg])
                G.stg_it += 1
                if j == 3 or oc == NPC - 1:
                    o0 = oc - j
                    S.dma("sp", pv[:, o0:oc + 1, t0:t0 + W], sg[:, 0:j + 1, 0:W], r=[sg])
                    G.stg_it = ((G.stg_it + 3) // 4) * 4
            linear(G, u, 16, W, G.wb_in[l], NPC, epi, ps, wts)


def phase_out_proj(G, l):
    nc, S = G.nc, G.S
    with ExitStack() as st:
        sb = lambda name, shape, dt=F32: G.sb(name, shape, dt, st)
        pst = lambda name: T(st.enter_context(nc.psum_tensor(name + "_%d" % G.nuid(), [128, 512], F32)))
        xts = [sb("op_x%d" % i, [128, 16, 512]) for i in range(2)]
        ms = [sb("op_m%d" % i, [128, 16, 512], BF16) for i in range(2)]
        sq = sb("op_sq", [128, 16, 512])
        rstd = sb("op_rstd", [128, 512])
        us = [sb("op_u%d" % i, [128, 16, 512], BF16) for i in range(1)]
        wts = [sb("op_w%d" % i, [128, 16, 512], BF16) for i in range(2)]
        gm = sb("op_gm", [128, 32])
        ps_ss = pst("op_pss")
        ps = [pst("op_ps%d" % i) for i in range(6)]
        gmod_cols(G, l, gm, "n2", 4)
        xv = G.XT.rearrange("(c p) t -> p c t", p=128)
        mv = G.MIX.rearrange("(c p) t -> p c t", p=128)
        uv = G.U2.rearrange("(c p) t -> p c t", p=128)
        for ti, (t0, W) in enumerate(TILES):
            which = 1 if ti == 0 else 0
            xt, u, m = xts[ti % 2], us[0], ms[ti % 2]
            S.dma("sp", xt[:, :, 0:W], xv[:, :, t0:t0 + W], w=[xt])
            S.dma("sp", m[:, :, 0:W], mv[:, :, t0:t0 + W], w=[m])

            def epi(oc, p_, W=W, xt=xt, which=which):
                S.op("dve", lambda e: e.scalar_tensor_tensor(out=xt[:, oc, 0:W], in0=p_[:, 0:W], scalar=mcol(G, l, 2, oc, which),
                                                             in1=xt[:, oc, 0:W], op0=ALU.mult, op1=ALU.add),
                     r=[p_, G.mod, xt], w=[xt])
            linear(G, m, 16, W, G.wb_out[l], 16, epi, ps, wts)
            S.dma("sp", xv[:, :, t0:t0 + W], xt[:, :, 0:W], r=[xt])
            rms_modulate(G, xt, W, lambda c: gm[:, which * 16 + c:which * 16 + c + 1], lambda c: mcol(G, l, 3, c, which),
                         lambda c: (u[:, c, 0:W], [u]), sq, rstd, ps_ss)
            S.dma("sp", uv[:, :, t0:t0 + W], u[:, :, 0:W], r=[u])


def phase_ffn(G, l):
    nc, S = G.nc, G.S
    with ExitStack() as st:
        sb = lambda name, shape, dt=F32: G.sb(name, shape, dt, st)
        pst = lambda name: T(st.enter_context(nc.psum_tensor(name + "_%d" % G.nuid(), [128, 512], F32)))
        uh = [sb("ff_u%d" % i, [128, 16, 514], BF16) for i in range(2)]
        gt = sb("ff_g", [128, 44, 512], BF16)
        wg = [sb("ff_wg%d" % i, [128, 16, 256], BF16) for i in range(2)]
        wu = [sb("ff_wu%d" % i, [128, 16, 256], BF16) for i in range(2)]
        wo = [sb("ff_wo%d" % i, [128, 44, 256], BF16) for i in range(2)]
        xc = [sb("ff_x%d" % i, [128, 512]) for i in range(3)]
        hh = [sb("ff_hh%d" % i, [128, 514]) for i in range(2)]
        tm = [sb("ff_tm%d" % i, [128, 512]) for i in range(2)]
        ge = [sb("ff_ge%d" % i, [128, 512]) for i in range(2)]
        psg = [pst("ff_pg%d" % i) for i in range(2)]
        psh = pst("ff_ph")
        psu = [pst("ff_pu%d" % i) for i in range(2)]
        pso = [pst("ff_po%d" % i) for i in range(3)]
        xv = G.XT.rearrange("(c p) t -> p c t", p=128)
        uv = G.U2.rearrange("(c p) t -> p c t", p=128)
        wiv = G.wb_fi[l].rearrange("(g p) (kc c) -> p g kc c", p=128, c=256)
        it = 0
        for ti, (t0, W) in enumerate(TILES):
            which = 1 if ti == 0 else 0
            u = uh[ti % 2]
            lz = any(t0 == a for a, b in SEQS)
            rz = any(t0 + W == b for a, b in SEQS)
            lo = t0 - (0 if lz else 1)
            hi = t0 + W + (0 if rz else 1)
            S.dma("sp", u[:, :, (1 if lz else 0):(1 if lz else 0) + hi - lo], uv[:, :, lo:hi], w=[u])
            if lz:
                S.op("dve", lambda e, u=u: e.memset(u[:, :, 0:1], 0.0), w=[u])
            if rz:
                S.op("dve", lambda e, u=u, W=W: e.memset(u[:, :, W + 1:W + 2], 0.0), w=[u])
            for g in range(22):
                a, b = wg[it % 2], wu[it % 2]
                S.dma("sp", a[:], wiv[:, g, :, :], w=[a])
                S.dma("sp", b[:], wiv[:, 22 + g, :, :], w=[b])
                it += 1
                for j in range(2):
                    ch = g * 2 + j
                    pg, pu = psg[ch % 2], psu[ch % 2]
                    h, t_, g_ = hh[ch % 2], tm[ch % 2], ge[ch % 2]
                    for kc in range(16):
                        S.op("pe", lambda e, kc=kc, j=j, a=a, pg=pg: e.matmul(pg[:, 0:W], lhsT=a[:, kc, j * 128:(j + 1) * 128],
                                                                         rhs=u[:, kc, 1:W + 1], start=(kc == 0), stop=(kc == 15)),
                             r=[a, u], w=[pg])
                    hs = psh[:, (ch % 8) * 2:(ch % 8) * 2 + 2]
                    for kc in range(16):
                        S.op("pe", lambda e, kc=kc, j=j, a=a, hs=hs: e.matmul(hs, lhsT=a[:, kc, j * 128:(j + 1) * 128],
                                                                         rhs=u[:, kc, 0:W + 2:W + 1], start=(kc == 0), stop=(kc == 15)),
                             r=[a, u], w=[psh])
                    for kc in range(16):
                        S.op("pe", lambda e, kc=kc, j=j, b=b, pu=pu: e.matmul(pu[:, 0:W], lhsT=b[:, kc, j * 128:(j + 1) * 128],
                                                                         rhs=u[:, kc, 1:W + 1], start=(kc == 0), stop=(kc == 15)),
                             r=[b, u], w=[pu])
                    S.op("act", lambda e, h=h, pg=pg: e.activation(out=h[:, 1:W + 1], in_=pg[:, 0:W], func=AF.Copy), r=[pg], w=[h])
                    S.op("act", lambda e, h=h, hs=hs: e.activation(out=h[:, 0:W + 2:W + 1], in_=hs, func=AF.Copy), r=[psh], w=[h])
                    S.op("act", lambda e, h=h, t_=t_, ch=ch: e.activation(out=t_[:, 0:W], in_=h[:, 1:W + 1], func=AF.Identity,
                                                                    scale=pcol(G, l, "cw1", ch), bias=pcol(G, l, "cb", ch)),
                         r=[h, G.par], w=[t_])
                    S.op("dve", lambda e, h=h, t_=t_, ch=ch: e.scalar_tensor_tensor(out=t_[:, 0:W], in0=h[:, 0:W], scalar=pcol(G, l, "cw0", ch),
                                                                             in1=t_[:, 0:W], op0=ALU.mult, op1=ALU.add),
                         r=[h, t_, G.par], w=[t_])
                    S.op("dve", lambda e, h=h, t_=t_, ch=ch: e.scalar_tensor_tensor(out=t_[:, 0:W], in0=h[:, 2:W + 2], scalar=pcol(G, l, "cw2", ch),
                                                                             in1=t_[:, 0:W], op0=ALU.mult, op1=ALU.add),
                         r=[h, t_, G.par], w=[t_])
                    S.op("act", lambda e, t_=t_, g_=g_: e.activation(out=g_[:, 0:W], in_=t_[:, 0:W], func=AF.Gelu), r=[t_], w=[g_])
                    S.op("dve", lambda e, g_=g_, pu=pu, ch=ch: e.tensor_tensor(out=gt[:, ch, 0:W], in0=g_[:, 0:W], in1=pu[:, 0:W], op=ALU.mult),
                         r=[g_, pu], w=[gt])

            def epi(oc, p_, W=W, t0=t0, which=which):
                x_ = xc[G.xc_it % 3]
                G.xc_it += 1
                S.dma("sp", x_[:, 0:W], xv[:, oc, t0:t0 + W], w=[x_])
                S.op("dve", lambda e: e.scalar_tensor_tensor(out=x_[:, 0:W], in0=p_[:, 0:W], scalar=mcol(G, l, 5, oc, which),
                                                             in1=x_[:, 0:W], op0=ALU.mult, op1=ALU.add),
                     r=[p_, G.mod, x_], w=[x_])
                S.dma("sp", xv[:, oc, t0:t0 + W], x_[:, 0:W], r=[x_])
            linear(G, gt, 44, W, G.wb_fo[l], 16, epi, pso, wo, gsz=2)


def final_norm(G):
    nc, S = G.nc, G.S
    with ExitStack() as st:
        sb = lambda name, shape, dt=F32: G.sb(name, shape, dt, st)
        xts = [sb("fn_x%d" % i, [128, 16, 512]) for i in range(2)]
        os_ = [sb("fn_o%d" % i, [128, 16, 512]) for i in range(2)]
        sq = sb("fn_sq", [128, 16, 512])
        rstd = sb("fn_rstd", [128, 512])
        ps_ss = T(st.enter_context(nc.psum_tensor("fin_pss", [128, 512], F32)))
        xv = G.XT.rearrange("(c p) t -> p c t", p=128)
        ov = G.out.rearrange("(c p) t -> p c t", p=128)
        for ti, (t0, W) in enumerate(TILES[1:]):
            xt, o = xts[ti % 2], os_[ti % 2]
            S.dma("sp", xt[:, :, 0:W], xv[:, :, t0:t0 + W], w=[xt])
            rms_modulate(G, xt, W, lambda c: G.par[:, P_FINAL + c:P_FINAL + c + 1], None,
                         lambda c: (o[:, c, 0:W], [o]), sq, rstd, ps_ss)
            S.dma("sp", ov[:, :, t0 - NCTX:t0 - NCTX + W], o[:, :, 0:W], r=[o])


def layer(G, l):
    S = G.S
    dbg = G.dbg or {}
    phase_norm_proj(G, l)
    S.barrier()
    if "PT" in dbg and l == 0:
        S.dma("sp", G.dbg_out["PT"][:, :], G.PT[:, :])
        S.barrier()
    if "mixin" in dbg:
        for c in range(16):
            S.dma("pool", G.MIX[c * 128:(c + 1) * 128, :], G.mixin[c * 128:(c + 1) * 128, :])
    else:
        phase_mixers(G, l)
    S.barrier()
    phase_out_proj(G, l)
    S.barrier()
    phase_ffn(G, l)
    S.barrier()
    if "YD" in dbg and l == 0:
        for d in range(2):
            for hh in range(4):
                S.dma("sp", G.dbg_out["YD"][d * 512 + hh * 128:d * 512 + (hh + 1) * 128, :], G.YD[d][hh * 128:(hh + 1) * 128, :])
        S.barrier()
    if "DER" in dbg and l == 0:
        for d in range(2):
            for a in range(DER_N):
                for hh in range(4):
                    S.dma("sp", G.dbg_out["DER"][(d * DER_N + a) * 512 + hh * 128:(d * DER_N + a) * 512 + (hh + 1) * 128, :], G.DER[d][a][hh * 128:(hh + 1) * 128, :])
        S.barrier()
    if "MIX" in dbg and l == 0:
        for c in range(16):
            S.dma("pool", G.dbg_out["MIX"][c * 128:(c + 1) * 128, :], G.MIX[c * 128:(c + 1) * 128, :])
        S.barrier()
    if "XT" in dbg and l == 0:
        S.dma("sp", G.dbg_out["XT"][:, :], G.XT[:, :])
        S.barrier()


_CONST = None


def const_inputs():
    global _CONST
    if _CONST is not None:
        return _CONST
    bf = ml_dtypes.bfloat16
    t = np.arange(NLAT, dtype=np.int64)
    tk = (t[:, None] * t[None, :]) % NLAT
    ang = 2.0 * np.pi * tk.astype(np.float64) / NLAT
    C = {}
    C["cosL"] = np.cos(ang).astype(np.float32).astype(bf)
    C["sinL"] = np.sin(ang).astype(np.float32).astype(bf)
    t = np.arange(NCTX, dtype=np.int64)
    ang = 2.0 * np.pi * ((t[:, None] * t[None, :]) % NCTX).astype(np.float64) / NCTX
    C["cosC"] = np.cos(ang).astype(np.float32).astype(bf)
    C["sinC"] = np.sin(ang).astype(np.float32).astype(bf)
    c = np.arange(128, dtype=np.int64)
    ang = 2.0 * np.pi * ((c[:, None] * c[None, :]) % 128).astype(np.float64) / 128
    C["cs128"] = np.concatenate([np.cos(ang), -np.sin(ang)], axis=1).astype(np.float32).astype(bf)
    pos = np.arange(NLAT)
    row = (pos // 64).astype(np.float64)
    colp = (pos % 64).astype(np.float64)
    inv = 10000.0 ** (-np.arange(32, dtype=np.float64) / 32)
    rc = np.zeros((128, NLAT), np.float64)
    rs = np.zeros((128, NLAT), np.float64)
    for d in range(128):
        axis, ab, pr = d // 64, (d % 64) // 32, d % 32
        a = (row if axis == 0 else colp) * inv[pr]
        rc[d] = np.cos(a)
        rs[d] = np.sin(a) * (-1.0 if ab == 0 else 1.0)
    C["ropeC"] = rc.astype(np.float32)
    C["ropeS"] = rs.astype(np.float32)
    j = np.arange(128)
    am = np.zeros((128, 3, 128), np.float32)
    am[:, 0, :] = (j[:, None] >= j[None, :])
    am[:, 1, :] = (j[:, None] <= j[None, :])
    am[:, 2, :] = np.eye(128)
    C["amask"] = am.astype(bf)
    j = np.arange(64)
    rwm = np.zeros((64, 5, 64), np.float32)
    rwm[:, 0, :] = (j[:, None] < j[None, :])
    rwm[:, 1, :] = (j[:, None] <= j[None, :])
    rwm[:, 2, :] = (j[None, :] < j[:, None])
    rwm[:, 3, :] = (j[None, :] <= j[:, None])
    rwm[:, 4, :] = np.eye(64)
    C["rwm"] = rwm
    rm = np.ones((128, 512), np.float32)
    rm[:, ::64] = 0.0
    C["rmask"] = rm
    _CONST = C
    return C


CONST_SHAPES = {"rwm": ([64, 5, 64], F32), "rmask": ([128, 512], F32), "cosL": ([NLAT, NLAT], BF16), "sinL": ([NLAT, NLAT], BF16), "cosC": ([NCTX, NCTX], BF16), "sinC": ([NCTX, NCTX], BF16),
                "cs128": ([128, 256], BF16), "ropeC": ([128, NLAT], F32), "ropeS": ([128, NLAT], F32), "amask": ([128, 3, 128], BF16)}


def phase_fnet(G, l):
    nc, S = G.nc, G.S
    pv = G.PT.rearrange("(c p) t -> p c t", p=128)
    mv = G.MIX.rearrange("(c p) t -> p c t", p=128)
    with ExitStack() as st:
        sb = lambda name, shape, dt=F32, s_=st: G.sb(name, shape, dt, s_)
        pst = lambda name: T(st.enter_context(nc.psum_tensor(name + "_%d" % G.nuid(), [128, 512], F32)))
        AT = sb("fn_AT", [128, 34, 4, 256], BF16)
        cs = sb("fn_cs", [128, 256], BF16)
        S.dma("sp", cs[:], G.C["cs128"][:, :], w=[cs])
        ps = [pst("fnp%d" % i) for i in range(6)]
        ps_ss = pst("fnpss")
        with ExitStack() as st1:
            zf = [sb("fn_zf%d" % i, [128, TOK], F32, st1) for i in range(2)]
            zb = sb("fn_zb", [128, 4, TOK], BF16, st1)
            for g in range(4):
                S.dma("sp", zf[g % 2][:], pv[:, C_F + g, :], w=[zf[g % 2]])
                S.op("act" if g % 2 == 0 else "dve", (lambda e, g=g: e.activation(out=zb[:, g, :], in_=zf[g % 2][:], func=AF.Copy)) if g % 2 == 0
                     else (lambda e, g=g: e.tensor_copy(out=zb[:, g, :], in_=zf[g % 2][:])), r=[zf[g % 2]], w=[zb])
            for tb in range(34):
                pa, pb = ps[(2 * tb) % 6], ps[(2 * tb + 1) % 6]
                for g in range(4):
                    p_ = pa if g < 2 else pb
                    S.op("pe", lambda e, g=g, tb=tb, p_=p_: e.matmul(p_[:, (g % 2) * 256:(g % 2) * 256 + 256], lhsT=zb[:, g, tb * 128:(tb + 1) * 128],
                                                                 rhs=cs[:, :], start=True, stop=True), r=[zb, cs], w=[p_])
                S.op("act", lambda e, tb=tb, pa=pa: e.activation(out=AT[:, tb, 0:2, :], in_=pa[:, :].rearrange("p (g c) -> p g c", g=2), func=AF.Copy),
                     r=[pa], w=[AT])
                S.op("dve", lambda e, tb=tb, pb=pb: e.tensor_copy(out=AT[:, tb, 2:4, :], in_=pb[:, :].rearrange("p (g c) -> p g c", g=2)),
                     r=[pb], w=[AT])
            S.barrier()
        tc_ = [sb("fn_tc%d" % i, [128, 16, 512], BF16) for i in range(2)]
        ts_ = [sb("fn_ts%d" % i, [128, 16, 512], BF16) for i in range(2)]
        fo = [sb("fn_fo%d" % i, [128, 4, 512]) for i in range(2)]
        sq = sb("fn_sq", [128, 4, 512])
        rstd = sb("fn_rstd", [128, 512])
        ob = [sb("fn_ob%d" % i, [128, 4, 512], BF16) for i in range(2)]
        it = 0
        for (base, L, ntb, tb0, KW, ctab, stab) in ((0, NCTX, 2, 0, 256, "cosC", "sinC"), (NCTX, NLAT, 32, 2, 512, "cosL", "sinL")):
            cv = G.C[ctab].rearrange("(tb p) k -> p tb k", p=128)
            sv = G.C[stab].rearrange("(tb p) k -> p tb k", p=128)
            scale = float(1.0 / np.sqrt(L * 128.0))
            for kt in range(L // KW):
                f_, o_ = fo[kt % 2], ob[kt % 2]
                for half in range((ntb + 15) // 16):
                    nb = min(16, ntb - half * 16)
                    a, b = tc_[it % 2], ts_[it % 2]
                    it += 1
                    S.dma("sp", a[:, 0:nb, 0:KW], cv[:, half * 16:half * 16 + nb, kt * KW:(kt + 1) * KW], w=[a])
                    S.dma("sp", b[:, 0:nb, 0:KW], sv[:, half * 16:half * 16 + nb, kt * KW:(kt + 1) * KW], w=[b])
                    for i in range(nb):
                        tb = half * 16 + i
                        for g in range(4):
                            S.op("pe", lambda e, g=g, i=i, tb=tb, a=a: e.matmul(ps[g][:, 0:KW], lhsT=AT[:, tb0 + tb, g, 0:128], rhs=a[:, i, 0:KW],
                                                                         start=(tb == 0), stop=False), r=[AT, a], w=[ps[g]])
                            S.op("pe", lambda e, g=g, i=i, tb=tb, b=b: e.matmul(ps[g][:, 0:KW], lhsT=AT[:, tb0 + tb, g, 128:256], rhs=b[:, i, 0:KW],
                                                                         start=False, stop=(tb == ntb - 1)), r=[AT, b], w=[ps[g]])
                for g in range(4):
                    S.op("act", lambda e, g=g, f_=f_: e.activation(out=f_[:, g, 0:KW], in_=ps[g][:, 0:KW], func=AF.Copy, scale=scale),
                         r=[ps[g]], w=[f_])
                rms_modulate(G, f_, KW, lambda c: pcol(G, l, "fg", c), None, lambda c: (o_[:, c, 0:KW], [o_]), sq, rstd, ps_ss, nch=4, dim=512)
                S.dma("sp", mv[:, 0:4, base + kt * KW:base + (kt + 1) * KW], o_[:, :, 0:KW], r=[o_])


def phase_attn(G, l):
    nc, S = G.nc, G.S
    pv = G.PT.rearrange("(c p) t -> p c t", p=128)
    mv = G.MIX.rearrange("(c p) t -> p c t", p=128)
    SCALE = 128.0 ** -0.5
    with ExitStack() as st:
        sb = lambda name, shape, dt=F32, s_=st: G.sb(name, shape, dt, s_)
        pst = lambda name: T(st.enter_context(nc.psum_tensor(name + "_%d" % G.nuid(), [128, 512], F32)))
        QR = sb("at_QR", [128, 8, TOK], BF16)
        KR = sb("at_KR", [128, 2, TOK], BF16)
        VT = sb("at_VT", [128, 34, 2, 128], BF16)
        am = sb("at_am", [128, 3, 128], BF16)
        esink = sb("at_es", [128, 8])
        S.dma("sp", am[:], G.C["amask"][:, :, :], w=[am])
        S.op("act", lambda e: e.activation(out=esink[:], in_=pcol(G, l, "sink", 0, 8), func=AF.Exp), r=[G.par], w=[esink])
        pT = T(st.enter_context(nc.psum_tensor("at_pT_%d" % G.nuid(), [128, 1024], BF16)))
        with ExitStack() as st1:
            rc = sb("at_rc", [128, NLAT], F32, st1)
            rs = sb("at_rs", [128, NLAT], F32, st1)
            S.dma("sp", rc[:], G.C["ropeC"][:, :], w=[rc])
            S.dma("sp", rs[:], G.C["ropeS"][:, :], w=[rs])
            qf = [sb("at_qf%d" % i, [128, 512], F32, st1) for i in range(2)]
            qs = [sb("at_qs%d" % i, [128, 512], F32, st1) for i in range(2)]
            t1 = [sb("at_t1%d" % i, [128, 512], F32, st1) for i in range(2)]
            t2 = [sb("at_t2%d" % i, [128, 512], F32, st1) for i in range(2)]
            vb = [sb("at_vb%d" % i, [128, 512], BF16, st1) for i in range(2)]
            it = 0
            for ch in range(10):
                src = C_Q + ch if ch < 8 else C_AK + (ch - 8)
                dst = (lambda a, b: QR[:, ch, a:b]) if ch < 8 else (lambda a, b: KR[:, ch - 8, a:b])
                dT = QR if ch < 8 else KR
                for ti, (t0, W) in enumerate(TILES):
                    q_, s_, a_, b_ = qf[it % 2], qs[it % 2], t1[it % 2], t2[it % 2]
                    it += 1
                    S.dma("sp", q_[:, 0:W], pv[:, src, t0:t0 + W], w=[q_])
                    if ti == 0:
                        S.op("act", lambda e, q_=q_, W=W, dst=dst, t0=t0: e.activation(out=dst(t0, t0 + W), in_=q_[:, 0:W], func=AF.Copy), r=[q_], w=[dT])
                        continue
                    for blk in range(4):
                        sp = (blk ^ 1) * 32
                        S.dma("sp", s_[blk * 32:(blk + 1) * 32, 0:W], G.PT[src * 128 + sp:src * 128 + sp + 32, t0:t0 + W], w=[s_])
                    p0 = t0 - NCTX
                    S.op("dve", lambda e, q_=q_, a_=a_, p0=p0, W=W: e.tensor_tensor(out=a_[:, 0:W], in0=q_[:, 0:W], in1=rc[:, p0:p0 + W], op=ALU.mult),
                         r=[q_, rc], w=[a_])
                    S.op("pool", lambda e, s_=s_, b_=b_, p0=p0, W=W: e.tensor_tensor(out=b_[:, 0:W], in0=s_[:, 0:W], in1=rs[:, p0:p0 + W], op=ALU.mult),
                         r=[s_, rs], w=[b_])
                    S.op("dve", lambda e, a_=a_, b_=b_, W=W, dst=dst, t0=t0: e.tensor_tensor(out=dst(t0, t0 + W), in0=a_[:, 0:W], in1=b_[:, 0:W], op=ALU.add),
                         r=[a_, b_], w=[dT])
            for g in range(2):
                for ti, (t0, W) in enumerate(TILES):
                    q_, v_ = qf[it % 2], vb[it % 2]
                    it += 1
                    S.dma("sp", q_[:, 0:W], pv[:, C_AV + g, t0:t0 + W], w=[q_])
                    S.op("act", lambda e, q_=q_, v_=v_, W=W: e.activation(out=v_[:, 0:W], in_=q_[:, 0:W], func=AF.Copy), r=[q_], w=[v_])
                    nb = W // 128
                    for i in range(nb):
                        S.op("pe", lambda e, i=i, v_=v_: e.transpose(pT[:, i * 128:(i + 1) * 128], v_[:, i * 128:(i + 1) * 128], am[:, 2, :]),
                             r=[v_, am], w=[pT])
                    b0 = t0 // 128
                    S.op("dve", lambda e, g=g, b0=b0, nb=nb: e.tensor_copy(out=VT[:, b0:b0 + nb, g, :],
                                                                     in_=pT[:, 0:nb * 128].rearrange("p (b d) -> p b d", b=nb)), r=[pT], w=[VT])
            S.barrier()
        ao = [sb("at_ao%d" % i, [128, 8, 512]) for i in range(2)]
        sq = sb("at_sq", [128, 8, 512])
        rstd = sb("at_rstd", [128, 512])
        ob = [sb("at_ob%d" % i, [128, 8, 512], BF16) for i in range(2)]
        pts = [sb("at_pt%d" % i, [128, 4, 128], BF16) for i in range(4)]
        den = [sb("at_den%d" % i, [128, 4, 128]) for i in range(2)]
        ps_s = [pst("at_ps%d" % i) for i in range(3)]
        ps_n = [pst("at_pn%d" % i) for i in range(2)]
        ps_d = [pst("at_pd%d" % i) for i in range(2)]
        it = 0
        ib = 0
        for ti, (t0, W) in enumerate(TILES):
            a_, o_ = ao[ti % 2], ob[ti % 2]
            for g in range(2):
                for n in range(W // 128):
                    q0 = t0 + n * 128
                    gb = q0 // 128
                    kbs = [(0, None), (1, None)]
                    if ti > 0:
                        if gb > 2:
                            kbs.append((gb - 1, 0))
                        kbs.append((gb, None))
                        if gb < 33:
                            kbs.append((gb + 1, 1))
                    pn, pd = ps_n[ib % 2], ps_d[ib % 2]
                    dn = den[ib % 2]
                    ib += 1
                    for ki, (kb, mk) in enumerate(kbs):
                        p_s, pt = ps_s[it % 3], pts[it % 4]
                        it += 1
                        S.op("pe", lambda e, kb=kb, p_s=p_s, q0=q0, g=g: e.matmul(p_s[:, :].rearrange("p (h q) -> p h q", h=4), lhsT=KR[:, g, kb * 128:(kb + 1) * 128],
                                                                          rhs=QR[:, 4 * g:4 * g + 4, q0:q0 + 128], start=True, stop=True), r=[KR, QR], w=[p_s])
                        S.op("act", lambda e, p_s=p_s, pt=pt: e.activation(out=pt[:], in_=p_s[:, :].rearrange("p (h q) -> p h q", h=4), func=AF.Exp, scale=SCALE),
                             r=[p_s], w=[pt])
                        if mk is not None:
                            S.op("dve", lambda e, pt=pt, mk=mk: e.tensor_tensor(out=pt[:], in0=pt[:], in1=am[:, mk:mk + 1, :].broadcast_to([128, 4, 128]), op=ALU.mult),
                                 r=[pt, am], w=[pt])
                        S.op("pe", lambda e, kb=kb, pt=pt, pn=pn, ki=ki, g=g: e.matmul(pn[:, :].rearrange("p (h q) -> p h q", h=4), lhsT=VT[:, kb, g, :], rhs=pt[:],
                                                                             start=(ki == 0), stop=(ki == len(kbs) - 1)), r=[VT, pt], w=[pn])
                        S.op("pe", lambda e, pt=pt, pd=pd, ki=ki: e.matmul(pd[:, :].rearrange("p (h q) -> p h q", h=4), lhsT=G.onesb[:], rhs=pt[:],
                                                                       start=(ki == 0), stop=(ki == len(kbs) - 1)), r=[G.onesb, pt], w=[pd])
                    S.op("dve", lambda e, pd=pd, dn=dn, g=g: e.tensor_tensor(out=dn[:], in0=pd[:, :].rearrange("p (h q) -> p h q", h=4),
                                                                       in1=esink[:, 4 * g:4 * g + 4].unsqueeze(2).broadcast_to([128, 4, 128]), op=ALU.add),
                         r=[pd, esink], w=[dn])
                    S.op("dve", lambda e, dn=dn: e.reciprocal(out=dn[:], in_=dn[:]), r=[dn], w=[dn])
                    S.op("dve", lambda e, pn=pn, dn=dn, a_=a_, g=g, n=n: e.tensor_tensor(out=a_[:, 4 * g:4 * g + 4, n * 128:(n + 1) * 128],
                                                                                 in0=pn[:, :].rearrange("p (h q) -> p h q", h=4), in1=dn[:], op=ALU.mult),
                         r=[pn, dn], w=[a_])
            rms_modulate(G, a_, W, lambda c: pcol(G, l, "ag", c), None, lambda c: (o_[:, c, 0:W], [o_]), sq, rstd, ps_s[0], nch=8, dim=1024)
            S.dma("sp", mv[:, 8:16, t0:t0 + W], o_[:, :, 0:W], r=[o_])


LD = 0.6065306597126334
NCHK = TOK // 64
SCW = 256
DER_N = 6


def phase_rwkv(G, l):
    S = G.S
    rwkv_r1(G, l)
    S.barrier()
    rwkv_r2(G, l)
    S.barrier()
    rwkv_r3(G, l)


def rwkv_r1(G, l):
    nc, S = G.nc, G.S
    pv = G.PT.rearrange("(c p) t -> p c t", p=128)
    with ExitStack() as st:
        sb = lambda name, shape, dt=F32, s_=st: G.sb(name, shape, dt, s_)
        pst = lambda name: T(st.enter_context(nc.psum_tensor(name + "_%d" % G.nuid(), [128, 512], F32)))
        w2t = sb("r1_w2", [128, 512]); a2t = sb("r1_a2", [128, 512]); g2t = sb("r1_g2", [128, 512])
        S.dma("sp", w2t[:], G.w2r[l], w=[w2t]); S.dma("sp", a2t[:], G.a2r[l], w=[a2t]); S.dma("sp", g2t[:], G.g2[l], w=[g2t])
        rmask = sb("r1_rm", [128, 512])
        S.dma("sp", rmask[:], G.C["rmask"][:, :], w=[rmask])
        bones = sb("r1_bo", [128, 128])
        S.op("dve", lambda e: e.memset(bones[:], 0.0), w=[bones])
        S.op("dve", lambda e: e.memset(bones[0:64, 0:64], 1.0), w=[bones])
        S.op("dve", lambda e: e.memset(bones[64:128, 64:128], 1.0), w=[bones])
        mmc = sb("r1_mmc", [128, 15]); omka = sb("r1_omka", [128, 4])
        S.op("dve", lambda e: e.tensor_tensor(out=mmc[:], in0=pcol(G, l, "mu0", 0, 15), in1=pcol(G, l, "mu1", 0, 15), op=ALU.add), r=[G.par], w=[mmc])
        S.op("dve", lambda e: e.tensor_scalar(out=mmc[:], in0=mmc[:], scalar1=-1.0, scalar2=1.0, op0=ALU.mult, op1=ALU.add), r=[mmc], w=[mmc])
        S.op("dve", lambda e: e.tensor_scalar(out=omka[:], in0=pcol(G, l, "ka", 0, 4), scalar1=-1.0, scalar2=1.0, op0=ALU.mult, op1=ALU.add), r=[G.par], w=[omka])
        rh = [sb("r1_rh%d" % i, [128, 514]) for i in range(3)]
        psx = sb("r1_psx", [128, 15, 512])
        tw = sb("r1_tw", [128, 512]); sgd = sb("r1_sgd", [128, 512])
        NT = 12
        tp = [sb("r1_t%d" % i, [128, 512]) for i in range(NT)]
        fixed = {n: sb("r1_f" + n, [128, 512]) for n in ("kk0", "sq", "kk", "kds", "bq")}
        ob = [sb("r1_o%d" % i, [128, 512]) for i in range(8)]
        ptt = [sb("r1_pt%d" % i, [128, 8]) for i in range(2)]
        ps = [pst("r1p%d" % i) for i in range(6)]
        cnt = {"t": 0, "o": 0, "p": 0, "rh": 0}

        def tmp():
            cnt["t"] += 1
            return tp[cnt["t"] % NT]

        def otile():
            cnt["o"] += 1
            return ob[cnt["o"] % 8]

        def psn():
            cnt["p"] += 1
            return ps[cnt["p"] % 6]

        def tt(eng, out, a, b, op, r, w):
            S.op(eng, lambda e: e.tensor_tensor(out=out, in0=a, in1=b, op=op), r=r, w=w)

        for ti, (t0, W) in enumerate(TILES):
            nck = W // 64
            lz = any(t0 == a for a, b in SEQS)
            rz = any(t0 + W == b for a, b in SEQS)
            lo = t0 - (0 if lz else 1)
            hi = t0 + W + (0 if rz else 1)
            for ci in range(15):
                cnt["rh"] += 1
                h = rh[cnt["rh"] % 3]
                S.dma("sp", h[:, (1 if lz else 0):(1 if lz else 0) + hi - lo], pv[:, C_R + ci, lo:hi], w=[h])
                if lz:
                    S.op("dve", lambda e, h=h: e.memset(h[:, 0:1], 0.0), w=[h])
                if rz:
                    S.op("dve", lambda e, h=h, W=W: e.memset(h[:, W + 1:W + 2], 0.0), w=[h])
                S.op("act", lambda e, h=h, ci=ci, W=W: e.activation(out=psx[:, ci, 0:W], in_=h[:, 1:W + 1], func=AF.Identity, scale=mmc[:, ci:ci + 1]),
                     r=[h, mmc], w=[psx])
                S.op("dve", lambda e, h=h, ci=ci, W=W: e.scalar_tensor_tensor(out=psx[:, ci, 0:W], in0=h[:, 0:W], scalar=pcol(G, l, "mu0", ci),
                                                                        in1=psx[:, ci, 0:W], op0=ALU.mult, op1=ALU.add), r=[h, psx, G.par], w=[psx])
                S.op("dve", lambda e, h=h, ci=ci, W=W: e.scalar_tensor_tensor(out=psx[:, ci, 0:W], in0=h[:, 2:W + 2], scalar=pcol(G, l, "mu1", ci),
                                                                        in1=psx[:, ci, 0:W], op0=ALU.mult, op1=ALU.add), r=[h, psx, G.par], w=[psx])
            S.op("act", lambda e, W=W: e.activation(out=tw[:, 0:W], in_=psx[:, 13, 0:W], func=AF.Tanh), r=[psx], w=[tw])
            S.op("act", lambda e, W=W: e.activation(out=sgd[:, 0:W], in_=psx[:, 4, 0:W], func=AF.Sigmoid), r=[psx], w=[sgd])
            for hc in range(4):
                r_, k_, v_ = psx[:, hc, 0:W], psx[:, 5 + hc, 0:W], psx[:, 9 + hc, 0:W]
                rows = slice(hc * 128, (hc + 1) * 128)
                p_ = psn()
                S.op("pe", lambda e, p_=p_, hc=hc, W=W: e.matmul(p_[:, 0:W], lhsT=g2t[:, hc * 128:(hc + 1) * 128], rhs=sgd[:, 0:W], start=True, stop=True),
                     r=[g2t, sgd], w=[p_])
                o = otile()
                S.op("act", lambda e, p_=p_, o=o, W=W: e.activation(out=o[:, 0:W], in_=p_[:, 0:W], func=AF.Copy), r=[p_], w=[o])
                S.dma("sp", G.GT[rows, t0:t0 + W], o[:, 0:W], r=[o])
                o = otile()
                S.op("act", lambda e, o=o, v_=v_, W=W: e.activation(out=o[:, 0:W], in_=v_, func=AF.Copy), r=[psx], w=[o])
                S.dma("sp", G.VS[rows, t0:t0 + W], o[:, 0:W], r=[o])
                kk0, sq, kk = fixed["kk0"], fixed["sq"], fixed["kk"]
                S.op("act", lambda e, kk0=kk0, k_=k_, hc=hc, W=W: e.activation(out=kk0[:, 0:W], in_=k_, func=AF.Identity, scale=pcol(G, l, "kks", hc)),
                     r=[psx, G.par], w=[kk0])
                S.op("act", lambda e, kk0=kk0, sq=sq, W=W: e.activation(out=sq[:, 0:W], in_=kk0[:, 0:W], func=AF.Square), r=[kk0], w=[sq])
                p_ = psn()
                S.op("pe", lambda e, p_=p_, sq=sq, W=W: e.matmul(p_[:, 0:W], lhsT=bones[:], rhs=sq[:, 0:W], start=True, stop=True), r=[bones, sq], w=[p_])
                S.op("act", lambda e, p_=p_, sq=sq, W=W: e.activation(out=sq[:, 0:W], in_=p_[:, 0:W], func=AF.Sqrt), r=[p_], w=[sq])
                S.op("dve", lambda e, sq=sq, W=W: e.tensor_scalar(out=sq[:, 0:W], in0=sq[:, 0:W], scalar1=1e-12, scalar2=None, op0=ALU.max), r=[sq], w=[sq])
                S.op("dve", lambda e, sq=sq, W=W: e.reciprocal(out=sq[:, 0:W], in_=sq[:, 0:W]), r=[sq], w=[sq])
                tt("dve", kk[:, 0:W], kk0[:, 0:W], sq[:, 0:W], ALU.mult, [kk0, sq], [kk])
                kds = fixed["kds"]
                for dr in range(2):
                    cnt["t"] = 0
                    prt = slice(dr * 64, (dr + 1) * 64)
                    pw, pa = psn(), psn()
                    S.op("pe", lambda e, pw=pw, dr=dr, hc=hc, W=W, prt=prt: e.matmul(pw[:, 0:W], lhsT=w2t[prt, hc * 128:(hc + 1) * 128], rhs=tw[prt, 0:W], start=True, stop=True),
                         r=[w2t, tw], w=[pw])
                    S.op("pe", lambda e, pa=pa, dr=dr, hc=hc, W=W, prt=prt: e.matmul(pa[:, 0:W], lhsT=a2t[prt, hc * 128:(hc + 1) * 128], rhs=psx[prt, 14, 0:W], start=True, stop=True),
                         r=[a2t, psx], w=[pa])
                    sg = tmp(); a_ = tmp()
                    S.op("act", lambda e, pw=pw, sg=sg, W=W, dr=dr, hc=hc: e.activation(out=sg[:, 0:W], in_=pw[:, 0:W], func=AF.Sigmoid, bias=pcol(G, l, "w0", dr * 4 + hc)),
                         r=[pw, G.par], w=[sg])
                    S.op("act", lambda e, pa=pa, a_=a_, W=W, dr=dr, hc=hc: e.activation(out=a_[:, 0:W], in_=pa[:, 0:W], func=AF.Sigmoid, bias=pcol(G, l, "a0", dr * 4 + hc)),
                         r=[pa, G.par], w=[a_])
                    kd = tmp(); nb = tmp()
                    S.op("act", lambda e, a_=a_, kd=kd, W=W, hc=hc: e.activation(out=kd[:, 0:W], in_=a_[:, 0:W], func=AF.Identity, scale=pcol(G, l, "ka", hc), bias=omka[:, hc:hc + 1]),
                         r=[a_, G.par, omka], w=[kd])
                    tt("dve", kd[:, 0:W], kd[:, 0:W], k_, ALU.mult, [kd, psx], [kd])
                    if dr == 0:
                        S.op("dve", lambda e, kds=kds, kd=kd, W=W: e.tensor_copy(out=kds[:, 0:W], in_=kd[:, 0:W]), r=[kd], w=[kds])
                    else:
                        tt("dve", kds[:, 0:W], kds[:, 0:W], kd[:, 0:W], ALU.add, [kds, kd], [kds])
                    S.op("dve", lambda e, a_=a_, nb=nb, kk=kk, W=W: e.scalar_tensor_tensor(out=nb[:, 0:W], in0=a_[:, 0:W], scalar=-1.0, in1=kk[:, 0:W],
                                                                                 op0=ALU.mult, op1=ALU.mult), r=[a_, kk], w=[nb])
                    c_ = tmp()
                    S.op("dve", lambda e, c_=c_, sg=sg, W=W: e.tensor_tensor_scan(out=c_[:, 0:W], data0=rmask[:, 0:W], data1=sg[:, 0:W], initial=0.0,
                                                                            op0=ALU.mult, op1=ALU.add), r=[rmask, sg], w=[c_])
                    c3 = c_[:, 0:W].rearrange("p (c t) -> p c t", t=64)
                    if dr == 1:
                        d_ = tmp()
                        tt("dve", d_[:, 0:W], sg[:, 0:W], c_[:, 0:W], ALU.subtract, [sg, c_], [d_])
                        tt("dve", d_[:, 0:W].rearrange("p (c t) -> p c t", t=64), d_[:, 0:W].rearrange("p (c t) -> p c t", t=64),
                           c3[:, :, 63:64].broadcast_to([128, nck, 64]), ALU.add, [d_, c_], [d_])
                        c_ = d_
                        c3 = c_[:, 0:W].rearrange("p (c t) -> p c t", t=64)
                        endi = 0
                    else:
                        endi = 63
                    e1 = tmp(); e2 = tmp(); e3 = tmp(); e4 = tmp()
                    S.op("act", lambda e, c_=c_, e1=e1, W=W: e.activation(out=e1[:, 0:W], in_=c_[:, 0:W], func=AF.Exp, scale=-LD), r=[c_], w=[e1])
                    S.op("act", lambda e, c_=c_, e2=e2, W=W: e.activation(out=e2[:, 0:W], in_=c_[:, 0:W], func=AF.Exp, scale=LD), r=[c_], w=[e2])
                    tt("dve", e3[:, 0:W], c_[:, 0:W], sg[:, 0:W], ALU.subtract, [c_, sg], [e3])
                    S.op("act", lambda e, e3=e3, W=W: e.activation(out=e3[:, 0:W], in_=e3[:, 0:W], func=AF.Exp, scale=-LD), r=[e3], w=[e3])
                    tt("dve", e4[:, 0:W].rearrange("p (c t) -> p c t", t=64), c3[:, :, endi:endi + 1].broadcast_to([128, nck, 64]), c3, ALU.subtract, [c_], [e4])
                    S.op("act", lambda e, e4=e4, W=W: e.activation(out=e4[:, 0:W], in_=e4[:, 0:W], func=AF.Exp, scale=-LD), r=[e4], w=[e4])
                    pt_ = ptt[(hc * 2 + dr) % 2]
                    S.op("dve", lambda e, pt_=pt_, e1=e1, W=W, endi=endi, nck=nck: e.tensor_copy(
                        out=pt_[:, 0:nck].unsqueeze(2), in_=e1[:, 0:W].rearrange("p (c t) -> p c t", t=64)[:, :, endi:endi + 1]), r=[e1], w=[pt_])
                    S.dma("sp", G.PTOT[dr][rows, t0 // 64:t0 // 64 + nck], pt_[:, 0:nck], r=[pt_])
                    for ai, (x_, y_, xd, yd) in enumerate(((kk, e3, kk, e3), (None, e1, psx, e1), (kd, e2, kd, e2), (nb, e2, nb, e2), (kd, e4, kd, e4), (nb, e4, nb, e4))):
                        o = otile()
                        xin_ = r_ if x_ is None else x_[:, 0:W]
                        tt("dve", o[:, 0:W], xin_, y_[:, 0:W], ALU.mult, [xd, yd], [o])
                        S.dma("sp", G.DER[dr][ai][rows, t0:t0 + W], o[:, 0:W], r=[o])
                bq = fixed["bq"]
                S.op("dve", lambda e, bq=bq, r_=r_, kds=kds, hc=hc, W=W: e.scalar_tensor_tensor(out=bq[:, 0:W], in0=r_, scalar=pcol(G, l, "rk", hc), in1=kds[:, 0:W],
                                                                                      op0=ALU.mult, op1=ALU.mult), r=[psx, kds, G.par], w=[bq])
                p_ = psn()
                S.op("pe", lambda e, p_=p_, bq=bq, W=W: e.matmul(p_[:, 0:W], lhsT=bones[:], rhs=bq[:, 0:W], start=True, stop=True), r=[bones, bq], w=[p_])
                o = otile()
                tt("dve", o[:, 0:W], p_[:, 0:W], v_, ALU.mult, [p_, psx], [o])
                S.dma("sp", G.BON[rows, t0:t0 + W], o[:, 0:W], r=[o])


class PV:
    def __init__(self, ap, d):
        self.ap = ap
        self.d = d


def rwkv_r2(G, l, HG=4):
    nc, S = G.nc, G.S
    with ExitStack() as st:
        sb = lambda name, shape, dt=F32, s_=st: G.sb(name, shape, dt, s_)
        banks = [st.enter_context(nc.psum_tensor("r2b%d_%d" % (i, G.nuid()), [128, 512], F32)) for i in range(8)]
        bdeps = [Dep(x=True) for i in range(8)]
        rwm = sb("r2_rwm", [64, 5, 64])
        S.dma("sp", rwm[:], G.C["rwm"][:, :, :], w=[rwm])
        ident = rwm[:, 4, :]
        m12 = [rwm[:, 0:2, :], rwm[:, 2:4, :]]
        mX = [rwm[:, 2, :], rwm[:, 0, :]]
        order = [list(range(NCHK)), [3, 2, 1, 0] + list(range(NCHK - 1, 3, -1))]
        nch = 2 * HG

        class Chain:
            pass
        chains = []
        for i in range(nch):
            c = Chain()
            c.inb = [sb("r2_in%d_%d" % (i, j), [64, 7, SCW]) for j in range(2)]
            c.tm = sb("r2_tm%d" % i, [64, 3, 64])
            c.AM1 = sb("r2_am1_%d" % i, [64, 2, 64]); c.AM2 = sb("r2_am2_%d" % i, [64, 2, 64])
            c.X = [sb("r2_x%d_%d" % (i, j), [64, 64]) for j in range(2)]
            c.Wk = [sb("r2_w%d_%d" % (i, j), [64, 2, 64]) for j in range(2)]
            c.Ti = sb("r2_ti%d" % i, [64, 64]); c.Z = sb("r2_z%d" % i, [64, 64]); c.U = sb("r2_u%d" % i, [64, 64])
            c.H = sb("r2_h%d" % i, [64, 64])
            c.ys = [sb("r2_ys%d_%d" % (i, j), [64, SCW]) for j in range(2)]
            c.pt = sb("r2_ptot%d" % i, [64, NCHK])
            c.regs = [PV(banks[i][0:64, k * 128:(k + 1) * 128], bdeps[i]) for k in range(4)]
            c.ri = 0
            chains.append(c)

        def prc(c):
            c.ri += 1
            return c.regs[c.ri % 4]

        def cp(eng, out, in_, r, w):
            if eng == "act":
                S.op("act", lambda e: e.activation(out=out, in_=in_, func=AF.Copy), r=r, w=w)
            else:
                S.op("dve", lambda e: e.tensor_copy(out=out, in_=in_), r=r, w=w)

        def mm(out_pv, out_ap, lhsT, rhs, start, stop, r):
            S.op("pe", lambda e: e.matmul(out_ap, lhsT=lhsT, rhs=rhs, start=start, stop=stop), r=r, w=[out_pv])

        for hg in range(8 // HG):
            for i, c in enumerate(chains):
                c.dr = i // HG
                c.head = hg * HG + i % HG
                c.rows = slice(c.head * 64, (c.head + 1) * 64)
                S.op("dve", lambda e, c=c: e.memset(c.H[:], 0.0), w=[c.H])
                S.dma("sp", c.pt[:], G.PTOT[c.dr][c.rows, :], w=[c.pt])
                c.nsc = 0
            for step in range(NCHK):
                for c in chains:
                    c.ck = order[c.dr][step]
                    c.off = (c.ck % 4) * 64
                    if step % 4 == 0:
                        c.nsc += 1
                        c.cur = c.inb[c.nsc % 2]
                        c.ysc = c.ys[c.nsc % 2]
                        sc = c.ck // 4
                        for ai in range(6):
                            S.dma("sp", c.cur[:, ai, :], G.DER[c.dr][ai][c.rows, sc * SCW:(sc + 1) * SCW], w=[c.cur])
                        S.dma("sp", c.cur[:, 6, :], G.VS[c.rows, sc * SCW:(sc + 1) * SCW], w=[c.cur])
                    c.cols = slice(c.off, c.off + 64)
                for c in chains:
                    c.p = prc(c)
                    for j, ai in enumerate((4, 5)):
                        S.op("pe", lambda e, c=c, j=j, ai=ai: e.transpose(c.p.ap[:, j * 64:(j + 1) * 64], c.cur[:, ai, c.cols], ident), r=[c.cur, rwm], w=[c.p])
                for c in chains:
                    pass
                for c in chains:
                    cp("act", c.tm[:, 0:2, :], c.p.ap[:, 0:128].rearrange("p (a t) -> p a t", a=2), [c.p], [c.tm])
                for c in chains:
                    c.p = prc(c)
                    S.op("pe", lambda e, c=c: e.transpose(c.p.ap[:, 0:64], c.cur[:, 6, c.cols], ident), r=[c.cur, rwm], w=[c.p])
                for c in chains:
                    cp("dve", c.tm[:, 2, :], c.p.ap[:, 0:64], [c.p], [c.tm])
                for c in chains:
                    c.p1, c.p2, c.p3 = prc(c), prc(c), prc(c)
                    kr = c.cur[:, 0:2, c.cols]
                    mm(c.p1, c.p1.ap[:, :].rearrange("p (a t) -> p a t", a=2), c.cur[:, 2, c.cols], kr, True, True, [c.cur])
                    mm(c.p2, c.p2.ap[:, :].rearrange("p (a t) -> p a t", a=2), c.cur[:, 3, c.cols], kr, True, True, [c.cur])
                    mm(c.p3, c.p3.ap[:, 0:64], c.cur[:, 0, c.cols], c.cur[:, 3, c.cols], True, True, [c.cur])
                for c in chains:
                    c.xi = 0
                    S.op("dve", lambda e, c=c: e.tensor_tensor(out=c.AM1[:], in0=c.p1.ap[:, :].rearrange("p (a t) -> p a t", a=2), in1=m12[c.dr], op=ALU.mult), r=[c.p1, rwm], w=[c.AM1])
                    S.op("dve", lambda e, c=c: e.tensor_tensor(out=c.AM2[:], in0=c.p2.ap[:, :].rearrange("p (a t) -> p a t", a=2), in1=m12[c.dr], op=ALU.mult), r=[c.p2, rwm], w=[c.AM2])
                    S.op("dve", lambda e, c=c: e.tensor_tensor(out=c.X[0][:], in0=c.p3.ap[:, 0:64], in1=mX[c.dr], op=ALU.mult), r=[c.p3, rwm], w=[c.X[0]])
                for c in chains:
                    c.p1, c.p2 = prc(c), prc(c)
                    mm(c.p1, c.p1.ap[:, 0:64], c.X[0][:], c.AM2[:, 0, :], True, True, [c.X[0], c.AM2])
                    mm(c.p2, c.p2.ap[:, 0:64], c.AM2[:, 0, :], c.X[0][:], True, True, [c.X[0], c.AM2])
                for c in chains:
                    cp("act", c.Wk[1][:, 0, :], c.p1.ap[:, 0:64], [c.p1], [c.Wk[1]])
                    S.op("dve", lambda e, c=c: e.tensor_tensor(out=c.Wk[1][:, 1, :], in0=c.AM2[:, 0, :], in1=ident, op=ALU.add), r=[c.AM2, rwm], w=[c.Wk[1]])
                    cp("act", c.X[1][:], c.p2.ap[:, 0:64], [c.p2], [c.X[1]])
                for k in range(1, 5):
                    a, b = k % 2, (k + 1) % 2
                    for c in chains:
                        c.p1, c.p2 = prc(c), prc(c)
                        mm(c.p1, c.p1.ap[:, :].rearrange("p (a t) -> p a t", a=2), c.X[a][:], c.Wk[a][:], True, True, [c.X[a], c.Wk[a]])
                        mm(c.p2, c.p2.ap[:, 0:64], c.Wk[a][:, 0, :], c.X[a][:], True, True, [c.X[a], c.Wk[a]])
                    for c in chains:
                        cp("act", c.Wk[b][:, 0, :], c.p1.ap[:, 0:64], [c.p1], [c.Wk[b]])
                        S.op("dve", lambda e, c=c, a=a, b=b: e.tensor_tensor(out=c.Wk[b][:, 1, :], in0=c.p1.ap[:, 64:128], in1=c.Wk[a][:, 1, :], op=ALU.add),
                             r=[c.p1, c.Wk[a]], w=[c.Wk[b]])
                        cp("dve", c.X[b][:], c.p2.ap[:, 0:64], [c.p2], [c.X[b]])
                for c in chains:
                    c.p1 = prc(c)
                    mm(c.p1, c.p1.ap[:, 0:64], c.X[1][:], c.Wk[1][:, 1, :], True, True, [c.X[1], c.Wk[1]])
                for c in chains:
                    S.op("dve", lambda e, c=c: e.tensor_tensor(out=c.Ti[:], in0=c.p1.ap[:, 0:64], in1=c.Wk[1][:, 1, :], op=ALU.add), r=[c.p1, c.Wk[1]], w=[c.Ti])
                for c in chains:
                    c.p1 = prc(c)
                    mm(c.p1, c.p1.ap[:, 0:64], c.cur[:, 0, c.cols], c.H[:], True, False, [c.cur, c.H])
                    mm(c.p1, c.p1.ap[:, 0:64], c.AM1[:, 0, :], c.tm[:, 2, :], False, True, [c.AM1, c.tm])
                for c in chains:
                    cp("act", c.Z[:], c.p1.ap[:, 0:64], [c.p1], [c.Z])
                for c in chains:
                    c.p1 = prc(c)
                    mm(c.p1, c.p1.ap[:, 0:64], c.Ti[:], c.Z[:], True, True, [c.Ti, c.Z])
                for c in chains:
                    cp("dve", c.U[:], c.p1.ap[:, 0:64], [c.p1], [c.U])
                for c in chains:
                    c.p1, c.p2 = prc(c), prc(c)
                    mm(c.p1, c.p1.ap[:, 0:64], c.H[:], c.cur[:, 1, c.cols], True, False, [c.H, c.cur])
                    mm(c.p1, c.p1.ap[:, 0:64], c.tm[:, 2, :], c.AM1[:, 1, :], False, False, [c.tm, c.AM1])
                    mm(c.p1, c.p1.ap[:, 0:64], c.U[:], c.AM2[:, 1, :], False, True, [c.U, c.AM2])
                    mm(c.p2, c.p2.ap[:, 0:64], c.tm[:, 0, :], c.tm[:, 2, :], True, False, [c.tm])
                    mm(c.p2, c.p2.ap[:, 0:64], c.tm[:, 1, :], c.U[:], False, True, [c.tm, c.U])
                for c in chains:
                    cp("act", c.ysc[:, c.cols], c.p1.ap[:, 0:64], [c.p1], [c.ysc])
                    S.op("dve", lambda e, c=c: e.scalar_tensor_tensor(out=c.H[:], in0=c.H[:], scalar=c.pt[:, c.ck:c.ck + 1], in1=c.p2.ap[:, 0:64],
                                                                  op0=ALU.mult, op1=ALU.add), r=[c.H, c.pt, c.p2], w=[c.H])
                if step % 4 == 3:
                    for c in chains:
                        sc = c.ck // 4
                        S.dma("sp", G.YD[c.dr][c.rows, sc * SCW:(sc + 1) * SCW], c.ysc[:], r=[c.ysc])


def rwkv_r3(G, l):
    nc, S = G.nc, G.S
    mv = G.MIX.rearrange("(c p) t -> p c t", p=128)
    with ExitStack() as st:
        sb = lambda name, shape, dt=F32, s_=st: G.sb(name, shape, dt, s_)
        pst = lambda name: T(st.enter_context(nc.psum_tensor(name + "_%d" % G.nuid(), [128, 512], F32)))
        bo = sb("r3_bo", [128, 128])
        S.op("dve", lambda e: e.memset(bo[:], 0.0), w=[bo])
        S.op("dve", lambda e: e.memset(bo[0:64, 0:64], 1.0 / 64), w=[bo])
        S.op("dve", lambda e: e.memset(bo[64:128, 64:128], 1.0 / 64), w=[bo])
        lnx = sb("r3_eps", [128, 1])
        S.op("dve", lambda e: e.memset(lnx[:], 64e-5), w=[lnx])
        ya = [sb("r3_ya%d" % i, [128, 512]) for i in range(2)]
        yb = [sb("r3_yb%d" % i, [128, 512]) for i in range(2)]
        bn = [sb("r3_bn%d" % i, [128, 512]) for i in range(2)]
        gg = [sb("r3_gg%d" % i, [128, 512]) for i in range(2)]
        t1 = [sb("r3_t1%d" % i, [128, 512]) for i in range(2)]
        t2 = [sb("r3_t2%d" % i, [128, 512]) for i in range(2)]
        ob = [sb("r3_ob%d" % i, [128, 512], BF16) for i in range(2)]
        ps = [pst("r3p%d" % i) for i in range(4)]
        it = 0
        for ti, (t0, W) in enumerate(TILES):
            for hc in range(4):
                rows = slice(hc * 128, (hc + 1) * 128)
                a, b, n_, g_, x1, x2, o = ya[it % 2], yb[it % 2], bn[it % 2], gg[it % 2], t1[it % 2], t2[it % 2], ob[it % 2]
                pm, pvv = ps[(2 * it) % 4], ps[(2 * it + 1) % 4]
                it += 1
                S.dma("sp", a[:, 0:W], G.YD[0][rows, t0:t0 + W], w=[a])
                S.dma("sp", b[:, 0:W], G.YD[1][rows, t0:t0 + W], w=[b])
                S.dma("sp", n_[:, 0:W], G.BON[rows, t0:t0 + W], w=[n_])
                S.dma("sp", g_[:, 0:W], G.GT[rows, t0:t0 + W], w=[g_])
                S.op("dve", lambda e, a=a, b=b, W=W: e.tensor_tensor(out=a[:, 0:W], in0=a[:, 0:W], in1=b[:, 0:W], op=ALU.add), r=[a, b], w=[a])
                S.op("pe", lambda e, a=a, pm=pm, W=W: e.matmul(pm[:, 0:W], lhsT=bo[:], rhs=a[:, 0:W], start=True, stop=True), r=[bo, a], w=[pm])
                S.op("dve", lambda e, a=a, pm=pm, x1=x1, W=W: e.tensor_tensor(out=x1[:, 0:W], in0=a[:, 0:W], in1=pm[:, 0:W], op=ALU.subtract), r=[a, pm], w=[x1])
                S.op("act", lambda e, x1=x1, x2=x2, W=W: e.activation(out=x2[:, 0:W], in_=x1[:, 0:W], func=AF.Square), r=[x1], w=[x2])
                S.op("pe", lambda e, x2=x2, pvv=pvv, W=W: e.matmul(pvv[:, 0:W], lhsT=bo[:], rhs=x2[:, 0:W], start=True, stop=True), r=[bo, x2], w=[pvv])
                S.op("act", lambda e, x2=x2, pvv=pvv, W=W: e.activation(out=x2[:, 0:W], in_=pvv[:, 0:W], func=AF.Sqrt, bias=lnx[:, 0:1]), r=[pvv, lnx], w=[x2])
                S.op("dve", lambda e, x2=x2, W=W: e.reciprocal(out=x2[:, 0:W], in_=x2[:, 0:W]), r=[x2], w=[x2])
                S.op("dve", lambda e, x1=x1, x2=x2, W=W: e.tensor_tensor(out=x1[:, 0:W], in0=x1[:, 0:W], in1=x2[:, 0:W], op=ALU.mult), r=[x1, x2], w=[x1])
                S.op("act", lambda e, x1=x1, W=W, hc=hc: e.activation(out=x1[:, 0:W], in_=x1[:, 0:W], func=AF.Identity, scale=pcol(G, l, "lnw", hc), bias=pcol(G, l, "lnb", hc)),
                     r=[x1, G.par], w=[x1])
                S.op("dve", lambda e, x1=x1, n_=n_, W=W: e.tensor_tensor(out=x1[:, 0:W], in0=x1[:, 0:W], in1=n_[:, 0:W], op=ALU.add), r=[x1, n_], w=[x1])
                S.op("dve", lambda e, x1=x1, g_=g_, o=o, W=W: e.tensor_tensor(out=o[:, 0:W], in0=x1[:, 0:W], in1=g_[:, 0:W], op=ALU.mult), r=[x1, g_], w=[o])
                S.dma("sp", mv[:, 4 + hc, t0:t0 + W], o[:, 0:W], r=[o])

def phase_mixers(G, l):
    S = G.S
    phase_fnet(G, l)
    S.barrier()
    phase_attn(G, l)
    S.barrier()
    phase_rwkv(G, l)
    S.barrier()


EXTRA_W = ("w2r", "a2r", "g2")


def extra_w(inp, k):
    if k == "w2r":
        return np.ascontiguousarray(np.asarray(inp["rwkv_w2"], np.float32).reshape(DEPTH, 128, 512))
    if k == "a2r":
        return np.ascontiguousarray(np.asarray(inp["rwkv_a2"], np.float32).reshape(DEPTH, 128, 512))
    return np.ascontiguousarray(np.asarray(inp["rwkv_g2"], np.float32))


def make_inputs(inp, b):
    x = np.asarray(inp["x"][b], np.float32)
    cx = np.asarray(inp["ctx"][b], np.float32)
    xin = np.ascontiguousarray(np.concatenate([cx, x], axis=0).T)
    cvec = np.concatenate([_col(inp["c"][b]), _col(inp["c_ctx"])], axis=1)
    return {"xin": xin, "cvec": np.ascontiguousarray(cvec)}


def kernel(**inp):
    nc, G = build_nc()
    shared = {"params": pack_params(inp)}
    shared.update(const_inputs())
    for k in ("w_ada", "w_in", "w_out", "w_ffn_in", "w_ffn_out"):
        shared[k] = np.ascontiguousarray(np.asarray(inp[k], np.float32))
    for k in EXTRA_W:
        shared[k] = extra_w(inp, k)
    in_maps = []
    for b in range(8):
        m = dict(shared)
        m.update(make_inputs(inp, b))
        in_maps.append(m)
    res = run_bass_kernel_spmd(nc, in_maps, core_ids=list(range(8)))
    out = np.stack([np.ascontiguousarray(r["out"].T) for r in res.results], axis=0)
    return out.astype(np.float32)
```

```python
import numpy as np
from contextlib import ExitStack
import ml_dtypes
import concourse.bass as bass
import concourse.mybir as mybir
from concourse.bass_utils import run_bass_kernel_spmd

F32 = mybir.dt.float32
F32R = mybir.dt.float32r
BF16 = mybir.dt.bfloat16
AF = mybir.ActivationFunctionType
ALU = mybir.AluOpType
AX = mybir.AxisListType

D = 2048
NCTX = 256
NLAT = 4096
TOK = NCTX + NLAT
DEPTH = 4
INW = 3968
DFF = 5632
TILES = [(0, 256)] + [(256 + 512 * i, 512) for i in range(8)]
SEQS = [(0, NCTX), (NCTX, TOK)]
RMS_EPS = 1e-6

C_F, C_Q, C_R, C_G, C_K, C_V, C_WD, C_AD, C_AK, C_AV = 0, 4, 12, 16, 17, 21, 25, 26, 27, 29
NPC = 31

PCOLS = {}
_off = 0
for _n, _w in [("n1", 16), ("n2", 16), ("mu0", 15), ("mu1", 15), ("w0", 8), ("a0", 8), ("kks", 4), ("ka", 4), ("rk", 4),
               ("lnw", 4), ("lnb", 4), ("fg", 4), ("ag", 8), ("cw0", 44), ("cw1", 44), ("cw2", 44), ("cb", 44),
               ("bada", 96), ("sink", 8)]:
    PCOLS[_n] = (_off, _w)
    _off += _w
PL = _off
P_FINAL = DEPTH * PL
NPAR = P_FINAL + 16


def _col(v):
    v = np.asarray(v, np.float32).reshape(-1)
    return np.ascontiguousarray(v.reshape(-1, 128).T)


def pack_params(inp):
    P = np.zeros((128, NPAR), np.float32)
    for l in range(DEPTH):
        def put(name, arr):
            o, w = PCOLS[name]
            P[:, l * PL + o: l * PL + o + w] = arr
        put("n1", _col(inp["norm1_g"][l])); put("n2", _col(inp["norm2_g"][l]))
        put("mu0", _col(inp["rwkv_mu"][l][0])); put("mu1", _col(inp["rwkv_mu"][l][1]))
        put("w0", _col(inp["rwkv_w0"][l])); put("a0", _col(inp["rwkv_a0"][l]))
        put("kks", _col(inp["rwkv_kk_scale"][l])); put("ka", _col(inp["rwkv_ka"][l])); put("rk", _col(inp["rwkv_rk"][l]))
        put("lnw", _col(inp["rwkv_lnx_w"][l])); put("lnb", _col(inp["rwkv_lnx_b"][l]))
        put("fg", _col(inp["fourier_out_g"][l])); put("ag", _col(inp["attn_out_g"][l]))
        for j in range(3):
            put("cw%d" % j, _col(inp["ffn_conv_w"][l][j]))
        put("cb", _col(inp["ffn_conv_b"][l])); put("bada", _col(inp["b_ada"][l]))
        put("sink", np.broadcast_to(np.asarray(inp["attn_sink"][l], np.float32)[None, :], (128, 8)))
    P[:, P_FINAL:P_FINAL + 16] = _col(inp["final_g"])
    return P


class Dep:
    __slots__ = ("w", "r", "x")

    def __init__(self, x=False):
        self.w = None
        self.r = {}
        self.x = x


class T:
    def __init__(self, t):
        self.t = t
        self.d = Dep()

    def __getitem__(self, idx):
        return self.t[idx]


def _d(x):
    return getattr(x, "d", x)


class Sched:
    def __init__(self, nc, es):
        self.nc = nc
        self.E = {"pe": nc.tensor, "dve": nc.vector, "act": nc.scalar, "pool": nc.gpsimd, "sp": nc.sync}
        self.csem = {k: es.enter_context(nc.semaphore("cs_" + k)) for k in ("pe", "dve", "act", "pool")}
        self.cnt = {k: 0 for k in self.csem}
        self.seen = {k: {} for k in self.E}
        self.dq = {}
        for q, n in (("sp", 16), ("pool", 8), ("act", 8)):
            self.dq[q] = dict(sems=[es.enter_context(nc.semaphore("d_%s%d" % (q, i))) for i in range(n)],
                              cnt=[0] * n, nxt=0)
        self.nins = 0

    def _wait(self, e, tok):
        if tok is None:
            return
        key, sem, val = tok
        if e == "pe" and key == "pe":
            return
        if self.seen[e].get(key, 0) >= val:
            return
        self.E[e].wait_ge(sem, val)
        self.seen[e][key] = val

    def _deps(self, e, r, w):
        for d in r:
            self._wait(e, _d(d).w)
        for d in w:
            d = _d(d)
            self._wait(e, d.w)
            for t in list(d.r.values()):
                self._wait(e, t)

    def _mark(self, tok, r, w):
        for d in r:
            _d(d).r[tok[0]] = tok
        for d in w:
            d = _d(d)
            d.w = tok
            d.r = {}

    def op(self, e, fn, r=(), w=()):
        xs = [d for d in r if _d(d).x]
        if xs:
            r = [d for d in r if not _d(d).x]
            w = list(w) + xs
        self._deps(e, r, w)
        ins = fn(self.E[e])
        self.cnt[e] += 1
        ins.then_inc(self.csem[e], 1)
        self._mark((e, self.csem[e], self.cnt[e]), r, w)
        self.nins += 1

    def dma(self, q, out, in_, r=(), w=(), **kw):
        Q = self.dq[q]
        i = Q["nxt"]
        Q["nxt"] = (i + 1) % len(Q["sems"])
        key = (q, i)
        if Q["cnt"][i]:
            self._wait(q, (key, Q["sems"][i], 16 * Q["cnt"][i]))
        self._deps(q, r, w)
        ins = self.E[q].dma_start(out=out, in_=in_, **kw)
        Q["cnt"][i] += 1
        ins.then_inc(Q["sems"][i], 16)
        self._mark((key, Q["sems"][i], 16 * Q["cnt"][i]), r, w)
        self.nins += 1

    def barrier(self, engines=("pe", "dve", "act", "pool", "sp")):
        for e in engines:
            for k in self.csem:
                if self.cnt[k]:
                    self._wait(e, (k, self.csem[k], self.cnt[k]))
            for q, Q in self.dq.items():
                for i, s in enumerate(Q["sems"]):
                    if Q["cnt"][i]:
                        self._wait(e, ((q, i), s, 16 * Q["cnt"][i]))


class Ctx:
    pass


def build_nc(nl=DEPTH, dbg=None):
    nc = bass.Bass("TRN2", target_bir_lowering=False)
    G = Ctx()
    G.nc = nc
    G.dbg = dbg
    dt_in = lambda name, shape, dt=F32: nc.dram_tensor(name, shape, dt, kind="ExternalInput").ap()
    dt_sc = lambda name, shape, dt=F32: nc.dram_tensor(name, shape, dt, kind="Internal").ap()
    G.xin = dt_in("xin", [D, TOK])
    G.cvec = dt_in("cvec", [128, 32])
    G.params = dt_in("params", [128, NPAR])
    G.w_ada = dt_in("w_ada", [DEPTH, D, 6 * D])
    G.w_in = dt_in("w_in", [DEPTH, D, INW])
    G.w_out = dt_in("w_out", [DEPTH, D, D])
    G.w_ffn_in = dt_in("w_ffn_in", [DEPTH, D, 2 * DFF])
    G.w_ffn_out = dt_in("w_ffn_out", [DEPTH, DFF, D])
    G.out = nc.dram_tensor("out", [D, NLAT], F32, kind="ExternalOutput").ap()
    G.XT = dt_sc("XT", [D, TOK])
    G.PT = dt_sc("PT", [NPC * 128, TOK])
    G.MIX = dt_sc("MIX", [D, TOK], BF16)
    G.U2 = dt_sc("U2", [D, TOK], BF16)
    if dbg and "mixin" in dbg:
        G.mixin = dt_in("mixin", [D, TOK])
    G.C = {k: dt_in(k, sh, dt) for k, (sh, dt) in CONST_SHAPES.items()}
    G.w2r = dt_in("w2r", [DEPTH, 128, 512])
    G.a2r = dt_in("a2r", [DEPTH, 128, 512])
    G.g2 = dt_in("g2", [DEPTH, 128, 512])
    G.GT = dt_sc("GT", [512, TOK])
    G.VS = dt_sc("VS", [512, TOK])
    G.BON = dt_sc("BON", [512, TOK])
    G.PTOT = [dt_sc("PTOT%d" % d, [512, NCHK]) for d in range(2)]
    G.DER = [[dt_sc("DER%d_%d" % (d, a), [512, TOK]) for a in range(DER_N)] for d in range(2)]
    G.YD = [dt_sc("YD%d" % d, [512, TOK]) for d in range(2)]
    G.wb_in = [dt_sc("wb_in%d" % l, [8 * 128, 16 * 512], BF16) for l in range(nl)]
    G.wb_out = [dt_sc("wb_out%d" % l, [4 * 128, 16 * 512], BF16) for l in range(nl)]
    G.wb_fi = [dt_sc("wb_fi%d" % l, [44 * 128, 16 * 256], BF16) for l in range(nl)]
    G.wb_fo = [dt_sc("wb_fo%d" % l, [8 * 128, 44 * 256], BF16) for l in range(nl)]
    if dbg:
        G.dbg_out = {}
        for name, shape in dbg.items():
            if name == "mixin":
                continue
            G.dbg_out[name] = nc.dram_tensor("dbg_" + name, shape, F32, kind="ExternalOutput").ap()

    with ExitStack() as es:
        S = Sched(nc, es)
        G.S = S
        G.es = es
        G.uid = 0

        def nuid():
            G.uid += 1
            return G.uid
        G.nuid = nuid

        def sb(name, shape, dt=F32, st=es):
            G.uid += 1
            return T(st.enter_context(nc.sbuf_tensor("%s_%d" % (name, G.uid), shape, dt)))
        G.sb = sb
        G.par = sb("par", [128, NPAR])
        G.mod = sb("mod", [128, nl * 96 * 2])
        G.ones = sb("ones_f", [128, 128])
        G.onesb = sb("ones_b", [128, 128], BF16)
        G.wcast = Dep()
        G.epsc = sb("epsc", [128, 1])
        S.op("dve", lambda e: e.memset(G.epsc[:], RMS_EPS), w=[G.epsc])
        G.lin_it = 0
        G.ps_it = 0
        G.stg_it = 0
        G.xc_it = 0
        S.dma("sp", G.par[:], G.params[:, :], w=[G.par])
        S.op("dve", lambda e: e.memset(G.ones[:], 1.0), w=[G.ones])
        S.op("dve", lambda e: e.memset(G.onesb[:], 1.0), w=[G.onesb])
        for l in range(nl):
            for src, dst, K, N, gs in ((G.w_in, G.wb_in, D, INW, 512), (G.w_out, G.wb_out, D, D, 512),
                                       (G.w_ffn_in, G.wb_fi, D, 2 * DFF, 256), (G.w_ffn_out, G.wb_fo, DFF, D, 256)):
                dv = dst[l].rearrange("(g p) (kc c) -> p g kc c", p=128, c=gs)
                nf = N // gs
                for kc in range(K // 128):
                    S.dma("pool", dv[:, 0:nf, kc, :], src[l, kc * 128:(kc + 1) * 128, 0:nf * gs].rearrange("p (g c) -> p g c", c=gs),
                          w=[G.wcast], max_dma_last_dim=4096)
                    if N > nf * gs:
                        S.dma("pool", dv[:, nf, kc, 0:N - nf * gs], src[l, kc * 128:(kc + 1) * 128, nf * gs:N], w=[G.wcast], max_dma_last_dim=4096)
        xd = Dep()
        for c in range(16):
            S.dma("sp", G.XT[c * 128:(c + 1) * 128, :], G.xin[c * 128:(c + 1) * 128, :], w=[xd])
        prologue_adaln(G, nl)
        S.barrier()
        if dbg and "mod" in dbg:
            S.dma("sp", G.dbg_out["mod"][:, :], G.mod[:], r=[G.mod])
        for l in range(nl):
            layer(G, l)
        final_norm(G)
        S.barrier()
    G.nins = S.nins
    return nc, G


def pcol(G, l, name, c=0, n=1):
    o, w = PCOLS[name]
    return G.par[:, l * PL + o + c: l * PL + o + c + n]


def mcol(G, l, idx, c, which):
    j = ((l * 96) + idx * 16 + c) * 2 + which
    return G.mod[:, j:j + 1]


def prologue_adaln(G, nl):
    nc, S = G.nc, G.S
    with ExitStack() as st:
        sb = lambda name, shape, dt=F32: G.sb(name, shape, dt, st)
        cv = sb("cv", [128, 32])
        s2 = sb("s2", [128, 32])
        wts = [sb("wada%d" % i, [128, 16, 512]) for i in range(2)]
        ps = [T(st.enter_context(nc.psum_tensor("ps_ada%d" % i, [128, 512], F32))) for i in range(4)]
        S.dma("sp", cv[:], G.cvec[:, :], w=[cv])
        S.op("act", lambda e: e.activation(out=s2[:].rearrange("p (k w) -> p w k", w=2),
                                           in_=cv[:].rearrange("p (w k) -> p w k", w=2), func=AF.Silu), r=[cv], w=[s2])
        it = 0
        for l in range(nl):
            wv = G.w_ada[l].rearrange("(kc p) n -> p kc n", p=128)
            for cg in range(24):
                wt = wts[it % 2]
                S.dma("sp", wt[:], wv[:, :, cg * 512:(cg + 1) * 512], w=[wt])
                for j in range(4):
                    ch = cg * 4 + j
                    p_ = ps[(it * 4 + j) % 4]
                    for kc in range(16):
                        S.op("pe", lambda e, kc=kc, j=j, p_=p_, wt=wt: e.matmul(
                            p_[:, 0:2], lhsT=wt[:, kc, j * 128:(j + 1) * 128], rhs=s2[:, kc * 2:kc * 2 + 2],
                            start=(kc == 0), stop=(kc == 15)), r=[wt, s2], w=[p_])
                    o = (l * 96 + ch) * 2
                    S.op("dve", lambda e, p_=p_, o=o, ch=ch, l=l: e.tensor_scalar(
                        out=G.mod[:, o:o + 2], in0=p_[:, 0:2], scalar1=pcol(G, l, "bada", ch), scalar2=None, op0=ALU.add),
                        r=[p_, G.par], w=[G.mod])
                it += 1


def rms_modulate(G, xt, W, scale_col, bias_col, out_fn, sq, rstd, ps_ss, nch=16, dim=D, ones=None):
    S = G.S
    ones = ones or G.ones
    S.op("act", lambda e: e.activation(out=sq[:, 0:nch, 0:W], in_=xt[:, 0:nch, 0:W], func=AF.Square), r=[xt], w=[sq])
    for c in range(nch):
        S.op("pe", lambda e, c=c: e.matmul(ps_ss[:, 0:W], lhsT=ones[:], rhs=sq[:, c, 0:W], start=(c == 0), stop=(c == nch - 1)),
             r=[sq, ones], w=[ps_ss])
    S.op("act", lambda e: e.activation(out=rstd[:, 0:W], in_=ps_ss[:, 0:W], func=AF.Sqrt, scale=1.0 / dim, bias=G.epsc[:, 0:1]),
         r=[ps_ss, G.epsc], w=[rstd])
    S.op("dve", lambda e: e.reciprocal(out=rstd[:, 0:W], in_=rstd[:, 0:W]), r=[rstd], w=[rstd])
    S.op("dve", lambda e: e.tensor_tensor(out=sq[:, 0:nch, 0:W], in0=xt[:, 0:nch, 0:W],
                                          in1=rstd[:, 0:W].unsqueeze(1).broadcast_to([128, nch, W]), op=ALU.mult),
         r=[xt, rstd], w=[sq])
    for c in range(nch):
        o, od = out_fn(c)
        b = bias_col(c) if bias_col is not None else 0.0
        S.op("act", lambda e, c=c, o=o, b=b: e.activation(out=o, in_=sq[:, c, 0:W], func=AF.Identity, scale=scale_col(c), bias=b),
             r=[sq, G.par, G.mod] + list(od), w=od)


def linear(G, act, KC, W, wdram, n_oc, epilogue, ps, wts, gsz=4, col0=0, a0=0):
    S = G.S
    wv = wdram.rearrange("(g p) (kc c) -> p g kc c", p=128, c=gsz * 128)
    ng = (n_oc + gsz - 1) // gsz
    st = G.lin_it
    for g in range(ng):
        wt = wts[(st + g) % len(wts)]
        n = min(gsz, n_oc - g * gsz)
        S.dma("sp", wt[:, 0:KC, 0:n * 128], wv[:, g, :, 0:n * 128], w=[wt])
        for j in range(n):
            oc = g * gsz + j
            p_ = ps[G.ps_it % len(ps)]
            G.ps_it += 1
            for kc in range(KC):
                S.op("pe", lambda e, kc=kc, j=j, p_=p_, wt=wt: e.matmul(
                    p_[:, 0:W], lhsT=wt[:, kc, j * 128:(j + 1) * 128], rhs=act[:, kc, a0:a0 + W],
                    start=(kc == 0), stop=(kc == KC - 1)), r=[wt, act], w=[p_])
            epilogue(oc, p_)
    G.lin_it += ng


def gmod_cols(G, l, gm, nname, sc_idx):
    S = G.S
    mv = G.mod[:, l * 192:(l + 1) * 192].rearrange("p (i c w) -> p i c w", i=6, c=16)
    o, _ = PCOLS[nname]
    for which in range(2):
        S.op("dve", lambda e, which=which: e.scalar_tensor_tensor(
            out=gm[:, which * 16:(which + 1) * 16], in0=mv[:, sc_idx, :, which], scalar=1.0,
            in1=G.par[:, l * PL + o:l * PL + o + 16], op0=ALU.add, op1=ALU.mult), r=[G.mod, G.par], w=[gm])


def phase_norm_proj(G, l):
    nc, S = G.nc, G.S
    with ExitStack() as st:
        sb = lambda name, shape, dt=F32: G.sb(name, shape, dt, st)
        pst = lambda name: T(st.enter_context(nc.psum_tensor(name + "_%d" % G.nuid(), [128, 512], F32)))
        xts = [sb("np_x%d" % i, [128, 16, 512]) for i in range(2)]
        sq = sb("np_sq", [128, 16, 512])
        rstd = sb("np_rstd", [128, 512])
        us = [sb("np_u%d" % i, [128, 16, 512], BF16) for i in range(2)]
        wts = [sb("np_w%d" % i, [128, 16, 512], BF16) for i in range(3)]
        stg = [sb("np_stg%d" % i, [128, 4, 512]) for i in range(2)]
        gm = sb("np_gm", [128, 32])
        ps_ss = pst("np_pss")
        ps = [pst("np_ps%d" % i) for i in range(6)]
        gmod_cols(G, l, gm, "n1", 1)
        xv = G.XT.rearrange("(c p) t -> p c t", p=128)
        pv = G.PT.rearrange("(c p) t -> p c t", p=128)
        def do_norm(ti):
            t0, W = TILES[ti]
            which = 1 if ti == 0 else 0
            xt, u = xts[ti % 2], us[ti % 2]
            S.dma("sp", xt[:, :, 0:W], xv[:, :, t0:t0 + W], w=[xt])
            rms_modulate(G, xt, W, lambda c: gm[:, which * 16 + c:which * 16 + c + 1], lambda c: mcol(G, l, 0, c, which),
                         lambda c: (u[:, c, 0:W], [u]), sq, rstd, ps_ss)
        do_norm(0)
        for ti, (t0, W) in enumerate(TILES):
            u = us[ti % 2]
            if ti + 1 < len(TILES):
                do_norm(ti + 1)
            state = {"k": 0}

            def epi(oc, p_, t0=t0, W=W):
                k = G.stg_it
                sg = stg[(k // 4) % 2]
                j = oc % 4
                eng = "act" if oc % 2 == 0 else "dve"
                if eng == "act":
                    S.op("act", lambda e: e.activation(out=sg[:, j, 0:W], in_=p_[:, 0:W], func=AF.Copy), r=[p_], w=[sg])
                else:
                    S.op("dve", lambda e: e.tensor_copy(out=sg[:, j, 0:W], in_=p_[:, 0:W]), r=[p_], w=[sg])
                G.stg_it += 1
                if j == 3 or oc == NPC - 1:
                    o0 = oc - j
                    S.dma("sp", pv[:, o0:oc + 1, t0:t0 + W], sg[:, 0:j + 1, 0:W], r=[sg])
                    G.stg_it = ((G.stg_it + 3) // 4) * 4
            linear(G, u, 16, W, G.wb_in[l], NPC, epi, ps, wts)


def phase_out_proj(G, l):
    nc, S = G.nc, G.S
    with ExitStack() as st:
        sb = lambda name, shape, dt=F32: G.sb(name, shape, dt, st)
        pst = lambda name: T(st.enter_context(nc.psum_tensor(name + "_%d" % G.nuid(), [128, 512], F32)))
        xts = [sb("op_x%d" % i, [128, 16, 512]) for i in range(2)]
        ms = [sb("op_m%d" % i, [128, 16, 512], BF16) for i in range(2)]
        sq = sb("op_sq", [128, 16, 512])
        rstd = sb("op_rstd", [128, 512])
        us = [sb("op_u%d" % i, [128, 16, 512], BF16) for i in range(1)]
        wts = [sb("op_w%d" % i, [128, 16, 512], BF16) for i in range(2)]
        gm = sb("op_gm", [128, 32])
        ps_ss = pst("op_pss")
        ps = [pst("op_ps%d" % i) for i in range(6)]
        gmod_cols(G, l, gm, "n2", 4)
        xv = G.XT.rearrange("(c p) t -> p c t", p=128)
        mv = G.MIX.rearrange("(c p) t -> p c t", p=128)
        uv = G.U2.rearrange("(c p) t -> p c t", p=128)
        for ti, (t0, W) in enumerate(TILES):
            which = 1 if ti == 0 else 0
            xt, u, m = xts[ti % 2], us[0], ms[ti % 2]
            S.dma("sp", xt[:, :, 0:W], xv[:, :, t0:t0 + W], w=[xt])
            S.dma("sp", m[:, :, 0:W], mv[:, :, t0:t0 + W], w=[m])

            def epi(oc, p_, W=W, xt=xt, which=which):
                S.op("dve", lambda e: e.scalar_tensor_tensor(out=xt[:, oc, 0:W], in0=p_[:, 0:W], scalar=mcol(G, l, 2, oc, which),
                                                             in1=xt[:, oc, 0:W], op0=ALU.mult, op1=ALU.add),
                     r=[p_, G.mod, xt], w=[xt])
            linear(G, m, 16, W, G.wb_out[l], 16, epi, ps, wts)
            S.dma("sp", xv[:, :, t0:t0 + W], xt[:, :, 0:W], r=[xt])
            rms_modulate(G, xt, W, lambda c: gm[:, which * 16 + c:which * 16 + c + 1], lambda c: mcol(G, l, 3, c, which),
                         lambda c: (u[:, c, 0:W], [u]), sq, rstd, ps_ss)
            S.dma("sp", uv[:, :, t0:t0 + W], u[:, :, 0:W], r=[u])


def phase_ffn(G, l):
    nc, S = G.nc, G.S
    with ExitStack() as st:
        sb = lambda name, shape, dt=F32: G.sb(name, shape, dt, st)
        pst = lambda name: T(st.enter_context(nc.psum_tensor(name + "_%d" % G.nuid(), [128, 512], F32)))
        uh = [sb("ff_u%d" % i, [128, 16, 514], BF16) for i in range(2)]
        gt = sb("ff_g", [128, 44, 512], BF16)
        wg = [sb("ff_wg%d" % i, [128, 16, 256], BF16) for i in range(2)]
        wu = [sb("ff_wu%d" % i, [128, 16, 256], BF16) for i in range(2)]
        wo = [sb("ff_wo%d" % i, [128, 44, 256], BF16) for i in range(2)]
        xc = [sb("ff_x%d" % i, [128, 512]) for i in range(3)]
        hh = [sb("ff_hh%d" % i, [128, 514]) for i in range(2)]
        tm = [sb("ff_tm%d" % i, [128, 512]) for i in range(2)]
        ge = [sb("ff_ge%d" % i, [128, 512]) for i in range(2)]
        psg = [pst("ff_pg%d" % i) for i in range(2)]
        psh = pst("ff_ph")
        psu = [pst("ff_pu%d" % i) for i in range(2)]
        pso = [pst("ff_po%d" % i) for i in range(3)]
        xv = G.XT.rearrange("(c p) t -> p c t", p=128)
        uv = G.U2.rearrange("(c p) t -> p c t", p=128)
        wiv = G.wb_fi[l].rearrange("(g p) (kc c) -> p g kc c", p=128, c=256)
        it = 0
        FT = [(0, 256)] + [(256 + 510 * i, 510) for i in range(8)] + [(256 + 4080, 16)]
        for ti, (t0, W) in enumerate(FT):
            which = 1 if ti == 0 else 0
            u = uh[ti % 2]
            lz = any(t0 == a for a, b in SEQS)
            rz = any(t0 + W == b for a, b in SEQS)
            lo = t0 - (0 if lz else 1)
            hi = t0 + W + (0 if rz else 1)
            S.dma("sp", u[:, :, (1 if lz else 0):(1 if lz else 0) + hi - lo], uv[:, :, lo:hi], w=[u])
            if lz:
                S.op("dve", lambda e, u=u: e.memset(u[:, :, 0:1], 0.0), w=[u])
            if rz:
                S.op("dve", lambda e, u=u, W=W: e.memset(u[:, :, W + 1:W + 2], 0.0), w=[u])
            for g in range(22):
                a, b = wg[it % 2], wu[it % 2]
                S.dma("sp", a[:], wiv[:, g, :, :], w=[a])
                S.dma("sp", b[:], wiv[:, 22 + g, :, :], w=[b])
                it += 1
                for j in range(2):
                    ch = g * 2 + j
                    pg, pu = psg[ch % 2], psu[ch % 2]
                    h, t_, g_ = hh[ch % 2], tm[ch % 2], ge[ch % 2]
                    for kc in range(16):
                        S.op("pe", lambda e, kc=kc, j=j, a=a, pg=pg: e.matmul(pg[:, 0:W + 2], lhsT=a[:, kc, j * 128:(j + 1) * 128],
                                                                         rhs=u[:, kc, 0:W + 2], start=(kc == 0), stop=(kc == 15)),
                             r=[a, u], w=[pg])
                    for kc in range(16):
                        S.op("pe", lambda e, kc=kc, j=j, b=b, pu=pu: e.matmul(pu[:, 0:W], lhsT=b[:, kc, j * 128:(j + 1) * 128],
                                                                         rhs=u[:, kc, 1:W + 1], start=(kc == 0), stop=(kc == 15)),
                             r=[b, u], w=[pu])
                    S.op("act", lambda e, h=h, pg=pg: e.activation(out=h[:, 0:W + 2], in_=pg[:, 0:W + 2], func=AF.Copy), r=[pg], w=[h])
                    S.op("act", lambda e, h=h, t_=t_, ch=ch: e.activation(out=t_[:, 0:W], in_=h[:, 1:W + 1], func=AF.Identity,
                                                                    scale=pcol(G, l, "cw1", ch), bias=pcol(G, l, "cb", ch)),
                         r=[h, G.par], w=[t_])
                    S.op("dve", lambda e, h=h, t_=t_, ch=ch: e.scalar_tensor_tensor(out=t_[:, 0:W], in0=h[:, 0:W], scalar=pcol(G, l, "cw0", ch),
                                                                             in1=t_[:, 0:W], op0=ALU.mult, op1=ALU.add),
                         r=[h, t_, G.par], w=[t_])
                    S.op("dve", lambda e, h=h, t_=t_, ch=ch: e.scalar_tensor_tensor(out=t_[:, 0:W], in0=h[:, 2:W + 2], scalar=pcol(G, l, "cw2", ch),
                                                                             in1=t_[:, 0:W], op0=ALU.mult, op1=ALU.add),
                         r=[h, t_, G.par], w=[t_])
                    S.op("act", lambda e, t_=t_, g_=g_: e.activation(out=g_[:, 0:W], in_=t_[:, 0:W], func=AF.Gelu), r=[t_], w=[g_])
                    S.op("dve", lambda e, g_=g_, pu=pu, ch=ch: e.tensor_tensor(out=gt[:, ch, 0:W], in0=g_[:, 0:W], in1=pu[:, 0:W], op=ALU.mult),
                         r=[g_, pu], w=[gt])

            def epi(oc, p_, W=W, t0=t0, which=which):
                x_ = xc[G.xc_it % 3]
                G.xc_it += 1
                S.dma("sp", x_[:, 0:W], xv[:, oc, t0:t0 + W], w=[x_])
                S.op("dve", lambda e: e.scalar_tensor_tensor(out=x_[:, 0:W], in0=p_[:, 0:W], scalar=mcol(G, l, 5, oc, which),
                                                             in1=x_[:, 0:W], op0=ALU.mult, op1=ALU.add),
                     r=[p_, G.mod, x_], w=[x_])
                S.dma("sp", xv[:, oc, t0:t0 + W], x_[:, 0:W], r=[x_])
            linear(G, gt, 44, W, G.wb_fo[l], 16, epi, pso, wo, gsz=2)


def final_norm(G):
    nc, S = G.nc, G.S
    with ExitStack() as st:
        sb = lambda name, shape, dt=F32: G.sb(name, shape, dt, st)
        xts = [sb("fn_x%d" % i, [128, 16, 512]) for i in range(2)]
        os_ = [sb("fn_o%d" % i, [128, 16, 512]) for i in range(2)]
        sq = sb("fn_sq", [128, 16, 512])
        rstd = sb("fn_rstd", [128, 512])
        ps_ss = T(st.enter_context(nc.psum_tensor("fin_pss", [128, 512], F32)))
        xv = G.XT.rearrange("(c p) t -> p c t", p=128)
        ov = G.out.rearrange("(c p) t -> p c t", p=128)
        for ti, (t0, W) in enumerate(TILES[1:]):
            xt, o = xts[ti % 2], os_[ti % 2]
            S.dma("sp", xt[:, :, 0:W], xv[:, :, t0:t0 + W], w=[xt])
            rms_modulate(G, xt, W, lambda c: G.par[:, P_FINAL + c:P_FINAL + c + 1], None,
                         lambda c: (o[:, c, 0:W], [o]), sq, rstd, ps_ss)
            S.dma("sp", ov[:, :, t0 - NCTX:t0 - NCTX + W], o[:, :, 0:W], r=[o])


def layer(G, l):
    S = G.S
    dbg = G.dbg or {}
    phase_norm_proj(G, l)
    S.barrier()
    if "PT" in dbg and l == 0:
        S.dma("sp", G.dbg_out["PT"][:, :], G.PT[:, :])
        S.barrier()
    if "mixin" in dbg:
        for c in range(16):
            S.dma("pool", G.MIX[c * 128:(c + 1) * 128, :], G.mixin[c * 128:(c + 1) * 128, :])
    else:
        phase_mixers(G, l)
    S.barrier()
    phase_out_proj(G, l)
    S.barrier()
    phase_ffn(G, l)
    S.barrier()
    if "YD" in dbg and l == 0:
        for d in range(2):
            for hh in range(4):
                S.dma("sp", G.dbg_out["YD"][d * 512 + hh * 128:d * 512 + (hh + 1) * 128, :], G.YD[d][hh * 128:(hh + 1) * 128, :])
        S.barrier()
    if "DER" in dbg and l == 0:
        for d in range(2):
            for a in range(DER_N):
                for hh in range(4):
                    S.dma("sp", G.dbg_out["DER"][(d * DER_N + a) * 512 + hh * 128:(d * DER_N + a) * 512 + (hh + 1) * 128, :], G.DER[d][a][hh * 128:(hh + 1) * 128, :])
        S.barrier()
    if "MIX" in dbg and l == 0:
        for c in range(16):
            S.dma("pool", G.dbg_out["MIX"][c * 128:(c + 1) * 128, :], G.MIX[c * 128:(c + 1) * 128, :])
        S.barrier()
    if "XT" in dbg and l == 0:
        S.dma("sp", G.dbg_out["XT"][:, :], G.XT[:, :])
        S.barrier()


_CONST = None


def const_inputs():
    global _CONST
    if _CONST is not None:
        return _CONST
    bf = ml_dtypes.bfloat16
    t = np.arange(NLAT, dtype=np.int64)
    tk = (t[:, None] * t[None, :]) % NLAT
    ang = 2.0 * np.pi * tk.astype(np.float64) / NLAT
    C = {}
    C["cosL"] = np.cos(ang).astype(np.float32).astype(bf)
    C["sinL"] = np.sin(ang).astype(np.float32).astype(bf)
    t = np.arange(NCTX, dtype=np.int64)
    ang = 2.0 * np.pi * ((t[:, None] * t[None, :]) % NCTX).astype(np.float64) / NCTX
    C["cosC"] = np.cos(ang).astype(np.float32).astype(bf)
    C["sinC"] = np.sin(ang).astype(np.float32).astype(bf)
    c = np.arange(128, dtype=np.int64)
    ang = 2.0 * np.pi * ((c[:, None] * c[None, :]) % 128).astype(np.float64) / 128
    C["cs128"] = np.concatenate([np.cos(ang), -np.sin(ang)], axis=1).astype(np.float32).astype(bf)
    pos = np.arange(NLAT)
    row = (pos // 64).astype(np.float64)
    colp = (pos % 64).astype(np.float64)
    inv = 10000.0 ** (-np.arange(32, dtype=np.float64) / 32)
    rc = np.zeros((128, NLAT), np.float64)
    rs = np.zeros((128, NLAT), np.float64)
    for d in range(128):
        axis, ab, pr = d // 64, (d % 64) // 32, d % 32
        a = (row if axis == 0 else colp) * inv[pr]
        rc[d] = np.cos(a)
        rs[d] = np.sin(a) * (-1.0 if ab == 0 else 1.0)
    C["ropeC"] = rc.astype(np.float32)
    C["ropeS"] = rs.astype(np.float32)
    j = np.arange(128)
    am = np.zeros((128, 3, 128), np.float32)
    am[:, 0, :] = (j[:, None] >= j[None, :])
    am[:, 1, :] = (j[:, None] <= j[None, :])
    am[:, 2, :] = np.eye(128)
    C["amask"] = am.astype(bf)
    j = np.arange(64)
    rwm = np.zeros((64, 5, 64), np.float32)
    rwm[:, 0, :] = (j[:, None] < j[None, :])
    rwm[:, 1, :] = (j[:, None] <= j[None, :])
    rwm[:, 2, :] = (j[None, :] < j[:, None])
    rwm[:, 3, :] = (j[None, :] <= j[:, None])
    rwm[:, 4, :] = np.eye(64)
    C["rwm"] = rwm
    rm = np.ones((128, 512), np.float32)
    rm[:, ::64] = 0.0
    C["rmask"] = rm
    _CONST = C
    return C


CONST_SHAPES = {"rwm": ([64, 5, 64], F32), "rmask": ([128, 512], F32), "cosL": ([NLAT, NLAT], BF16), "sinL": ([NLAT, NLAT], BF16), "cosC": ([NCTX, NCTX], BF16), "sinC": ([NCTX, NCTX], BF16),
                "cs128": ([128, 256], BF16), "ropeC": ([128, NLAT], F32), "ropeS": ([128, NLAT], F32), "amask": ([128, 3, 128], BF16)}


def phase_fnet(G, l):
    nc, S = G.nc, G.S
    pv = G.PT.rearrange("(c p) t -> p c t", p=128)
    mv = G.MIX.rearrange("(c p) t -> p c t", p=128)
    with ExitStack() as st:
        sb = lambda name, shape, dt=F32, s_=st: G.sb(name, shape, dt, s_)
        pst = lambda name: T(st.enter_context(nc.psum_tensor(name + "_%d" % G.nuid(), [128, 512], F32)))
        AT = sb("fn_AT", [128, 34, 4, 256], BF16)
        cs = sb("fn_cs", [128, 256], BF16)
        S.dma("sp", cs[:], G.C["cs128"][:, :], w=[cs])
        ps = [pst("fnp%d" % i) for i in range(6)]
        ps_ss = pst("fnpss")
        with ExitStack() as st1:
            zf = [sb("fn_zf%d" % i, [128, TOK], F32, st1) for i in range(2)]
            zb = sb("fn_zb", [128, 4, TOK], BF16, st1)
            for g in range(4):
                S.dma("sp", zf[g % 2][:], pv[:, C_F + g, :], w=[zf[g % 2]])
                S.op("act" if g % 2 == 0 else "dve", (lambda e, g=g: e.activation(out=zb[:, g, :], in_=zf[g % 2][:], func=AF.Copy)) if g % 2 == 0
                     else (lambda e, g=g: e.tensor_copy(out=zb[:, g, :], in_=zf[g % 2][:])), r=[zf[g % 2]], w=[zb])
            for tb in range(34):
                pa, pb = ps[(2 * tb) % 6], ps[(2 * tb + 1) % 6]
                for g in range(4):
                    p_ = pa if g < 2 else pb
                    S.op("pe", lambda e, g=g, tb=tb, p_=p_: e.matmul(p_[:, (g % 2) * 256:(g % 2) * 256 + 256], lhsT=zb[:, g, tb * 128:(tb + 1) * 128],
                                                                 rhs=cs[:, :], start=True, stop=True), r=[zb, cs], w=[p_])
                S.op("act", lambda e, tb=tb, pa=pa: e.activation(out=AT[:, tb, 0:2, :], in_=pa[:, :].rearrange("p (g c) -> p g c", g=2), func=AF.Copy),
                     r=[pa], w=[AT])
                S.op("dve", lambda e, tb=tb, pb=pb: e.tensor_copy(out=AT[:, tb, 2:4, :], in_=pb[:, :].rearrange("p (g c) -> p g c", g=2)),
                     r=[pb], w=[AT])
            S.barrier()
        tc_ = [sb("fn_tc%d" % i, [128, 16, 512], BF16) for i in range(2)]
        ts_ = [sb("fn_ts%d" % i, [128, 16, 512], BF16) for i in range(2)]
        fo = [sb("fn_fo%d" % i, [128, 4, 512]) for i in range(2)]
        sq = sb("fn_sq", [128, 4, 512])
        rstd = sb("fn_rstd", [128, 512])
        ob = [sb("fn_ob%d" % i, [128, 4, 512], BF16) for i in range(2)]
        it = 0
        for (base, L, ntb, tb0, KW, ctab, stab) in ((0, NCTX, 2, 0, 256, "cosC", "sinC"), (NCTX, NLAT, 32, 2, 512, "cosL", "sinL")):
            cv = G.C[ctab].rearrange("(tb p) k -> p tb k", p=128)
            sv = G.C[stab].rearrange("(tb p) k -> p tb k", p=128)
            scale = float(1.0 / np.sqrt(L * 128.0))
            for kt in range(L // KW):
                f_, o_ = fo[kt % 2], ob[kt % 2]
                for half in range((ntb + 15) // 16):
                    nb = min(16, ntb - half * 16)
                    a, b = tc_[it % 2], ts_[it % 2]
                    it += 1
                    S.dma("sp", a[:, 0:nb, 0:KW], cv[:, half * 16:half * 16 + nb, kt * KW:(kt + 1) * KW], w=[a])
                    S.dma("sp", b[:, 0:nb, 0:KW], sv[:, half * 16:half * 16 + nb, kt * KW:(kt + 1) * KW], w=[b])
                    for i in range(nb):
                        tb = half * 16 + i
                        for g in range(4):
                            S.op("pe", lambda e, g=g, i=i, tb=tb, a=a: e.matmul(ps[g][:, 0:KW], lhsT=AT[:, tb0 + tb, g, 0:128], rhs=a[:, i, 0:KW],
                                                                         start=(tb == 0), stop=False), r=[AT, a], w=[ps[g]])
                            S.op("pe", lambda e, g=g, i=i, tb=tb, b=b: e.matmul(ps[g][:, 0:KW], lhsT=AT[:, tb0 + tb, g, 128:256], rhs=b[:, i, 0:KW],
                                                                         start=False, stop=(tb == ntb - 1)), r=[AT, b], w=[ps[g]])
                for g in range(4):
                    S.op("act", lambda e, g=g, f_=f_: e.activation(out=f_[:, g, 0:KW], in_=ps[g][:, 0:KW], func=AF.Copy, scale=scale),
                         r=[ps[g]], w=[f_])
                rms_modulate(G, f_, KW, lambda c: pcol(G, l, "fg", c), None, lambda c: (o_[:, c, 0:KW], [o_]), sq, rstd, ps_ss, nch=4, dim=512)
                S.dma("sp", mv[:, 0:4, base + kt * KW:base + (kt + 1) * KW], o_[:, :, 0:KW], r=[o_])


def phase_attn(G, l):
    nc, S = G.nc, G.S
    pv = G.PT.rearrange("(c p) t -> p c t", p=128)
    mv = G.MIX.rearrange("(c p) t -> p c t", p=128)
    SCALE = 128.0 ** -0.5
    with ExitStack() as st:
        sb = lambda name, shape, dt=F32, s_=st: G.sb(name, shape, dt, s_)
        pst = lambda name: T(st.enter_context(nc.psum_tensor(name + "_%d" % G.nuid(), [128, 512], F32)))
        QR = sb("at_QR", [128, 8, TOK], BF16)
        KR = sb("at_KR", [128, 2, TOK], BF16)
        VT = sb("at_VT", [128, 34, 2, 128], BF16)
        am = sb("at_am", [128, 3, 128], BF16)
        esink = sb("at_es", [128, 8])
        S.dma("sp", am[:], G.C["amask"][:, :, :], w=[am])
        S.op("act", lambda e: e.activation(out=esink[:], in_=pcol(G, l, "sink", 0, 8), func=AF.Exp), r=[G.par], w=[esink])
        pT = T(st.enter_context(nc.psum_tensor("at_pT_%d" % G.nuid(), [128, 1024], BF16)))
        with ExitStack() as st1:
            rc = sb("at_rc", [128, NLAT], F32, st1)
            rs = sb("at_rs", [128, NLAT], F32, st1)
            S.dma("sp", rc[:], G.C["ropeC"][:, :], w=[rc])
            S.dma("sp", rs[:], G.C["ropeS"][:, :], w=[rs])
            qf = [sb("at_qf%d" % i, [128, 512], F32, st1) for i in range(2)]
            qs = [sb("at_qs%d" % i, [128, 512], F32, st1) for i in range(2)]
            t1 = [sb("at_t1%d" % i, [128, 512], F32, st1) for i in range(2)]
            t2 = [sb("at_t2%d" % i, [128, 512], F32, st1) for i in range(2)]
            vb = [sb("at_vb%d" % i, [128, 512], BF16, st1) for i in range(2)]
            it = 0
            for ch in range(10):
                src = C_Q + ch if ch < 8 else C_AK + (ch - 8)
                dst = (lambda a, b: QR[:, ch, a:b]) if ch < 8 else (lambda a, b: KR[:, ch - 8, a:b])
                dT = QR if ch < 8 else KR
                for ti, (t0, W) in enumerate(TILES):
                    q_, s_, a_, b_ = qf[it % 2], qs[it % 2], t1[it % 2], t2[it % 2]
                    it += 1
                    S.dma("sp", q_[:, 0:W], pv[:, src, t0:t0 + W], w=[q_])
                    if ti == 0:
                        S.op("act", lambda e, q_=q_, W=W, dst=dst, t0=t0: e.activation(out=dst(t0, t0 + W), in_=q_[:, 0:W], func=AF.Copy), r=[q_], w=[dT])
                        continue
                    for blk in range(4):
                        sp = (blk ^ 1) * 32
                        S.dma("sp", s_[blk * 32:(blk + 1) * 32, 0:W], G.PT[src * 128 + sp:src * 128 + sp + 32, t0:t0 + W], w=[s_])
                    p0 = t0 - NCTX
                    S.op("dve", lambda e, q_=q_, a_=a_, p0=p0, W=W: e.tensor_tensor(out=a_[:, 0:W], in0=q_[:, 0:W], in1=rc[:, p0:p0 + W], op=ALU.mult),
                         r=[q_, rc], w=[a_])
                    S.op("pool", lambda e, s_=s_, b_=b_, p0=p0, W=W: e.tensor_tensor(out=b_[:, 0:W], in0=s_[:, 0:W], in1=rs[:, p0:p0 + W], op=ALU.mult),
                         r=[s_, rs], w=[b_])
                    S.op("dve", lambda e, a_=a_, b_=b_, W=W, dst=dst, t0=t0: e.tensor_tensor(out=dst(t0, t0 + W), in0=a_[:, 0:W], in1=b_[:, 0:W], op=ALU.add),
                         r=[a_, b_], w=[dT])
            for g in range(2):
                for ti, (t0, W) in enumerate(TILES):
                    q_, v_ = qf[it % 2], vb[it % 2]
                    it += 1
                    S.dma("sp", q_[:, 0:W], pv[:, C_AV + g, t0:t0 + W], w=[q_])
                    S.op("act", lambda e, q_=q_, v_=v_, W=W: e.activation(out=v_[:, 0:W], in_=q_[:, 0:W], func=AF.Copy), r=[q_], w=[v_])
                    nb = W // 128
                    for i in range(nb):
                        S.op("pe", lambda e, i=i, v_=v_: e.transpose(pT[:, i * 128:(i + 1) * 128], v_[:, i * 128:(i + 1) * 128], am[:, 2, :]),
                             r=[v_, am], w=[pT])
                    b0 = t0 // 128
                    S.op("dve", lambda e, g=g, b0=b0, nb=nb: e.tensor_copy(out=VT[:, b0:b0 + nb, g, :],
                                                                     in_=pT[:, 0:nb * 128].rearrange("p (b d) -> p b d", b=nb)), r=[pT], w=[VT])
            S.barrier()
        ao = [sb("at_ao%d" % i, [128, 8, 512]) for i in range(2)]
        sq = sb("at_sq", [128, 8, 512])
        rstd = sb("at_rstd", [128, 512])
        ob = [sb("at_ob%d" % i, [128, 8, 512], BF16) for i in range(2)]
        pts = [sb("at_pt%d" % i, [128, 4, 128], BF16) for i in range(4)]
        den = [sb("at_den%d" % i, [128, 4, 128]) for i in range(2)]
        ps_s = [pst("at_ps%d" % i) for i in range(3)]
        ps_n = [pst("at_pn%d" % i) for i in range(2)]
        ps_d = [pst("at_pd%d" % i) for i in range(2)]
        it = 0
        ib = 0
        for ti, (t0, W) in enumerate(TILES):
            a_, o_ = ao[ti % 2], ob[ti % 2]
            for g in range(2):
                for n in range(W // 128):
                    q0 = t0 + n * 128
                    gb = q0 // 128
                    kbs = [(0, None), (1, None)]
                    if ti > 0:
                        if gb > 2:
                            kbs.append((gb - 1, 0))
                        kbs.append((gb, None))
                        if gb < 33:
                            kbs.append((gb + 1, 1))
                    pn, pd = ps_n[ib % 2], ps_d[ib % 2]
                    dn = den[ib % 2]
                    ib += 1
                    for ki, (kb, mk) in enumerate(kbs):
                        p_s, pt = ps_s[it % 3], pts[it % 4]
                        it += 1
                        S.op("pe", lambda e, kb=kb, p_s=p_s, q0=q0, g=g: e.matmul(p_s[:, :].rearrange("p (h q) -> p h q", h=4), lhsT=KR[:, g, kb * 128:(kb + 1) * 128],
                                                                          rhs=QR[:, 4 * g:4 * g + 4, q0:q0 + 128], start=True, stop=True), r=[KR, QR], w=[p_s])
                        S.op("act", lambda e, p_s=p_s, pt=pt: e.activation(out=pt[:], in_=p_s[:, :].rearrange("p (h q) -> p h q", h=4), func=AF.Exp, scale=SCALE),
                             r=[p_s], w=[pt])
                        if mk is not None:
                            S.op("dve", lambda e, pt=pt, mk=mk: e.tensor_tensor(out=pt[:], in0=pt[:], in1=am[:, mk:mk + 1, :].broadcast_to([128, 4, 128]), op=ALU.mult),
                                 r=[pt, am], w=[pt])
                        S.op("pe", lambda e, kb=kb, pt=pt, pn=pn, ki=ki, g=g: e.matmul(pn[:, :].rearrange("p (h q) -> p h q", h=4), lhsT=VT[:, kb, g, :], rhs=pt[:],
                                                                             start=(ki == 0), stop=(ki == len(kbs) - 1)), r=[VT, pt], w=[pn])
                        S.op("pe", lambda e, pt=pt, pd=pd, ki=ki: e.matmul(pd[:, :].rearrange("p (h q) -> p h q", h=4), lhsT=G.onesb[:], rhs=pt[:],
                                                                       start=(ki == 0), stop=(ki == len(kbs) - 1)), r=[G.onesb, pt], w=[pd])
                    S.op("dve", lambda e, pd=pd, dn=dn, g=g: e.tensor_tensor(out=dn[:], in0=pd[:, :].rearrange("p (h q) -> p h q", h=4),
                                                                       in1=esink[:, 4 * g:4 * g + 4].unsqueeze(2).broadcast_to([128, 4, 128]), op=ALU.add),
                         r=[pd, esink], w=[dn])
                    S.op("dve", lambda e, dn=dn: e.reciprocal(out=dn[:], in_=dn[:]), r=[dn], w=[dn])
                    S.op("dve", lambda e, pn=pn, dn=dn, a_=a_, g=g, n=n: e.tensor_tensor(out=a_[:, 4 * g:4 * g + 4, n * 128:(n + 1) * 128],
                                                                                 in0=pn[:, :].rearrange("p (h q) -> p h q", h=4), in1=dn[:], op=ALU.mult),
                         r=[pn, dn], w=[a_])
            rms_modulate(G, a_, W, lambda c: pcol(G, l, "ag", c), None, lambda c: (o_[:, c, 0:W], [o_]), sq, rstd, ps_s[0], nch=8, dim=1024)
            S.dma("sp", mv[:, 8:16, t0:t0 + W], o_[:, :, 0:W], r=[o_])


LD = 0.6065306597126334
NCHK = TOK // 64
SCW = 256
DER_N = 6


def phase_rwkv(G, l):
    S = G.S
    rwkv_r1(G, l)
    S.barrier()
    rwkv_r2(G, l)
    S.barrier()
    rwkv_r3(G, l)


def rwkv_r1(G, l):
    nc, S = G.nc, G.S
    pv = G.PT.rearrange("(c p) t -> p c t", p=128)
    with ExitStack() as st:
        sb = lambda name, shape, dt=F32, s_=st: G.sb(name, shape, dt, s_)
        pst = lambda name: T(st.enter_context(nc.psum_tensor(name + "_%d" % G.nuid(), [128, 512], F32)))
        w2t = sb("r1_w2", [128, 512]); a2t = sb("r1_a2", [128, 512]); g2t = sb("r1_g2", [128, 512])
        S.dma("sp", w2t[:], G.w2r[l], w=[w2t]); S.dma("sp", a2t[:], G.a2r[l], w=[a2t]); S.dma("sp", g2t[:], G.g2[l], w=[g2t])
        rmask = sb("r1_rm", [128, 512])
        S.dma("sp", rmask[:], G.C["rmask"][:, :], w=[rmask])
        bones = sb("r1_bo", [128, 128])
        S.op("dve", lambda e: e.memset(bones[:], 0.0), w=[bones])
        S.op("dve", lambda e: e.memset(bones[0:64, 0:64], 1.0), w=[bones])
        S.op("dve", lambda e: e.memset(bones[64:128, 64:128], 1.0), w=[bones])
        mmc = sb("r1_mmc", [128, 15]); omka = sb("r1_omka", [128, 4])
        S.op("dve", lambda e: e.tensor_tensor(out=mmc[:], in0=pcol(G, l, "mu0", 0, 15), in1=pcol(G, l, "mu1", 0, 15), op=ALU.add), r=[G.par], w=[mmc])
        S.op("dve", lambda e: e.tensor_scalar(out=mmc[:], in0=mmc[:], scalar1=-1.0, scalar2=1.0, op0=ALU.mult, op1=ALU.add), r=[mmc], w=[mmc])
        S.op("dve", lambda e: e.tensor_scalar(out=omka[:], in0=pcol(G, l, "ka", 0, 4), scalar1=-1.0, scalar2=1.0, op0=ALU.mult, op1=ALU.add), r=[G.par], w=[omka])
        rh = [sb("r1_rh%d" % i, [128, 514]) for i in range(3)]
        psx = sb("r1_psx", [128, 15, 512])
        tw = sb("r1_tw", [128, 512]); sgd = sb("r1_sgd", [128, 512])
        NT = 12
        tp = [sb("r1_t%d" % i, [128, 512]) for i in range(NT)]
        fixed = {n: sb("r1_f" + n, [128, 512]) for n in ("kk0", "sq", "kk", "kds", "bq")}
        ob = [sb("r1_o%d" % i, [128, 512]) for i in range(8)]
        ptt = [sb("r1_pt%d" % i, [128, 8]) for i in range(2)]
        ps = [pst("r1p%d" % i) for i in range(6)]
        cnt = {"t": 0, "o": 0, "p": 0, "rh": 0}

        def tmp():
            cnt["t"] += 1
            return tp[cnt["t"] % NT]

        def otile():
            cnt["o"] += 1
            return ob[cnt["o"] % 8]

        def psn():
            cnt["p"] += 1
            return ps[cnt["p"] % 6]

        def tt(eng, out, a, b, op, r, w):
            S.op(eng, lambda e: e.tensor_tensor(out=out, in0=a, in1=b, op=op), r=r, w=w)

        for ti, (t0, W) in enumerate(TILES):
            nck = W // 64
            lz = any(t0 == a for a, b in SEQS)
            rz = any(t0 + W == b for a, b in SEQS)
            lo = t0 - (0 if lz else 1)
            hi = t0 + W + (0 if rz else 1)
            for ci in range(15):
                cnt["rh"] += 1
                h = rh[cnt["rh"] % 3]
                S.dma("sp", h[:, (1 if lz else 0):(1 if lz else 0) + hi - lo], pv[:, C_R + ci, lo:hi], w=[h])
                if lz:
                    S.op("dve", lambda e, h=h: e.memset(h[:, 0:1], 0.0), w=[h])
                if rz:
                    S.op("dve", lambda e, h=h, W=W: e.memset(h[:, W + 1:W + 2], 0.0), w=[h])
                S.op("act", lambda e, h=h, ci=ci, W=W: e.activation(out=psx[:, ci, 0:W], in_=h[:, 1:W + 1], func=AF.Identity, scale=mmc[:, ci:ci + 1]),
                     r=[h, mmc], w=[psx])
                S.op("dve", lambda e, h=h, ci=ci, W=W: e.scalar_tensor_tensor(out=psx[:, ci, 0:W], in0=h[:, 0:W], scalar=pcol(G, l, "mu0", ci),
                                                                        in1=psx[:, ci, 0:W], op0=ALU.mult, op1=ALU.add), r=[h, psx, G.par], w=[psx])
                S.op("dve", lambda e, h=h, ci=ci, W=W: e.scalar_tensor_tensor(out=psx[:, ci, 0:W], in0=h[:, 2:W + 2], scalar=pcol(G, l, "mu1", ci),
                                                                        in1=psx[:, ci, 0:W], op0=ALU.mult, op1=ALU.add), r=[h, psx, G.par], w=[psx])
            S.op("act", lambda e, W=W: e.activation(out=tw[:, 0:W], in_=psx[:, 13, 0:W], func=AF.Tanh), r=[psx], w=[tw])
            S.op("act", lambda e, W=W: e.activation(out=sgd[:, 0:W], in_=psx[:, 4, 0:W], func=AF.Sigmoid), r=[psx], w=[sgd])
            for hc in range(4):
                r_, k_, v_ = psx[:, hc, 0:W], psx[:, 5 + hc, 0:W], psx[:, 9 + hc, 0:W]
                rows = slice(hc * 128, (hc + 1) * 128)
                p_ = psn()
                S.op("pe", lambda e, p_=p_, hc=hc, W=W: e.matmul(p_[:, 0:W], lhsT=g2t[:, hc * 128:(hc + 1) * 128], rhs=sgd[:, 0:W], start=True, stop=True),
                     r=[g2t, sgd], w=[p_])
                o = otile()
                S.op("act", lambda e, p_=p_, o=o, W=W: e.activation(out=o[:, 0:W], in_=p_[:, 0:W], func=AF.Copy), r=[p_], w=[o])
                S.dma("sp", G.GT[rows, t0:t0 + W], o[:, 0:W], r=[o])
                o = otile()
                S.op("act", lambda e, o=o, v_=v_, W=W: e.activation(out=o[:, 0:W], in_=v_, func=AF.Copy), r=[psx], w=[o])
                S.dma("sp", G.VS[rows, t0:t0 + W], o[:, 0:W], r=[o])
                kk0, sq, kk = fixed["kk0"], fixed["sq"], fixed["kk"]
                S.op("act", lambda e, kk0=kk0, k_=k_, hc=hc, W=W: e.activation(out=kk0[:, 0:W], in_=k_, func=AF.Identity, scale=pcol(G, l, "kks", hc)),
                     r=[psx, G.par], w=[kk0])
                S.op("act", lambda e, kk0=kk0, sq=sq, W=W: e.activation(out=sq[:, 0:W], in_=kk0[:, 0:W], func=AF.Square), r=[kk0], w=[sq])
                p_ = psn()
                S.op("pe", lambda e, p_=p_, sq=sq, W=W: e.matmul(p_[:, 0:W], lhsT=bones[:], rhs=sq[:, 0:W], start=True, stop=True), r=[bones, sq], w=[p_])
                S.op("act", lambda e, p_=p_, sq=sq, W=W: e.activation(out=sq[:, 0:W], in_=p_[:, 0:W], func=AF.Sqrt), r=[p_], w=[sq])
                S.op("dve", lambda e, sq=sq, W=W: e.tensor_scalar(out=sq[:, 0:W], in0=sq[:, 0:W], scalar1=1e-12, scalar2=None, op0=ALU.max), r=[sq], w=[sq])
                S.op("dve", lambda e, sq=sq, W=W: e.reciprocal(out=sq[:, 0:W], in_=sq[:, 0:W]), r=[sq], w=[sq])
                tt("dve", kk[:, 0:W], kk0[:, 0:W], sq[:, 0:W], ALU.mult, [kk0, sq], [kk])
                kds = fixed["kds"]
                for dr in range(2):
                    cnt["t"] = 0
                    prt = slice(dr * 64, (dr + 1) * 64)
                    pw, pa = psn(), psn()
                    S.op("pe", lambda e, pw=pw, dr=dr, hc=hc, W=W, prt=prt: e.matmul(pw[:, 0:W], lhsT=w2t[prt, hc * 128:(hc + 1) * 128], rhs=tw[prt, 0:W], start=True, stop=True),
                         r=[w2t, tw], w=[pw])
                    S.op("pe", lambda e, pa=pa, dr=dr, hc=hc, W=W, prt=prt: e.matmul(pa[:, 0:W], lhsT=a2t[prt, hc * 128:(hc + 1) * 128], rhs=psx[prt, 14, 0:W], start=True, stop=True),
                         r=[a2t, psx], w=[pa])
                    sg = tmp(); a_ = tmp()
                    S.op("act", lambda e, pw=pw, sg=sg, W=W, dr=dr, hc=hc: e.activation(out=sg[:, 0:W], in_=pw[:, 0:W], func=AF.Sigmoid, bias=pcol(G, l, "w0", dr * 4 + hc)),
                         r=[pw, G.par], w=[sg])
                    S.op("act", lambda e, pa=pa, a_=a_, W=W, dr=dr, hc=hc: e.activation(out=a_[:, 0:W], in_=pa[:, 0:W], func=AF.Sigmoid, bias=pcol(G, l, "a0", dr * 4 + hc)),
                         r=[pa, G.par], w=[a_])
                    kd = tmp(); nb = tmp()
                    S.op("act", lambda e, a_=a_, kd=kd, W=W, hc=hc: e.activation(out=kd[:, 0:W], in_=a_[:, 0:W], func=AF.Identity, scale=pcol(G, l, "ka", hc), bias=omka[:, hc:hc + 1]),
                         r=[a_, G.par, omka], w=[kd])
                    tt("dve", kd[:, 0:W], kd[:, 0:W], k_, ALU.mult, [kd, psx], [kd])
                    if dr == 0:
                        S.op("dve", lambda e, kds=kds, kd=kd, W=W: e.tensor_copy(out=kds[:, 0:W], in_=kd[:, 0:W]), r=[kd], w=[kds])
                    else:
                        tt("dve", kds[:, 0:W], kds[:, 0:W], kd[:, 0:W], ALU.add, [kds, kd], [kds])
                    S.op("dve", lambda e, a_=a_, nb=nb, kk=kk, W=W: e.scalar_tensor_tensor(out=nb[:, 0:W], in0=a_[:, 0:W], scalar=-1.0, in1=kk[:, 0:W],
                                                                                 op0=ALU.mult, op1=ALU.mult), r=[a_, kk], w=[nb])
                    c_ = tmp()
                    S.op("dve", lambda e, c_=c_, sg=sg, W=W: e.tensor_tensor_scan(out=c_[:, 0:W], data0=rmask[:, 0:W], data1=sg[:, 0:W], initial=0.0,
                                                                            op0=ALU.mult, op1=ALU.add), r=[rmask, sg], w=[c_])
                    c3 = c_[:, 0:W].rearrange("p (c t) -> p c t", t=64)
                    if dr == 1:
                        d_ = tmp()
                        tt("dve", d_[:, 0:W], sg[:, 0:W], c_[:, 0:W], ALU.subtract, [sg, c_], [d_])
                        tt("dve", d_[:, 0:W].rearrange("p (c t) -> p c t", t=64), d_[:, 0:W].rearrange("p (c t) -> p c t", t=64),
                           c3[:, :, 63:64].broadcast_to([128, nck, 64]), ALU.add, [d_, c_], [d_])
                        c_ = d_
                        c3 = c_[:, 0:W].rearrange("p (c t) -> p c t", t=64)
                        endi = 0
                    else:
                        endi = 63
                    e1 = tmp(); e2 = tmp(); e3 = tmp(); e4 = tmp()
                    S.op("act", lambda e, c_=c_, e1=e1, W=W: e.activation(out=e1[:, 0:W], in_=c_[:, 0:W], func=AF.Exp, scale=-LD), r=[c_], w=[e1])
                    S.op("act", lambda e, c_=c_, e2=e2, W=W: e.activation(out=e2[:, 0:W], in_=c_[:, 0:W], func=AF.Exp, scale=LD), r=[c_], w=[e2])
                    tt("dve", e3[:, 0:W], c_[:, 0:W], sg[:, 0:W], ALU.subtract, [c_, sg], [e3])
                    S.op("act", lambda e, e3=e3, W=W: e.activation(out=e3[:, 0:W], in_=e3[:, 0:W], func=AF.Exp, scale=-LD), r=[e3], w=[e3])
                    tt("dve", e4[:, 0:W].rearrange("p (c t) -> p c t", t=64), c3[:, :, endi:endi + 1].broadcast_to([128, nck, 64]), c3, ALU.subtract, [c_], [e4])
                    S.op("act", lambda e, e4=e4, W=W: e.activation(out=e4[:, 0:W], in_=e4[:, 0:W], func=AF.Exp, scale=-LD), r=[e4], w=[e4])
                    pt_ = ptt[(hc * 2 + dr) % 2]
                    S.op("dve", lambda e, pt_=pt_, e1=e1, W=W, endi=endi, nck=nck: e.tensor_copy(
                        out=pt_[:, 0:nck].unsqueeze(2), in_=e1[:, 0:W].rearrange("p (c t) -> p c t", t=64)[:, :, endi:endi + 1]), r=[e1], w=[pt_])
                    S.dma("sp", G.PTOT[dr][rows, t0 // 64:t0 // 64 + nck], pt_[:, 0:nck], r=[pt_])
                    for ai, (x_, y_, xd, yd) in enumerate(((kk, e3, kk, e3), (None, e1, psx, e1), (kd, e2, kd, e2), (nb, e2, nb, e2), (kd, e4, kd, e4), (nb, e4, nb, e4))):
                        o = otile()
                        xin_ = r_ if x_ is None else x_[:, 0:W]
                        tt("dve", o[:, 0:W], xin_, y_[:, 0:W], ALU.mult, [xd, yd], [o])
                        S.dma("sp", G.DER[dr][ai][rows, t0:t0 + W], o[:, 0:W], r=[o])
                bq = fixed["bq"]
                S.op("dve", lambda e, bq=bq, r_=r_, kds=kds, hc=hc, W=W: e.scalar_tensor_tensor(out=bq[:, 0:W], in0=r_, scalar=pcol(G, l, "rk", hc), in1=kds[:, 0:W],
                                                                                      op0=ALU.mult, op1=ALU.mult), r=[psx, kds, G.par], w=[bq])
                p_ = psn()
                S.op("pe", lambda e, p_=p_, bq=bq, W=W: e.matmul(p_[:, 0:W], lhsT=bones[:], rhs=bq[:, 0:W], start=True, stop=True), r=[bones, bq], w=[p_])
                o = otile()
                tt("dve", o[:, 0:W], p_[:, 0:W], v_, ALU.mult, [p_, psx], [o])
                S.dma("sp", G.BON[rows, t0:t0 + W], o[:, 0:W], r=[o])


class PV:
    def __init__(self, ap, d):
        self.ap = ap
        self.d = d


def rwkv_r2(G, l, HG=4):
    nc, S = G.nc, G.S
    with ExitStack() as st:
        sb = lambda name, shape, dt=F32, s_=st: G.sb(name, shape, dt, s_)
        banks = [st.enter_context(nc.psum_tensor("r2b%d_%d" % (i, G.nuid()), [128, 512], F32)) for i in range(8)]
        bdeps = [Dep(x=True) for i in range(8)]
        rwm = sb("r2_rwm", [64, 5, 64])
        S.dma("sp", rwm[:], G.C["rwm"][:, :, :], w=[rwm])
        ident = rwm[:, 4, :]
        m12 = [rwm[:, 0:2, :], rwm[:, 2:4, :]]
        mX = [rwm[:, 2, :], rwm[:, 0, :]]
        order = [list(range(NCHK)), [3, 2, 1, 0] + list(range(NCHK - 1, 3, -1))]
        nch = 2 * HG

        class Chain:
            pass
        chains = []
        for i in range(nch):
            c = Chain()
            c.inb = [sb("r2_in%d_%d" % (i, j), [64, 7, SCW]) for j in range(2)]
            c.tm = sb("r2_tm%d" % i, [64, 3, 64])
            c.AM1 = sb("r2_am1_%d" % i, [64, 2, 64]); c.AM2 = sb("r2_am2_%d" % i, [64, 2, 64])
            c.X = [sb("r2_x%d_%d" % (i, j), [64, 64]) for j in range(2)]
            c.Wk = [sb("r2_w%d_%d" % (i, j), [64, 2, 64]) for j in range(2)]
            c.Ti = sb("r2_ti%d" % i, [64, 64]); c.Z = sb("r2_z%d" % i, [64, 64]); c.U = sb("r2_u%d" % i, [64, 64])
            c.H = sb("r2_h%d" % i, [64, 64])
            c.ys = [sb("r2_ys%d_%d" % (i, j), [64, SCW]) for j in range(2)]
            c.pt = sb("r2_ptot%d" % i, [64, NCHK])
            c.regs = [PV(banks[i][0:64, k * 128:(k + 1) * 128], bdeps[i]) for k in range(4)]
            c.ri = 0
            chains.append(c)

        def prc(c):
            c.ri += 1
            return c.regs[c.ri % 4]

        def cp(eng, out, in_, r, w):
            if eng == "act":
                S.op("act", lambda e: e.activation(out=out, in_=in_, func=AF.Copy), r=r, w=w)
            else:
                S.op("dve", lambda e: e.tensor_copy(out=out, in_=in_), r=r, w=w)

        def mm(out_pv, out_ap, lhsT, rhs, start, stop, r):
            S.op("pe", lambda e: e.matmul(out_ap, lhsT=lhsT, rhs=rhs, start=start, stop=stop), r=r, w=[out_pv])

        for hg in range(8 // HG):
            for i, c in enumerate(chains):
                c.dr = i // HG
                c.head = hg * HG + i % HG
                c.rows = slice(c.head * 64, (c.head + 1) * 64)
                S.op("dve", lambda e, c=c: e.memset(c.H[:], 0.0), w=[c.H])
                S.dma("sp", c.pt[:], G.PTOT[c.dr][c.rows, :], w=[c.pt])
                c.nsc = 0
            for step in range(NCHK):
                for c in chains:
                    c.ck = order[c.dr][step]
                    c.off = (c.ck % 4) * 64
                    if step % 4 == 0:
                        c.nsc += 1
                        c.cur = c.inb[c.nsc % 2]
                        c.ysc = c.ys[c.nsc % 2]
                        sc = c.ck // 4
                        for ai in range(6):
                            S.dma("sp", c.cur[:, ai, :], G.DER[c.dr][ai][c.rows, sc * SCW:(sc + 1) * SCW], w=[c.cur])
                        S.dma("sp", c.cur[:, 6, :], G.VS[c.rows, sc * SCW:(sc + 1) * SCW], w=[c.cur])
                    c.cols = slice(c.off, c.off + 64)
                for c in chains:
                    c.p = prc(c)
                    for j, ai in enumerate((4, 5)):
                        S.op("pe", lambda e, c=c, j=j, ai=ai: e.transpose(c.p.ap[:, j * 64:(j + 1) * 64], c.cur[:, ai, c.cols], ident), r=[c.cur, rwm], w=[c.p])
                for c in chains:
                    pass
                for c in chains:
                    cp("act", c.tm[:, 0:2, :], c.p.ap[:, 0:128].rearrange("p (a t) -> p a t", a=2), [c.p], [c.tm])
                for c in chains:
                    c.p = prc(c)
                    S.op("pe", lambda e, c=c: e.transpose(c.p.ap[:, 0:64], c.cur[:, 6, c.cols], ident), r=[c.cur, rwm], w=[c.p])
                for c in chains:
                    cp("dve", c.tm[:, 2, :], c.p.ap[:, 0:64], [c.p], [c.tm])
                for c in chains:
                    c.p1, c.p2, c.p3 = prc(c), prc(c), prc(c)
                    kr = c.cur[:, 0:2, c.cols]
                    mm(c.p1, c.p1.ap[:, :].rearrange("p (a t) -> p a t", a=2), c.cur[:, 2, c.cols], kr, True, True, [c.cur])
                    mm(c.p2, c.p2.ap[:, :].rearrange("p (a t) -> p a t", a=2), c.cur[:, 3, c.cols], kr, True, True, [c.cur])
                    mm(c.p3, c.p3.ap[:, 0:64], c.cur[:, 0, c.cols], c.cur[:, 3, c.cols], True, True, [c.cur])
                for c in chains:
                    c.xi = 0
                    S.op("dve", lambda e, c=c: e.tensor_tensor(out=c.AM1[:], in0=c.p1.ap[:, :].rearrange("p (a t) -> p a t", a=2), in1=m12[c.dr], op=ALU.mult), r=[c.p1, rwm], w=[c.AM1])
                    S.op("dve", lambda e, c=c: e.tensor_tensor(out=c.AM2[:], in0=c.p2.ap[:, :].rearrange("p (a t) -> p a t", a=2), in1=m12[c.dr], op=ALU.mult), r=[c.p2, rwm], w=[c.AM2])
                    S.op("dve", lambda e, c=c: e.tensor_tensor(out=c.X[0][:], in0=c.p3.ap[:, 0:64], in1=mX[c.dr], op=ALU.mult), r=[c.p3, rwm], w=[c.X[0]])
                for c in chains:
                    c.p1, c.p2 = prc(c), prc(c)
                    mm(c.p1, c.p1.ap[:, 0:64], c.X[0][:], c.AM2[:, 0, :], True, True, [c.X[0], c.AM2])
                    mm(c.p2, c.p2.ap[:, 0:64], c.AM2[:, 0, :], c.X[0][:], True, True, [c.X[0], c.AM2])
                for c in chains:
                    cp("act", c.Wk[1][:, 0, :], c.p1.ap[:, 0:64], [c.p1], [c.Wk[1]])
                    S.op("dve", lambda e, c=c: e.tensor_tensor(out=c.Wk[1][:, 1, :], in0=c.AM2[:, 0, :], in1=ident, op=ALU.add), r=[c.AM2, rwm], w=[c.Wk[1]])
                    cp("act", c.X[1][:], c.p2.ap[:, 0:64], [c.p2], [c.X[1]])
                for k in range(1, 5):
                    a, b = k % 2, (k + 1) % 2
                    for c in chains:
                        c.p1, c.p2 = prc(c), prc(c)
                        mm(c.p1, c.p1.ap[:, :].rearrange("p (a t) -> p a t", a=2), c.X[a][:], c.Wk[a][:], True, True, [c.X[a], c.Wk[a]])
                        mm(c.p2, c.p2.ap[:, 0:64], c.Wk[a][:, 0, :], c.X[a][:], True, True, [c.X[a], c.Wk[a]])
                    for c in chains:
                        cp("act", c.Wk[b][:, 0, :], c.p1.ap[:, 0:64], [c.p1], [c.Wk[b]])
                        S.op("dve", lambda e, c=c, a=a, b=b: e.tensor_tensor(out=c.Wk[b][:, 1, :], in0=c.p1.ap[:, 64:128], in1=c.Wk[a][:, 1, :], op=ALU.add),
                             r=[c.p1, c.Wk[a]], w=[c.Wk[b]])
                        cp("dve", c.X[b][:], c.p2.ap[:, 0:64], [c.p2], [c.X[b]])
                for c in chains:
                    c.p1 = prc(c)
                    mm(c.p1, c.p1.ap[:, 0:64], c.X[1][:], c.Wk[1][:, 1, :], True, True, [c.X[1], c.Wk[1]])
                for c in chains:
                    S.op("dve", lambda e, c=c: e.tensor_tensor(out=c.Ti[:], in0=c.p1.ap[:, 0:64], in1=c.Wk[1][:, 1, :], op=ALU.add), r=[c.p1, c.Wk[1]], w=[c.Ti])
                for c in chains:
                    c.p1 = prc(c)
                    mm(c.p1, c.p1.ap[:, 0:64], c.cur[:, 0, c.cols], c.H[:], True, False, [c.cur, c.H])
                    mm(c.p1, c.p1.ap[:, 0:64], c.AM1[:, 0, :], c.tm[:, 2, :], False, True, [c.AM1, c.tm])
                for c in chains:
                    cp("act", c.Z[:], c.p1.ap[:, 0:64], [c.p1], [c.Z])
                for c in chains:
                    c.p1 = prc(c)
                    mm(c.p1, c.p1.ap[:, 0:64], c.Ti[:], c.Z[:], True, True, [c.Ti, c.Z])
                for c in chains:
                    cp("dve", c.U[:], c.p1.ap[:, 0:64], [c.p1], [c.U])
                for c in chains:
                    c.p1, c.p2 = prc(c), prc(c)
                    mm(c.p1, c.p1.ap[:, 0:64], c.H[:], c.cur[:, 1, c.cols], True, False, [c.H, c.cur])
                    mm(c.p1, c.p1.ap[:, 0:64], c.tm[:, 2, :], c.AM1[:, 1, :], False, False, [c.tm, c.AM1])
                    mm(c.p1, c.p1.ap[:, 0:64], c.U[:], c.AM2[:, 1, :], False, True, [c.U, c.AM2])
                    mm(c.p2, c.p2.ap[:, 0:64], c.tm[:, 0, :], c.tm[:, 2, :], True, False, [c.tm])
                    mm(c.p2, c.p2.ap[:, 0:64], c.tm[:, 1, :], c.U[:], False, True, [c.tm, c.U])
                for c in chains:
                    cp("act", c.ysc[:, c.cols], c.p1.ap[:, 0:64], [c.p1], [c.ysc])
                    S.op("dve", lambda e, c=c: e.scalar_tensor_tensor(out=c.H[:], in0=c.H[:], scalar=c.pt[:, c.ck:c.ck + 1], in1=c.p2.ap[:, 0:64],
                                                                  op0=ALU.mult, op1=ALU.add), r=[c.H, c.pt, c.p2], w=[c.H])
                if step % 4 == 3:
                    for c in chains:
                        sc = c.ck // 4
                        S.dma("sp", G.YD[c.dr][c.rows, sc * SCW:(sc + 1) * SCW], c.ysc[:], r=[c.ysc])


def rwkv_r3(G, l):
    nc, S = G.nc, G.S
    mv = G.MIX.rearrange("(c p) t -> p c t", p=128)
    with ExitStack() as st:
        sb = lambda name, shape, dt=F32, s_=st: G.sb(name, shape, dt, s_)
        pst = lambda name: T(st.enter_context(nc.psum_tensor(name + "_%d" % G.nuid(), [128, 512], F32)))
        bo = sb("r3_bo", [128, 128])
        S.op("dve", lambda e: e.memset(bo[:], 0.0), w=[bo])
        S.op("dve", lambda e: e.memset(bo[0:64, 0:64], 1.0 / 64), w=[bo])
        S.op("dve", lambda e: e.memset(bo[64:128, 64:128], 1.0 / 64), w=[bo])
        lnx = sb("r3_eps", [128, 1])
        S.op("dve", lambda e: e.memset(lnx[:], 64e-5), w=[lnx])
        ya = [sb("r3_ya%d" % i, [128, 512]) for i in range(2)]
        yb = [sb("r3_yb%d" % i, [128, 512]) for i in range(2)]
        bn = [sb("r3_bn%d" % i, [128, 512]) for i in range(2)]
        gg = [sb("r3_gg%d" % i, [128, 512]) for i in range(2)]
        t1 = [sb("r3_t1%d" % i, [128, 512]) for i in range(2)]
        t2 = [sb("r3_t2%d" % i, [128, 512]) for i in range(2)]
        ob = [sb("r3_ob%d" % i, [128, 512], BF16) for i in range(2)]
        ps = [pst("r3p%d" % i) for i in range(4)]
        it = 0
        for ti, (t0, W) in enumerate(TILES):
            for hc in range(4):
                rows = slice(hc * 128, (hc + 1) * 128)
                a, b, n_, g_, x1, x2, o = ya[it % 2], yb[it % 2], bn[it % 2], gg[it % 2], t1[it % 2], t2[it % 2], ob[it % 2]
                pm, pvv = ps[(2 * it) % 4], ps[(2 * it + 1) % 4]
                it += 1
                S.dma("sp", a[:, 0:W], G.YD[0][rows, t0:t0 + W], w=[a])
                S.dma("sp", b[:, 0:W], G.YD[1][rows, t0:t0 + W], w=[b])
                S.dma("sp", n_[:, 0:W], G.BON[rows, t0:t0 + W], w=[n_])
                S.dma("sp", g_[:, 0:W], G.GT[rows, t0:t0 + W], w=[g_])
                S.op("dve", lambda e, a=a, b=b, W=W: e.tensor_tensor(out=a[:, 0:W], in0=a[:, 0:W], in1=b[:, 0:W], op=ALU.add), r=[a, b], w=[a])
                S.op("pe", lambda e, a=a, pm=pm, W=W: e.matmul(pm[:, 0:W], lhsT=bo[:], rhs=a[:, 0:W], start=True, stop=True), r=[bo, a], w=[pm])
                S.op("dve", lambda e, a=a, pm=pm, x1=x1, W=W: e.tensor_tensor(out=x1[:, 0:W], in0=a[:, 0:W], in1=pm[:, 0:W], op=ALU.subtract), r=[a, pm], w=[x1])
                S.op("act", lambda e, x1=x1, x2=x2, W=W: e.activation(out=x2[:, 0:W], in_=x1[:, 0:W], func=AF.Square), r=[x1], w=[x2])
                S.op("pe", lambda e, x2=x2, pvv=pvv, W=W: e.matmul(pvv[:, 0:W], lhsT=bo[:], rhs=x2[:, 0:W], start=True, stop=True), r=[bo, x2], w=[pvv])
                S.op("act", lambda e, x2=x2, pvv=pvv, W=W: e.activation(out=x2[:, 0:W], in_=pvv[:, 0:W], func=AF.Sqrt, bias=lnx[:, 0:1]), r=[pvv, lnx], w=[x2])
                S.op("dve", lambda e, x2=x2, W=W: e.reciprocal(out=x2[:, 0:W], in_=x2[:, 0:W]), r=[x2], w=[x2])
                S.op("dve", lambda e, x1=x1, x2=x2, W=W: e.tensor_tensor(out=x1[:, 0:W], in0=x1[:, 0:W], in1=x2[:, 0:W], op=ALU.mult), r=[x1, x2], w=[x1])
                S.op("act", lambda e, x1=x1, W=W, hc=hc: e.activation(out=x1[:, 0:W], in_=x1[:, 0:W], func=AF.Identity, scale=pcol(G, l, "lnw", hc), bias=pcol(G, l, "lnb", hc)),
                     r=[x1, G.par], w=[x1])
                S.op("dve", lambda e, x1=x1, n_=n_, W=W: e.tensor_tensor(out=x1[:, 0:W], in0=x1[:, 0:W], in1=n_[:, 0:W], op=ALU.add), r=[x1, n_], w=[x1])
                S.op("dve", lambda e, x1=x1, g_=g_, o=o, W=W: e.tensor_tensor(out=o[:, 0:W], in0=x1[:, 0:W], in1=g_[:, 0:W], op=ALU.mult), r=[x1, g_], w=[o])
                S.dma("sp", mv[:, 4 + hc, t0:t0 + W], o[:, 0:W], r=[o])

def phase_mixers(G, l):
    S = G.S
    phase_fnet(G, l)
    S.barrier()
    phase_attn(G, l)
    S.barrier()
    phase_rwkv(G, l)
    S.barrier()


EXTRA_W = ("w2r", "a2r", "g2")


def extra_w(inp, k):
    if k == "w2r":
        return np.ascontiguousarray(np.asarray(inp["rwkv_w2"], np.float32).reshape(DEPTH, 128, 512))
    if k == "a2r":
        return np.ascontiguousarray(np.asarray(inp["rwkv_a2"], np.float32).reshape(DEPTH, 128, 512))
    return np.ascontiguousarray(np.asarray(inp["rwkv_g2"], np.float32))


def make_inputs(inp, b):
    x = np.asarray(inp["x"][b], np.float32)
    cx = np.asarray(inp["ctx"][b], np.float32)
    xin = np.ascontiguousarray(np.concatenate([cx, x], axis=0).T)
    cvec = np.concatenate([_col(inp["c"][b]), _col(inp["c_ctx"])], axis=1)
    return {"xin": xin, "cvec": np.ascontiguousarray(cvec)}


def kernel(**inp):
    nc, G = build_nc()
    shared = {"params": pack_params(inp)}
    shared.update(const_inputs())
    for k in ("w_ada", "w_in", "w_out", "w_ffn_in", "w_ffn_out"):
        shared[k] = np.ascontiguousarray(np.asarray(inp[k], np.float32))
    for k in EXTRA_W:
        shared[k] = extra_w(inp, k)
    in_maps = []
    for b in range(8):
        m = dict(shared)
        m.update(make_inputs(inp, b))
        in_maps.append(m)
    res = run_bass_kernel_spmd(nc, in_maps, core_ids=list(range(8)))
    out = np.stack([np.ascontiguousarray(r["out"].T) for r in res.results], axis=0)
    return out.astype(np.float32)
```

```python
import numpy as np
from contextlib import ExitStack
import ml_dtypes
import concourse.bass as bass
import concourse.mybir as mybir
from concourse.bass_utils import run_bass_kernel_spmd

F32 = mybir.dt.float32
F32R = mybir.dt.float32r
BF16 = mybir.dt.bfloat16
AF = mybir.ActivationFunctionType
ALU = mybir.AluOpType
AX = mybir.AxisListType

D = 2048
NCTX = 256
NLAT = 4096
TOK = NCTX + NLAT
DEPTH = 4
INW = 3968
DFF = 5632
TILES = [(0, 256)] + [(256 + 512 * i, 512) for i in range(8)]
SEQS = [(0, NCTX), (NCTX, TOK)]
RMS_EPS = 1e-6

C_F, C_Q, C_R, C_G, C_K, C_V, C_WD, C_AD, C_AK, C_AV = 0, 4, 12, 16, 17, 21, 25, 26, 27, 29
NPC = 31

PCOLS = {}
_off = 0
for _n, _w in [("n1", 16), ("n2", 16), ("mu0", 15), ("mu1", 15), ("w0", 8), ("a0", 8), ("kks", 4), ("ka", 4), ("rk", 4),
               ("lnw", 4), ("lnb", 4), ("fg", 4), ("ag", 8), ("cw0", 44), ("cw1", 44), ("cw2", 44), ("cb", 44),
               ("bada", 96), ("sink", 8)]:
    PCOLS[_n] = (_off, _w)
    _off += _w
PL = _off
P_FINAL = DEPTH * PL
NPAR = P_FINAL + 16


def _col(v):
    v = np.asarray(v, np.float32).reshape(-1)
    return np.ascontiguousarray(v.reshape(-1, 128).T)


def pack_params(inp):
    P = np.zeros((128, NPAR), np.float32)
    for l in range(DEPTH):
        def put(name, arr):
            o, w = PCOLS[name]
            P[:, l * PL + o: l * PL + o + w] = arr
        put("n1", _col(inp["norm1_g"][l])); put("n2", _col(inp["norm2_g"][l]))
        put("mu0", _col(inp["rwkv_mu"][l][0])); put("mu1", _col(inp["rwkv_mu"][l][1]))
        put("w0", _col(inp["rwkv_w0"][l])); put("a0", _col(inp["rwkv_a0"][l]))
        put("kks", _col(inp["rwkv_kk_scale"][l])); put("ka", _col(inp["rwkv_ka"][l])); put("rk", _col(inp["rwkv_rk"][l]))
        put("lnw", _col(inp["rwkv_lnx_w"][l])); put("lnb", _col(inp["rwkv_lnx_b"][l]))
        put("fg", _col(inp["fourier_out_g"][l])); put("ag", _col(inp["attn_out_g"][l]))
        for j in range(3):
            put("cw%d" % j, _col(inp["ffn_conv_w"][l][j]))
        put("cb", _col(inp["ffn_conv_b"][l])); put("bada", _col(inp["b_ada"][l]))
        put("sink", np.broadcast_to(np.asarray(inp["attn_sink"][l], np.float32)[None, :], (128, 8)))
    P[:, P_FINAL:P_FINAL + 16] = _col(inp["final_g"])
    return P


class Dep:
    __slots__ = ("w", "r", "x")

    def __init__(self, x=False):
        self.w = None
        self.r = {}
        self.x = x


class T:
    def __init__(self, t):
        self.t = t
        self.d = Dep()

    def __getitem__(self, idx):
        return self.t[idx]


def _d(x):
    return getattr(x, "d", x)


class Sched:
    def __init__(self, nc, es):
        self.nc = nc
        self.E = {"pe": nc.tensor, "dve": nc.vector, "act": nc.scalar, "pool": nc.gpsimd, "sp": nc.sync}
        self.csem = {k: es.enter_context(nc.semaphore("cs_" + k)) for k in ("pe", "dve", "act", "pool")}
        self.cnt = {k: 0 for k in self.csem}
        self.seen = {k: {} for k in self.E}
        self.dq = {}
        for q, n in (("sp", 16), ("pool", 16), ("act", 4)):
            self.dq[q] = dict(sems=[es.enter_context(nc.semaphore("d_%s%d" % (q, i))) for i in range(n)],
                              cnt=[0] * n, nxt=0)
        self.nins = 0

    def _wait(self, e, tok):
        if tok is None:
            return
        key, sem, val = tok
        if e == "pe" and key == "pe":
            return
        if self.seen[e].get(key, 0) >= val:
            return
        self.E[e].wait_ge(sem, val)
        self.seen[e][key] = val

    def _deps(self, e, r, w):
        for d in r:
            self._wait(e, _d(d).w)
        for d in w:
            d = _d(d)
            self._wait(e, d.w)
            for t in list(d.r.values()):
                self._wait(e, t)

    def _mark(self, tok, r, w):
        for d in r:
            _d(d).r[tok[0]] = tok
        for d in w:
            d = _d(d)
            d.w = tok
            d.r = {}

    def op(self, e, fn, r=(), w=()):
        xs = [d for d in r if _d(d).x]
        if xs:
            r = [d for d in r if not _d(d).x]
            w = list(w) + xs
        self._deps(e, r, w)
        ins = fn(self.E[e])
        self.cnt[e] += 1
        ins.then_inc(self.csem[e], 1)
        self._mark((e, self.csem[e], self.cnt[e]), r, w)
        self.nins += 1

    def dma(self, q, out, in_, r=(), w=(), **kw):
        if q == "sp" and len(w) == 0 and len(r) > 0:
            q = "pool"
        Q = self.dq[q]
        i = Q["nxt"]
        Q["nxt"] = (i + 1) % len(Q["sems"])
        key = (q, i)
        if Q["cnt"][i]:
            self._wait(q, (key, Q["sems"][i], 16 * Q["cnt"][i]))
        self._deps(q, r, w)
        ins = self.E[q].dma_start(out=out, in_=in_, **kw)
        Q["cnt"][i] += 1
        ins.then_inc(Q["sems"][i], 16)
        self._mark((key, Q["sems"][i], 16 * Q["cnt"][i]), r, w)
        self.nins += 1

    def barrier(self, engines=("pe", "dve", "act", "pool", "sp")):
        for e in engines:
            for k in self.csem:
                if self.cnt[k]:
                    self._wait(e, (k, self.csem[k], self.cnt[k]))
            for q, Q in self.dq.items():
                for i, s in enumerate(Q["sems"]):
                    if Q["cnt"][i]:
                        self._wait(e, ((q, i), s, 16 * Q["cnt"][i]))


class Ctx:
    pass


def build_nc(nl=DEPTH, dbg=None):
    nc = bass.Bass("TRN2", target_bir_lowering=False)
    G = Ctx()
    G.nc = nc
    G.dbg = dbg
    dt_in = lambda name, shape, dt=F32: nc.dram_tensor(name, shape, dt, kind="ExternalInput").ap()
    dt_sc = lambda name, shape, dt=F32: nc.dram_tensor(name, shape, dt, kind="Internal").ap()
    G.xin = dt_in("xin", [D, TOK])
    G.cvec = dt_in("cvec", [128, 32])
    G.params = dt_in("params", [128, NPAR])
    G.w_ada = dt_in("w_ada", [DEPTH, D, 6 * D])
    G.w_in = dt_in("w_in", [DEPTH, D, INW])
    G.w_out = dt_in("w_out", [DEPTH, D, D])
    G.w_ffn_in = dt_in("w_ffn_in", [DEPTH, D, 2 * DFF])
    G.w_ffn_out = dt_in("w_ffn_out", [DEPTH, DFF, D])
    G.out = nc.dram_tensor("out", [D, NLAT], F32, kind="ExternalOutput").ap()
    G.XT = dt_sc("XT", [D, TOK])
    G.PT = dt_sc("PT", [NPC * 128, TOK])
    G.MIX = dt_sc("MIX", [D, TOK], BF16)
    G.U2 = dt_sc("U2", [D, TOK], BF16)
    if dbg and "mixin" in dbg:
        G.mixin = dt_in("mixin", [D, TOK])
    G.C = {k: dt_in(k, sh, dt) for k, (sh, dt) in CONST_SHAPES.items()}
    G.w2r = dt_in("w2r", [DEPTH, 128, 512])
    G.a2r = dt_in("a2r", [DEPTH, 128, 512])
    G.g2 = dt_in("g2", [DEPTH, 128, 512])
    G.GT = dt_sc("GT", [512, TOK])
    G.VS = dt_sc("VS", [512, TOK])
    G.BON = dt_sc("BON", [512, TOK])
    G.PTOT = [dt_sc("PTOT%d" % d, [512, NCHK]) for d in range(2)]
    G.DER = [[dt_sc("DER%d_%d" % (d, a), [512, TOK]) for a in range(DER_N)] for d in range(2)]
    G.YD = [dt_sc("YD%d" % d, [512, TOK]) for d in range(2)]
    G.wb_in = [dt_sc("wb_in%d" % l, [8 * 128, 16 * 512], BF16) for l in range(nl)]
    G.wb_out = [dt_sc("wb_out%d" % l, [4 * 128, 16 * 512], BF16) for l in range(nl)]
    G.wb_fi = [dt_sc("wb_fi%d" % l, [44 * 128, 16 * 256], BF16) for l in range(nl)]
    G.wb_fo = [dt_sc("wb_fo%d" % l, [8 * 128, 44 * 256], BF16) for l in range(nl)]
    if dbg:
        G.dbg_out = {}
        for name, shape in dbg.items():
            if name == "mixin":
                continue
            G.dbg_out[name] = nc.dram_tensor("dbg_" + name, shape, F32, kind="ExternalOutput").ap()

    with ExitStack() as es:
        S = Sched(nc, es)
        G.S = S
        G.es = es
        G.uid = 0

        def nuid():
            G.uid += 1
            return G.uid
        G.nuid = nuid

        def sb(name, shape, dt=F32, st=es):
            G.uid += 1
            return T(st.enter_context(nc.sbuf_tensor("%s_%d" % (name, G.uid), shape, dt)))
        G.sb = sb
        G.par = sb("par", [128, NPAR])
        G.mod = sb("mod", [128, nl * 96 * 2])
        G.ones = sb("ones_f", [128, 128])
        G.onesb = sb("ones_b", [128, 128], BF16)
        G.wcast = Dep()
        G.epsc = sb("epsc", [128, 1])
        S.op("dve", lambda e: e.memset(G.epsc[:], RMS_EPS), w=[G.epsc])
        G.lin_it = 0
        G.ps_it = 0
        G.stg_it = 0
        G.xc_it = 0
        S.dma("sp", G.par[:], G.params[:, :], w=[G.par])
        S.op("dve", lambda e: e.memset(G.ones[:], 1.0), w=[G.ones])
        S.op("dve", lambda e: e.memset(G.onesb[:], 1.0), w=[G.onesb])
        for l in range(nl):
            for src, dst, K, N, gs in ((G.w_in, G.wb_in, D, INW, 512), (G.w_out, G.wb_out, D, D, 512),
                                       (G.w_ffn_in, G.wb_fi, D, 2 * DFF, 256), (G.w_ffn_out, G.wb_fo, DFF, D, 256)):
                dv = dst[l].rearrange("(g p) (kc c) -> p g kc c", p=128, c=gs)
                nf = N // gs
                for kc in range(K // 128):
                    S.dma("pool", dv[:, 0:nf, kc, :], src[l, kc * 128:(kc + 1) * 128, 0:nf * gs].rearrange("p (g c) -> p g c", c=gs),
                          w=[G.wcast], max_dma_last_dim=4096)
                    if N > nf * gs:
                        S.dma("pool", dv[:, nf, kc, 0:N - nf * gs], src[l, kc * 128:(kc + 1) * 128, nf * gs:N], w=[G.wcast], max_dma_last_dim=4096)
        xd = Dep()
        for c in range(16):
            S.dma("sp", G.XT[c * 128:(c + 1) * 128, :], G.xin[c * 128:(c + 1) * 128, :], w=[xd])
        prologue_adaln(G, nl)
        S.barrier()
        if dbg and "mod" in dbg:
            S.dma("sp", G.dbg_out["mod"][:, :], G.mod[:], r=[G.mod])
        for l in range(nl):
            layer(G, l)
        final_norm(G)
        S.barrier()
    G.nins = S.nins
    return nc, G


def pcol(G, l, name, c=0, n=1):
    o, w = PCOLS[name]
    return G.par[:, l * PL + o + c: l * PL + o + c + n]


def mcol(G, l, idx, c, which):
    j = ((l * 96) + idx * 16 + c) * 2 + which
    return G.mod[:, j:j + 1]


def prologue_adaln(G, nl):
    nc, S = G.nc, G.S
    with ExitStack() as st:
        sb = lambda name, shape, dt=F32: G.sb(name, shape, dt, st)
        cv = sb("cv", [128, 32])
        s2 = sb("s2", [128, 32])
        wts = [sb("wada%d" % i, [128, 16, 512]) for i in range(2)]
        ps = [T(st.enter_context(nc.psum_tensor("ps_ada%d" % i, [128, 512], F32))) for i in range(4)]
        S.dma("sp", cv[:], G.cvec[:, :], w=[cv])
        S.op("act", lambda e: e.activation(out=s2[:].rearrange("p (k w) -> p w k", w=2),
                                           in_=cv[:].rearrange("p (w k) -> p w k", w=2), func=AF.Silu), r=[cv], w=[s2])
        it = 0
        for l in range(nl):
            wv = G.w_ada[l].rearrange("(kc p) n -> p kc n", p=128)
            for cg in range(24):
                wt = wts[it % 2]
                S.dma("sp", wt[:], wv[:, :, cg * 512:(cg + 1) * 512], w=[wt])
                for j in range(4):
                    ch = cg * 4 + j
                    p_ = ps[(it * 4 + j) % 4]
                    for kc in range(16):
                        S.op("pe", lambda e, kc=kc, j=j, p_=p_, wt=wt: e.matmul(
                            p_[:, 0:2], lhsT=wt[:, kc, j * 128:(j + 1) * 128], rhs=s2[:, kc * 2:kc * 2 + 2],
                            start=(kc == 0), stop=(kc == 15)), r=[wt, s2], w=[p_])
                    o = (l * 96 + ch) * 2
                    S.op("dve", lambda e, p_=p_, o=o, ch=ch, l=l: e.tensor_scalar(
                        out=G.mod[:, o:o + 2], in0=p_[:, 0:2], scalar1=pcol(G, l, "bada", ch), scalar2=None, op0=ALU.add),
                        r=[p_, G.par], w=[G.mod])
                it += 1


def rms_modulate(G, xt, W, scale_col, bias_col, out_fn, sq, rstd, ps_ss, nch=16, dim=D, ones=None):
    S = G.S
    ones = ones or G.ones
    S.op("act", lambda e: e.activation(out=sq[:, 0:nch, 0:W], in_=xt[:, 0:nch, 0:W], func=AF.Square), r=[xt], w=[sq])
    for c in range(nch):
        S.op("pe", lambda e, c=c: e.matmul(ps_ss[:, 0:W], lhsT=ones[:], rhs=sq[:, c, 0:W], start=(c == 0), stop=(c == nch - 1)),
             r=[sq, ones], w=[ps_ss])
    S.op("act", lambda e: e.activation(out=rstd[:, 0:W], in_=ps_ss[:, 0:W], func=AF.Sqrt, scale=1.0 / dim, bias=G.epsc[:, 0:1]),
         r=[ps_ss, G.epsc], w=[rstd])
    S.op("dve", lambda e: e.reciprocal(out=rstd[:, 0:W], in_=rstd[:, 0:W]), r=[rstd], w=[rstd])
    S.op("dve", lambda e: e.tensor_tensor(out=sq[:, 0:nch, 0:W], in0=xt[:, 0:nch, 0:W],
                                          in1=rstd[:, 0:W].unsqueeze(1).broadcast_to([128, nch, W]), op=ALU.mult),
         r=[xt, rstd], w=[sq])
    for c in range(nch):
        o, od = out_fn(c)
        b = bias_col(c) if bias_col is not None else 0.0
        S.op("act", lambda e, c=c, o=o, b=b: e.activation(out=o, in_=sq[:, c, 0:W], func=AF.Identity, scale=scale_col(c), bias=b),
             r=[sq, G.par, G.mod] + list(od), w=od)


def linear(G, act, KC, W, wdram, n_oc, epilogue, ps, wts, gsz=4, col0=0, a0=0):
    S = G.S
    wv = wdram.rearrange("(g p) (kc c) -> p g kc c", p=128, c=gsz * 128)
    ng = (n_oc + gsz - 1) // gsz
    st = G.lin_it
    for g in range(ng):
        wt = wts[(st + g) % len(wts)]
        n = min(gsz, n_oc - g * gsz)
        S.dma("sp", wt[:, 0:KC, 0:n * 128], wv[:, g, :, 0:n * 128], w=[wt])
        for j in range(n):
            oc = g * gsz + j
            p_ = ps[G.ps_it % len(ps)]
            G.ps_it += 1
            for kc in range(KC):
                S.op("pe", lambda e, kc=kc, j=j, p_=p_, wt=wt: e.matmul(
                    p_[:, 0:W], lhsT=wt[:, kc, j * 128:(j + 1) * 128], rhs=act[:, kc, a0:a0 + W],
                    start=(kc == 0), stop=(kc == KC - 1)), r=[wt, act], w=[p_])
            epilogue(oc, p_)
    G.lin_it += ng


def gmod_cols(G, l, gm, nname, sc_idx):
    S = G.S
    mv = G.mod[:, l * 192:(l + 1) * 192].rearrange("p (i c w) -> p i c w", i=6, c=16)
    o, _ = PCOLS[nname]
    for which in range(2):
        S.op("dve", lambda e, which=which: e.scalar_tensor_tensor(
            out=gm[:, which * 16:(which + 1) * 16], in0=mv[:, sc_idx, :, which], scalar=1.0,
            in1=G.par[:, l * PL + o:l * PL + o + 16], op0=ALU.add, op1=ALU.mult), r=[G.mod, G.par], w=[gm])


def phase_norm_proj(G, l):
    nc, S = G.nc, G.S
    with ExitStack() as st:
        sb = lambda name, shape, dt=F32: G.sb(name, shape, dt, st)
        pst = lambda name: T(st.enter_context(nc.psum_tensor(name + "_%d" % G.nuid(), [128, 512], F32)))
        xts = [sb("np_x%d" % i, [128, 16, 512]) for i in range(2)]
        sq = sb("np_sq", [128, 16, 512])
        rstd = sb("np_rstd", [128, 512])
        us = [sb("np_u%d" % i, [128, 16, 512], BF16) for i in range(2)]
        wts = [sb("np_w%d" % i, [128, 16, 512], BF16) for i in range(3)]
        stg = [sb("np_stg%d" % i, [128, 4, 512]) for i in range(2)]
        gm = sb("np_gm", [128, 32])
        ps_ss = pst("np_pss")
        ps = [pst("np_ps%d" % i) for i in range(6)]
        gmod_cols(G, l, gm, "n1", 1)
        xv = G.XT.rearrange("(c p) t -> p c t", p=128)
        pv = G.PT.rearrange("(c p) t -> p c t", p=128)
        def do_norm(ti):
            t0, W = TILES[ti]
            which = 1 if ti == 0 else 0
            xt, u = xts[ti % 2], us[ti % 2]
            S.dma("sp", xt[:, :, 0:W], xv[:, :, t0:t0 + W], w=[xt])
            rms_modulate(G, xt, W, lambda c: gm[:, which * 16 + c:which * 16 + c + 1], lambda c: mcol(G, l, 0, c, which),
                         lambda c: (u[:, c, 0:W], [u]), sq, rstd, ps_ss)
        do_norm(0)
        for ti, (t0, W) in enumerate(TILES):
            u = us[ti % 2]
            if ti + 1 < len(TILES):
                do_norm(ti + 1)
            state = {"k": 0}

            def epi(oc, p_, t0=t0, W=W):
                k = G.stg_it
                sg = stg[(k // 4) % 2]
                j = oc % 4
                eng = "act" if oc % 2 == 0 else "dve"
                if eng == "act":
                    S.op("act", lambda e: e.activation(out=sg[:, j, 0:W], in_=p_[:, 0:W], func=AF.Copy), r=[p_], w=[sg])
                else:
                    S.op("dve", lambda e: e.tensor_copy(out=sg[:, j, 0:W], in_=p_[:, 0:W]), r=[p_], w=[sg])
                G.stg_it += 1
                if j == 3 or oc == NPC - 1:
                    o0 = oc - j
                    S.dma("sp", pv[:, o0:oc + 1, t0:t0 + W], sg[:, 0:j + 1, 0:W], r=[sg])
                    G.stg_it = ((G.stg_it + 3) // 4) * 4
            linear(G, u, 16, W, G.wb_in[l], NPC, epi, ps, wts)


def phase_out_proj(G, l):
    nc, S = G.nc, G.S
    with ExitStack() as st:
        sb = lambda name, shape, dt=F32: G.sb(name, shape, dt, st)
        pst = lambda name: T(st.enter_context(nc.psum_tensor(name + "_%d" % G.nuid(), [128, 512], F32)))
        xts = [sb("op_x%d" % i, [128, 16, 512]) for i in range(2)]
        ms = [sb("op_m%d" % i, [128, 16, 512], BF16) for i in range(2)]
        sq = sb("op_sq", [128, 16, 512])
        rstd = sb("op_rstd", [128, 512])
        us = [sb("op_u%d" % i, [128, 16, 512], BF16) for i in range(1)]
        wts = [sb("op_w%d" % i, [128, 16, 512], BF16) for i in range(2)]
        gm = sb("op_gm", [128, 32])
        ps_ss = pst("op_pss")
        ps = [pst("op_ps%d" % i) for i in range(6)]
        gmod_cols(G, l, gm, "n2", 4)
        xv = G.XT.rearrange("(c p) t -> p c t", p=128)
        mv = G.MIX.rearrange("(c p) t -> p c t", p=128)
        uv = G.U2.rearrange("(c p) t -> p c t", p=128)
        for ti, (t0, W) in enumerate(TILES):
            which = 1 if ti == 0 else 0
            xt, u, m = xts[ti % 2], us[0], ms[ti % 2]
            S.dma("sp", xt[:, :, 0:W], xv[:, :, t0:t0 + W], w=[xt])
            S.dma("sp", m[:, :, 0:W], mv[:, :, t0:t0 + W], w=[m])

            def epi(oc, p_, W=W, xt=xt, which=which):
                S.op("dve", lambda e: e.scalar_tensor_tensor(out=xt[:, oc, 0:W], in0=p_[:, 0:W], scalar=mcol(G, l, 2, oc, which),
                                                             in1=xt[:, oc, 0:W], op0=ALU.mult, op1=ALU.add),
                     r=[p_, G.mod, xt], w=[xt])
            linear(G, m, 16, W, G.wb_out[l], 16, epi, ps, wts)
            S.dma("sp", xv[:, :, t0:t0 + W], xt[:, :, 0:W], r=[xt])
            rms_modulate(G, xt, W, lambda c: gm[:, which * 16 + c:which * 16 + c + 1], lambda c: mcol(G, l, 3, c, which),
                         lambda c: (u[:, c, 0:W], [u]), sq, rstd, ps_ss)
            S.dma("sp", uv[:, :, t0:t0 + W], u[:, :, 0:W], r=[u])


def phase_ffn(G, l):
    nc, S = G.nc, G.S
    with ExitStack() as st:
        sb = lambda name, shape, dt=F32: G.sb(name, shape, dt, st)
        pst = lambda name: T(st.enter_context(nc.psum_tensor(name + "_%d" % G.nuid(), [128, 512], F32)))
        uh = [sb("ff_u%d" % i, [128, 16, 514], BF16) for i in range(2)]
        gt = sb("ff_g", [128, 44, 512], BF16)
        wg = [sb("ff_wg%d" % i, [128, 16, 256], BF16) for i in range(2)]
        wu = [sb("ff_wu%d" % i, [128, 16, 256], BF16) for i in range(2)]
        wo = [sb("ff_wo%d" % i, [128, 44, 256], BF16) for i in range(2)]
        xc = [sb("ff_x%d" % i, [128, 512]) for i in range(3)]
        hh = [sb("ff_hh%d" % i, [128, 514]) for i in range(2)]
        tm = [sb("ff_tm%d" % i, [128, 512]) for i in range(2)]
        ge = [sb("ff_ge%d" % i, [128, 512]) for i in range(2)]
        psg = [pst("ff_pg%d" % i) for i in range(2)]
        psh = pst("ff_ph")
        psu = [pst("ff_pu%d" % i) for i in range(2)]
        pso = [pst("ff_po%d" % i) for i in range(3)]
        xv = G.XT.rearrange("(c p) t -> p c t", p=128)
        uv = G.U2.rearrange("(c p) t -> p c t", p=128)
        wiv = G.wb_fi[l].rearrange("(g p) (kc c) -> p g kc c", p=128, c=256)
        it = 0
        for ti, (t0, W) in enumerate(TILES):
            which = 1 if ti == 0 else 0
            u = uh[ti % 2]
            lz = any(t0 == a for a, b in SEQS)
            rz = any(t0 + W == b for a, b in SEQS)
            lo = t0 - (0 if lz else 1)
            hi = t0 + W + (0 if rz else 1)
            S.dma("sp", u[:, :, (1 if lz else 0):(1 if lz else 0) + hi - lo], uv[:, :, lo:hi], w=[u])
            if lz:
                S.op("dve", lambda e, u=u: e.memset(u[:, :, 0:1], 0.0), w=[u])
            if rz:
                S.op("dve", lambda e, u=u, W=W: e.memset(u[:, :, W + 1:W + 2], 0.0), w=[u])
            for g in range(22):
                a, b = wg[it % 2], wu[it % 2]
                S.dma("sp", a[:], wiv[:, g, :, :], w=[a])
                S.dma("sp", b[:], wiv[:, 22 + g, :, :], w=[b])
                it += 1
                for j in range(2):
                    ch = g * 2 + j
                    pg, pu = psg[ch % 2], psu[ch % 2]
                    h, t_, g_ = hh[ch % 2], tm[ch % 2], ge[ch % 2]
                    for kc in range(16):
                        S.op("pe", lambda e, kc=kc, j=j, a=a, pg=pg: e.matmul(pg[:, 0:W], lhsT=a[:, kc, j * 128:(j + 1) * 128],
                                                                         rhs=u[:, kc, 1:W + 1], start=(kc == 0), stop=(kc == 15)),
                             r=[a, u], w=[pg])
                    hs = psh[:, (ch % 8) * 2:(ch % 8) * 2 + 2]
                    for kc in range(16):
                        S.op("pe", lambda e, kc=kc, j=j, a=a, hs=hs: e.matmul(hs, lhsT=a[:, kc, j * 128:(j + 1) * 128],
                                                                         rhs=u[:, kc, 0:W + 2:W + 1], start=(kc == 0), stop=(kc == 15)),
                             r=[a, u], w=[psh])
                    for kc in range(16):
                        S.op("pe", lambda e, kc=kc, j=j, b=b, pu=pu: e.matmul(pu[:, 0:W], lhsT=b[:, kc, j * 128:(j + 1) * 128],
                                                                         rhs=u[:, kc, 1:W + 1], start=(kc == 0), stop=(kc == 15)),
                             r=[b, u], w=[pu])
                    S.op("act", lambda e, h=h, pg=pg: e.activation(out=h[:, 1:W + 1], in_=pg[:, 0:W], func=AF.Copy), r=[pg], w=[h])
                    S.op("act", lambda e, h=h, hs=hs: e.activation(out=h[:, 0:W + 2:W + 1], in_=hs, func=AF.Copy), r=[psh], w=[h])
                    S.op("act", lambda e, h=h, t_=t_, ch=ch: e.activation(out=t_[:, 0:W], in_=h[:, 1:W + 1], func=AF.Identity,
                                                                    scale=pcol(G, l, "cw1", ch), bias=pcol(G, l, "cb", ch)),
                         r=[h, G.par], w=[t_])
                    S.op("dve", lambda e, h=h, t_=t_, ch=ch: e.scalar_tensor_tensor(out=t_[:, 0:W], in0=h[:, 0:W], scalar=pcol(G, l, "cw0", ch),
                                                                             in1=t_[:, 0:W], op0=ALU.mult, op1=ALU.add),
                         r=[h, t_, G.par], w=[t_])
                    S.op("dve", lambda e, h=h, t_=t_, ch=ch: e.scalar_tensor_tensor(out=t_[:, 0:W], in0=h[:, 2:W + 2], scalar=pcol(G, l, "cw2", ch),
                                                                             in1=t_[:, 0:W], op0=ALU.mult, op1=ALU.add),
                         r=[h, t_, G.par], w=[t_])
                    S.op("act", lambda e, t_=t_, g_=g_: e.activation(out=g_[:, 0:W], in_=t_[:, 0:W], func=AF.Gelu), r=[t_], w=[g_])
                    S.op("dve", lambda e, g_=g_, pu=pu, ch=ch: e.tensor_tensor(out=gt[:, ch, 0:W], in0=g_[:, 0:W], in1=pu[:, 0:W], op=ALU.mult),
                         r=[g_, pu], w=[gt])

            def epi(oc, p_, W=W, t0=t0, which=which):
                x_ = xc[G.xc_it % 3]
                G.xc_it += 1
                S.dma("sp", x_[:, 0:W], xv[:, oc, t0:t0 + W], w=[x_])
                S.op("dve", lambda e: e.scalar_tensor_tensor(out=x_[:, 0:W], in0=p_[:, 0:W], scalar=mcol(G, l, 5, oc, which),
                                                             in1=x_[:, 0:W], op0=ALU.mult, op1=ALU.add),
                     r=[p_, G.mod, x_], w=[x_])
                S.dma("sp", xv[:, oc, t0:t0 + W], x_[:, 0:W], r=[x_])
            linear(G, gt, 44, W, G.wb_fo[l], 16, epi, pso, wo, gsz=2)


def final_norm(G):
    nc, S = G.nc, G.S
    with ExitStack() as st:
        sb = lambda name, shape, dt=F32: G.sb(name, shape, dt, st)
        xts = [sb("fn_x%d" % i, [128, 16, 512]) for i in range(2)]
        os_ = [sb("fn_o%d" % i, [128, 16, 512]) for i in range(2)]
        sq = sb("fn_sq", [128, 16, 512])
        rstd = sb("fn_rstd", [128, 512])
        ps_ss = T(st.enter_context(nc.psum_tensor("fin_pss", [128, 512], F32)))
        xv = G.XT.rearrange("(c p) t -> p c t", p=128)
        ov = G.out.rearrange("(c p) t -> p c t", p=128)
        for ti, (t0, W) in enumerate(TILES[1:]):
            xt, o = xts[ti % 2], os_[ti % 2]
            S.dma("sp", xt[:, :, 0:W], xv[:, :, t0:t0 + W], w=[xt])
            rms_modulate(G, xt, W, lambda c: G.par[:, P_FINAL + c:P_FINAL + c + 1], None,
                         lambda c: (o[:, c, 0:W], [o]), sq, rstd, ps_ss)
            S.dma("sp", ov[:, :, t0 - NCTX:t0 - NCTX + W], o[:, :, 0:W], r=[o])


def layer(G, l):
    S = G.S
    dbg = G.dbg or {}
    phase_norm_proj(G, l)
    S.barrier()
    if "PT" in dbg and l == 0:
        S.dma("sp", G.dbg_out["PT"][:, :], G.PT[:, :])
        S.barrier()
    if "mixin" in dbg:
        for c in range(16):
            S.dma("pool", G.MIX[c * 128:(c + 1) * 128, :], G.mixin[c * 128:(c + 1) * 128, :])
    else:
        phase_mixers(G, l)
    S.barrier()
    phase_out_proj(G, l)
    S.barrier()
    phase_ffn(G, l)
    S.barrier()
    if "YD" in dbg and l == 0:
        for d in range(2):
            for hh in range(4):
                S.dma("sp", G.dbg_out["YD"][d * 512 + hh * 128:d * 512 + (hh + 1) * 128, :], G.YD[d][hh * 128:(hh + 1) * 128, :])
        S.barrier()
    if "DER" in dbg and l == 0:
        for d in range(2):
            for a in range(DER_N):
                for hh in range(4):
                    S.dma("sp", G.dbg_out["DER"][(d * DER_N + a) * 512 + hh * 128:(d * DER_N + a) * 512 + (hh + 1) * 128, :], G.DER[d][a][hh * 128:(hh + 1) * 128, :])
        S.barrier()
    if "MIX" in dbg and l == 0:
        for c in range(16):
            S.dma("pool", G.dbg_out["MIX"][c * 128:(c + 1) * 128, :], G.MIX[c * 128:(c + 1) * 128, :])
        S.barrier()
    if "XT" in dbg and l == 0:
        S.dma("sp", G.dbg_out["XT"][:, :], G.XT[:, :])
        S.barrier()


_CONST = None


def const_inputs():
    global _CONST
    if _CONST is not None:
        return _CONST
    bf = ml_dtypes.bfloat16
    t = np.arange(NLAT, dtype=np.int64)
    tk = (t[:, None] * t[None, :]) % NLAT
    ang = 2.0 * np.pi * tk.astype(np.float64) / NLAT
    C = {}
    C["cosL"] = np.cos(ang).astype(np.float32).astype(bf)
    C["sinL"] = np.sin(ang).astype(np.float32).astype(bf)
    t = np.arange(NCTX, dtype=np.int64)
    ang = 2.0 * np.pi * ((t[:, None] * t[None, :]) % NCTX).astype(np.float64) / NCTX
    C["cosC"] = np.cos(ang).astype(np.float32).astype(bf)
    C["sinC"] = np.sin(ang).astype(np.float32).astype(bf)
    c = np.arange(128, dtype=np.int64)
    ang = 2.0 * np.pi * ((c[:, None] * c[None, :]) % 128).astype(np.float64) / 128
    C["cs128"] = np.concatenate([np.cos(ang), -np.sin(ang)], axis=1).astype(np.float32).astype(bf)
    pos = np.arange(NLAT)
    row = (pos // 64).astype(np.float64)
    colp = (pos % 64).astype(np.float64)
    inv = 10000.0 ** (-np.arange(32, dtype=np.float64) / 32)
    rc = np.zeros((128, NLAT), np.float64)
    rs = np.zeros((128, NLAT), np.float64)
    for d in range(128):
        axis, ab, pr = d // 64, (d % 64) // 32, d % 32
        a = (row if axis == 0 else colp) * inv[pr]
        rc[d] = np.cos(a)
        rs[d] = np.sin(a) * (-1.0 if ab == 0 else 1.0)
    C["ropeC"] = rc.astype(np.float32)
    C["ropeS"] = rs.astype(np.float32)
    j = np.arange(128)
    am = np.zeros((128, 3, 128), np.float32)
    am[:, 0, :] = (j[:, None] >= j[None, :])
    am[:, 1, :] = (j[:, None] <= j[None, :])
    am[:, 2, :] = np.eye(128)
    C["amask"] = am.astype(bf)
    j = np.arange(64)
    rwm = np.zeros((64, 5, 64), np.float32)
    rwm[:, 0, :] = (j[:, None] < j[None, :])
    rwm[:, 1, :] = (j[:, None] <= j[None, :])
    rwm[:, 2, :] = (j[None, :] < j[:, None])
    rwm[:, 3, :] = (j[None, :] <= j[:, None])
    rwm[:, 4, :] = np.eye(64)
    C["rwm"] = rwm
    rm = np.ones((128, 512), np.float32)
    rm[:, ::64] = 0.0
    C["rmask"] = rm
    _CONST = C
    return C


CONST_SHAPES = {"rwm": ([64, 5, 64], F32), "rmask": ([128, 512], F32), "cosL": ([NLAT, NLAT], BF16), "sinL": ([NLAT, NLAT], BF16), "cosC": ([NCTX, NCTX], BF16), "sinC": ([NCTX, NCTX], BF16),
                "cs128": ([128, 256], BF16), "ropeC": ([128, NLAT], F32), "ropeS": ([128, NLAT], F32), "amask": ([128, 3, 128], BF16)}


def phase_fnet(G, l):
    nc, S = G.nc, G.S
    pv = G.PT.rearrange("(c p) t -> p c t", p=128)
    mv = G.MIX.rearrange("(c p) t -> p c t", p=128)
    with ExitStack() as st:
        sb = lambda name, shape, dt=F32, s_=st: G.sb(name, shape, dt, s_)
        pst = lambda name: T(st.enter_context(nc.psum_tensor(name + "_%d" % G.nuid(), [128, 512], F32)))
        AT = sb("fn_AT", [128, 34, 4, 256], BF16)
        cs = sb("fn_cs", [128, 256], BF16)
        S.dma("sp", cs[:], G.C["cs128"][:, :], w=[cs])
        ps = [pst("fnp%d" % i) for i in range(6)]
        ps_ss = pst("fnpss")
        with ExitStack() as st1:
            zf = [sb("fn_zf%d" % i, [128, TOK], F32, st1) for i in range(2)]
            zb = sb("fn_zb", [128, 4, TOK], BF16, st1)
            for g in range(4):
                S.dma("sp", zf[g % 2][:], pv[:, C_F + g, :], w=[zf[g % 2]])
                S.op("act" if g % 2 == 0 else "dve", (lambda e, g=g: e.activation(out=zb[:, g, :], in_=zf[g % 2][:], func=AF.Copy)) if g % 2 == 0
                     else (lambda e, g=g: e.tensor_copy(out=zb[:, g, :], in_=zf[g % 2][:])), r=[zf[g % 2]], w=[zb])
            for tb in range(34):
                pa, pb = ps[(2 * tb) % 6], ps[(2 * tb + 1) % 6]
                for g in range(4):
                    p_ = pa if g < 2 else pb
                    S.op("pe", lambda e, g=g, tb=tb, p_=p_: e.matmul(p_[:, (g % 2) * 256:(g % 2) * 256 + 256], lhsT=zb[:, g, tb * 128:(tb + 1) * 128],
                                                                 rhs=cs[:, :], start=True, stop=True), r=[zb, cs], w=[p_])
                S.op("act", lambda e, tb=tb, pa=pa: e.activation(out=AT[:, tb, 0:2, :], in_=pa[:, :].rearrange("p (g c) -> p g c", g=2), func=AF.Copy),
                     r=[pa], w=[AT])
                S.op("dve", lambda e, tb=tb, pb=pb: e.tensor_copy(out=AT[:, tb, 2:4, :], in_=pb[:, :].rearrange("p (g c) -> p g c", g=2)),
                     r=[pb], w=[AT])
            S.barrier()
        tc_ = [sb("fn_tc%d" % i, [128, 16, 512], BF16) for i in range(2)]
        ts_ = [sb("fn_ts%d" % i, [128, 16, 512], BF16) for i in range(2)]
        fo = [sb("fn_fo%d" % i, [128, 4, 512]) for i in range(2)]
        sq = sb("fn_sq", [128, 4, 512])
        rstd = sb("fn_rstd", [128, 512])
        ob = [sb("fn_ob%d" % i, [128, 4, 512], BF16) for i in range(2)]
        it = 0
        for (base, L, ntb, tb0, KW, ctab, stab) in ((0, NCTX, 2, 0, 256, "cosC", "sinC"), (NCTX, NLAT, 32, 2, 512, "cosL", "sinL")):
            cv = G.C[ctab].rearrange("(tb p) k -> p tb k", p=128)
            sv = G.C[stab].rearrange("(tb p) k -> p tb k", p=128)
            scale = float(1.0 / np.sqrt(L * 128.0))
            for kt in range(L // KW):
                f_, o_ = fo[kt % 2], ob[kt % 2]
                for half in range((ntb + 15) // 16):
                    nb = min(16, ntb - half * 16)
                    a, b = tc_[it % 2], ts_[it % 2]
                    it += 1
                    S.dma("sp", a[:, 0:nb, 0:KW], cv[:, half * 16:half * 16 + nb, kt * KW:(kt + 1) * KW], w=[a])
                    S.dma("sp", b[:, 0:nb, 0:KW], sv[:, half * 16:half * 16 + nb, kt * KW:(kt + 1) * KW], w=[b])
                    for i in range(nb):
                        tb = half * 16 + i
                        for g in range(4):
                            S.op("pe", lambda e, g=g, i=i, tb=tb, a=a: e.matmul(ps[g][:, 0:KW], lhsT=AT[:, tb0 + tb, g, 0:128], rhs=a[:, i, 0:KW],
                                                                         start=(tb == 0), stop=False), r=[AT, a], w=[ps[g]])
                            S.op("pe", lambda e, g=g, i=i, tb=tb, b=b: e.matmul(ps[g][:, 0:KW], lhsT=AT[:, tb0 + tb, g, 128:256], rhs=b[:, i, 0:KW],
                                                                         start=False, stop=(tb == ntb - 1)), r=[AT, b], w=[ps[g]])
                for g in range(4):
                    S.op("act", lambda e, g=g, f_=f_: e.activation(out=f_[:, g, 0:KW], in_=ps[g][:, 0:KW], func=AF.Copy, scale=scale),
                         r=[ps[g]], w=[f_])
                rms_modulate(G, f_, KW, lambda c: pcol(G, l, "fg", c), None, lambda c: (o_[:, c, 0:KW], [o_]), sq, rstd, ps_ss, nch=4, dim=512)
                S.dma("sp", mv[:, 0:4, base + kt * KW:base + (kt + 1) * KW], o_[:, :, 0:KW], r=[o_])


def phase_attn(G, l):
    nc, S = G.nc, G.S
    pv = G.PT.rearrange("(c p) t -> p c t", p=128)
    mv = G.MIX.rearrange("(c p) t -> p c t", p=128)
    SCALE = 128.0 ** -0.5
    with ExitStack() as st:
        sb = lambda name, shape, dt=F32, s_=st: G.sb(name, shape, dt, s_)
        pst = lambda name: T(st.enter_context(nc.psum_tensor(name + "_%d" % G.nuid(), [128, 512], F32)))
        QR = sb("at_QR", [128, 8, TOK], BF16)
        KR = sb("at_KR", [128, 2, TOK], BF16)
        VT = sb("at_VT", [128, 34, 2, 128], BF16)
        am = sb("at_am", [128, 3, 128], BF16)
        esink = sb("at_es", [128, 8])
        S.dma("sp", am[:], G.C["amask"][:, :, :], w=[am])
        S.op("act", lambda e: e.activation(out=esink[:], in_=pcol(G, l, "sink", 0, 8), func=AF.Exp), r=[G.par], w=[esink])
        pT = T(st.enter_context(nc.psum_tensor("at_pT_%d" % G.nuid(), [128, 1024], BF16)))
        with ExitStack() as st1:
            rc = sb("at_rc", [128, NLAT], F32, st1)
            rs = sb("at_rs", [128, NLAT], F32, st1)
            S.dma("sp", rc[:], G.C["ropeC"][:, :], w=[rc])
            S.dma("sp", rs[:], G.C["ropeS"][:, :], w=[rs])
            qf = [sb("at_qf%d" % i, [128, 512], F32, st1) for i in range(2)]
            qs = [sb("at_qs%d" % i, [128, 512], F32, st1) for i in range(2)]
            t1 = [sb("at_t1%d" % i, [128, 512], F32, st1) for i in range(2)]
            t2 = [sb("at_t2%d" % i, [128, 512], F32, st1) for i in range(2)]
            vb = [sb("at_vb%d" % i, [128, 512], BF16, st1) for i in range(2)]
            it = 0
            for ch in range(10):
                src = C_Q + ch if ch < 8 else C_AK + (ch - 8)
                dst = (lambda a, b: QR[:, ch, a:b]) if ch < 8 else (lambda a, b: KR[:, ch - 8, a:b])
                dT = QR if ch < 8 else KR
                for ti, (t0, W) in enumerate(TILES):
                    q_, s_, a_, b_ = qf[it % 2], qs[it % 2], t1[it % 2], t2[it % 2]
                    it += 1
                    S.dma("sp", q_[:, 0:W], pv[:, src, t0:t0 + W], w=[q_])
                    if ti == 0:
                        S.op("act", lambda e, q_=q_, W=W, dst=dst, t0=t0: e.activation(out=dst(t0, t0 + W), in_=q_[:, 0:W], func=AF.Copy), r=[q_], w=[dT])
                        continue
                    for blk in range(4):
                        sp = (blk ^ 1) * 32
                        S.dma("sp", s_[blk * 32:(blk + 1) * 32, 0:W], G.PT[src * 128 + sp:src * 128 + sp + 32, t0:t0 + W], w=[s_])
                    p0 = t0 - NCTX
                    S.op("dve", lambda e, q_=q_, a_=a_, p0=p0, W=W: e.tensor_tensor(out=a_[:, 0:W], in0=q_[:, 0:W], in1=rc[:, p0:p0 + W], op=ALU.mult),
                         r=[q_, rc], w=[a_])
                    S.op("pool", lambda e, s_=s_, b_=b_, p0=p0, W=W: e.tensor_tensor(out=b_[:, 0:W], in0=s_[:, 0:W], in1=rs[:, p0:p0 + W], op=ALU.mult),
                         r=[s_, rs], w=[b_])
                    S.op("dve", lambda e, a_=a_, b_=b_, W=W, dst=dst, t0=t0: e.tensor_tensor(out=dst(t0, t0 + W), in0=a_[:, 0:W], in1=b_[:, 0:W], op=ALU.add),
                         r=[a_, b_], w=[dT])
            for g in range(2):
                for ti, (t0, W) in enumerate(TILES):
                    q_, v_ = qf[it % 2], vb[it % 2]
                    it += 1
                    S.dma("sp", q_[:, 0:W], pv[:, C_AV + g, t0:t0 + W], w=[q_])
                    S.op("act", lambda e, q_=q_, v_=v_, W=W: e.activation(out=v_[:, 0:W], in_=q_[:, 0:W], func=AF.Copy), r=[q_], w=[v_])
                    nb = W // 128
                    for i in range(nb):
                        S.op("pe", lambda e, i=i, v_=v_: e.transpose(pT[:, i * 128:(i + 1) * 128], v_[:, i * 128:(i + 1) * 128], am[:, 2, :]),
                             r=[v_, am], w=[pT])
                    b0 = t0 // 128
                    S.op("dve", lambda e, g=g, b0=b0, nb=nb: e.tensor_copy(out=VT[:, b0:b0 + nb, g, :],
                                                                     in_=pT[:, 0:nb * 128].rearrange("p (b d) -> p b d", b=nb)), r=[pT], w=[VT])
            S.barrier()
        ao = [sb("at_ao%d" % i, [128, 8, 512]) for i in range(2)]
        sq = sb("at_sq", [128, 8, 512])
        rstd = sb("at_rstd", [128, 512])
        ob = [sb("at_ob%d" % i, [128, 8, 512], BF16) for i in range(2)]
        pts = [sb("at_pt%d" % i, [128, 4, 128], BF16) for i in range(4)]
        den = [sb("at_den%d" % i, [128, 4, 128]) for i in range(2)]
        ps_s = [pst("at_ps%d" % i) for i in range(3)]
        ps_n = [pst("at_pn%d" % i) for i in range(2)]
        ps_d = [pst("at_pd%d" % i) for i in range(2)]
        it = 0
        ib = 0
        for ti, (t0, W) in enumerate(TILES):
            a_, o_ = ao[ti % 2], ob[ti % 2]
            for g in range(2):
                for n in range(W // 128):
                    q0 = t0 + n * 128
                    gb = q0 // 128
                    kbs = [(0, None), (1, None)]
                    if ti > 0:
                        if gb > 2:
                            kbs.append((gb - 1, 0))
                        kbs.append((gb, None))
                        if gb < 33:
                            kbs.append((gb + 1, 1))
                    pn, pd = ps_n[ib % 2], ps_d[ib % 2]
                    dn = den[ib % 2]
                    ib += 1
                    for ki, (kb, mk) in enumerate(kbs):
                        p_s, pt = ps_s[it % 3], pts[it % 4]
                        it += 1
                        S.op("pe", lambda e, kb=kb, p_s=p_s, q0=q0, g=g: e.matmul(p_s[:, :].rearrange("p (h q) -> p h q", h=4), lhsT=KR[:, g, kb * 128:(kb + 1) * 128],
                                                                          rhs=QR[:, 4 * g:4 * g + 4, q0:q0 + 128], start=True, stop=True), r=[KR, QR], w=[p_s])
                        S.op("act", lambda e, p_s=p_s, pt=pt: e.activation(out=pt[:], in_=p_s[:, :].rearrange("p (h q) -> p h q", h=4), func=AF.Exp, scale=SCALE),
                             r=[p_s], w=[pt])
                        if mk is not None:
                            S.op("dve", lambda e, pt=pt, mk=mk: e.tensor_tensor(out=pt[:], in0=pt[:], in1=am[:, mk:mk + 1, :].broadcast_to([128, 4, 128]), op=ALU.mult),
                                 r=[pt, am], w=[pt])
                        S.op("pe", lambda e, kb=kb, pt=pt, pn=pn, ki=ki, g=g: e.matmul(pn[:, :].rearrange("p (h q) -> p h q", h=4), lhsT=VT[:, kb, g, :], rhs=pt[:],
                                                                             start=(ki == 0), stop=(ki == len(kbs) - 1)), r=[VT, pt], w=[pn])
                        S.op("pe", lambda e, pt=pt, pd=pd, ki=ki: e.matmul(pd[:, :].rearrange("p (h q) -> p h q", h=4), lhsT=G.onesb[:], rhs=pt[:],
                                                                       start=(ki == 0), stop=(ki == len(kbs) - 1)), r=[G.onesb, pt], w=[pd])
                    S.op("dve", lambda e, pd=pd, dn=dn, g=g: e.tensor_tensor(out=dn[:], in0=pd[:, :].rearrange("p (h q) -> p h q", h=4),
                                                                       in1=esink[:, 4 * g:4 * g + 4].unsqueeze(2).broadcast_to([128, 4, 128]), op=ALU.add),
                         r=[pd, esink], w=[dn])
                    S.op("dve", lambda e, dn=dn: e.reciprocal(out=dn[:], in_=dn[:]), r=[dn], w=[dn])
                    S.op("dve", lambda e, pn=pn, dn=dn, a_=a_, g=g, n=n: e.tensor_tensor(out=a_[:, 4 * g:4 * g + 4, n * 128:(n + 1) * 128],
                                                                                 in0=pn[:, :].rearrange("p (h q) -> p h q", h=4), in1=dn[:], op=ALU.mult),
                         r=[pn, dn], w=[a_])
            rms_modulate(G, a_, W, lambda c: pcol(G, l, "ag", c), None, lambda c: (o_[:, c, 0:W], [o_]), sq, rstd, ps_s[0], nch=8, dim=1024)
            S.dma("sp", mv[:, 8:16, t0:t0 + W], o_[:, :, 0:W], r=[o_])


LD = 0.6065306597126334
NCHK = TOK // 64
SCW = 256
DER_N = 6


def phase_rwkv(G, l):
    S = G.S
    rwkv_r1(G, l)
    S.barrier()
    rwkv_r2(G, l)
    S.barrier()
    rwkv_r3(G, l)


def rwkv_r1(G, l):
    nc, S = G.nc, G.S
    pv = G.PT.rearrange("(c p) t -> p c t", p=128)
    with ExitStack() as st:
        sb = lambda name, shape, dt=F32, s_=st: G.sb(name, shape, dt, s_)
        pst = lambda name: T(st.enter_context(nc.psum_tensor(name + "_%d" % G.nuid(), [128, 512], F32)))
        w2t = sb("r1_w2", [128, 512]); a2t = sb("r1_a2", [128, 512]); g2t = sb("r1_g2", [128, 512])
        S.dma("sp", w2t[:], G.w2r[l], w=[w2t]); S.dma("sp", a2t[:], G.a2r[l], w=[a2t]); S.dma("sp", g2t[:], G.g2[l], w=[g2t])
        rmask = sb("r1_rm", [128, 512])
        S.dma("sp", rmask[:], G.C["rmask"][:, :], w=[rmask])
        bones = sb("r1_bo", [128, 128])
        S.op("dve", lambda e: e.memset(bones[:], 0.0), w=[bones])
        S.op("dve", lambda e: e.memset(bones[0:64, 0:64], 1.0), w=[bones])
        S.op("dve", lambda e: e.memset(bones[64:128, 64:128], 1.0), w=[bones])
        mmc = sb("r1_mmc", [128, 15]); omka = sb("r1_omka", [128, 4])
        S.op("dve", lambda e: e.tensor_tensor(out=mmc[:], in0=pcol(G, l, "mu0", 0, 15), in1=pcol(G, l, "mu1", 0, 15), op=ALU.add), r=[G.par], w=[mmc])
        S.op("dve", lambda e: e.tensor_scalar(out=mmc[:], in0=mmc[:], scalar1=-1.0, scalar2=1.0, op0=ALU.mult, op1=ALU.add), r=[mmc], w=[mmc])
        S.op("dve", lambda e: e.tensor_scalar(out=omka[:], in0=pcol(G, l, "ka", 0, 4), scalar1=-1.0, scalar2=1.0, op0=ALU.mult, op1=ALU.add), r=[G.par], w=[omka])
        rh = [sb("r1_rh%d" % i, [128, 514]) for i in range(3)]
        psx = sb("r1_psx", [128, 15, 512])
        tw = sb("r1_tw", [128, 512]); sgd = sb("r1_sgd", [128, 512])
        NT = 12
        tp = [sb("r1_t%d" % i, [128, 512]) for i in range(NT)]
        fixed = {n: sb("r1_f" + n, [128, 512]) for n in ("kk0", "sq", "kk", "kds", "bq")}
        ob = [sb("r1_o%d" % i, [128, 512]) for i in range(8)]
        ptt = [sb("r1_pt%d" % i, [128, 8]) for i in range(2)]
        ps = [pst("r1p%d" % i) for i in range(6)]
        cnt = {"t": 0, "o": 0, "p": 0, "rh": 0}

        def tmp():
            cnt["t"] += 1
            return tp[cnt["t"] % NT]

        def otile():
            cnt["o"] += 1
            return ob[cnt["o"] % 8]

        def psn():
            cnt["p"] += 1
            return ps[cnt["p"] % 6]

        def tt(eng, out, a, b, op, r, w):
            S.op(eng, lambda e: e.tensor_tensor(out=out, in0=a, in1=b, op=op), r=r, w=w)

        for ti, (t0, W) in enumerate(TILES):
            nck = W // 64
            lz = any(t0 == a for a, b in SEQS)
            rz = any(t0 + W == b for a, b in SEQS)
            lo = t0 - (0 if lz else 1)
            hi = t0 + W + (0 if rz else 1)
            for ci in range(15):
                cnt["rh"] += 1
                h = rh[cnt["rh"] % 3]
                S.dma("sp", h[:, (1 if lz else 0):(1 if lz else 0) + hi - lo], pv[:, C_R + ci, lo:hi], w=[h])
                if lz:
                    S.op("dve", lambda e, h=h: e.memset(h[:, 0:1], 0.0), w=[h])
                if rz:
                    S.op("dve", lambda e, h=h, W=W: e.memset(h[:, W + 1:W + 2], 0.0), w=[h])
                S.op("act", lambda e, h=h, ci=ci, W=W: e.activation(out=psx[:, ci, 0:W], in_=h[:, 1:W + 1], func=AF.Identity, scale=mmc[:, ci:ci + 1]),
                     r=[h, mmc], w=[psx])
                S.op("dve", lambda e, h=h, ci=ci, W=W: e.scalar_tensor_tensor(out=psx[:, ci, 0:W], in0=h[:, 0:W], scalar=pcol(G, l, "mu0", ci),
                                                                        in1=psx[:, ci, 0:W], op0=ALU.mult, op1=ALU.add), r=[h, psx, G.par], w=[psx])
                S.op("dve", lambda e, h=h, ci=ci, W=W: e.scalar_tensor_tensor(out=psx[:, ci, 0:W], in0=h[:, 2:W + 2], scalar=pcol(G, l, "mu1", ci),
                                                                        in1=psx[:, ci, 0:W], op0=ALU.mult, op1=ALU.add), r=[h, psx, G.par], w=[psx])
            S.op("act", lambda e, W=W: e.activation(out=tw[:, 0:W], in_=psx[:, 13, 0:W], func=AF.Tanh), r=[psx], w=[tw])
            S.op("act", lambda e, W=W: e.activation(out=sgd[:, 0:W], in_=psx[:, 4, 0:W], func=AF.Sigmoid), r=[psx], w=[sgd])
            for hc in range(4):
                r_, k_, v_ = psx[:, hc, 0:W], psx[:, 5 + hc, 0:W], psx[:, 9 + hc, 0:W]
                rows = slice(hc * 128, (hc + 1) * 128)
                p_ = psn()
                S.op("pe", lambda e, p_=p_, hc=hc, W=W: e.matmul(p_[:, 0:W], lhsT=g2t[:, hc * 128:(hc + 1) * 128], rhs=sgd[:, 0:W], start=True, stop=True),
                     r=[g2t, sgd], w=[p_])
                o = otile()
                S.op("act", lambda e, p_=p_, o=o, W=W: e.activation(out=o[:, 0:W], in_=p_[:, 0:W], func=AF.Copy), r=[p_], w=[o])
                S.dma("sp", G.GT[rows, t0:t0 + W], o[:, 0:W], r=[o])
                o = otile()
                S.op("act", lambda e, o=o, v_=v_, W=W: e.activation(out=o[:, 0:W], in_=v_, func=AF.Copy), r=[psx], w=[o])
                S.dma("sp", G.VS[rows, t0:t0 + W], o[:, 0:W], r=[o])
                kk0, sq, kk = fixed["kk0"], fixed["sq"], fixed["kk"]
                S.op("act", lambda e, kk0=kk0, k_=k_, hc=hc, W=W: e.activation(out=kk0[:, 0:W], in_=k_, func=AF.Identity, scale=pcol(G, l, "kks", hc)),
                     r=[psx, G.par], w=[kk0])
                S.op("act", lambda e, kk0=kk0, sq=sq, W=W: e.activation(out=sq[:, 0:W], in_=kk0[:, 0:W], func=AF.Square), r=[kk0], w=[sq])
                p_ = psn()
                S.op("pe", lambda e, p_=p_, sq=sq, W=W: e.matmul(p_[:, 0:W], lhsT=bones[:], rhs=sq[:, 0:W], start=True, stop=True), r=[bones, sq], w=[p_])
                S.op("act", lambda e, p_=p_, sq=sq, W=W: e.activation(out=sq[:, 0:W], in_=p_[:, 0:W], func=AF.Sqrt), r=[p_], w=[sq])
                S.op("dve", lambda e, sq=sq, W=W: e.tensor_scalar(out=sq[:, 0:W], in0=sq[:, 0:W], scalar1=1e-12, scalar2=None, op0=ALU.max), r=[sq], w=[sq])
                S.op("dve", lambda e, sq=sq, W=W: e.reciprocal(out=sq[:, 0:W], in_=sq[:, 0:W]), r=[sq], w=[sq])
                tt("dve", kk[:, 0:W], kk0[:, 0:W], sq[:, 0:W], ALU.mult, [kk0, sq], [kk])
                kds = fixed["kds"]
                for dr in range(2):
                    cnt["t"] = 0
                    prt = slice(dr * 64, (dr + 1) * 64)
                    pw, pa = psn(), psn()
                    S.op("pe", lambda e, pw=pw, dr=dr, hc=hc, W=W, prt=prt: e.matmul(pw[:, 0:W], lhsT=w2t[prt, hc * 128:(hc + 1) * 128], rhs=tw[prt, 0:W], start=True, stop=True),
                         r=[w2t, tw], w=[pw])
                    S.op("pe", lambda e, pa=pa, dr=dr, hc=hc, W=W, prt=prt: e.matmul(pa[:, 0:W], lhsT=a2t[prt, hc * 128:(hc + 1) * 128], rhs=psx[prt, 14, 0:W], start=True, stop=True),
                         r=[a2t, psx], w=[pa])
                    sg = tmp(); a_ = tmp()
                    S.op("act", lambda e, pw=pw, sg=sg, W=W, dr=dr, hc=hc: e.activation(out=sg[:, 0:W], in_=pw[:, 0:W], func=AF.Sigmoid, bias=pcol(G, l, "w0", dr * 4 + hc)),
                         r=[pw, G.par], w=[sg])
                    S.op("act", lambda e, pa=pa, a_=a_, W=W, dr=dr, hc=hc: e.activation(out=a_[:, 0:W], in_=pa[:, 0:W], func=AF.Sigmoid, bias=pcol(G, l, "a0", dr * 4 + hc)),
                         r=[pa, G.par], w=[a_])
                    kd = tmp(); nb = tmp()
                    S.op("act", lambda e, a_=a_, kd=kd, W=W, hc=hc: e.activation(out=kd[:, 0:W], in_=a_[:, 0:W], func=AF.Identity, scale=pcol(G, l, "ka", hc), bias=omka[:, hc:hc + 1]),
                         r=[a_, G.par, omka], w=[kd])
                    tt("dve", kd[:, 0:W], kd[:, 0:W], k_, ALU.mult, [kd, psx], [kd])
                    if dr == 0:
                        S.op("dve", lambda e, kds=kds, kd=kd, W=W: e.tensor_copy(out=kds[:, 0:W], in_=kd[:, 0:W]), r=[kd], w=[kds])
                    else:
                        tt("dve", kds[:, 0:W], kds[:, 0:W], kd[:, 0:W], ALU.add, [kds, kd], [kds])
                    S.op("dve", lambda e, a_=a_, nb=nb, kk=kk, W=W: e.scalar_tensor_tensor(out=nb[:, 0:W], in0=a_[:, 0:W], scalar=-1.0, in1=kk[:, 0:W],
                                                                                 op0=ALU.mult, op1=ALU.mult), r=[a_, kk], w=[nb])
                    c_ = tmp()
                    S.op("dve", lambda e, c_=c_, sg=sg, W=W: e.tensor_tensor_scan(out=c_[:, 0:W], data0=rmask[:, 0:W], data1=sg[:, 0:W], initial=0.0,
                                                                            op0=ALU.mult, op1=ALU.add), r=[rmask, sg], w=[c_])
                    c3 = c_[:, 0:W].rearrange("p (c t) -> p c t", t=64)
                    if dr == 1:
                        d_ = tmp()
                        tt("dve", d_[:, 0:W], sg[:, 0:W], c_[:, 0:W], ALU.subtract, [sg, c_], [d_])
                        tt("dve", d_[:, 0:W].rearrange("p (c t) -> p c t", t=64), d_[:, 0:W].rearrange("p (c t) -> p c t", t=64),
                           c3[:, :, 63:64].broadcast_to([128, nck, 64]), ALU.add, [d_, c_], [d_])
                        c_ = d_
                        c3 = c_[:, 0:W].rearrange("p (c t) -> p c t", t=64)
                        endi = 0
                    else:
                        endi = 63
                    e1 = tmp(); e2 = tmp(); e3 = tmp(); e4 = tmp()
                    S.op("act", lambda e, c_=c_, e1=e1, W=W: e.activation(out=e1[:, 0:W], in_=c_[:, 0:W], func=AF.Exp, scale=-LD), r=[c_], w=[e1])
                    S.op("act", lambda e, c_=c_, e2=e2, W=W: e.activation(out=e2[:, 0:W], in_=c_[:, 0:W], func=AF.Exp, scale=LD), r=[c_], w=[e2])
                    tt("dve", e3[:, 0:W], c_[:, 0:W], sg[:, 0:W], ALU.subtract, [c_, sg], [e3])
                    S.op("act", lambda e, e3=e3, W=W: e.activation(out=e3[:, 0:W], in_=e3[:, 0:W], func=AF.Exp, scale=-LD), r=[e3], w=[e3])
                    tt("dve", e4[:, 0:W].rearrange("p (c t) -> p c t", t=64), c3[:, :, endi:endi + 1].broadcast_to([128, nck, 64]), c3, ALU.subtract, [c_], [e4])
                    S.op("act", lambda e, e4=e4, W=W: e.activation(out=e4[:, 0:W], in_=e4[:, 0:W], func=AF.Exp, scale=-LD), r=[e4], w=[e4])
                    pt_ = ptt[(hc * 2 + dr) % 2]
                    S.op("dve", lambda e, pt_=pt_, e1=e1, W=W, endi=endi, nck=nck: e.tensor_copy(
                        out=pt_[:, 0:nck].unsqueeze(2), in_=e1[:, 0:W].rearrange("p (c t) -> p c t", t=64)[:, :, endi:endi + 1]), r=[e1], w=[pt_])
                    S.dma("sp", G.PTOT[dr][rows, t0 // 64:t0 // 64 + nck], pt_[:, 0:nck], r=[pt_])
                    for ai, (x_, y_, xd, yd) in enumerate(((kk, e3, kk, e3), (None, e1, psx, e1), (kd, e2, kd, e2), (nb, e2, nb, e2), (kd, e4, kd, e4), (nb, e4, nb, e4))):
                        o = otile()
                        xin_ = r_ if x_ is None else x_[:, 0:W]
                        tt("dve", o[:, 0:W], xin_, y_[:, 0:W], ALU.mult, [xd, yd], [o])
                        S.dma("sp", G.DER[dr][ai][rows, t0:t0 + W], o[:, 0:W], r=[o])
                bq = fixed["bq"]
                S.op("dve", lambda e, bq=bq, r_=r_, kds=kds, hc=hc, W=W: e.scalar_tensor_tensor(out=bq[:, 0:W], in0=r_, scalar=pcol(G, l, "rk", hc), in1=kds[:, 0:W],
                                                                                      op0=ALU.mult, op1=ALU.mult), r=[psx, kds, G.par], w=[bq])
                p_ = psn()
                S.op("pe", lambda e, p_=p_, bq=bq, W=W: e.matmul(p_[:, 0:W], lhsT=bones[:], rhs=bq[:, 0:W], start=True, stop=True), r=[bones, bq], w=[p_])
                o = otile()
                tt("dve", o[:, 0:W], p_[:, 0:W], v_, ALU.mult, [p_, psx], [o])
                S.dma("sp", G.BON[rows, t0:t0 + W], o[:, 0:W], r=[o])


class PV:
    def __init__(self, ap, d):
        self.ap = ap
        self.d = d


def rwkv_r2(G, l, HG=4):
    nc, S = G.nc, G.S
    with ExitStack() as st:
        sb = lambda name, shape, dt=F32, s_=st: G.sb(name, shape, dt, s_)
        banks = [st.enter_context(nc.psum_tensor("r2b%d_%d" % (i, G.nuid()), [128, 512], F32)) for i in range(8)]
        bdeps = [Dep(x=True) for i in range(8)]
        rwm = sb("r2_rwm", [64, 5, 64])
        S.dma("sp", rwm[:], G.C["rwm"][:, :, :], w=[rwm])
        ident = rwm[:, 4, :]
        m12 = [rwm[:, 0:2, :], rwm[:, 2:4, :]]
        mX = [rwm[:, 2, :], rwm[:, 0, :]]
        order = [list(range(NCHK)), [3, 2, 1, 0] + list(range(NCHK - 1, 3, -1))]
        nch = 2 * HG

        class Chain:
            pass
        chains = []
        for i in range(nch):
            c = Chain()
            c.inb = [sb("r2_in%d_%d" % (i, j), [64, 7, SCW]) for j in range(2)]
            c.tm = sb("r2_tm%d" % i, [64, 3, 64])
            c.AM1 = sb("r2_am1_%d" % i, [64, 2, 64]); c.AM2 = sb("r2_am2_%d" % i, [64, 2, 64])
            c.X = [sb("r2_x%d_%d" % (i, j), [64, 64]) for j in range(2)]
            c.Wk = [sb("r2_w%d_%d" % (i, j), [64, 2, 64]) for j in range(2)]
            c.Ti = sb("r2_ti%d" % i, [64, 64]); c.Z = sb("r2_z%d" % i, [64, 64]); c.U = sb("r2_u%d" % i, [64, 64])
            c.H = sb("r2_h%d" % i, [64, 64])
            c.ys = [sb("r2_ys%d_%d" % (i, j), [64, SCW]) for j in range(2)]
            c.pt = sb("r2_ptot%d" % i, [64, NCHK])
            c.regs = [PV(banks[i][0:64, k * 128:(k + 1) * 128], bdeps[i]) for k in range(4)]
            c.ri = 0
            chains.append(c)

        def prc(c):
            c.ri += 1
            return c.regs[c.ri % 4]

        def cp(eng, out, in_, r, w):
            if eng == "act":
                S.op("act", lambda e: e.activation(out=out, in_=in_, func=AF.Copy), r=r, w=w)
            else:
                S.op("dve", lambda e: e.tensor_copy(out=out, in_=in_), r=r, w=w)

        def mm(out_pv, out_ap, lhsT, rhs, start, stop, r):
            S.op("pe", lambda e: e.matmul(out_ap, lhsT=lhsT, rhs=rhs, start=start, stop=stop), r=r, w=[out_pv])

        for hg in range(8 // HG):
            for i, c in enumerate(chains):
                c.dr = i // HG
                c.head = hg * HG + i % HG
                c.rows = slice(c.head * 64, (c.head + 1) * 64)
                S.op("dve", lambda e, c=c: e.memset(c.H[:], 0.0), w=[c.H])
                S.dma("sp", c.pt[:], G.PTOT[c.dr][c.rows, :], w=[c.pt])
                c.nsc = 0
            for step in range(NCHK):
                for c in chains:
                    c.ck = order[c.dr][step]
                    c.off = (c.ck % 4) * 64
                    if step % 4 == 0:
                        c.nsc += 1
                        c.cur = c.inb[c.nsc % 2]
                        c.ysc = c.ys[c.nsc % 2]
                        sc = c.ck // 4
                        for ai in range(6):
                            S.dma("sp", c.cur[:, ai, :], G.DER[c.dr][ai][c.rows, sc * SCW:(sc + 1) * SCW], w=[c.cur])
                        S.dma("sp", c.cur[:, 6, :], G.VS[c.rows, sc * SCW:(sc + 1) * SCW], w=[c.cur])
                    c.cols = slice(c.off, c.off + 64)
                for c in chains:
                    c.p = prc(c)
                    for j, ai in enumerate((4, 5)):
                        S.op("pe", lambda e, c=c, j=j, ai=ai: e.transpose(c.p.ap[:, j * 64:(j + 1) * 64], c.cur[:, ai, c.cols], ident), r=[c.cur, rwm], w=[c.p])
                for c in chains:
                    pass
                for c in chains:
                    cp("act", c.tm[:, 0:2, :], c.p.ap[:, 0:128].rearrange("p (a t) -> p a t", a=2), [c.p], [c.tm])
                for c in chains:
                    c.p = prc(c)
                    S.op("pe", lambda e, c=c: e.transpose(c.p.ap[:, 0:64], c.cur[:, 6, c.cols], ident), r=[c.cur, rwm], w=[c.p])
                for c in chains:
                    cp("dve", c.tm[:, 2, :], c.p.ap[:, 0:64], [c.p], [c.tm])
                for c in chains:
                    c.p1, c.p2, c.p3 = prc(c), prc(c), prc(c)
                    kr = c.cur[:, 0:2, c.cols]
                    mm(c.p1, c.p1.ap[:, :].rearrange("p (a t) -> p a t", a=2), c.cur[:, 2, c.cols], kr, True, True, [c.cur])
                    mm(c.p2, c.p2.ap[:, :].rearrange("p (a t) -> p a t", a=2), c.cur[:, 3, c.cols], kr, True, True, [c.cur])
                    mm(c.p3, c.p3.ap[:, 0:64], c.cur[:, 0, c.cols], c.cur[:, 3, c.cols], True, True, [c.cur])
                for c in chains:
                    c.xi = 0
                    S.op("dve", lambda e, c=c: e.tensor_tensor(out=c.AM1[:], in0=c.p1.ap[:, :].rearrange("p (a t) -> p a t", a=2), in1=m12[c.dr], op=ALU.mult), r=[c.p1, rwm], w=[c.AM1])
                    S.op("dve", lambda e, c=c: e.tensor_tensor(out=c.AM2[:], in0=c.p2.ap[:, :].rearrange("p (a t) -> p a t", a=2), in1=m12[c.dr], op=ALU.mult), r=[c.p2, rwm], w=[c.AM2])
                    S.op("dve", lambda e, c=c: e.tensor_tensor(out=c.X[0][:], in0=c.p3.ap[:, 0:64], in1=mX[c.dr], op=ALU.mult), r=[c.p3, rwm], w=[c.X[0]])
                for c in chains:
                    c.p1, c.p2 = prc(c), prc(c)
                    mm(c.p1, c.p1.ap[:, 0:64], c.X[0][:], c.AM2[:, 0, :], True, True, [c.X[0], c.AM2])
                    mm(c.p2, c.p2.ap[:, 0:64], c.AM2[:, 0, :], c.X[0][:], True, True, [c.X[0], c.AM2])
                for c in chains:
                    cp("act", c.Wk[1][:, 0, :], c.p1.ap[:, 0:64], [c.p1], [c.Wk[1]])
                    S.op("dve", lambda e, c=c: e.tensor_tensor(out=c.Wk[1][:, 1, :], in0=c.AM2[:, 0, :], in1=ident, op=ALU.add), r=[c.AM2, rwm], w=[c.Wk[1]])
                    cp("act", c.X[1][:], c.p2.ap[:, 0:64], [c.p2], [c.X[1]])
                for k in range(1, 5):
                    a, b = k % 2, (k + 1) % 2
                    for c in chains:
                        c.p1, c.p2 = prc(c), prc(c)
                        mm(c.p1, c.p1.ap[:, :].rearrange("p (a t) -> p a t", a=2), c.X[a][:], c.Wk[a][:], True, True, [c.X[a], c.Wk[a]])
                        mm(c.p2, c.p2.ap[:, 0:64], c.Wk[a][:, 0, :], c.X[a][:], True, True, [c.X[a], c.Wk[a]])
                    for c in chains:
                        cp("act", c.Wk[b][:, 0, :], c.p1.ap[:, 0:64], [c.p1], [c.Wk[b]])
                        S.op("dve", lambda e, c=c, a=a, b=b: e.tensor_tensor(out=c.Wk[b][:, 1, :], in0=c.p1.ap[:, 64:128], in1=c.Wk[a][:, 1, :], op=ALU.add),
                             r=[c.p1, c.Wk[a]], w=[c.Wk[b]])
                        cp("dve", c.X[b][:], c.p2.ap[:, 0:64], [c.p2], [c.X[b]])
                for c in chains:
                    c.p1 = prc(c)
                    mm(c.p1, c.p1.ap[:, 0:64], c.X[1][:], c.Wk[1][:, 1, :], True, True, [c.X[1], c.Wk[1]])
                for c in chains:
                    S.op("dve", lambda e, c=c: e.tensor_tensor(out=c.Ti[:], in0=c.p1.ap[:, 0:64], in1=c.Wk[1][:, 1, :], op=ALU.add), r=[c.p1, c.Wk[1]], w=[c.Ti])
                for c in chains:
                    c.p1 = prc(c)
                    mm(c.p1, c.p1.ap[:, 0:64], c.cur[:, 0, c.cols], c.H[:], True, False, [c.cur, c.H])
                    mm(c.p1, c.p1.ap[:, 0:64], c.AM1[:, 0, :], c.tm[:, 2, :], False, True, [c.AM1, c.tm])
                for c in chains:
                    cp("act", c.Z[:], c.p1.ap[:, 0:64], [c.p1], [c.Z])
                for c in chains:
                    c.p1 = prc(c)
                    mm(c.p1, c.p1.ap[:, 0:64], c.Ti[:], c.Z[:], True, True, [c.Ti, c.Z])
                for c in chains:
                    cp("dve", c.U[:], c.p1.ap[:, 0:64], [c.p1], [c.U])
                for c in chains:
                    c.p1, c.p2 = prc(c), prc(c)
                    mm(c.p1, c.p1.ap[:, 0:64], c.H[:], c.cur[:, 1, c.cols], True, False, [c.H, c.cur])
                    mm(c.p1, c.p1.ap[:, 0:64], c.tm[:, 2, :], c.AM1[:, 1, :], False, False, [c.tm, c.AM1])
                    mm(c.p1, c.p1.ap[:, 0:64], c.U[:], c.AM2[:, 1, :], False, True, [c.U, c.AM2])
                    mm(c.p2, c.p2.ap[:, 0:64], c.tm[:, 0, :], c.tm[:, 2, :], True, False, [c.tm])
                    mm(c.p2, c.p2.ap[:, 0:64], c.tm[:, 1, :], c.U[:], False, True, [c.tm, c.U])
                for c in chains:
                    cp("act", c.ysc[:, c.cols], c.p1.ap[:, 0:64], [c.p1], [c.ysc])
                    S.op("dve", lambda e, c=c: e.scalar_tensor_tensor(out=c.H[:], in0=c.H[:], scalar=c.pt[:, c.ck:c.ck + 1], in1=c.p2.ap[:, 0:64],
                                                                  op0=ALU.mult, op1=ALU.add), r=[c.H, c.pt, c.p2], w=[c.H])
                if step % 4 == 3:
                    for c in chains:
                        sc = c.ck // 4
                        S.dma("sp", G.YD[c.dr][c.rows, sc * SCW:(sc + 1) * SCW], c.ysc[:], r=[c.ysc])


def rwkv_r3(G, l):
    nc, S = G.nc, G.S
    mv = G.MIX.rearrange("(c p) t -> p c t", p=128)
    with ExitStack() as st:
        sb = lambda name, shape, dt=F32, s_=st: G.sb(name, shape, dt, s_)
        pst = lambda name: T(st.enter_context(nc.psum_tensor(name + "_%d" % G.nuid(), [128, 512], F32)))
        bo = sb("r3_bo", [128, 128])
        S.op("dve", lambda e: e.memset(bo[:], 0.0), w=[bo])
        S.op("dve", lambda e: e.memset(bo[0:64, 0:64], 1.0 / 64), w=[bo])
        S.op("dve", lambda e: e.memset(bo[64:128, 64:128], 1.0 / 64), w=[bo])
        lnx = sb("r3_eps", [128, 1])
        S.op("dve", lambda e: e.memset(lnx[:], 64e-5), w=[lnx])
        ya = [sb("r3_ya%d" % i, [128, 512]) for i in range(2)]
        yb = [sb("r3_yb%d" % i, [128, 512]) for i in range(2)]
        bn = [sb("r3_bn%d" % i, [128, 512]) for i in range(2)]
        gg = [sb("r3_gg%d" % i, [128, 512]) for i in range(2)]
        t1 = [sb("r3_t1%d" % i, [128, 512]) for i in range(2)]
        t2 = [sb("r3_t2%d" % i, [128, 512]) for i in range(2)]
        ob = [sb("r3_ob%d" % i, [128, 512], BF16) for i in range(2)]
        ps = [pst("r3p%d" % i) for i in range(4)]
        it = 0
        for ti, (t0, W) in enumerate(TILES):
            for hc in range(4):
                rows = slice(hc * 128, (hc + 1) * 128)
                a, b, n_, g_, x1, x2, o = ya[it % 2], yb[it % 2], bn[it % 2], gg[it % 2], t1[it % 2], t2[it % 2], ob[it % 2]
                pm, pvv = ps[(2 * it) % 4], ps[(2 * it + 1) % 4]
                it += 1
                S.dma("sp", a[:, 0:W], G.YD[0][rows, t0:t0 + W], w=[a])
                S.dma("sp", b[:, 0:W], G.YD[1][rows, t0:t0 + W], w=[b])
                S.dma("sp", n_[:, 0:W], G.BON[rows, t0:t0 + W], w=[n_])
                S.dma("sp", g_[:, 0:W], G.GT[rows, t0:t0 + W], w=[g_])
                S.op("dve", lambda e, a=a, b=b, W=W: e.tensor_tensor(out=a[:, 0:W], in0=a[:, 0:W], in1=b[:, 0:W], op=ALU.add), r=[a, b], w=[a])
                S.op("pe", lambda e, a=a, pm=pm, W=W: e.matmul(pm[:, 0:W], lhsT=bo[:], rhs=a[:, 0:W], start=True, stop=True), r=[bo, a], w=[pm])
                S.op("dve", lambda e, a=a, pm=pm, x1=x1, W=W: e.tensor_tensor(out=x1[:, 0:W], in0=a[:, 0:W], in1=pm[:, 0:W], op=ALU.subtract), r=[a, pm], w=[x1])
                S.op("act", lambda e, x1=x1, x2=x2, W=W: e.activation(out=x2[:, 0:W], in_=x1[:, 0:W], func=AF.Square), r=[x1], w=[x2])
                S.op("pe", lambda e, x2=x2, pvv=pvv, W=W: e.matmul(pvv[:, 0:W], lhsT=bo[:], rhs=x2[:, 0:W], start=True, stop=True), r=[bo, x2], w=[pvv])
                S.op("act", lambda e, x2=x2, pvv=pvv, W=W: e.activation(out=x2[:, 0:W], in_=pvv[:, 0:W], func=AF.Sqrt, bias=lnx[:, 0:1]), r=[pvv, lnx], w=[x2])
                S.op("dve", lambda e, x2=x2, W=W: e.reciprocal(out=x2[:, 0:W], in_=x2[:, 0:W]), r=[x2], w=[x2])
                S.op("dve", lambda e, x1=x1, x2=x2, W=W: e.tensor_tensor(out=x1[:, 0:W], in0=x1[:, 0:W], in1=x2[:, 0:W], op=ALU.mult), r=[x1, x2], w=[x1])
                S.op("act", lambda e, x1=x1, W=W, hc=hc: e.activation(out=x1[:, 0:W], in_=x1[:, 0:W], func=AF.Identity, scale=pcol(G, l, "lnw", hc), bias=pcol(G, l, "lnb", hc)),
                     r=[x1, G.par], w=[x1])
                S.op("dve", lambda e, x1=x1, n_=n_, W=W: e.tensor_tensor(out=x1[:, 0:W], in0=x1[:, 0:W], in1=n_[:, 0:W], op=ALU.add), r=[x1, n_], w=[x1])
                S.op("dve", lambda e, x1=x1, g_=g_, o=o, W=W: e.tensor_tensor(out=o[:, 0:W], in0=x1[:, 0:W], in1=g_[:, 0:W], op=ALU.mult), r=[x1, g_], w=[o])
                S.dma("sp", mv[:, 4 + hc, t0:t0 + W], o[:, 0:W], r=[o])

def phase_mixers(G, l):
    S = G.S
    phase_fnet(G, l)
    S.barrier()
    phase_attn(G, l)
    S.barrier()
    phase_rwkv(G, l)
    S.barrier()


EXTRA_W = ("w2r", "a2r", "g2")


def extra_w(inp, k):
    if k == "w2r":
        return np.ascontiguousarray(np.asarray(inp["rwkv_w2"], np.float32).reshape(DEPTH, 128, 512))
    if k == "a2r":
        return np.ascontiguousarray(np.asarray(inp["rwkv_a2"], np.float32).reshape(DEPTH, 128, 512))
    return np.ascontiguousarray(np.asarray(inp["rwkv_g2"], np.float32))


def make_inputs(inp, b):
    x = np.asarray(inp["x"][b], np.float32)
    cx = np.asarray(inp["ctx"][b], np.float32)
    xin = np.ascontiguousarray(np.concatenate([cx, x], axis=0).T)
    cvec = np.concatenate([_col(inp["c"][b]), _col(inp["c_ctx"])], axis=1)
    return {"xin": xin, "cvec": np.ascontiguousarray(cvec)}


def kernel(**inp):
    nc, G = build_nc()
    shared = {"params": pack_params(inp)}
    shared.update(const_inputs())
    for k in ("w_ada", "w_in", "w_out", "w_ffn_in", "w_ffn_out"):
        shared[k] = np.ascontiguousarray(np.asarray(inp[k], np.float32))
    for k in EXTRA_W:
        shared[k] = extra_w(inp, k)
    in_maps = []
    for b in range(8):
        m = dict(shared)
        m.update(make_inputs(inp, b))
        in_maps.append(m)
    res = run_bass_kernel_spmd(nc, in_maps, core_ids=list(range(8)))
    out = np.stack([np.ascontiguousarray(r["out"].T) for r in res.results], axis=0)
    return out.astype(np.float32)
```
